# Optimizing a Trainium2 kernel written in Bass

```python
import math
import jax, jax.numpy as jnp
from jax import lax
import numpy as np

D_MODEL = 1024
BATCH = 16
SEQ = 2048
DEPTH = 1

N_MEM = 256
MEM_HEADS = 4
MEM_HEAD_DIM = D_MODEL // MEM_HEADS
D_MIX = D_MODEL
D_LRU = D_MIX // 2
LRU_BLOCKS = 8
LRU_BLOCK_DIM = D_LRU // LRU_BLOCKS
CONV_WIDTH = 4
LRU_C = 8.0
MLA_HEADS = 4
QK_NOPE_DIM = 128
QK_ROPE_DIM = 64
V_HEAD_DIM = 128
Q_LORA_RANK = 384
KV_LORA_RANK = 256
D_MLA_OUT = MLA_HEADS * V_HEAD_DIM
ROPE_THETA = 10000.0
Q_BLOCK = 128
OFF_LRU_X = 0
OFF_LRU_GATE = OFF_LRU_X + D_LRU
OFF_CQ = OFF_LRU_GATE + D_LRU
OFF_CKV = OFF_CQ + Q_LORA_RANK
OFF_KPE = OFF_CKV + KV_LORA_RANK
D_IN = OFF_KPE + QK_ROPE_DIM
D_FF = -(-8 * D_MODEL // (3 * 256)) * 256
RMS_EPS = 1e-6
NEG_INF = -1e30

kernel_name = "hymba_rglru_mla_memory_layer"


def rmsnorm(x, g):
    xf = x.astype(jnp.float32)
    y = xf * lax.rsqrt(jnp.mean(xf * xf, axis=-1, keepdims=True) + RMS_EPS)
    return (y * g.astype(jnp.float32)).astype(x.dtype)


def causal_depthwise_conv(x, w, b):
    s = x.shape[1]
    xp = jnp.pad(x, ((0, 0), (CONV_WIDTH - 1, 0), (0, 0)))
    y = b
    for k in range(CONV_WIDTH):
        y = y + w[k] * xp[:, k:k + s]
    return y


def rg_lru(x, w_a, b_a, w_x, b_x, lam):
    bsz, s, _ = x.shape
    xb = x.reshape(bsz, s, LRU_BLOCKS, LRU_BLOCK_DIM)
    r = jax.nn.sigmoid(jnp.einsum('bshi,hij->bshj', xb, w_a).reshape(bsz, s, D_LRU) + b_a)
    i = jax.nn.sigmoid(jnp.einsum('bshi,hij->bshj', xb, w_x).reshape(bsz, s, D_LRU) + b_x)
    log_a = -LRU_C * r.astype(jnp.float32) * jax.nn.softplus(-lam.astype(jnp.float32))
    a = jnp.exp(log_a)
    u = jnp.sqrt(-jnp.expm1(2.0 * log_a)) * (i * x).astype(jnp.float32)

    def combine(c1, c2):
        a1, b1 = c1
        a2, b2 = c2
        return a1 * a2, a2 * b1 + b2

    _, h = lax.associative_scan(combine, (a, u), axis=1)
    return h.astype(x.dtype)


def apply_rope(x, cos, sin):
    half = x.shape[-1] // 2
    x1, x2 = x[..., :half], x[..., half:]
    return jnp.concatenate([x1 * cos - x2 * sin, x2 * cos + x1 * sin], axis=-1)


def causal_block_attention(q, k, v, scale):
    s = q.shape[1]
    outs = []
    for blk in range(s // Q_BLOCK):
        q0 = blk * Q_BLOCK
        kv_len = q0 + Q_BLOCK
        qb = q[:, q0:kv_len]
        kb = k[:, :kv_len]
        vb = v[:, :kv_len]
        sc = jnp.einsum('bqhd,bkhd->bhqk', qb, kb).astype(jnp.float32) * scale
        mask = (q0 + jnp.arange(Q_BLOCK))[:, None] >= jnp.arange(kv_len)[None, :]
        sc = jnp.where(mask[None, None], sc, NEG_INF)
        p = jax.nn.softmax(sc, axis=-1).astype(vb.dtype)
        outs.append(jnp.einsum('bhqk,bkhd->bqhd', p, vb))
    return jnp.concatenate(outs, axis=1)


def setup_inputs(seed: int = 0) -> dict:
    key = jax.random.key(seed)
    ks = iter(jax.random.split(key, 48))

    def w(shape, fan_in):
        return jax.random.normal(next(ks), shape, jnp.float32) * fan_in ** -0.5

    def gain(n):
        return 1.0 + 0.05 * jax.random.normal(next(ks), (DEPTH, n), jnp.float32)

    def bias(n, s=0.02):
        return s * jax.random.normal(next(ks), (DEPTH, n), jnp.float32)

    x = jax.random.normal(next(ks), (BATCH, SEQ, D_MODEL), jnp.float32)
    mem = jax.random.normal(next(ks), (BATCH, N_MEM, D_MODEL), jnp.float32)
    start = jax.random.randint(next(ks), (BATCH, 1), 0, 4096, dtype=jnp.int32)
    positions = (start + jnp.arange(SEQ, dtype=jnp.int32)[None, :]).astype(jnp.int32)
    a0 = jax.random.uniform(next(ks), (DEPTH, D_LRU), jnp.float32, minval=0.9, maxval=0.999)
    lru_lambda = jnp.log(a0) - jnp.log1p(-a0)

    return {
        "x": x,
        "mem": mem,
        "positions": positions,
        "g_pre_mix": gain(D_MODEL),
        "w_in": w((DEPTH, D_MODEL, D_IN), D_MODEL),
        "conv_w": w((DEPTH, CONV_WIDTH, D_LRU), CONV_WIDTH),
        "conv_b": bias(D_LRU),
        "lru_wa": w((DEPTH, LRU_BLOCKS, LRU_BLOCK_DIM, LRU_BLOCK_DIM), LRU_BLOCK_DIM),
        "lru_ba": bias(D_LRU, 0.1),
        "lru_wx": w((DEPTH, LRU_BLOCKS, LRU_BLOCK_DIM, LRU_BLOCK_DIM), LRU_BLOCK_DIM),
        "lru_bx": bias(D_LRU, 0.1),
        "lru_lambda": lru_lambda,
        "g_q_lat": gain(Q_LORA_RANK),
        "w_uq": w((DEPTH, Q_LORA_RANK, MLA_HEADS * (QK_NOPE_DIM + QK_ROPE_DIM)), Q_LORA_RANK),
        "g_kv_lat": gain(KV_LORA_RANK),
        "w_ukv": w((DEPTH, KV_LORA_RANK, MLA_HEADS * (QK_NOPE_DIM + V_HEAD_DIM)), KV_LORA_RANK),
        "g_lru_out": gain(D_LRU),
        "g_mla_out": gain(D_MLA_OUT),
        "w_out": w((DEPTH, D_MIX, D_MODEL), D_MIX),
        "g_post_mix": gain(D_MODEL),
        "g_pre_mem": gain(D_MODEL),
        "g_mem_kv": gain(D_MODEL),
        "w_mq": w((DEPTH, D_MODEL, D_MODEL), D_MODEL),
        "w_mk": w((DEPTH, D_MODEL, D_MODEL), D_MODEL),
        "w_mv": w((DEPTH, D_MODEL, D_MODEL), D_MODEL),
        "w_mo": w((DEPTH, D_MODEL, D_MODEL), D_MODEL),
        "g_post_mem": gain(D_MODEL),
        "g_pre_ffn": gain(D_MODEL),
        "w_gate": w((DEPTH, D_MODEL, D_FF), D_MODEL),
        "w_up": w((DEPTH, D_MODEL, D_FF), D_MODEL),
        "w_down": w((DEPTH, D_FF, D_MODEL), D_FF),
        "g_post_ffn": gain(D_MODEL),
    }


def reference(x, mem, positions, g_pre_mix, w_in, conv_w, conv_b, lru_wa, lru_ba, lru_wx,
              lru_bx, lru_lambda, g_q_lat, w_uq, g_kv_lat, w_ukv, g_lru_out, g_mla_out, w_out,
              g_post_mix, g_pre_mem, g_mem_kv, w_mq, w_mk, w_mv, w_mo, g_post_mem, g_pre_ffn,
              w_gate, w_up, w_down, g_post_ffn):
    bsz, s, _ = x.shape
    n_mem = mem.shape[1]
    mla_scale = 1.0 / math.sqrt(QK_NOPE_DIM + QK_ROPE_DIM)
    mem_scale = 1.0 / math.sqrt(MEM_HEAD_DIM)

    inv_freq = ROPE_THETA ** (-jnp.arange(0, QK_ROPE_DIM, 2, dtype=jnp.float32) / QK_ROPE_DIM)
    ang = positions.astype(jnp.float32)[..., None] * inv_freq
    cos = jnp.cos(ang)[:, :, None, :].astype(x.dtype)
    sin = jnp.sin(ang)[:, :, None, :].astype(x.dtype)

    h = x
    for l in range(DEPTH):
        xn = rmsnorm(h, g_pre_mix[l])
        z = xn @ w_in[l]
        lru_x = z[..., OFF_LRU_X:OFF_LRU_GATE]
        lru_gate = z[..., OFF_LRU_GATE:OFF_CQ]
        c_q = z[..., OFF_CQ:OFF_CKV]
        c_kv = z[..., OFF_CKV:OFF_KPE]
        k_pe = z[..., OFF_KPE:D_IN]

        u = causal_depthwise_conv(lru_x, conv_w[l], conv_b[l])
        hl = rg_lru(u, lru_wa[l], lru_ba[l], lru_wx[l], lru_bx[l], lru_lambda[l])
        y_lru = hl * jax.nn.gelu(lru_gate, approximate=True)

        q = (rmsnorm(c_q, g_q_lat[l]) @ w_uq[l]).reshape(bsz, s, MLA_HEADS, QK_NOPE_DIM + QK_ROPE_DIM)
        kv = (rmsnorm(c_kv, g_kv_lat[l]) @ w_ukv[l]).reshape(bsz, s, MLA_HEADS, QK_NOPE_DIM + V_HEAD_DIM)
        q_nope, q_pe = q[..., :QK_NOPE_DIM], q[..., QK_NOPE_DIM:]
        k_nope, v = kv[..., :QK_NOPE_DIM], kv[..., QK_NOPE_DIM:]
        q_pe = apply_rope(q_pe, cos, sin)
        k_pe_r = apply_rope(k_pe[:, :, None, :], cos, sin)
        q_full = jnp.concatenate([q_nope, q_pe], axis=-1)
        k_full = jnp.concatenate(
            [k_nope, jnp.broadcast_to(k_pe_r, (bsz, s, MLA_HEADS, QK_ROPE_DIM))], axis=-1)
        y_mla = causal_block_attention(q_full, k_full, v, mla_scale).reshape(bsz, s, D_MLA_OUT)

        y = jnp.concatenate([rmsnorm(y_lru, g_lru_out[l]), rmsnorm(y_mla, g_mla_out[l])], axis=-1)
        h = h + rmsnorm(y @ w_out[l], g_post_mix[l])

        xn = rmsnorm(h, g_pre_mem[l])
        mn = rmsnorm(mem, g_mem_kv[l])
        mq = (xn @ w_mq[l]).reshape(bsz, s, MEM_HEADS, MEM_HEAD_DIM)
        mk = (mn @ w_mk[l]).reshape(bsz, n_mem, MEM_HEADS, MEM_HEAD_DIM)
        mv = (mn @ w_mv[l]).reshape(bsz, n_mem, MEM_HEADS, MEM_HEAD_DIM)
        sc = jnp.einsum('bshd,bnhd->bhsn', mq, mk).astype(jnp.float32) * mem_scale
        p = jax.nn.softmax(sc, axis=-1).astype(mv.dtype)
        o = jnp.einsum('bhsn,bnhd->bshd', p, mv).reshape(bsz, s, D_MODEL)
        h = h + rmsnorm(o @ w_mo[l], g_post_mem[l])

        xn = rmsnorm(h, g_pre_ffn[l])
        f = (jax.nn.silu(xn @ w_gate[l]) * (xn @ w_up[l])) @ w_down[l]
        h = h + rmsnorm(f, g_post_ffn[l])
    return h
```

```python
import math
from contextlib import ExitStack

import numpy as np
import concourse.bass as bass
import concourse.mybir as mybir
from concourse.bass_utils import run_bass_kernel_spmd

F32 = mybir.dt.float32
BF16 = mybir.dt.bfloat16
I32 = mybir.dt.int32
AF = mybir.ActivationFunctionType
ALU = mybir.AluOpType

NCORES = 8
D = 1024
DFF = 2816
NJ = DFF // 128
NMEM = 256
EPS = 1e-6
NSLOT = 27
SLOT_MK, SLOT_MV = 23, 25
GPM, CW, CB, BA, BX, LAM, GQ, GKV, GLO, GMO, GPMEM, GMKV, GPF = 0, 8, 24, 28, 32, 36, 40, 43, 45, 49, 53, 61, 69
NPV = 77
HC, H2, BAH, BXH = 77, 81, 85, 89
NPVT = 96
SEM_LIMIT = 12000
NRING = 5
import os
F_PIPEA = os.environ.get('K_PIPEA', '1') == '1'
F_LATSKEW = os.environ.get('K_LATSKEW', '0') == '1'
F_P2EARLY = os.environ.get('K_P2EARLY', '1') == '1'
F_WPRE = os.environ.get('K_WPRE', '1') == '1'


class Buf:
    __slots__ = ("name", "w", "r", "ld", "ldc", "ldk", "st", "stc", "stk")

    def __init__(self, name):
        self.name = name
        self.w = {}
        self.r = {}
        self.ld = None
        self.ldc = 0
        self.ldk = None
        self.st = None
        self.stc = 0
        self.stk = None


class _Eng:
    def __init__(self, sch, name, h):
        self.sch = sch
        self.name = name
        self.h = h
        self.sem = None
        self.key = None
        self.cnt = 0
        self.nsem = 0
        self.known = {}
        self.pending = False

    def newsem(self):
        self.sem = self.sch.alloc_sem(f"e_{self.name}{self.nsem}")
        self.key = (self.name, self.nsem)
        self.nsem += 1
        self.cnt = 0


class Sched:
    def __init__(self, nc, es):
        self.nc = nc
        self.es = es
        self.nsems = 0
        self.E = {
            "pe": _Eng(self, "pe", nc.tensor),
            "act": _Eng(self, "act", nc.scalar),
            "dve": _Eng(self, "dve", nc.vector),
            "pool": _Eng(self, "pool", nc.gpsimd),
            "sp": _Eng(self, "sp", nc.sync),
        }
        for e in self.E.values():
            e.newsem()
        self.dma_tokens = {}
        self.nwaits = 0
        self.ninst = 0

    def alloc_sem(self, name):
        self.nsems += 1
        return self.es.enter_context(self.nc.semaphore(name))

    @staticmethod
    def _merge(d, src):
        for k, (s, v) in src.items():
            if k not in d or d[k][1] < v:
                d[k] = (s, v)

    def _wait(self, E, key, sem, val):
        if E.known.get(key, 0) >= val:
            return
        E.h.wait_ge(sem, val)
        E.known[key] = val
        self.nwaits += 1

    def _deps(self, E, R, W, skip_keys=()):
        deps = {}
        for b in R:
            self._merge(deps, b.w)
        for b in W:
            self._merge(deps, b.w)
            self._merge(deps, b.r)
        for key, (sem, val) in deps.items():
            if key in skip_keys:
                continue
            if E.name == "pe" and key[0] == "pe":
                continue
            self._wait(E, key, sem, val)

    def op(self, en, fn, R=(), W=(), inc=True):
        E = self.E[en]
        self._deps(E, R, W)
        ins = fn(E.h)
        self.ninst += 1
        if inc:
            if E.cnt >= SEM_LIMIT and not E.pending:
                E.newsem()
            E.cnt += 1
            ins.then_inc(E.sem, 1)
            E.pending = False
            tok = (E.key, E.sem, E.cnt)
        else:
            E.pending = True
            tok = (E.key, E.sem, E.cnt + 1)
        for b in R:
            k = tok[0]
            if k not in b.r or b.r[k][1] < tok[2]:
                b.r[k] = (tok[1], tok[2])
        for b in W:
            b.w = {tok[0]: (tok[1], tok[2])}
            b.r = {}
        return ins

    def dma(self, q, out, in_, R=(), W=(), **kw):
        E = self.E[q]
        skip = ()
        if W and W[0].ldk is not None:
            skip = (W[0].ldk,)
        self._deps(E, R, W, skip_keys=skip)
        ins = E.h.dma_start(out=out, in_=in_, **kw)
        self.ninst += 1
        if W:
            b = W[0]
            if b.ld is None:
                b.ld = self.alloc_sem("ld_" + b.name)
                b.ldk = ("ld", b.name)
            b.ldc += 16
            ins.then_inc(b.ld, 16)
            tok = (b.ldk, b.ld, b.ldc)
        else:
            b = R[0]
            if b.st is None:
                b.st = self.alloc_sem("st_" + b.name)
                b.stk = ("st", b.name)
            b.stc += 16
            ins.then_inc(b.st, 16)
            tok = (b.stk, b.st, b.stc)
        self.dma_tokens[tok[0]] = (tok[1], tok[2])
        for bb in R:
            k = tok[0]
            if k not in bb.r or bb.r[k][1] < tok[2]:
                bb.r[k] = (tok[1], tok[2])
        for bb in W:
            if bb is W[0] and skip:
                bb.w[tok[0]] = (tok[1], tok[2])
                bb.r = {}
            else:
                bb.w = {tok[0]: (tok[1], tok[2])}
                bb.r = {}
        return ins

    def barrier(self, bar_ap):
        Dv = self.E["dve"]
        for E in self.E.values():
            if E is Dv:
                continue
            assert not E.pending
            if E.cnt > 0:
                self._wait(Dv, E.key, E.sem, E.cnt)
        for k, (s, v) in self.dma_tokens.items():
            self._wait(Dv, k, s, v)
        if Dv.cnt > 0:
            self._wait(Dv, Dv.key, Dv.sem, Dv.cnt)
        ins = Dv.h.memset(bar_ap, 0.0)
        if Dv.cnt >= SEM_LIMIT:
            Dv.newsem()
        Dv.cnt += 1
        ins.then_inc(Dv.sem, 1)
        for E in self.E.values():
            if E is Dv:
                continue
            self._wait(E, Dv.key, Dv.sem, Dv.cnt)

    def final_wait(self):
        sp = self.E["sp"]
        for k, (s, v) in self.dma_tokens.items():
            self._wait(sp, k, s, v)
        for E in self.E.values():
            if E is sp or E.cnt == 0:
                continue
            self._wait(sp, E.key, E.sem, E.cnt)


def build(S=2048, NSEQ=2, dbg=False):
    NTB = S // 128
    NQT = S // 512
    nc = bass.Bass("TRN2", target_bir_lowering=False)

    def din(name, shape, dt=F32):
        return nc.dram_tensor(name, list(shape), dt, kind="ExternalInput").ap()

    x_d = din("x", [NSEQ, S, D])
    mem_d = din("mem", [NSEQ, NMEM, D])
    pos_d = din("pos", [128, NSEQ * NTB], I32)
    pv_d = din("pv", [128, NPV])
    grow_d = din("grow", [3, 128, D])
    invf_d = din("invf", [128, 32])
    ident_d = din("ident", [128, 128])
    tri_d = din("tri", [128, 128])
    winA_d = din("winA", [128, 8 * 1024])
    winB_d = din("winB", [128, 8 * 768])
    wuq_d = din("wuq", [128, 3 * 768])
    wukv_d = din("wukv", [128, 2 * 1024])
    wlru_d = din("wlru", [128, 4 * 2 * 128])
    wsl_d = din("wslots", [NSLOT, 128, 4096])
    scr_d = nc.dram_tensor("scr", [NSLOT, 128, 4096], BF16, kind="Internal").ap()
    out_d = nc.dram_tensor("out", [NSEQ, S, D], F32, kind="ExternalOutput").ap()
    if dbg:
        dbg_yT = nc.dram_tensor("dbg_yT", [NSEQ, 128, 8 * S], BF16, kind="ExternalOutput").ap()

    es = ExitStack()
    sch = Sched(nc, es)

    def T(scope, name, cols, dt):
        return scope.enter_context(nc.sbuf_tensor("sb_" + name, [128, cols], dt))

    mla_scale = 1.0 / math.sqrt(192.0)
    mem_scale = 1.0 / math.sqrt(256.0)

    def body():
        pers = es
        ident = T(pers, "ident", 128, BF16)
        tri = T(pers, "tri", 128, BF16)
        onesb = T(pers, "onesb", 128, BF16)
        onesf = T(pers, "onesf", 128, F32)
        pv = T(pers, "pv", NPVT, F32)
        cst = T(pers, "cst", 8, F32)
        invf = T(pers, "invf", 32, F32)
        posi = T(pers, "posi", NSEQ * NTB, I32)
        smallt = T(pers, "smallt", 32 * 4, F32)
        bar = T(pers, "bar", 4, F32)
        yT = T(pers, "yT", 8 * S, BF16)
        PS = pers.enter_context(nc.psum_tensor("ps", [128, 6 * 512], F32))
        TPS = pers.enter_context(nc.psum_tensor("tps", [128, 2 * 1024], BF16))

        constB = Buf("const")
        identB = Buf("identb")
        triB = Buf("trib")
        pvB = Buf("pvb")
        invfB = Buf("invfb")
        posB = Buf("posb")
        bankB = [Buf(f"bank{i}") for i in range(6)]
        tpB = [Buf(f"tpb{i}") for i in range(2)]
        smallB = [Buf(f"small{i}") for i in range(32)]
        yTB = [Buf(f"yT{i}") for i in range(NQT)]
        scrB = Buf("scr")
        st = {"small": 0, "bank": 0, "pair": 0, "tp": 0, "alt": 0}

        def small():
            i = st["small"] % 32
            st["small"] += 1
            return smallt[:, i * 4:(i + 1) * 4], smallB[i]

        def nbank():
            i = st["bank"] % 6
            st["bank"] += 1
            return PS[:, i * 512:(i + 1) * 512], bankB[i]

        def npair():
            i = st["pair"] % 3
            st["pair"] += 1
            return PS[:, i * 1024:(i + 1) * 1024], [bankB[2 * i], bankB[2 * i + 1]]

        def ntp():
            i = st["tp"] % 2
            st["tp"] += 1
            return TPS[:, i * 1024:(i + 1) * 1024], tpB[i]

        def MM(out, lhsT, rhs, start, stop, R, W, inc=None):
            return sch.op("pe", lambda e: e.matmul(out, lhsT=lhsT, rhs=rhs, start=start, stop=stop),
                          R=R, W=W, inc=(stop if inc is None else inc))

        def TR(out, in_, R, W, last):
            return sch.op("pe", lambda e: e.transpose(out, in_, ident[:]), R=list(R) + [identB], W=W, inc=last)

        def ACT(out, in_, func, R, W, **kw):
            return sch.op("act", lambda e: e.activation(out=out, in_=in_, func=func, **kw), R=R, W=W)

        def TS(out, in0, s1, s2, op0, op1, R, W, eng="dve"):
            if s2 is None:
                return sch.op(eng, lambda e: e.tensor_scalar(out=out, in0=in0, scalar1=s1, scalar2=None, op0=op0),
                              R=R, W=W)
            return sch.op(eng, lambda e: e.tensor_scalar(out=out, in0=in0, scalar1=s1, scalar2=s2, op0=op0, op1=op1),
                          R=R, W=W)

        def TT(out, in0, in1, op, R, W, eng="dve"):
            return sch.op(eng, lambda e: e.tensor_tensor(out=out, in0=in0, in1=in1, op=op), R=R, W=W)

        def STT(out, in0, scalar, in1, op0, op1, R, W):
            return sch.op("dve", lambda e: e.scalar_tensor_tensor(out=out, in0=in0, scalar=scalar, in1=in1,
                                                                  op0=op0, op1=op1), R=R, W=W)

        def COPY(out, in_, R, W, eng=None):
            if eng is None:
                eng = "act" if st["alt"] % 2 == 0 else "dve"
                st["alt"] += 1
            if eng == "act":
                return sch.op("act", lambda e: e.activation(out=out, in_=in_, func=AF.Copy), R=R, W=W)
            return sch.op(eng, lambda e: e.tensor_copy(out=out, in_=in_), R=R, W=W)

        def RECIP(out, in_, R, W):
            return sch.op("dve", lambda e: e.reciprocal(out=out, in_=in_), R=R, W=W)

        def rstd_from_ss(ss, ssB, n, invd):
            for i in range(n):
                TS(ss[:, i:i + 1], ss[:, i:i + 1], invd[i], EPS, ALU.mult, ALU.add, R=[ssB], W=[ssB])
            ACT(ss[:, 0:n], ss[:, 0:n], AF.Sqrt, R=[ssB], W=[ssB])
            RECIP(ss[:, 0:n], ss[:, 0:n], R=[ssB], W=[ssB])

        sch.dma("sp", pv[:, 0:NPV], pv_d[:, :], W=[pvB])
        sch.dma("sp", invf[:], invf_d[:, :], W=[invfB])
        sch.dma("sp", posi[:], pos_d[:, :], W=[posB])
        sch.dma("pool", ident[:], ident_d[:, :], W=[identB])
        sch.dma("pool", tri[:], tri_d[:, :], W=[triB])
        sch.op("dve", lambda e: e.memset(onesb[:], 1.0), W=[constB])
        sch.op("dve", lambda e: e.memset(onesf[:], 1.0), W=[constB])
        sch.op("dve", lambda e: e.memset(cst[:, 0:1], 1.0), W=[constB])
        sch.op("dve", lambda e: e.memset(cst[:, 1:2], math.pi), W=[constB])
        tmpc, tmpB = small()
        tmpc2, tmpB2 = small()
        TS(tmpc2, pv[:, LAM:LAM + 4], -1.0, None, ALU.mult, None, R=[pvB], W=[tmpB2])
        TT(tmpc, pv[:, LAM:LAM + 4], tmpc2, ALU.max, R=[pvB, tmpB2], W=[tmpB])
        ACT(tmpc, tmpc, AF.Exp, R=[tmpB], W=[tmpB], scale=-1.0)
        TS(tmpc, tmpc, 1.0, None, ALU.add, None, R=[tmpB], W=[tmpB])
        ACT(tmpc, tmpc, AF.Ln, R=[tmpB], W=[tmpB])
        TS(tmpc2, tmpc2, 0.0, None, ALU.max, None, R=[tmpB2], W=[tmpB2])
        TT(tmpc, tmpc, tmpc2, ALU.add, R=[tmpB, tmpB2], W=[tmpB])
        TS(pv[:, HC:HC + 4], tmpc, -4.0, None, ALU.mult, None, R=[tmpB], W=[pvB])
        TS(pv[:, H2:H2 + 4], tmpc, -8.0, None, ALU.mult, None, R=[tmpB], W=[pvB])
        TS(pv[:, BAH:BAH + 4], pv[:, BA:BA + 4], 0.5, None, ALU.mult, None, R=[pvB], W=[pvB])
        TS(pv[:, BXH:BXH + 4], pv[:, BX:BX + 4], 0.5, None, ALU.mult, None, R=[pvB], W=[pvB])

        scr_state = {"done": False}

        def convert_scratch():
            if scr_state["done"]:
                return
            scr_state["done"] = True
            order = [23, 24, 25, 26] + list(range(23))
            for sl in order:
                sch.dma("pool", scr_d[sl].rearrange("p (a b) -> p a b", a=2),
                        wsl_d[sl].rearrange("p (a b) -> p a b", a=2), W=[scrB])

        def nt_a(src, srcB, Dn, junk, junkB):
            ss, ssB = small()
            ACT(junk[:, 0:Dn], src, AF.Square, R=srcB, W=[junkB, ssB], accum_out=ss[:, 0:1])
            rstd_from_ss(ss, ssB, 1, [1.0 / Dn])
            return ss, ssB

        def nt_b(src, srcB, Dn, ss, ssB, gcols, dst3, dstB, xs, xsB):
            nk = Dn // 128
            ACT(xs[:, 0:Dn], src, AF.Copy, R=list(srcB) + [ssB], W=[xsB], scale=ss[:, 0:1])
            tp, tB = ntp()
            for k in range(nk):
                TR(tp[:, k * 128:(k + 1) * 128], xs[:, k * 128:(k + 1) * 128], R=[xsB], W=[tB], last=(k == nk - 1))
            g3 = gcols.unsqueeze(2).broadcast_to([128, nk, 128])
            TT(dst3, tp[:, 0:Dn].rearrange("p (k t) -> p k t", k=nk), g3, ALU.mult, R=[tB, pvB], W=dstB)

        def nt(src, srcB, Dn, gcols, dst3, dstB, xs, xsB, junk, junkB):
            ss, ssB = nt_a(src, srcB, Dn, junk, junkB)
            nt_b(src, srcB, Dn, ss, ssB, gcols, dst3, dstB, xs, xsB)

        def rope(src3, srcB, G, cos, sin, csB, dst3, dstB, tA, tAB, tB_, tBB):
            x1 = src3[:, :, 0:32]
            x2 = src3[:, :, 32:64]
            cb = cos.unsqueeze(1).broadcast_to([128, G, 32])
            sb = sin.unsqueeze(1).broadcast_to([128, G, 32])
            a3 = tA[:, 0:G * 32].rearrange("p (g d) -> p g d", g=G)
            b3 = tB_[:, 0:G * 32].rearrange("p (g d) -> p g d", g=G)
            TT(a3, x1, cb, ALU.mult, R=srcB + [csB], W=[tAB])
            TT(b3, x2, sb, ALU.mult, R=srcB + [csB], W=[tBB])
            TT(dst3[:, :, 0:32], a3, b3, ALU.subtract, R=[tAB, tBB], W=dstB)
            TT(a3, x2, cb, ALU.mult, R=srcB + [csB], W=[tAB])
            TT(b3, x1, sb, ALU.mult, R=srcB + [csB], W=[tBB])
            TT(dst3[:, :, 32:64], a3, b3, ALU.add, R=[tAB, tBB], W=dstB)

        def feat_norm(srcs, srcBs, gcol0, Dn, tt, dst_off, sq, sqB, rt, rtB):
            n = len(srcs)
            bank, bB = nbank()
            for i in range(n):
                q, qB = sq[i % 2], sqB[i % 2]
                ACT(q[:, :], srcs[i][:, tt * 512:(tt + 1) * 512], AF.Square, R=[srcBs[i]], W=[qB])
                MM(bank, onesf[:], q[:, :], i == 0, i == n - 1, R=[qB, constB], W=[bB], inc=True)
            TS(rt[:, :], bank, 1.0 / Dn, EPS, ALU.mult, ALU.add, R=[bB], W=[rtB])
            ACT(rt[:, :], rt[:, :], AF.Sqrt, R=[rtB], W=[rtB])
            RECIP(rt[:, :], rt[:, :], R=[rtB], W=[rtB])
            for i in range(n):
                c = dst_off + i
                STT(yT[:, c * S + tt * 512: c * S + (tt + 1) * 512], srcs[i][:, tt * 512:(tt + 1) * 512],
                    pv[:, gcol0 + i:gcol0 + i + 1], rt[:, :], ALU.mult, ALU.mult,
                    R=[srcBs[i], pvB, rtB], W=[yTB[tt]])

        for s in range(NSEQ):
            sfx = f"_{s}"
            with ExitStack() as s1:
                cqnT = T(s1, "cqnT" + sfx, 3 * S, BF16)
                ckvnT = T(s1, "ckvnT" + sfx, 2 * S, BF16)
                kpeT = T(s1, "kpeT" + sfx, S, BF16)
                cs = T(s1, "cs" + sfx, NTB * 64, F32)
                latB = [Buf(f"lat{s}_{i}") for i in range(NTB)]
                csB = Buf(f"cs{s}")
                with ExitStack() as s2:
                    xnT = T(s2, "xnT" + sfx, 8 * S, BF16)
                    xnT3 = xnT[:, :].rearrange("p (k t) -> p k t", k=8)
                    xnTB = [Buf(f"xnT{s}_{i}") for i in range(NTB)]
                    winA = T(s2, "winA" + sfx, 8 * 1024, BF16)
                    wlru = T(s2, "wlru" + sfx, 4 * 2 * 128, BF16)
                    winAB = Buf(f"winA{s}")
                    wlruB = Buf(f"wlru{s}")
                    sch.dma("pool", winA[:, :].rearrange("p (k n) -> p k n", k=8),
                            winA_d[:, :].rearrange("p (k n) -> p k n", k=8), W=[winAB])
                    sch.dma("pool", wlru[:], wlru_d[:, :], W=[wlruB])
                    convert_scratch()
                    with ExitStack() as s2a:
                        xtmp = [T(s2a, f"xtmp{i}" + sfx, 1024, F32) for i in range(3)]
                        xtmpB = [Buf(f"xtmp{s}_{i}") for i in range(3)]
                        xs = [T(s2a, f"xs{i}" + sfx, 1024, BF16) for i in range(2)]
                        xsB = [Buf(f"xs{s}_{i}") for i in range(2)]
                        junk = T(s2a, "junk" + sfx, 1024, BF16)
                        junkB = Buf(f"junk{s}")
                        ang = T(s2a, "ang" + sfx, NTB * 64, F32)
                        kf = T(s2a, "kf" + sfx, NTB * 64, F32)
                        ki = T(s2a, "ki" + sfx, NTB * 64, I32)
                        posf = T(s2a, "posf" + sfx, NTB, F32)
                        angB = Buf(f"ang{s}")
                        kfB = Buf(f"kf{s}")
                        kiB = Buf(f"ki{s}")
                        posfB = Buf(f"posf{s}")
                        pend = None
                        for tb in range(NTB + 1):
                            cur = None
                            if tb < NTB:
                                i = tb % 3
                                sch.dma("sp", xtmp[i][:], x_d[s, tb * 128:(tb + 1) * 128, :], W=[xtmpB[i]])
                                cur = (tb,) + nt_a(xtmp[i][:, :], [xtmpB[i]], 1024, junk, junkB)
                            if not F_PIPEA:
                                pend = cur
                                cur = None
                            if pend is not None:
                                ptb, pss, pssB = pend
                                pi = ptb % 3
                                nt_b(xtmp[pi][:, :], [xtmpB[pi]], 1024, pss, pssB, pv[:, GPM:GPM + 8],
                                     xnT3[:, :, ptb * 128:(ptb + 1) * 128], [xnTB[ptb]], xs[ptb % 2], xsB[ptb % 2])
                            pend = cur
                        COPY(posf[:], posi[:, s * NTB:(s + 1) * NTB], R=[posB], W=[posfB], eng="dve")
                        ang3 = ang[:, :].rearrange("p (t d) -> p t d", t=NTB)
                        TT(ang3[:, :, 0:32], posf[:, :].unsqueeze(2).broadcast_to([128, NTB, 32]),
                           invf[:, :].unsqueeze(1).broadcast_to([128, NTB, 32]), ALU.mult,
                           R=[posfB, invfB], W=[angB])
                        TS(ang3[:, :, 32:64], ang3[:, :, 0:32], math.pi / 2, None, ALU.add, None, R=[angB], W=[angB])
                        TS(kf[:], ang[:], 1.0 / (2 * math.pi), None, ALU.mult, None, R=[angB], W=[kfB])
                        COPY(ki[:], kf[:], R=[kfB], W=[kiB], eng="dve")
                        COPY(kf[:], ki[:], R=[kiB], W=[kfB], eng="dve")
                        C1 = 6.28125
                        C2 = 2 * math.pi - C1
                        STT(ang[:], kf[:], -C1, ang[:], ALU.mult, ALU.add, R=[kfB, angB], W=[angB])
                        STT(ang[:], kf[:], -C2, ang[:], ALU.mult, ALU.add, R=[kfB, angB], W=[angB])
                        TS(kf[:], ang[:], math.pi, None, ALU.is_gt, None, R=[angB], W=[kfB])
                        STT(ang[:], kf[:], -2 * math.pi, ang[:], ALU.mult, ALU.add, R=[kfB, angB], W=[angB])
                        TS(kf[:], ang[:], -math.pi, None, ALU.is_lt, None, R=[angB], W=[kfB])
                        STT(ang[:], kf[:], 2 * math.pi, ang[:], ALU.mult, ALU.add, R=[kfB, angB], W=[angB])
                        TS(ang[:], ang[:], math.pi, -math.pi, ALU.min, ALU.max, R=[angB], W=[angB])
                        ACT(cs[:], ang[:], AF.Sin, R=[angB], W=[csB])
                        sch.barrier(bar[:, 0:1])

                    with ExitStack() as s3:
                        NSEG = NQT
                        Rb = [T(s3, f"R{i}" + sfx, S, F32) for i in range(6)]
                        RB = [[Buf(f"R{s}_{i}_{g}") for g in range(NSEG)] for i in range(6)]
                        Ubf = T(s3, "Ubf" + sfx, S, BF16)
                        UbfB = [Buf(f"Ubf{s}_{g}") for g in range(NSEG)]
                        YL = [T(s3, f"YL{i}" + sfx, S, F32) for i in range(4)]
                        YLB = [[Buf(f"YL{s}_{i}_{g}") for g in range(NSEG)] for i in range(4)]
                        sq = [T(s3, f"sq{i}" + sfx, 512, F32) for i in range(2)]
                        sqB = [Buf(f"sq{s}_{i}") for i in range(2)]
                        rt = T(s3, "rt" + sfx, 512, F32)
                        rtB = Buf(f"rt{s}")
                        LX, LG, U, A, TI, R6 = Rb
                        LXB, LGB, UB, AB, TIB, R6B = RB
                        segs = [slice(g * 512, (g + 1) * 512) for g in range(NSEG)]
                        for c in range(4):
                            def col(base):
                                return pv[:, base + c:base + c + 1]
                            for g in range(NSEG):
                                sl = segs[g]
                                bank, bB = nbank()
                                for kc in range(8):
                                    MM(bank, winA[:, kc * 1024 + c * 128: kc * 1024 + (c + 1) * 128],
                                       xnT[:, kc * S + g * 512: kc * S + (g + 1) * 512], kc == 0, kc == 7,
                                       R=[winAB] + xnTB[g * 4:(g + 1) * 4], W=[bB])
                                COPY(LX[:, sl], bank, R=[bB], W=[LXB[g]], eng="act")
                                bank, bB = nbank()
                                for kc in range(8):
                                    MM(bank, winA[:, kc * 1024 + 512 + c * 128: kc * 1024 + 512 + (c + 1) * 128],
                                       xnT[:, kc * S + g * 512: kc * S + (g + 1) * 512], kc == 0, kc == 7,
                                       R=[winAB] + xnTB[g * 4:(g + 1) * 4], W=[bB])
                                COPY(LG[:, sl], bank, R=[bB], W=[LGB[g]], eng="dve")
                            for g in range(NSEG):
                                sl = segs[g]
                                e = (g + 1) * 512
                                TS(U[:, sl], LX[:, sl], pv[:, CW + 3 * 4 + c:CW + 3 * 4 + c + 1], col(CB), ALU.mult, ALU.add,
                                   R=[LXB[g], pvB], W=[UB[g]])
                                for k, sh in ((2, 1), (1, 2), (0, 3)):
                                    lo = max(g * 512, sh)
                                    rr = [LXB[g], pvB, UB[g]] + ([LXB[g - 1]] if g > 0 else [])
                                    STT(U[:, lo:e], LX[:, lo - sh:e - sh], pv[:, CW + k * 4 + c:CW + k * 4 + c + 1], U[:, lo:e],
                                        ALU.mult, ALU.add, R=rr, W=[UB[g]])
                                COPY(Ubf[:, sl], U[:, sl], R=[UB[g]], W=[UbfB[g]], eng="act")
                            for g in range(NSEG):
                                sl = segs[g]
                                bank, bB = nbank()
                                MM(bank, wlru[:, (c * 2 + 0) * 128:(c * 2 + 1) * 128], Ubf[:, sl], True, True,
                                   R=[wlruB, UbfB[g]], W=[bB])
                                ACT(LX[:, sl], bank, AF.Tanh, R=[bB, pvB], W=[LXB[g]], scale=0.5, bias=col(BAH))
                                bank, bB = nbank()
                                MM(bank, wlru[:, (c * 2 + 1) * 128:(c * 2 + 2) * 128], Ubf[:, sl], True, True,
                                   R=[wlruB, UbfB[g]], W=[bB])
                                ACT(TI[:, sl], bank, AF.Tanh, R=[bB, pvB], W=[TIB[g]], scale=0.5, bias=col(BXH))
                            for g in range(NSEG):
                                sl = segs[g]
                                ACT(A[:, sl], LX[:, sl], AF.Exp, R=[LXB[g], pvB], W=[AB[g]], scale=col(HC), bias=col(HC))
                                ACT(R6[:, sl], LX[:, sl], AF.Exp, R=[LXB[g], pvB], W=[R6B[g]], scale=col(H2), bias=col(H2))
                            for g in range(NSEG):
                                sl = segs[g]
                                ACT(LX[:, sl], LG[:, sl], AF.Square, R=[LGB[g]], W=[LXB[g]])
                                TS(LX[:, sl], LX[:, sl], 0.044715, 1.0, ALU.mult, ALU.add, R=[LXB[g]], W=[LXB[g]])
                                TT(LX[:, sl], LX[:, sl], LG[:, sl], ALU.mult, R=[LXB[g], LGB[g]], W=[LXB[g]])
                            for g in range(NSEG):
                                sl = segs[g]
                                ACT(LX[:, sl], LX[:, sl], AF.Tanh, R=[LXB[g]], W=[LXB[g]], scale=0.7978845608028654)
                            for g in range(NSEG):
                                sl = segs[g]
                                ACT(R6[:, sl], R6[:, sl], AF.Sqrt, R=[R6B[g], constB], W=[R6B[g]], scale=-1.0, bias=cst[:, 0:1])
                            for g in range(NSEG):
                                sl = segs[g]
                                STT(TI[:, sl], TI[:, sl], 1.0, U[:, sl], ALU.add, ALU.mult, R=[TIB[g], UB[g]], W=[TIB[g]])
                                STT(TI[:, sl], TI[:, sl], 0.5, R6[:, sl], ALU.mult, ALU.mult, R=[TIB[g], R6B[g]], W=[TIB[g]])
                            for g in range(NSEG):
                                sl = segs[g]
                                if g == 0:
                                    sch.op("dve", lambda e, sl=sl: e.tensor_tensor_scan(
                                        out=U[:, sl], data0=A[:, sl], data1=TI[:, sl], initial=0.0,
                                        op0=ALU.mult, op1=ALU.add), R=[AB[g], TIB[g]], W=[UB[g]])
                                else:
                                    sch.op("dve", lambda e, sl=sl, g=g: e.tensor_tensor_scan(
                                        out=U[:, sl], data0=A[:, sl], data1=TI[:, sl],
                                        initial=U[:, g * 512 - 1:g * 512],
                                        op0=ALU.mult, op1=ALU.add), R=[AB[g], TIB[g], UB[g - 1]], W=[UB[g]])
                                STT(LX[:, sl], LX[:, sl], 1.0, LG[:, sl], ALU.add, ALU.mult, R=[LXB[g], LGB[g]], W=[LXB[g]])
                                STT(YL[c][:, sl], LX[:, sl], 0.5, U[:, sl], ALU.mult, ALU.mult, R=[LXB[g], UB[g]], W=[YLB[c][g]])
                        for tt in range(NQT):
                            feat_norm([YL[i] for i in range(4)], [YLB[i][tt] for i in range(4)], GLO, 512, tt, 0, sq, sqB, rt, rtB)
                        sch.barrier(bar[:, 0:1])

                    with ExitStack() as s4:
                        winB = T(s4, "winB" + sfx, 8 * 768, BF16)
                        winBB = Buf(f"winB{s}")
                        sch.dma("pool", winB[:, :].rearrange("p (k n) -> p k n", k=8),
                                winB_d[:, :].rearrange("p (k n) -> p k n", k=8), W=[winBB])
                        lat = [T(s4, f"latbf{i}" + sfx, 768, BF16) for i in range(2)]
                        latbB = [Buf(f"latbf{s}_{i}") for i in range(2)]
                        junk = T(s4, "junkb" + sfx, 512, BF16)
                        junkB = Buf(f"junkb{s}")
                        rtmp = [T(s4, f"rtmp{i}" + sfx, 128, F32) for i in range(2)]
                        rtmpB = [Buf(f"rtmp{s}_{i}") for i in range(2)]
                        cqnT3 = cqnT[:, :].rearrange("p (k t) -> p k t", k=3)
                        ckvnT3 = ckvnT[:, :].rearrange("p (k t) -> p k t", k=2)
                        def lat_front(tb):
                            bA, bAB = nbank()
                            bBk, bBB = nbank()
                            for kc in range(8):
                                lhs = xnT[:, kc * S + tb * 128: kc * S + (tb + 1) * 128]
                                MM(bA, lhs, winB[:, kc * 768: kc * 768 + 512], kc == 0, kc == 7,
                                   R=[winBB, xnTB[tb]], W=[bAB])
                            for kc in range(8):
                                lhs = xnT[:, kc * S + tb * 128: kc * S + (tb + 1) * 128]
                                MM(bBk[:, 0:256], lhs, winB[:, kc * 768 + 512: kc * 768 + 768], kc == 0, kc == 7,
                                   R=[winBB, xnTB[tb]], W=[bBB])
                            ss, ssB = small()
                            ACT(junk[:, 0:384], bA[:, 0:384], AF.Square, R=[bAB], W=[junkB, ssB], accum_out=ss[:, 0:1])
                            ACT(junk[:, 0:256], bBk[:, 0:256], AF.Square, R=[bBB], W=[junkB, ssB], accum_out=ss[:, 1:2])
                            rstd_from_ss(ss, ssB, 2, [1.0 / 384, 1.0 / 256])
                            return (tb, bA, bAB, bBk, bBB, ss, ssB)

                        def lat_back(stt):
                            tb, bA, bAB, bBk, bBB, ss, ssB = stt
                            bsl = slice(tb * 128, (tb + 1) * 128)
                            L = lat[tb % 2]
                            LB = latbB[tb % 2]
                            ACT(L[:, 0:384], bA[:, 0:384], AF.Copy, R=[bAB, ssB], W=[LB], scale=ss[:, 0:1])
                            ACT(L[:, 384:640], bBk[:, 0:256], AF.Copy, R=[bBB, ssB], W=[LB], scale=ss[:, 1:2])
                            rope(bA[:, 384:512].rearrange("p (g d) -> p g d", g=2), [bAB], 2,
                                 cs[:, tb * 64 + 32: tb * 64 + 64], cs[:, tb * 64: tb * 64 + 32], csB,
                                 L[:, 640:768].rearrange("p (g d) -> p g d", g=2), [LB],
                                 rtmp[0], rtmpB[0], rtmp[1], rtmpB[1])
                            tp, tB = ntp()
                            for k in range(6):
                                TR(tp[:, k * 128:(k + 1) * 128], L[:, k * 128:(k + 1) * 128], R=[LB], W=[tB], last=(k == 5))
                            TT(cqnT3[:, :, bsl], tp[:, 0:384].rearrange("p (k t) -> p k t", k=3),
                               pv[:, GQ:GQ + 3].unsqueeze(2).broadcast_to([128, 3, 128]), ALU.mult,
                               R=[tB, pvB], W=[latB[tb]])
                            TT(ckvnT3[:, :, bsl], tp[:, 384:640].rearrange("p (k t) -> p k t", k=2),
                               pv[:, GKV:GKV + 2].unsqueeze(2).broadcast_to([128, 2, 128]), ALU.mult,
                               R=[tB, pvB], W=[latB[tb]])
                            COPY(kpeT[:, bsl], tp[:, 640:768], R=[tB], W=[latB[tb]], eng="act")

                        pend = None
                        for tb in range(NTB + 1):
                            cur = lat_front(tb) if tb < NTB else None
                            if not F_LATSKEW:
                                pend = cur
                                cur = None
                            if pend is not None:
                                lat_back(pend)
                            pend = cur
                        sch.barrier(bar[:, 0:1])
                with ExitStack() as s5:
                    qnT = T(s5, "qnT" + sfx, 4 * S, BF16)
                    qpeT = T(s5, "qpeT" + sfx, 2 * S, BF16)
                    knT = T(s5, "knT" + sfx, 4 * S, BF16)
                    Vt = T(s5, "Vt" + sfx, NTB * 512, BF16)
                    qB = [Buf(f"q{s}_{i}") for i in range(NQT)]
                    kB = [Buf(f"k{s}_{i}") for i in range(NQT)]
                    qpeB = [Buf(f"qpe{s}_{i}") for i in range(NTB)]
                    vB = [Buf(f"v{s}_{i}") for i in range(NTB)]
                    qpeT3 = qpeT[:, :].rearrange("p (k t) -> p k t", k=2)
                    with ExitStack() as s5b:
                        wuq = T(s5b, "wuq" + sfx, 3 * 768, BF16)
                        wukv = T(s5b, "wukv" + sfx, 2 * 1024, BF16)
                        wuqB = Buf(f"wuq{s}")
                        wukvB = Buf(f"wukv{s}")
                        sch.dma("pool", wuq[:, :].rearrange("p (k n) -> p k n", k=3),
                                wuq_d[:, :].rearrange("p (k n) -> p k n", k=3), W=[wuqB])
                        sch.dma("pool", wukv[:, :].rearrange("p (k n) -> p k n", k=2),
                                wukv_d[:, :].rearrange("p (k n) -> p k n", k=2), W=[wukvB])
                        qpb = [T(s5b, f"qpb{i}" + sfx, 256, BF16) for i in range(2)]
                        qpbB = [Buf(f"qpb{s}_{i}") for i in range(2)]
                        rtmp = [T(s5b, f"rtq{i}" + sfx, 128, F32) for i in range(2)]
                        rtmpB = [Buf(f"rtq{s}_{i}") for i in range(2)]
                        for tt in range(NQT):
                            tsl = slice(tt * 512, (tt + 1) * 512)
                            lb4 = latB[tt * 4:(tt + 1) * 4]
                            for h in range(4):
                                bank, bB = nbank()
                                for kc in range(3):
                                    MM(bank, wuq[:, kc * 768 + h * 128: kc * 768 + (h + 1) * 128],
                                       cqnT[:, kc * S + tt * 512: kc * S + (tt + 1) * 512], kc == 0, kc == 2,
                                       R=[wuqB] + lb4, W=[bB])
                                COPY(qnT[:, h * S + tt * 512: h * S + (tt + 1) * 512], bank, R=[bB], W=[qB[tt]])
                            for h in range(4):
                                bank, bB = nbank()
                                for kc in range(2):
                                    MM(bank, wukv[:, kc * 1024 + h * 128: kc * 1024 + (h + 1) * 128],
                                       ckvnT[:, kc * S + tt * 512: kc * S + (tt + 1) * 512], kc == 0, kc == 1,
                                       R=[wukvB] + lb4, W=[bB])
                                COPY(knT[:, h * S + tt * 512: h * S + (tt + 1) * 512], bank, R=[bB], W=[kB[tt]])
                            for tb in range(tt * 4, tt * 4 + 4):
                                bsl = slice(tb * 128, (tb + 1) * 128)
                                bank, bB = nbank()
                                for kc in range(3):
                                    MM(bank[:, 0:256], cqnT[:, kc * S + tb * 128: kc * S + (tb + 1) * 128],
                                       wuq[:, kc * 768 + 512: kc * 768 + 768], kc == 0, kc == 2,
                                       R=[wuqB, latB[tb]], W=[bB])
                                Q = qpb[tb % 2]
                                QB = qpbB[tb % 2]
                                rope(bank[:, 0:256].rearrange("p (g d) -> p g d", g=4), [bB], 4,
                                     cs[:, tb * 64 + 32: tb * 64 + 64], cs[:, tb * 64: tb * 64 + 32], csB,
                                     Q[:, :].rearrange("p (g d) -> p g d", g=4), [QB],
                                     rtmp[0], rtmpB[0], rtmp[1], rtmpB[1])
                                tp, tB = ntp()
                                for k in range(2):
                                    TR(tp[:, k * 128:(k + 1) * 128], Q[:, k * 128:(k + 1) * 128], R=[QB], W=[tB], last=(k == 1))
                                COPY(qpeT3[:, :, bsl], tp[:, 0:256].rearrange("p (k t) -> p k t", k=2),
                                     R=[tB], W=[qpeB[tb]], eng="act")
                                bank, bB = nbank()
                                for kc in range(2):
                                    MM(bank, ckvnT[:, kc * S + tb * 128: kc * S + (tb + 1) * 128],
                                       wukv[:, kc * 1024 + 512: kc * 1024 + 1024], kc == 0, kc == 1,
                                       R=[wukvB, latB[tb]], W=[bB])
                                COPY(Vt[:, tb * 512:(tb + 1) * 512], bank, R=[bB], W=[vB[tb]], eng="dve")
                        sch.barrier(bar[:, 0:1])
                    with ExitStack() as s6:
                        YM = [T(s6, f"YM{i}" + sfx, S, F32) for i in range(4)]
                        YMB = [Buf(f"YM{s}_{i}") for i in range(4)]
                        Pb = [T(s6, f"Pb{i}" + sfx, 512, BF16) for i in range(3)]
                        PbB = [Buf(f"Pb{s}_{i}") for i in range(3)]
                        rden = [T(s6, f"rden{i}" + sfx, 512, F32) for i in range(2)]
                        rdenB = [Buf(f"rden{s}_{i}") for i in range(2)]
                        sq = [T(s6, f"sqm{i}" + sfx, 512, F32) for i in range(2)]
                        sqB = [Buf(f"sqm{s}_{i}") for i in range(2)]
                        rt = T(s6, "rtm" + sfx, 512, F32)
                        rtB = Buf(f"rtm{s}")
                        it = 0
                        pcount = 0
                        for qi in range(NQT):
                            for h in range(4):
                                hb = (h % 2) * 64
                                nkc = 4 * qi + 4
                                ob = 4 if it % 2 == 0 else 2
                                O, OB = PS[:, ob * 512:(ob + 1) * 512], bankB[ob]
                                DN, DNB = PS[:, (ob + 1) * 512:(ob + 2) * 512], bankB[ob + 1]

                                def c0_of(kc):
                                    return 0 if kc < 4 * qi else (kc - 4 * qi) * 128

                                def emitS(kc):
                                    sbi = kc % 2
                                    Sb, SB = PS[:, sbi * 512:(sbi + 1) * 512], bankB[sbi]
                                    c0 = c0_of(kc)
                                    MM(Sb[:, c0:512], knT[:, h * S + kc * 128: h * S + (kc + 1) * 128],
                                       qnT[:, h * S + qi * 512 + c0: h * S + (qi + 1) * 512], True, False,
                                       R=[kB[kc // 4], qB[qi]], W=[SB])
                                    MM(Sb[:, c0:512], kpeT[hb:hb + 64, kc * 128:(kc + 1) * 128],
                                       qpeT[hb:hb + 64, (h // 2) * S + qi * 512 + c0: (h // 2) * S + (qi + 1) * 512],
                                       False, True, R=[latB[kc]] + qpeB[qi * 4:(qi + 1) * 4], W=[SB])

                                emitS(0)
                                for kc in range(nkc):
                                    if kc + 1 < nkc:
                                        emitS(kc + 1)
                                    sbi = kc % 2
                                    Sb, SB = PS[:, sbi * 512:(sbi + 1) * 512], bankB[sbi]
                                    c0 = c0_of(kc)
                                    P, PB = Pb[pcount % 3], PbB[pcount % 3]
                                    pcount += 1
                                    ACT(P[:, c0:512], Sb[:, c0:512], AF.Exp, R=[SB], W=[PB], scale=mla_scale)
                                    if kc >= 4 * qi:
                                        TT(P[:, c0:c0 + 128], P[:, c0:c0 + 128], tri[:, :], ALU.mult, R=[PB, triB], W=[PB])
                                    MM(O[:, c0:512], Vt[:, kc * 512 + h * 128: kc * 512 + (h + 1) * 128], P[:, c0:512],
                                       kc == 0, kc == nkc - 1, R=[vB[kc], PB], W=[OB])
                                    MM(DN[:, c0:512], onesb[:, :], P[:, c0:512], kc == 0, kc == nkc - 1,
                                       R=[constB, PB], W=[DNB])
                                rd, rdB = rden[it % 2], rdenB[it % 2]
                                RECIP(rd[:, :], DN, R=[DNB], W=[rdB])
                                TT(YM[h][:, qi * 512:(qi + 1) * 512], O, rd[:, :], ALU.mult, R=[OB, rdB], W=[YMB[h]])
                                it += 1
                            feat_norm(YM, YMB, GMO, 512, qi, 4, sq, sqB, rt, rtB)
                        sch.barrier(bar[:, 0:1])
            if dbg:
                dB = Buf(f"dbgy{s}")
                sch.dma("sp", dbg_yT[s], yT[:, :], R=yTB + [dB])

            with ExitStack() as p2:
                G = [T(p2, f"G{i}" + sfx, 1024, F32) for i in range(3)]
                GB = [Buf(f"G{s}_{i}") for i in range(3)]
                for i in range(3):
                    sch.dma("sp", G[i][:], grow_d[i], W=[GB[i]])
                ring = T(p2, "ring" + sfx, NRING * 4096, BF16)
                ringB = [Buf(f"ring{s}_{i}") for i in range(NRING)]
                mkT = T(p2, "mkT" + sfx, 8 * 256, BF16)
                mv = T(p2, "mv" + sfx, 2 * 1024, BF16)
                mnT = T(p2, "mnT" + sfx, 8 * 256, BF16)
                mkB = Buf(f"mk{s}")
                mvB = Buf(f"mv{s}")
                mnB = [Buf(f"mn{s}_{i}") for i in range(2)]
                memt = [T(p2, f"memt{i}" + sfx, 1024, F32) for i in range(2)]
                memtB = [Buf(f"memt{s}_{i}") for i in range(2)]
                Ht = T(p2, "Ht" + sfx, 4 * 1024, F32)
                HtB = [Buf(f"Ht{s}_{i}") for i in range(4)]
                xn2 = T(p2, "xn2" + sfx, 8 * 512, BF16)
                xn2B = [Buf(f"xn2{s}_{i}") for i in range(4)]
                xn23 = xn2[:, :].rearrange("p (k t) -> p k t", k=8)
                PL = T(p2, "PL" + sfx, NJ * 512, BF16)
                PLB = [Buf(f"PL{s}_{i}") for i in range(NJ)]
                Pb = [T(p2, f"Pm{i}" + sfx, 512, BF16) for i in range(4)]
                PbB = [Buf(f"Pm{s}_{i}") for i in range(4)]
                rden = [T(p2, f"rdm{i}" + sfx, 512, F32) for i in range(2)]
                rdenB = [Buf(f"rdm{s}_{i}") for i in range(2)]
                sg = [T(p2, f"sg{i}" + sfx, 512, F32) for i in range(2)]
                sgB = [Buf(f"sg{s}_{i}") for i in range(2)]
                tmpo = [T(p2, f"tmpo{i}" + sfx, 1024, F32) for i in range(2)]
                tmpoB = [Buf(f"tmpo{s}_{i}") for i in range(2)]
                Fsb = T(p2, "Fsb" + sfx, 4 * 1024, F32)
                FsbB = [Buf(f"Fsb{s}_{i}") for i in range(4)]
                xs = [T(p2, f"xs2{i}" + sfx, 1024, BF16) for i in range(2)]
                xsB = [Buf(f"xs2{s}_{i}") for i in range(2)]
                junk = T(p2, "junk2" + sfx, 1024, BF16)
                junkB = Buf(f"junk2{s}")

                tile_uses = [0, 1, 2, 3, 4, 5] + list(range(6, 17)) + list(range(17, 23))
                uses = [23, 24, 25, 26] + tile_uses * NQT
                rs = {"issued": 0, "k": 0}

                def ring_get():
                    k = rs["k"]
                    rs["k"] += 1
                    while rs["issued"] <= min(k + 3, len(uses) - 1):
                        i = rs["issued"]
                        r = i % NRING
                        sch.dma("sp", ring[:, r * 4096:(r + 1) * 4096], scr_d[uses[i]], R=[scrB], W=[ringB[r]])
                        rs["issued"] += 1
                    r = k % NRING
                    return ring[:, r * 4096:(r + 1) * 4096], ringB[r], uses[k]

                mnT3 = mnT[:, :].rearrange("p (k t) -> p k t", k=8)
                for mc in range(2):
                    sch.dma("sp", memt[mc][:], mem_d[s, mc * 128:(mc + 1) * 128, :], W=[memtB[mc]])
                for mc in range(2):
                    nt(memt[mc][:, :], [memtB[mc]], 1024, pv[:, GMKV:GMKV + 8],
                       mnT3[:, :, mc * 128:(mc + 1) * 128], [mnB[mc]], xs[mc], xsB[mc], junk, junkB)
                for sl in range(2):
                    W_, WB, uid = ring_get()
                    assert uid == SLOT_MK + sl
                    for cc in range(4):
                        c = sl * 4 + cc
                        bank, bB = nbank()
                        for kc in range(8):
                            MM(bank[:, 0:256], W_[:, cc * 1024 + kc * 128: cc * 1024 + (kc + 1) * 128],
                               mnT[:, kc * 256:(kc + 1) * 256], kc == 0, kc == 7, R=[WB] + mnB, W=[bB])
                        COPY(mkT[:, c * 256:(c + 1) * 256], bank[:, 0:256], R=[bB], W=[mkB])
                for fh in range(2):
                    W_, WB, uid = ring_get()
                    assert uid == SLOT_MV + fh
                    for mc in range(2):
                        bank, bB = nbank()
                        for kc in range(8):
                            MM(bank, mnT[:, kc * 256 + mc * 128: kc * 256 + (mc + 1) * 128],
                               W_[:, kc * 512:(kc + 1) * 512], kc == 0, kc == 7, R=[WB, mnB[mc]], W=[bB])
                        COPY(mv[:, mc * 1024 + fh * 512: mc * 1024 + (fh + 1) * 512], bank, R=[bB], W=[mvB])

                def postres(src, srcB, gi, tb, oi):
                    ss, ssB = small()
                    ACT(junk[:, :], src, AF.Square, R=srcB, W=[junkB, ssB], accum_out=ss[:, 0:1])
                    rstd_from_ss(ss, ssB, 1, [1.0 / 1024])
                    to, toB = tmpo[oi % 2], tmpoB[oi % 2]
                    STT(to[:, :], src, ss[:, 0:1], G[gi][:, :], ALU.mult, ALU.mult, R=list(srcB) + [ssB, GB[gi]], W=[toB])
                    TT(Ht[:, tb * 1024:(tb + 1) * 1024], Ht[:, tb * 1024:(tb + 1) * 1024], to[:, :], ALU.add,
                       R=[HtB[tb], toB], W=[HtB[tb]])

                oi = 0
                for tt in range(NQT):
                    for tb in range(4):
                        sch.dma("pool", Ht[:, tb * 1024:(tb + 1) * 1024],
                                x_d[s, tt * 512 + tb * 128: tt * 512 + (tb + 1) * 128, :], W=[HtB[tb]])
                    W0, W0B, _ = ring_get()
                    W1, W1B, _ = ring_get()
                    for tb in range(4):
                        pair, pB = npair()
                        for fh, (W_, WB) in enumerate(((W0, W0B), (W1, W1B))):
                            for kc in range(8):
                                MM(pair[:, fh * 512:(fh + 1) * 512],
                                   yT[:, kc * S + tt * 512 + tb * 128: kc * S + tt * 512 + (tb + 1) * 128],
                                   W_[:, kc * 512:(kc + 1) * 512], kc == 0, kc == 7, R=[yTB[tt], WB], W=[pB[fh]])
                        postres(pair, pB, 0, tb, oi)
                        oi += 1
                        if F_P2EARLY:
                            nt(Ht[:, tb * 1024:(tb + 1) * 1024], [HtB[tb]], 1024, pv[:, GPMEM:GPMEM + 8],
                               xn23[:, :, tb * 128:(tb + 1) * 128], [xn2B[tb]], xs[tb % 2], xsB[tb % 2], junk, junkB)
                    if not F_P2EARLY:
                        for tb in range(4):
                            nt(Ht[:, tb * 1024:(tb + 1) * 1024], [HtB[tb]], 1024, pv[:, GPMEM:GPMEM + 8],
                               xn23[:, :, tb * 128:(tb + 1) * 128], [xn2B[tb]], xs[tb % 2], xsB[tb % 2], junk, junkB)
                    for sl in range(2):
                        W_, WB, _ = ring_get()
                        for cc in range(4):
                            c = sl * 4 + cc
                            bank, bB = nbank()
                            for kc in range(8):
                                MM(bank, W_[:, cc * 1024 + kc * 128: cc * 1024 + (kc + 1) * 128],
                                   xn2[:, kc * 512:(kc + 1) * 512], kc == 0, kc == 7, R=[WB] + xn2B, W=[bB])
                            COPY(PL[:, c * 512:(c + 1) * 512], bank, R=[bB], W=[PLB[c]])
                    pc = 0
                    for h in range(4):
                        Ps = []
                        for mc in range(2):
                            bank, bB = nbank()
                            for dc in range(2):
                                c = h * 2 + dc
                                MM(bank, mkT[:, c * 256 + mc * 128: c * 256 + (mc + 1) * 128],
                                   PL[:, c * 512:(c + 1) * 512], dc == 0, dc == 1, R=[mkB, PLB[c]], W=[bB])
                            P, PB = Pb[pc % 4], PbB[pc % 4]
                            pc += 1
                            ACT(P[:, :], bank, AF.Exp, R=[bB], W=[PB], scale=mem_scale)
                            Ps.append((P, PB))
                        dn, dnB = nbank()
                        for mc in range(2):
                            MM(dn, onesb[:, :], Ps[mc][0][:, :], mc == 0, mc == 1, R=[constB, Ps[mc][1]], W=[dnB])
                        rd, rdB = rden[h % 2], rdenB[h % 2]
                        RECIP(rd[:, :], dn, R=[dnB], W=[rdB])
                        for dvc in range(2):
                            c = h * 2 + dvc
                            bank, bB = nbank()
                            for mc in range(2):
                                MM(bank, mv[:, mc * 1024 + c * 128: mc * 1024 + (c + 1) * 128], Ps[mc][0][:, :],
                                   mc == 0, mc == 1, R=[mvB, Ps[mc][1]], W=[bB])
                            TT(PL[:, (8 + c) * 512:(9 + c) * 512], bank, rd[:, :], ALU.mult, R=[bB, rdB], W=[PLB[8 + c]])
                    W0, W0B, _ = ring_get()
                    W1, W1B, _ = ring_get()
                    for tb in range(4):
                        pair, pB = npair()
                        for fh, (W_, WB) in enumerate(((W0, W0B), (W1, W1B))):
                            for kc in range(8):
                                MM(pair[:, fh * 512:(fh + 1) * 512],
                                   PL[:, (8 + kc) * 512 + tb * 128: (8 + kc) * 512 + (tb + 1) * 128],
                                   W_[:, kc * 512:(kc + 1) * 512], kc == 0, kc == 7, R=[PLB[8 + kc], WB], W=[pB[fh]])
                        postres(pair, pB, 1, tb, oi)
                        oi += 1
                        if F_P2EARLY:
                            nt(Ht[:, tb * 1024:(tb + 1) * 1024], [HtB[tb]], 1024, pv[:, GPF:GPF + 8],
                               xn23[:, :, tb * 128:(tb + 1) * 128], [xn2B[tb]], xs[tb % 2], xsB[tb % 2], junk, junkB)
                    if not F_P2EARLY:
                        for tb in range(4):
                            nt(Ht[:, tb * 1024:(tb + 1) * 1024], [HtB[tb]], 1024, pv[:, GPF:GPF + 8],
                               xn23[:, :, tb * 128:(tb + 1) * 128], [xn2B[tb]], xs[tb % 2], xsB[tb % 2], junk, junkB)
                    for jj in range(11):
                        W_, WB, _ = ring_get()
                        for jl in range(2):
                            j = jj * 2 + jl
                            bg, bgB = nbank()
                            for kc in range(8):
                                MM(bg, W_[:, (jl * 2 + 0) * 1024 + kc * 128: (jl * 2 + 0) * 1024 + (kc + 1) * 128],
                                   xn2[:, kc * 512:(kc + 1) * 512], kc == 0, kc == 7, R=[WB] + xn2B, W=[bgB])
                            bu, buB = nbank()
                            for kc in range(8):
                                MM(bu, W_[:, (jl * 2 + 1) * 1024 + kc * 128: (jl * 2 + 1) * 1024 + (kc + 1) * 128],
                                   xn2[:, kc * 512:(kc + 1) * 512], kc == 0, kc == 7, R=[WB] + xn2B, W=[buB])
                            g_, gB_ = sg[j % 2], sgB[j % 2]
                            ACT(g_[:, :], bg, AF.Silu, R=[bgB], W=[gB_])
                            TT(PL[:, j * 512:(j + 1) * 512], bu, g_[:, :], ALU.mult, R=[buB, gB_], W=[PLB[j]])
                    for fh in range(2):
                        for sl3 in range(3):
                            W_, WB, _ = ring_get()
                            nj = 8 if sl3 < 2 else 6
                            for jl in range(nj):
                                j = sl3 * 8 + jl
                                for tb in range(4):
                                    MM(PS[:, tb * 512:(tb + 1) * 512], PL[:, j * 512 + tb * 128: j * 512 + (tb + 1) * 128],
                                       W_[:, jl * 512:(jl + 1) * 512], j == 0, j == NJ - 1, R=[PLB[j], WB], W=[bankB[tb]])
                        for tb in range(4):
                            COPY(Fsb[:, tb * 1024 + fh * 512: tb * 1024 + (fh + 1) * 512], PS[:, tb * 512:(tb + 1) * 512],
                                 R=[bankB[tb]], W=[FsbB[tb]])
                    st["bank"] = 4
                    for tb in range(4):
                        postres(Fsb[:, tb * 1024:(tb + 1) * 1024], [FsbB[tb]], 2, tb, oi)
                        oi += 1
                        sch.dma("pool", out_d[s, tt * 512 + tb * 128: tt * 512 + (tb + 1) * 128, :],
                                Ht[:, tb * 1024:(tb + 1) * 1024], R=[HtB[tb]])
                sch.barrier(bar[:, 0:1])
        sch.final_wait()

    with nc.Block() as block:
        @block.sync
        def _(sync):
            body()
    es.close()
    return nc, sch


def _kc(W):
    K, N = W.shape
    return np.ascontiguousarray(W.reshape(K // 128, 128, N).transpose(1, 0, 2)).reshape(128, -1)


def _cols(v):
    return np.ascontiguousarray(v.reshape(-1, 128).T)


def pack_shared(inp):
    f = np.float32
    w_in = inp["w_in"][0]
    winA = _kc(w_in[:, 0:1024])
    winB = _kc(np.concatenate([w_in[:, 1024:1408], w_in[:, 1664:1728], w_in[:, 1664:1728], w_in[:, 1408:1664]], axis=1))
    w_uq = inp["w_uq"][0].reshape(384, 4, 192)
    wuq = _kc(np.concatenate([w_uq[:, :, 0:128].reshape(384, 512), w_uq[:, :, 128:192].reshape(384, 256)], axis=1))
    w_ukv = inp["w_ukv"][0].reshape(256, 4, 256)
    wukv = _kc(np.concatenate([w_ukv[:, :, 0:128].reshape(256, 512), w_ukv[:, :, 128:256].reshape(256, 512)], axis=1))
    wlru = np.zeros((128, 4, 2, 128), f)
    for c in range(4):
        for g, key in enumerate(("lru_wa", "lru_wx")):
            for hh in range(2):
                wlru[hh * 64:(hh + 1) * 64, c, g, hh * 64:(hh + 1) * 64] = inp[key][0][2 * c + hh]
    wlru = wlru.reshape(128, -1)
    slots = np.zeros((NSLOT, 128, 4096), f)

    def moving(W, fh):
        return _kc(W[:, fh * 512:(fh + 1) * 512])

    def stationary(W, sl):
        Wr = W.reshape(8, 128, 8, 128)[:, :, sl * 4:(sl + 1) * 4, :]
        return np.ascontiguousarray(Wr.transpose(1, 2, 0, 3)).reshape(128, -1)

    for fh in range(2):
        slots[0 + fh] = moving(inp["w_out"][0], fh)
        slots[2 + fh] = stationary(inp["w_mq"][0], fh)
        slots[4 + fh] = moving(inp["w_mo"][0], fh)
        slots[SLOT_MK + fh] = stationary(inp["w_mk"][0], fh)
        slots[SLOT_MV + fh] = moving(inp["w_mv"][0], fh)
    wg = inp["w_gate"][0].reshape(8, 128, NJ, 128)
    wu = inp["w_up"][0].reshape(8, 128, NJ, 128)
    for jj in range(11):
        blk = np.zeros((128, 2, 2, 8, 128), f)
        for jl in range(2):
            j = jj * 2 + jl
            blk[:, jl, 0] = wg[:, :, j, :].transpose(1, 0, 2)
            blk[:, jl, 1] = wu[:, :, j, :].transpose(1, 0, 2)
        slots[6 + jj] = blk.reshape(128, -1)
    wd = inp["w_down"][0].reshape(NJ, 128, 1024)
    for fh in range(2):
        for sl3 in range(3):
            nj = 8 if sl3 < 2 else 6
            blk = np.zeros((128, 8, 512), f)
            blk[:, 0:nj, :] = wd[sl3 * 8: sl3 * 8 + nj, :, fh * 512:(fh + 1) * 512].transpose(1, 0, 2)
            slots[17 + fh * 3 + sl3] = blk.reshape(128, -1)
    pv = np.zeros((128, NPV), f)
    pv[:, GPM:GPM + 8] = _cols(inp["g_pre_mix"][0])
    for k in range(4):
        pv[:, CW + k * 4: CW + k * 4 + 4] = _cols(inp["conv_w"][0][k])
    pv[:, CB:CB + 4] = _cols(inp["conv_b"][0])
    pv[:, BA:BA + 4] = _cols(inp["lru_ba"][0])
    pv[:, BX:BX + 4] = _cols(inp["lru_bx"][0])
    pv[:, LAM:LAM + 4] = _cols(inp["lru_lambda"][0])
    pv[:, GQ:GQ + 3] = _cols(inp["g_q_lat"][0])
    pv[:, GKV:GKV + 2] = _cols(inp["g_kv_lat"][0])
    pv[:, GLO:GLO + 4] = _cols(inp["g_lru_out"][0])
    pv[:, GMO:GMO + 4] = _cols(inp["g_mla_out"][0])
    pv[:, GPMEM:GPMEM + 8] = _cols(inp["g_pre_mem"][0])
    pv[:, GMKV:GMKV + 8] = _cols(inp["g_mem_kv"][0])
    pv[:, GPF:GPF + 8] = _cols(inp["g_pre_ffn"][0])
    grow = np.stack([np.broadcast_to(inp[k][0][None, :], (128, 1024)) for k in
                     ("g_post_mix", "g_post_mem", "g_post_ffn")]).astype(f)
    invf = (10000.0 ** (-np.arange(0, 64, 2, dtype=np.float32) / 64)).astype(f)
    invf = np.ascontiguousarray(np.broadcast_to(invf[None, :], (128, 32)))
    ident = np.eye(128, dtype=f)
    tri = (np.arange(128)[None, :] >= np.arange(128)[:, None]).astype(f)
    return dict(pv=pv, grow=np.ascontiguousarray(grow), invf=invf, ident=ident, tri=tri, winA=winA, winB=winB,
                wuq=wuq, wukv=wukv, wlru=wlru, wslots=slots)


def pack_core(inp, b0, nseq, S):
    x = np.ascontiguousarray(inp["x"][b0:b0 + nseq], dtype=np.float32)
    mem = np.ascontiguousarray(inp["mem"][b0:b0 + nseq], dtype=np.float32)
    pos = np.asarray(inp["positions"][b0:b0 + nseq], dtype=np.int32)
    pos = np.ascontiguousarray(pos.reshape(nseq, S // 128, 128).transpose(2, 0, 1).reshape(128, -1))
    return dict(x=x, mem=mem, pos=pos)


_CACHE = {}


def kernel(**inputs):
    inp = {k: np.asarray(v) for k, v in inputs.items()}
    B, S, _ = inp["x"].shape
    nseq = B // NCORES
    key = (S, nseq)
    if key not in _CACHE:
        _CACHE[key] = build(S, nseq)[0]
    nc = _CACHE[key]
    shared = pack_shared(inp)
    in_maps = []
    for c in range(NCORES):
        m = dict(shared)
        m.update(pack_core(inp, c * nseq, nseq, S))
        in_maps.append(m)
    res = run_bass_kernel_spmd(nc, in_maps, core_ids=list(range(NCORES)))
    out = np.concatenate([np.asarray(r["out"]) for r in res.results], axis=0)
    return out.astype(np.float32)
```

```python
import math
from contextlib import ExitStack

import numpy as np
import concourse.bass as bass
import concourse.mybir as mybir
from concourse.bass_utils import run_bass_kernel_spmd

F32 = mybir.dt.float32
BF16 = mybir.dt.bfloat16
I32 = mybir.dt.int32
AF = mybir.ActivationFunctionType
ALU = mybir.AluOpType

NCORES = 8
D = 1024
DFF = 2816
NJ = DFF // 128
NMEM = 256
EPS = 1e-6
NSLOT = 27
SLOT_MK, SLOT_MV = 23, 25
GPM, CW, CB, BA, BX, LAM, GQ, GKV, GLO, GMO, GPMEM, GMKV, GPF = 0, 8, 24, 28, 32, 36, 40, 43, 45, 49, 53, 61, 69
NPV = 77
HC, H2, BAH, BXH = 77, 81, 85, 89
NPVT = 96
SEM_LIMIT = 12000
NRING = 5
import os
F_PIPEA = os.environ.get('K_PIPEA', '1') == '1'
F_LATSKEW = os.environ.get('K_LATSKEW', '0') == '1'
F_P2EARLY = os.environ.get('K_P2EARLY', '0') == '1'
F_WPRE = os.environ.get('K_WPRE', '0') == '1'


class Buf:
    __slots__ = ("name", "w", "r", "ld", "ldc", "ldk", "st", "stc", "stk")

    def __init__(self, name):
        self.name = name
        self.w = {}
        self.r = {}
        self.ld = None
        self.ldc = 0
        self.ldk = None
        self.st = None
        self.stc = 0
        self.stk = None


class _Eng:
    def __init__(self, sch, name, h):
        self.sch = sch
        self.name = name
        self.h = h
        self.sem = None
        self.key = None
        self.cnt = 0
        self.nsem = 0
        self.known = {}
        self.pending = False

    def newsem(self):
        self.sem = self.sch.alloc_sem(f"e_{self.name}{self.nsem}")
        self.key = (self.name, self.nsem)
        self.nsem += 1
        self.cnt = 0


class Sched:
    def __init__(self, nc, es):
        self.nc = nc
        self.es = es
        self.nsems = 0
        self.E = {
            "pe": _Eng(self, "pe", nc.tensor),
            "act": _Eng(self, "act", nc.scalar),
            "dve": _Eng(self, "dve", nc.vector),
            "pool": _Eng(self, "pool", nc.gpsimd),
            "sp": _Eng(self, "sp", nc.sync),
        }
        for e in self.E.values():
            e.newsem()
        self.dma_tokens = {}
        self.nwaits = 0
        self.ninst = 0

    def alloc_sem(self, name):
        self.nsems += 1
        return self.es.enter_context(self.nc.semaphore(name))

    @staticmethod
    def _merge(d, src):
        for k, (s, v) in src.items():
            if k not in d or d[k][1] < v:
                d[k] = (s, v)

    def _wait(self, E, key, sem, val):
        if E.known.get(key, 0) >= val:
            return
        E.h.wait_ge(sem, val)
        E.known[key] = val
        self.nwaits += 1

    def _deps(self, E, R, W, skip_keys=()):
        deps = {}
        for b in R:
            self._merge(deps, b.w)
        for b in W:
            self._merge(deps, b.w)
            self._merge(deps, b.r)
        for key, (sem, val) in deps.items():
            if key in skip_keys:
                continue
            if E.name == "pe" and key[0] == "pe":
                continue
            self._wait(E, key, sem, val)

    def op(self, en, fn, R=(), W=(), inc=True):
        E = self.E[en]
        self._deps(E, R, W)
        ins = fn(E.h)
        self.ninst += 1
        if inc:
            if E.cnt >= SEM_LIMIT and not E.pending:
                E.newsem()
            E.cnt += 1
            ins.then_inc(E.sem, 1)
            E.pending = False
            tok = (E.key, E.sem, E.cnt)
        else:
            E.pending = True
            tok = (E.key, E.sem, E.cnt + 1)
        for b in R:
            k = tok[0]
            if k not in b.r or b.r[k][1] < tok[2]:
                b.r[k] = (tok[1], tok[2])
        for b in W:
            b.w = {tok[0]: (tok[1], tok[2])}
            b.r = {}
        return ins

    def dma(self, q, out, in_, R=(), W=(), **kw):
        E = self.E[q]
        skip = ()
        if W and W[0].ldk is not None:
            skip = (W[0].ldk,)
        self._deps(E, R, W, skip_keys=skip)
        ins = E.h.dma_start(out=out, in_=in_, **kw)
        self.ninst += 1
        if W:
            b = W[0]
            if b.ld is None:
                b.ld = self.alloc_sem("ld_" + b.name)
                b.ldk = ("ld", b.name)
            b.ldc += 16
            ins.then_inc(b.ld, 16)
            tok = (b.ldk, b.ld, b.ldc)
        else:
            b = R[0]
            if b.st is None:
                b.st = self.alloc_sem("st_" + b.name)
                b.stk = ("st", b.name)
            b.stc += 16
            ins.then_inc(b.st, 16)
            tok = (b.stk, b.st, b.stc)
        self.dma_tokens[tok[0]] = (tok[1], tok[2])
        for bb in R:
            k = tok[0]
            if k not in bb.r or bb.r[k][1] < tok[2]:
                bb.r[k] = (tok[1], tok[2])
        for bb in W:
            if bb is W[0] and skip:
                bb.w[tok[0]] = (tok[1], tok[2])
                bb.r = {}
            else:
                bb.w = {tok[0]: (tok[1], tok[2])}
                bb.r = {}
        return ins

    def barrier(self, bar_ap):
        Dv = self.E["dve"]
        for E in self.E.values():
            if E is Dv:
                continue
            assert not E.pending
            if E.cnt > 0:
                self._wait(Dv, E.key, E.sem, E.cnt)
        for k, (s, v) in self.dma_tokens.items():
            self._wait(Dv, k, s, v)
        if Dv.cnt > 0:
            self._wait(Dv, Dv.key, Dv.sem, Dv.cnt)
        ins = Dv.h.memset(bar_ap, 0.0)
        if Dv.cnt >= SEM_LIMIT:
            Dv.newsem()
        Dv.cnt += 1
        ins.then_inc(Dv.sem, 1)
        for E in self.E.values():
            if E is Dv:
                continue
            self._wait(E, Dv.key, Dv.sem, Dv.cnt)

    def final_wait(self):
        sp = self.E["sp"]
        for k, (s, v) in self.dma_tokens.items():
            self._wait(sp, k, s, v)
        for E in self.E.values():
            if E is sp or E.cnt == 0:
                continue
            self._wait(sp, E.key, E.sem, E.cnt)


def build(S=2048, NSEQ=2, dbg=False):
    NTB = S // 128
    NQT = S // 512
    nc = bass.Bass("TRN2", target_bir_lowering=False)

    def din(name, shape, dt=F32):
        return nc.dram_tensor(name, list(shape), dt, kind="ExternalInput").ap()

    x_d = din("x", [NSEQ, S, D])
    mem_d = din("mem", [NSEQ, NMEM, D])
    pos_d = din("pos", [128, NSEQ * NTB], I32)
    pv_d = din("pv", [128, NPV])
    grow_d = din("grow", [3, 128, D])
    invf_d = din("invf", [128, 32])
    ident_d = din("ident", [128, 128])
    tri_d = din("tri", [128, 128])
    winA_d = din("winA", [128, 8 * 1024])
    winB_d = din("winB", [128, 8 * 768])
    wuq_d = din("wuq", [128, 3 * 768])
    wukv_d = din("wukv", [128, 2 * 1024])
    wlru_d = din("wlru", [128, 4 * 2 * 128])
    wsl_d = din("wslots", [NSLOT, 128, 4096])
    scr_d = nc.dram_tensor("scr", [NSLOT, 128, 4096], BF16, kind="Internal").ap()
    out_d = nc.dram_tensor("out", [NSEQ, S, D], F32, kind="ExternalOutput").ap()
    if dbg:
        dbg_yT = nc.dram_tensor("dbg_yT", [NSEQ, 128, 8 * S], BF16, kind="ExternalOutput").ap()

    es = ExitStack()
    sch = Sched(nc, es)

    def T(scope, name, cols, dt):
        return scope.enter_context(nc.sbuf_tensor("sb_" + name, [128, cols], dt))

    mla_scale = 1.0 / math.sqrt(192.0)
    mem_scale = 1.0 / math.sqrt(256.0)

    def body():
        pers = es
        ident = T(pers, "ident", 128, BF16)
        tri = T(pers, "tri", 128, BF16)
        onesb = T(pers, "onesb", 128, BF16)
        onesf = T(pers, "onesf", 128, F32)
        pv = T(pers, "pv", NPVT, F32)
        cst = T(pers, "cst", 8, F32)
        invf = T(pers, "invf", 32, F32)
        posi = T(pers, "posi", NSEQ * NTB, I32)
        smallt = T(pers, "smallt", 32 * 4, F32)
        bar = T(pers, "bar", 4, F32)
        yT = T(pers, "yT", 8 * S, BF16)
        PS = pers.enter_context(nc.psum_tensor("ps", [128, 6 * 512], F32))
        TPS = pers.enter_context(nc.psum_tensor("tps", [128, 2 * 1024], BF16))

        constB = Buf("const")
        identB = Buf("identb")
        triB = Buf("trib")
        pvB = Buf("pvb")
        invfB = Buf("invfb")
        posB = Buf("posb")
        bankB = [Buf(f"bank{i}") for i in range(6)]
        tpB = [Buf(f"tpb{i}") for i in range(2)]
        smallB = [Buf(f"small{i}") for i in range(32)]
        yTB = [Buf(f"yT{i}") for i in range(NQT)]
        scrB = Buf("scr")
        st = {"small": 0, "bank": 0, "pair": 0, "tp": 0, "alt": 0}

        def small():
            i = st["small"] % 32
            st["small"] += 1
            return smallt[:, i * 4:(i + 1) * 4], smallB[i]

        def nbank():
            i = st["bank"] % 6
            st["bank"] += 1
            return PS[:, i * 512:(i + 1) * 512], bankB[i]

        def npair():
            i = st["pair"] % 3
            st["pair"] += 1
            return PS[:, i * 1024:(i + 1) * 1024], [bankB[2 * i], bankB[2 * i + 1]]

        def ntp():
            i = st["tp"] % 2
            st["tp"] += 1
            return TPS[:, i * 1024:(i + 1) * 1024], tpB[i]

        def MM(out, lhsT, rhs, start, stop, R, W, inc=None):
            return sch.op("pe", lambda e: e.matmul(out, lhsT=lhsT, rhs=rhs, start=start, stop=stop),
                          R=R, W=W, inc=(stop if inc is None else inc))

        def TR(out, in_, R, W, last):
            return sch.op("pe", lambda e: e.transpose(out, in_, ident[:]), R=list(R) + [identB], W=W, inc=last)

        def ACT(out, in_, func, R, W, **kw):
            return sch.op("act", lambda e: e.activation(out=out, in_=in_, func=func, **kw), R=R, W=W)

        def TS(out, in0, s1, s2, op0, op1, R, W, eng="dve"):
            if s2 is None:
                return sch.op(eng, lambda e: e.tensor_scalar(out=out, in0=in0, scalar1=s1, scalar2=None, op0=op0),
                              R=R, W=W)
            return sch.op(eng, lambda e: e.tensor_scalar(out=out, in0=in0, scalar1=s1, scalar2=s2, op0=op0, op1=op1),
                          R=R, W=W)

        def TT(out, in0, in1, op, R, W, eng="dve"):
            return sch.op(eng, lambda e: e.tensor_tensor(out=out, in0=in0, in1=in1, op=op), R=R, W=W)

        def STT(out, in0, scalar, in1, op0, op1, R, W):
            return sch.op("dve", lambda e: e.scalar_tensor_tensor(out=out, in0=in0, scalar=scalar, in1=in1,
                                                                  op0=op0, op1=op1), R=R, W=W)

        def COPY(out, in_, R, W, eng=None):
            if eng is None:
                eng = "act" if st["alt"] % 2 == 0 else "dve"
                st["alt"] += 1
            if eng == "act":
                return sch.op("act", lambda e: e.activation(out=out, in_=in_, func=AF.Copy), R=R, W=W)
            return sch.op(eng, lambda e: e.tensor_copy(out=out, in_=in_), R=R, W=W)

        def RECIP(out, in_, R, W):
            return sch.op("dve", lambda e: e.reciprocal(out=out, in_=in_), R=R, W=W)

        def rstd_from_ss(ss, ssB, n, invd):
            for i in range(n):
                TS(ss[:, i:i + 1], ss[:, i:i + 1], invd[i], EPS, ALU.mult, ALU.add, R=[ssB], W=[ssB])
            ACT(ss[:, 0:n], ss[:, 0:n], AF.Sqrt, R=[ssB], W=[ssB])
            RECIP(ss[:, 0:n], ss[:, 0:n], R=[ssB], W=[ssB])

        sch.dma("sp", pv[:, 0:NPV], pv_d[:, :], W=[pvB])
        sch.dma("sp", invf[:], invf_d[:, :], W=[invfB])
        sch.dma("sp", posi[:], pos_d[:, :], W=[posB])
        sch.dma("pool", ident[:], ident_d[:, :], W=[identB])
        sch.dma("pool", tri[:], tri_d[:, :], W=[triB])
        sch.op("dve", lambda e: e.memset(onesb[:], 1.0), W=[constB])
        sch.op("dve", lambda e: e.memset(onesf[:], 1.0), W=[constB])
        sch.op("dve", lambda e: e.memset(cst[:, 0:1], 1.0), W=[constB])
        sch.op("dve", lambda e: e.memset(cst[:, 1:2], math.pi), W=[constB])
        tmpc, tmpB = small()
        tmpc2, tmpB2 = small()
        TS(tmpc2, pv[:, LAM:LAM + 4], -1.0, None, ALU.mult, None, R=[pvB], W=[tmpB2])
        TT(tmpc, pv[:, LAM:LAM + 4], tmpc2, ALU.max, R=[pvB, tmpB2], W=[tmpB])
        ACT(tmpc, tmpc, AF.Exp, R=[tmpB], W=[tmpB], scale=-1.0)
        TS(tmpc, tmpc, 1.0, None, ALU.add, None, R=[tmpB], W=[tmpB])
        ACT(tmpc, tmpc, AF.Ln, R=[tmpB], W=[tmpB])
        TS(tmpc2, tmpc2, 0.0, None, ALU.max, None, R=[tmpB2], W=[tmpB2])
        TT(tmpc, tmpc, tmpc2, ALU.add, R=[tmpB, tmpB2], W=[tmpB])
        TS(pv[:, HC:HC + 4], tmpc, -4.0, None, ALU.mult, None, R=[tmpB], W=[pvB])
        TS(pv[:, H2:H2 + 4], tmpc, -8.0, None, ALU.mult, None, R=[tmpB], W=[pvB])
        TS(pv[:, BAH:BAH + 4], pv[:, BA:BA + 4], 0.5, None, ALU.mult, None, R=[pvB], W=[pvB])
        TS(pv[:, BXH:BXH + 4], pv[:, BX:BX + 4], 0.5, None, ALU.mult, None, R=[pvB], W=[pvB])

        scr_state = {"done": False}

        def convert_scratch():
            if scr_state["done"]:
                return
            scr_state["done"] = True
            order = [23, 24, 25, 26] + list(range(23))
            for sl in order:
                sch.dma("pool", scr_d[sl].rearrange("p (a b) -> p a b", a=2),
                        wsl_d[sl].rearrange("p (a b) -> p a b", a=2), W=[scrB])

        def nt_a(src, srcB, Dn, junk, junkB):
            ss, ssB = small()
            ACT(junk[:, 0:Dn], src, AF.Square, R=srcB, W=[junkB, ssB], accum_out=ss[:, 0:1])
            rstd_from_ss(ss, ssB, 1, [1.0 / Dn])
            return ss, ssB

        def nt_b(src, srcB, Dn, ss, ssB, gcols, dst3, dstB, xs, xsB):
            nk = Dn // 128
            ACT(xs[:, 0:Dn], src, AF.Copy, R=list(srcB) + [ssB], W=[xsB], scale=ss[:, 0:1])
            tp, tB = ntp()
            for k in range(nk):
                TR(tp[:, k * 128:(k + 1) * 128], xs[:, k * 128:(k + 1) * 128], R=[xsB], W=[tB], last=(k == nk - 1))
            g3 = gcols.unsqueeze(2).broadcast_to([128, nk, 128])
            TT(dst3, tp[:, 0:Dn].rearrange("p (k t) -> p k t", k=nk), g3, ALU.mult, R=[tB, pvB], W=dstB)

        def nt(src, srcB, Dn, gcols, dst3, dstB, xs, xsB, junk, junkB):
            ss, ssB = nt_a(src, srcB, Dn, junk, junkB)
            nt_b(src, srcB, Dn, ss, ssB, gcols, dst3, dstB, xs, xsB)

        def rope(src3, srcB, G, cos, sin, csB, dst3, dstB, tA, tAB, tB_, tBB):
            x1 = src3[:, :, 0:32]
            x2 = src3[:, :, 32:64]
            cb = cos.unsqueeze(1).broadcast_to([128, G, 32])
            sb = sin.unsqueeze(1).broadcast_to([128, G, 32])
            a3 = tA[:, 0:G * 32].rearrange("p (g d) -> p g d", g=G)
            b3 = tB_[:, 0:G * 32].rearrange("p (g d) -> p g d", g=G)
            TT(a3, x1, cb, ALU.mult, R=srcB + [csB], W=[tAB])
            TT(b3, x2, sb, ALU.mult, R=srcB + [csB], W=[tBB])
            TT(dst3[:, :, 0:32], a3, b3, ALU.subtract, R=[tAB, tBB], W=dstB)
            TT(a3, x2, cb, ALU.mult, R=srcB + [csB], W=[tAB])
            TT(b3, x1, sb, ALU.mult, R=srcB + [csB], W=[tBB])
            TT(dst3[:, :, 32:64], a3, b3, ALU.add, R=[tAB, tBB], W=dstB)

        def feat_norm(srcs, srcBs, gcol0, Dn, tt, dst_off, sq, sqB, rt, rtB):
            n = len(srcs)
            bank, bB = nbank()
            for i in range(n):
                q, qB = sq[i % 2], sqB[i % 2]
                ACT(q[:, :], srcs[i][:, tt * 512:(tt + 1) * 512], AF.Square, R=[srcBs[i]], W=[qB])
                MM(bank, onesf[:], q[:, :], i == 0, i == n - 1, R=[qB, constB], W=[bB], inc=True)
            TS(rt[:, :], bank, 1.0 / Dn, EPS, ALU.mult, ALU.add, R=[bB], W=[rtB])
            ACT(rt[:, :], rt[:, :], AF.Sqrt, R=[rtB], W=[rtB])
            RECIP(rt[:, :], rt[:, :], R=[rtB], W=[rtB])
            for i in range(n):
                c = dst_off + i
                STT(yT[:, c * S + tt * 512: c * S + (tt + 1) * 512], srcs[i][:, tt * 512:(tt + 1) * 512],
                    pv[:, gcol0 + i:gcol0 + i + 1], rt[:, :], ALU.mult, ALU.mult,
                    R=[srcBs[i], pvB, rtB], W=[yTB[tt]])

        for s in range(NSEQ):
            sfx = f"_{s}"
            with ExitStack() as s1:
                cqnT = T(s1, "cqnT" + sfx, 3 * S, BF16)
                ckvnT = T(s1, "ckvnT" + sfx, 2 * S, BF16)
                kpeT = T(s1, "kpeT" + sfx, S, BF16)
                cs = T(s1, "cs" + sfx, NTB * 64, F32)
                latB = [Buf(f"lat{s}_{i}") for i in range(NTB)]
                csB = Buf(f"cs{s}")
                with ExitStack() as s2:
                    xnT = T(s2, "xnT" + sfx, 8 * S, BF16)
                    xnT3 = xnT[:, :].rearrange("p (k t) -> p k t", k=8)
                    xnTB = [Buf(f"xnT{s}_{i}") for i in range(NTB)]
                    winA = T(s2, "winA" + sfx, 8 * 1024, BF16)
                    wlru = T(s2, "wlru" + sfx, 4 * 2 * 128, BF16)
                    winAB = Buf(f"winA{s}")
                    wlruB = Buf(f"wlru{s}")
                    sch.dma("pool", winA[:, :].rearrange("p (k n) -> p k n", k=8),
                            winA_d[:, :].rearrange("p (k n) -> p k n", k=8), W=[winAB])
                    sch.dma("pool", wlru[:], wlru_d[:, :], W=[wlruB])
                    if F_WPRE:
                        convert_scratch()
                    with ExitStack() as s2a:
                        xtmp = [T(s2a, f"xtmp{i}" + sfx, 1024, F32) for i in range(3)]
                        xtmpB = [Buf(f"xtmp{s}_{i}") for i in range(3)]
                        xs = [T(s2a, f"xs{i}" + sfx, 1024, BF16) for i in range(2)]
                        xsB = [Buf(f"xs{s}_{i}") for i in range(2)]
                        junk = T(s2a, "junk" + sfx, 1024, BF16)
                        junkB = Buf(f"junk{s}")
                        ang = T(s2a, "ang" + sfx, NTB * 64, F32)
                        kf = T(s2a, "kf" + sfx, NTB * 64, F32)
                        ki = T(s2a, "ki" + sfx, NTB * 64, I32)
                        posf = T(s2a, "posf" + sfx, NTB, F32)
                        angB = Buf(f"ang{s}")
                        kfB = Buf(f"kf{s}")
                        kiB = Buf(f"ki{s}")
                        posfB = Buf(f"posf{s}")
                        pend = None
                        for tb in range(NTB + 1):
                            cur = None
                            if tb < NTB:
                                i = tb % 3
                                sch.dma("sp", xtmp[i][:], x_d[s, tb * 128:(tb + 1) * 128, :], W=[xtmpB[i]])
                                cur = (tb,) + nt_a(xtmp[i][:, :], [xtmpB[i]], 1024, junk, junkB)
                            if not F_PIPEA:
                                pend = cur
                                cur = None
                            if pend is not None:
                                ptb, pss, pssB = pend
                                pi = ptb % 3
                                nt_b(xtmp[pi][:, :], [xtmpB[pi]], 1024, pss, pssB, pv[:, GPM:GPM + 8],
                                     xnT3[:, :, ptb * 128:(ptb + 1) * 128], [xnTB[ptb]], xs[ptb % 2], xsB[ptb % 2])
                            pend = cur
                        COPY(posf[:], posi[:, s * NTB:(s + 1) * NTB], R=[posB], W=[posfB], eng="dve")
                        ang3 = ang[:, :].rearrange("p (t d) -> p t d", t=NTB)
                        TT(ang3[:, :, 0:32], posf[:, :].unsqueeze(2).broadcast_to([128, NTB, 32]),
                           invf[:, :].unsqueeze(1).broadcast_to([128, NTB, 32]), ALU.mult,
                           R=[posfB, invfB], W=[angB])
                        TS(ang3[:, :, 32:64], ang3[:, :, 0:32], math.pi / 2, None, ALU.add, None, R=[angB], W=[angB])
                        TS(kf[:], ang[:], 1.0 / (2 * math.pi), None, ALU.mult, None, R=[angB], W=[kfB])
                        COPY(ki[:], kf[:], R=[kfB], W=[kiB], eng="dve")
                        COPY(kf[:], ki[:], R=[kiB], W=[kfB], eng="dve")
                        C1 = 6.28125
                        C2 = 2 * math.pi - C1
                        STT(ang[:], kf[:], -C1, ang[:], ALU.mult, ALU.add, R=[kfB, angB], W=[angB])
                        STT(ang[:], kf[:], -C2, ang[:], ALU.mult, ALU.add, R=[kfB, angB], W=[angB])
                        TS(kf[:], ang[:], math.pi, None, ALU.is_gt, None, R=[angB], W=[kfB])
                        STT(ang[:], kf[:], -2 * math.pi, ang[:], ALU.mult, ALU.add, R=[kfB, angB], W=[angB])
                        TS(kf[:], ang[:], -math.pi, None, ALU.is_lt, None, R=[angB], W=[kfB])
                        STT(ang[:], kf[:], 2 * math.pi, ang[:], ALU.mult, ALU.add, R=[kfB, angB], W=[angB])
                        TS(ang[:], ang[:], math.pi, -math.pi, ALU.min, ALU.max, R=[angB], W=[angB])
                        ACT(cs[:], ang[:], AF.Sin, R=[angB], W=[csB])
                        sch.barrier(bar[:, 0:1])

                    with ExitStack() as s3:
                        convert_scratch()
                        NSEG = NQT
                        Rb = [T(s3, f"R{i}" + sfx, S, F32) for i in range(6)]
                        RB = [[Buf(f"R{s}_{i}_{g}") for g in range(NSEG)] for i in range(6)]
                        Ubf = T(s3, "Ubf" + sfx, S, BF16)
                        UbfB = [Buf(f"Ubf{s}_{g}") for g in range(NSEG)]
                        YL = [T(s3, f"YL{i}" + sfx, S, F32) for i in range(4)]
                        YLB = [[Buf(f"YL{s}_{i}_{g}") for g in range(NSEG)] for i in range(4)]
                        sq = [T(s3, f"sq{i}" + sfx, 512, F32) for i in range(2)]
                        sqB = [Buf(f"sq{s}_{i}") for i in range(2)]
                        rt = T(s3, "rt" + sfx, 512, F32)
                        rtB = Buf(f"rt{s}")
                        LX, LG, U, A, TI, R6 = Rb
                        LXB, LGB, UB, AB, TIB, R6B = RB
                        segs = [slice(g * 512, (g + 1) * 512) for g in range(NSEG)]
                        for c in range(4):
                            def col(base):
                                return pv[:, base + c:base + c + 1]
                            for g in range(NSEG):
                                sl = segs[g]
                                bank, bB = nbank()
                                for kc in range(8):
                                    MM(bank, winA[:, kc * 1024 + c * 128: kc * 1024 + (c + 1) * 128],
                                       xnT[:, kc * S + g * 512: kc * S + (g + 1) * 512], kc == 0, kc == 7,
                                       R=[winAB] + xnTB[g * 4:(g + 1) * 4], W=[bB])
                                COPY(LX[:, sl], bank, R=[bB], W=[LXB[g]], eng="act")
                                bank, bB = nbank()
                                for kc in range(8):
                                    MM(bank, winA[:, kc * 1024 + 512 + c * 128: kc * 1024 + 512 + (c + 1) * 128],
                                       xnT[:, kc * S + g * 512: kc * S + (g + 1) * 512], kc == 0, kc == 7,
                                       R=[winAB] + xnTB[g * 4:(g + 1) * 4], W=[bB])
                                COPY(LG[:, sl], bank, R=[bB], W=[LGB[g]], eng="dve")
                            for g in range(NSEG):
                                sl = segs[g]
                                e = (g + 1) * 512
                                TS(U[:, sl], LX[:, sl], pv[:, CW + 3 * 4 + c:CW + 3 * 4 + c + 1], col(CB), ALU.mult, ALU.add,
                                   R=[LXB[g], pvB], W=[UB[g]])
                                for k, sh in ((2, 1), (1, 2), (0, 3)):
                                    lo = max(g * 512, sh)
                                    rr = [LXB[g], pvB, UB[g]] + ([LXB[g - 1]] if g > 0 else [])
                                    STT(U[:, lo:e], LX[:, lo - sh:e - sh], pv[:, CW + k * 4 + c:CW + k * 4 + c + 1], U[:, lo:e],
                                        ALU.mult, ALU.add, R=rr, W=[UB[g]])
                                COPY(Ubf[:, sl], U[:, sl], R=[UB[g]], W=[UbfB[g]], eng="act")
                            for g in range(NSEG):
                                sl = segs[g]
                                bank, bB = nbank()
                                MM(bank, wlru[:, (c * 2 + 0) * 128:(c * 2 + 1) * 128], Ubf[:, sl], True, True,
                                   R=[wlruB, UbfB[g]], W=[bB])
                                ACT(LX[:, sl], bank, AF.Tanh, R=[bB, pvB], W=[LXB[g]], scale=0.5, bias=col(BAH))
                                bank, bB = nbank()
                                MM(bank, wlru[:, (c * 2 + 1) * 128:(c * 2 + 2) * 128], Ubf[:, sl], True, True,
                                   R=[wlruB, UbfB[g]], W=[bB])
                                ACT(TI[:, sl], bank, AF.Tanh, R=[bB, pvB], W=[TIB[g]], scale=0.5, bias=col(BXH))
                            for g in range(NSEG):
                                sl = segs[g]
                                ACT(A[:, sl], LX[:, sl], AF.Exp, R=[LXB[g], pvB], W=[AB[g]], scale=col(HC), bias=col(HC))
                                ACT(R6[:, sl], LX[:, sl], AF.Exp, R=[LXB[g], pvB], W=[R6B[g]], scale=col(H2), bias=col(H2))
                            for g in range(NSEG):
                                sl = segs[g]
                                ACT(LX[:, sl], LG[:, sl], AF.Square, R=[LGB[g]], W=[LXB[g]])
                                TS(LX[:, sl], LX[:, sl], 0.044715, 1.0, ALU.mult, ALU.add, R=[LXB[g]], W=[LXB[g]])
                                TT(LX[:, sl], LX[:, sl], LG[:, sl], ALU.mult, R=[LXB[g], LGB[g]], W=[LXB[g]])
                            for g in range(NSEG):
                                sl = segs[g]
                                ACT(LX[:, sl], LX[:, sl], AF.Tanh, R=[LXB[g]], W=[LXB[g]], scale=0.7978845608028654)
                            for g in range(NSEG):
                                sl = segs[g]
                                ACT(R6[:, sl], R6[:, sl], AF.Sqrt, R=[R6B[g], constB], W=[R6B[g]], scale=-1.0, bias=cst[:, 0:1])
                            for g in range(NSEG):
                                sl = segs[g]
                                STT(TI[:, sl], TI[:, sl], 1.0, U[:, sl], ALU.add, ALU.mult, R=[TIB[g], UB[g]], W=[TIB[g]])
                                STT(TI[:, sl], TI[:, sl], 0.5, R6[:, sl], ALU.mult, ALU.mult, R=[TIB[g], R6B[g]], W=[TIB[g]])
                            for g in range(NSEG):
                                sl = segs[g]
                                if g == 0:
                                    sch.op("dve", lambda e, sl=sl: e.tensor_tensor_scan(
                                        out=U[:, sl], data0=A[:, sl], data1=TI[:, sl], initial=0.0,
                                        op0=ALU.mult, op1=ALU.add), R=[AB[g], TIB[g]], W=[UB[g]])
                                else:
                                    sch.op("dve", lambda e, sl=sl, g=g: e.tensor_tensor_scan(
                                        out=U[:, sl], data0=A[:, sl], data1=TI[:, sl],
                                        initial=U[:, g * 512 - 1:g * 512],
                                        op0=ALU.mult, op1=ALU.add), R=[AB[g], TIB[g], UB[g - 1]], W=[UB[g]])
                                STT(LX[:, sl], LX[:, sl], 1.0, LG[:, sl], ALU.add, ALU.mult, R=[LXB[g], LGB[g]], W=[LXB[g]])
                                STT(YL[c][:, sl], LX[:, sl], 0.5, U[:, sl], ALU.mult, ALU.mult, R=[LXB[g], UB[g]], W=[YLB[c][g]])
                        for tt in range(NQT):
                            feat_norm([YL[i] for i in range(4)], [YLB[i][tt] for i in range(4)], GLO, 512, tt, 0, sq, sqB, rt, rtB)
                        sch.barrier(bar[:, 0:1])

                    with ExitStack() as s4:
                        winB = T(s4, "winB" + sfx, 8 * 768, BF16)
                        winBB = Buf(f"winB{s}")
                        sch.dma("pool", winB[:, :].rearrange("p (k n) -> p k n", k=8),
                                winB_d[:, :].rearrange("p (k n) -> p k n", k=8), W=[winBB])
                        lat = [T(s4, f"latbf{i}" + sfx, 768, BF16) for i in range(2)]
                        latbB = [Buf(f"latbf{s}_{i}") for i in range(2)]
                        junk = T(s4, "junkb" + sfx, 512, BF16)
                        junkB = Buf(f"junkb{s}")
                        rtmp = [T(s4, f"rtmp{i}" + sfx, 128, F32) for i in range(2)]
                        rtmpB = [Buf(f"rtmp{s}_{i}") for i in range(2)]
                        cqnT3 = cqnT[:, :].rearrange("p (k t) -> p k t", k=3)
                        ckvnT3 = ckvnT[:, :].rearrange("p (k t) -> p k t", k=2)
                        def lat_front(tb):
                            bA, bAB = nbank()
                            bBk, bBB = nbank()
                            for kc in range(8):
                                lhs = xnT[:, kc * S + tb * 128: kc * S + (tb + 1) * 128]
                                MM(bA, lhs, winB[:, kc * 768: kc * 768 + 512], kc == 0, kc == 7,
                                   R=[winBB, xnTB[tb]], W=[bAB])
                            for kc in range(8):
                                lhs = xnT[:, kc * S + tb * 128: kc * S + (tb + 1) * 128]
                                MM(bBk[:, 0:256], lhs, winB[:, kc * 768 + 512: kc * 768 + 768], kc == 0, kc == 7,
                                   R=[winBB, xnTB[tb]], W=[bBB])
                            ss, ssB = small()
                            ACT(junk[:, 0:384], bA[:, 0:384], AF.Square, R=[bAB], W=[junkB, ssB], accum_out=ss[:, 0:1])
                            ACT(junk[:, 0:256], bBk[:, 0:256], AF.Square, R=[bBB], W=[junkB, ssB], accum_out=ss[:, 1:2])
                            rstd_from_ss(ss, ssB, 2, [1.0 / 384, 1.0 / 256])
                            return (tb, bA, bAB, bBk, bBB, ss, ssB)

                        def lat_back(stt):
                            tb, bA, bAB, bBk, bBB, ss, ssB = stt
                            bsl = slice(tb * 128, (tb + 1) * 128)
                            L = lat[tb % 2]
                            LB = latbB[tb % 2]
                            ACT(L[:, 0:384], bA[:, 0:384], AF.Copy, R=[bAB, ssB], W=[LB], scale=ss[:, 0:1])
                            ACT(L[:, 384:640], bBk[:, 0:256], AF.Copy, R=[bBB, ssB], W=[LB], scale=ss[:, 1:2])
                            rope(bA[:, 384:512].rearrange("p (g d) -> p g d", g=2), [bAB], 2,
                                 cs[:, tb * 64 + 32: tb * 64 + 64], cs[:, tb * 64: tb * 64 + 32], csB,
                                 L[:, 640:768].rearrange("p (g d) -> p g d", g=2), [LB],
                                 rtmp[0], rtmpB[0], rtmp[1], rtmpB[1])
                            tp, tB = ntp()
                            for k in range(6):
                                TR(tp[:, k * 128:(k + 1) * 128], L[:, k * 128:(k + 1) * 128], R=[LB], W=[tB], last=(k == 5))
                            TT(cqnT3[:, :, bsl], tp[:, 0:384].rearrange("p (k t) -> p k t", k=3),
                               pv[:, GQ:GQ + 3].unsqueeze(2).broadcast_to([128, 3, 128]), ALU.mult,
                               R=[tB, pvB], W=[latB[tb]])
                            TT(ckvnT3[:, :, bsl], tp[:, 384:640].rearrange("p (k t) -> p k t", k=2),
                               pv[:, GKV:GKV + 2].unsqueeze(2).broadcast_to([128, 2, 128]), ALU.mult,
                               R=[tB, pvB], W=[latB[tb]])
                            COPY(kpeT[:, bsl], tp[:, 640:768], R=[tB], W=[latB[tb]], eng="act")

                        pend = None
                        for tb in range(NTB + 1):
                            cur = lat_front(tb) if tb < NTB else None
                            if not F_LATSKEW:
                                pend = cur
                                cur = None
                            if pend is not None:
                                lat_back(pend)
                            pend = cur
                        sch.barrier(bar[:, 0:1])
                with ExitStack() as s5:
                    qnT = T(s5, "qnT" + sfx, 4 * S, BF16)
                    qpeT = T(s5, "qpeT" + sfx, 2 * S, BF16)
                    knT = T(s5, "knT" + sfx, 4 * S, BF16)
                    Vt = T(s5, "Vt" + sfx, NTB * 512, BF16)
                    qB = [Buf(f"q{s}_{i}") for i in range(NQT)]
                    kB = [Buf(f"k{s}_{i}") for i in range(NQT)]
                    qpeB = [Buf(f"qpe{s}_{i}") for i in range(NTB)]
                    vB = [Buf(f"v{s}_{i}") for i in range(NTB)]
                    qpeT3 = qpeT[:, :].rearrange("p (k t) -> p k t", k=2)
                    with ExitStack() as s5b:
                        wuq = T(s5b, "wuq" + sfx, 3 * 768, BF16)
                        wukv = T(s5b, "wukv" + sfx, 2 * 1024, BF16)
                        wuqB = Buf(f"wuq{s}")
                        wukvB = Buf(f"wukv{s}")
                        sch.dma("pool", wuq[:, :].rearrange("p (k n) -> p k n", k=3),
                                wuq_d[:, :].rearrange("p (k n) -> p k n", k=3), W=[wuqB])
                        sch.dma("pool", wukv[:, :].rearrange("p (k n) -> p k n", k=2),
                                wukv_d[:, :].rearrange("p (k n) -> p k n", k=2), W=[wukvB])
                        qpb = [T(s5b, f"qpb{i}" + sfx, 256, BF16) for i in range(2)]
                        qpbB = [Buf(f"qpb{s}_{i}") for i in range(2)]
                        rtmp = [T(s5b, f"rtq{i}" + sfx, 128, F32) for i in range(2)]
                        rtmpB = [Buf(f"rtq{s}_{i}") for i in range(2)]
                        for tt in range(NQT):
                            tsl = slice(tt * 512, (tt + 1) * 512)
                            lb4 = latB[tt * 4:(tt + 1) * 4]
                            for h in range(4):
                                bank, bB = nbank()
                                for kc in range(3):
                                    MM(bank, wuq[:, kc * 768 + h * 128: kc * 768 + (h + 1) * 128],
                                       cqnT[:, kc * S + tt * 512: kc * S + (tt + 1) * 512], kc == 0, kc == 2,
                                       R=[wuqB] + lb4, W=[bB])
                                COPY(qnT[:, h * S + tt * 512: h * S + (tt + 1) * 512], bank, R=[bB], W=[qB[tt]])
                            for h in range(4):
                                bank, bB = nbank()
                                for kc in range(2):
                                    MM(bank, wukv[:, kc * 1024 + h * 128: kc * 1024 + (h + 1) * 128],
                                       ckvnT[:, kc * S + tt * 512: kc * S + (tt + 1) * 512], kc == 0, kc == 1,
                                       R=[wukvB] + lb4, W=[bB])
                                COPY(knT[:, h * S + tt * 512: h * S + (tt + 1) * 512], bank, R=[bB], W=[kB[tt]])
                            for tb in range(tt * 4, tt * 4 + 4):
                                bsl = slice(tb * 128, (tb + 1) * 128)
                                bank, bB = nbank()
                                for kc in range(3):
                                    MM(bank[:, 0:256], cqnT[:, kc * S + tb * 128: kc * S + (tb + 1) * 128],
                                       wuq[:, kc * 768 + 512: kc * 768 + 768], kc == 0, kc == 2,
                                       R=[wuqB, latB[tb]], W=[bB])
                                Q = qpb[tb % 2]
                                QB = qpbB[tb % 2]
                                rope(bank[:, 0:256].rearrange("p (g d) -> p g d", g=4), [bB], 4,
                                     cs[:, tb * 64 + 32: tb * 64 + 64], cs[:, tb * 64: tb * 64 + 32], csB,
                                     Q[:, :].rearrange("p (g d) -> p g d", g=4), [QB],
                                     rtmp[0], rtmpB[0], rtmp[1], rtmpB[1])
                                tp, tB = ntp()
                                for k in range(2):
                                    TR(tp[:, k * 128:(k + 1) * 128], Q[:, k * 128:(k + 1) * 128], R=[QB], W=[tB], last=(k == 1))
                                COPY(qpeT3[:, :, bsl], tp[:, 0:256].rearrange("p (k t) -> p k t", k=2),
                                     R=[tB], W=[qpeB[tb]], eng="act")
                                bank, bB = nbank()
                                for kc in range(2):
                                    MM(bank, ckvnT[:, kc * S + tb * 128: kc * S + (tb + 1) * 128],
                                       wukv[:, kc * 1024 + 512: kc * 1024 + 1024], kc == 0, kc == 1,
                                       R=[wukvB, latB[tb]], W=[bB])
                                COPY(Vt[:, tb * 512:(tb + 1) * 512], bank, R=[bB], W=[vB[tb]], eng="dve")
                        sch.barrier(bar[:, 0:1])
                    with ExitStack() as s6:
                        YM = [T(s6, f"YM{i}" + sfx, S, F32) for i in range(4)]
                        YMB = [Buf(f"YM{s}_{i}") for i in range(4)]
                        Pb = [T(s6, f"Pb{i}" + sfx, 512, BF16) for i in range(3)]
                        PbB = [Buf(f"Pb{s}_{i}") for i in range(3)]
                        rden = [T(s6, f"rden{i}" + sfx, 512, F32) for i in range(2)]
                        rdenB = [Buf(f"rden{s}_{i}") for i in range(2)]
                        sq = [T(s6, f"sqm{i}" + sfx, 512, F32) for i in range(2)]
                        sqB = [Buf(f"sqm{s}_{i}") for i in range(2)]
                        rt = T(s6, "rtm" + sfx, 512, F32)
                        rtB = Buf(f"rtm{s}")
                        it = 0
                        pcount = 0
                        for qi in range(NQT):
                            for h in range(4):
                                hb = (h % 2) * 64
                                nkc = 4 * qi + 4
                                ob = 4 if it % 2 == 0 else 2
                                O, OB = PS[:, ob * 512:(ob + 1) * 512], bankB[ob]
                                DN, DNB = PS[:, (ob + 1) * 512:(ob + 2) * 512], bankB[ob + 1]

                                def c0_of(kc):
                                    return 0 if kc < 4 * qi else (kc - 4 * qi) * 128

                                def emitS(kc):
                                    sbi = kc % 2
                                    Sb, SB = PS[:, sbi * 512:(sbi + 1) * 512], bankB[sbi]
                                    c0 = c0_of(kc)
                                    MM(Sb[:, c0:512], knT[:, h * S + kc * 128: h * S + (kc + 1) * 128],
                                       qnT[:, h * S + qi * 512 + c0: h * S + (qi + 1) * 512], True, False,
                                       R=[kB[kc // 4], qB[qi]], W=[SB])
                                    MM(Sb[:, c0:512], kpeT[hb:hb + 64, kc * 128:(kc + 1) * 128],
                                       qpeT[hb:hb + 64, (h // 2) * S + qi * 512 + c0: (h // 2) * S + (qi + 1) * 512],
                                       False, True, R=[latB[kc]] + qpeB[qi * 4:(qi + 1) * 4], W=[SB])

                                emitS(0)
                                for kc in range(nkc):
                                    if kc + 1 < nkc:
                                        emitS(kc + 1)
                                    sbi = kc % 2
                                    Sb, SB = PS[:, sbi * 512:(sbi + 1) * 512], bankB[sbi]
                                    c0 = c0_of(kc)
                                    P, PB = Pb[pcount % 3], PbB[pcount % 3]
                                    pcount += 1
                                    ACT(P[:, c0:512], Sb[:, c0:512], AF.Exp, R=[SB], W=[PB], scale=mla_scale)
                                    if kc >= 4 * qi:
                                        TT(P[:, c0:c0 + 128], P[:, c0:c0 + 128], tri[:, :], ALU.mult, R=[PB, triB], W=[PB])
                                    MM(O[:, c0:512], Vt[:, kc * 512 + h * 128: kc * 512 + (h + 1) * 128], P[:, c0:512],
                                       kc == 0, kc == nkc - 1, R=[vB[kc], PB], W=[OB])
                                    MM(DN[:, c0:512], onesb[:, :], P[:, c0:512], kc == 0, kc == nkc - 1,
                                       R=[constB, PB], W=[DNB])
                                rd, rdB = rden[it % 2], rdenB[it % 2]
                                RECIP(rd[:, :], DN, R=[DNB], W=[rdB])
                                TT(YM[h][:, qi * 512:(qi + 1) * 512], O, rd[:, :], ALU.mult, R=[OB, rdB], W=[YMB[h]])
                                it += 1
                            feat_norm(YM, YMB, GMO, 512, qi, 4, sq, sqB, rt, rtB)
                        sch.barrier(bar[:, 0:1])
            if dbg:
                dB = Buf(f"dbgy{s}")
                sch.dma("sp", dbg_yT[s], yT[:, :], R=yTB + [dB])

            with ExitStack() as p2:
                G = [T(p2, f"G{i}" + sfx, 1024, F32) for i in range(3)]
                GB = [Buf(f"G{s}_{i}") for i in range(3)]
                for i in range(3):
                    sch.dma("sp", G[i][:], grow_d[i], W=[GB[i]])
                ring = T(p2, "ring" + sfx, NRING * 4096, BF16)
                ringB = [Buf(f"ring{s}_{i}") for i in range(NRING)]
                mkT = T(p2, "mkT" + sfx, 8 * 256, BF16)
                mv = T(p2, "mv" + sfx, 2 * 1024, BF16)
                mnT = T(p2, "mnT" + sfx, 8 * 256, BF16)
                mkB = Buf(f"mk{s}")
                mvB = Buf(f"mv{s}")
                mnB = [Buf(f"mn{s}_{i}") for i in range(2)]
                memt = [T(p2, f"memt{i}" + sfx, 1024, F32) for i in range(2)]
                memtB = [Buf(f"memt{s}_{i}") for i in range(2)]
                Ht = T(p2, "Ht" + sfx, 4 * 1024, F32)
                HtB = [Buf(f"Ht{s}_{i}") for i in range(4)]
                xn2 = T(p2, "xn2" + sfx, 8 * 512, BF16)
                xn2B = [Buf(f"xn2{s}_{i}") for i in range(4)]
                xn23 = xn2[:, :].rearrange("p (k t) -> p k t", k=8)
                PL = T(p2, "PL" + sfx, NJ * 512, BF16)
                PLB = [Buf(f"PL{s}_{i}") for i in range(NJ)]
                Pb = [T(p2, f"Pm{i}" + sfx, 512, BF16) for i in range(4)]
                PbB = [Buf(f"Pm{s}_{i}") for i in range(4)]
                rden = [T(p2, f"rdm{i}" + sfx, 512, F32) for i in range(2)]
                rdenB = [Buf(f"rdm{s}_{i}") for i in range(2)]
                sg = [T(p2, f"sg{i}" + sfx, 512, F32) for i in range(2)]
                sgB = [Buf(f"sg{s}_{i}") for i in range(2)]
                tmpo = [T(p2, f"tmpo{i}" + sfx, 1024, F32) for i in range(2)]
                tmpoB = [Buf(f"tmpo{s}_{i}") for i in range(2)]
                Fsb = T(p2, "Fsb" + sfx, 4 * 1024, F32)
                FsbB = [Buf(f"Fsb{s}_{i}") for i in range(4)]
                xs = [T(p2, f"xs2{i}" + sfx, 1024, BF16) for i in range(2)]
                xsB = [Buf(f"xs2{s}_{i}") for i in range(2)]
                junk = T(p2, "junk2" + sfx, 1024, BF16)
                junkB = Buf(f"junk2{s}")

                tile_uses = [0, 1, 2, 3, 4, 5] + list(range(6, 17)) + list(range(17, 23))
                uses = [23, 24, 25, 26] + tile_uses * NQT
                rs = {"issued": 0, "k": 0}

                def ring_get():
                    k = rs["k"]
                    rs["k"] += 1
                    while rs["issued"] <= min(k + 3, len(uses) - 1):
                        i = rs["issued"]
                        r = i % NRING
                        sch.dma("sp", ring[:, r * 4096:(r + 1) * 4096], scr_d[uses[i]], R=[scrB], W=[ringB[r]])
                        rs["issued"] += 1
                    r = k % NRING
                    return ring[:, r * 4096:(r + 1) * 4096], ringB[r], uses[k]

                mnT3 = mnT[:, :].rearrange("p (k t) -> p k t", k=8)
                for mc in range(2):
                    sch.dma("sp", memt[mc][:], mem_d[s, mc * 128:(mc + 1) * 128, :], W=[memtB[mc]])
                for mc in range(2):
                    nt(memt[mc][:, :], [memtB[mc]], 1024, pv[:, GMKV:GMKV + 8],
                       mnT3[:, :, mc * 128:(mc + 1) * 128], [mnB[mc]], xs[mc], xsB[mc], junk, junkB)
                for sl in range(2):
                    W_, WB, uid = ring_get()
                    assert uid == SLOT_MK + sl
                    for cc in range(4):
                        c = sl * 4 + cc
                        bank, bB = nbank()
                        for kc in range(8):
                            MM(bank[:, 0:256], W_[:, cc * 1024 + kc * 128: cc * 1024 + (kc + 1) * 128],
                               mnT[:, kc * 256:(kc + 1) * 256], kc == 0, kc == 7, R=[WB] + mnB, W=[bB])
                        COPY(mkT[:, c * 256:(c + 1) * 256], bank[:, 0:256], R=[bB], W=[mkB])
                for fh in range(2):
                    W_, WB, uid = ring_get()
                    assert uid == SLOT_MV + fh
                    for mc in range(2):
                        bank, bB = nbank()
                        for kc in range(8):
                            MM(bank, mnT[:, kc * 256 + mc * 128: kc * 256 + (mc + 1) * 128],
                               W_[:, kc * 512:(kc + 1) * 512], kc == 0, kc == 7, R=[WB, mnB[mc]], W=[bB])
                        COPY(mv[:, mc * 1024 + fh * 512: mc * 1024 + (fh + 1) * 512], bank, R=[bB], W=[mvB])

                def postres(src, srcB, gi, tb, oi):
                    ss, ssB = small()
                    ACT(junk[:, :], src, AF.Square, R=srcB, W=[junkB, ssB], accum_out=ss[:, 0:1])
                    rstd_from_ss(ss, ssB, 1, [1.0 / 1024])
                    to, toB = tmpo[oi % 2], tmpoB[oi % 2]
                    STT(to[:, :], src, ss[:, 0:1], G[gi][:, :], ALU.mult, ALU.mult, R=list(srcB) + [ssB, GB[gi]], W=[toB])
                    TT(Ht[:, tb * 1024:(tb + 1) * 1024], Ht[:, tb * 1024:(tb + 1) * 1024], to[:, :], ALU.add,
                       R=[HtB[tb], toB], W=[HtB[tb]])

                oi = 0
                for tt in range(NQT):
                    for tb in range(4):
                        sch.dma("pool", Ht[:, tb * 1024:(tb + 1) * 1024],
                                x_d[s, tt * 512 + tb * 128: tt * 512 + (tb + 1) * 128, :], W=[HtB[tb]])
                    W0, W0B, _ = ring_get()
                    W1, W1B, _ = ring_get()
                    for tb in range(4):
                        pair, pB = npair()
                        for fh, (W_, WB) in enumerate(((W0, W0B), (W1, W1B))):
                            for kc in range(8):
                                MM(pair[:, fh * 512:(fh + 1) * 512],
                                   yT[:, kc * S + tt * 512 + tb * 128: kc * S + tt * 512 + (tb + 1) * 128],
                                   W_[:, kc * 512:(kc + 1) * 512], kc == 0, kc == 7, R=[yTB[tt], WB], W=[pB[fh]])
                        postres(pair, pB, 0, tb, oi)
                        oi += 1
                        if F_P2EARLY:
                            nt(Ht[:, tb * 1024:(tb + 1) * 1024], [HtB[tb]], 1024, pv[:, GPMEM:GPMEM + 8],
                               xn23[:, :, tb * 128:(tb + 1) * 128], [xn2B[tb]], xs[tb % 2], xsB[tb % 2], junk, junkB)
                    if not F_P2EARLY:
                        for tb in range(4):
                            nt(Ht[:, tb * 1024:(tb + 1) * 1024], [HtB[tb]], 1024, pv[:, GPMEM:GPMEM + 8],
                               xn23[:, :, tb * 128:(tb + 1) * 128], [xn2B[tb]], xs[tb % 2], xsB[tb % 2], junk, junkB)
                    for sl in range(2):
                        W_, WB, _ = ring_get()
                        for cc in range(4):
                            c = sl * 4 + cc
                            bank, bB = nbank()
                            for kc in range(8):
                                MM(bank, W_[:, cc * 1024 + kc * 128: cc * 1024 + (kc + 1) * 128],
                                   xn2[:, kc * 512:(kc + 1) * 512], kc == 0, kc == 7, R=[WB] + xn2B, W=[bB])
                            COPY(PL[:, c * 512:(c + 1) * 512], bank, R=[bB], W=[PLB[c]])
                    pc = 0
                    for h in range(4):
                        Ps = []
                        for mc in range(2):
                            bank, bB = nbank()
                            for dc in range(2):
                                c = h * 2 + dc
                                MM(bank, mkT[:, c * 256 + mc * 128: c * 256 + (mc + 1) * 128],
                                   PL[:, c * 512:(c + 1) * 512], dc == 0, dc == 1, R=[mkB, PLB[c]], W=[bB])
                            P, PB = Pb[pc % 4], PbB[pc % 4]
                            pc += 1
                            ACT(P[:, :], bank, AF.Exp, R=[bB], W=[PB], scale=mem_scale)
                            Ps.append((P, PB))
                        dn, dnB = nbank()
                        for mc in range(2):
                            MM(dn, onesb[:, :], Ps[mc][0][:, :], mc == 0, mc == 1, R=[constB, Ps[mc][1]], W=[dnB])
                        rd, rdB = rden[h % 2], rdenB[h % 2]
                        RECIP(rd[:, :], dn, R=[dnB], W=[rdB])
                        for dvc in range(2):
                            c = h * 2 + dvc
                            bank, bB = nbank()
                            for mc in range(2):
                                MM(bank, mv[:, mc * 1024 + c * 128: mc * 1024 + (c + 1) * 128], Ps[mc][0][:, :],
                                   mc == 0, mc == 1, R=[mvB, Ps[mc][1]], W=[bB])
                            TT(PL[:, (8 + c) * 512:(9 + c) * 512], bank, rd[:, :], ALU.mult, R=[bB, rdB], W=[PLB[8 + c]])
                    W0, W0B, _ = ring_get()
                    W1, W1B, _ = ring_get()
                    for tb in range(4):
                        pair, pB = npair()
                        for fh, (W_, WB) in enumerate(((W0, W0B), (W1, W1B))):
                            for kc in range(8):
                                MM(pair[:, fh * 512:(fh + 1) * 512],
                                   PL[:, (8 + kc) * 512 + tb * 128: (8 + kc) * 512 + (tb + 1) * 128],
                                   W_[:, kc * 512:(kc + 1) * 512], kc == 0, kc == 7, R=[PLB[8 + kc], WB], W=[pB[fh]])
                        postres(pair, pB, 1, tb, oi)
                        oi += 1
                        if F_P2EARLY:
                            nt(Ht[:, tb * 1024:(tb + 1) * 1024], [HtB[tb]], 1024, pv[:, GPF:GPF + 8],
                               xn23[:, :, tb * 128:(tb + 1) * 128], [xn2B[tb]], xs[tb % 2], xsB[tb % 2], junk, junkB)
                    if not F_P2EARLY:
                        for tb in range(4):
                            nt(Ht[:, tb * 1024:(tb + 1) * 1024], [HtB[tb]], 1024, pv[:, GPF:GPF + 8],
                               xn23[:, :, tb * 128:(tb + 1) * 128], [xn2B[tb]], xs[tb % 2], xsB[tb % 2], junk, junkB)
                    for jj in range(11):
                        W_, WB, _ = ring_get()
                        for jl in range(2):
                            j = jj * 2 + jl
                            bg, bgB = nbank()
                            for kc in range(8):
                                MM(bg, W_[:, (jl * 2 + 0) * 1024 + kc * 128: (jl * 2 + 0) * 1024 + (kc + 1) * 128],
                                   xn2[:, kc * 512:(kc + 1) * 512], kc == 0, kc == 7, R=[WB] + xn2B, W=[bgB])
                            bu, buB = nbank()
                            for kc in range(8):
                                MM(bu, W_[:, (jl * 2 + 1) * 1024 + kc * 128: (jl * 2 + 1) * 1024 + (kc + 1) * 128],
                                   xn2[:, kc * 512:(kc + 1) * 512], kc == 0, kc == 7, R=[WB] + xn2B, W=[buB])
                            g_, gB_ = sg[j % 2], sgB[j % 2]
                            ACT(g_[:, :], bg, AF.Silu, R=[bgB], W=[gB_])
                            TT(PL[:, j * 512:(j + 1) * 512], bu, g_[:, :], ALU.mult, R=[buB, gB_], W=[PLB[j]])
                    for fh in range(2):
                        for sl3 in range(3):
                            W_, WB, _ = ring_get()
                            nj = 8 if sl3 < 2 else 6
                            for jl in range(nj):
                                j = sl3 * 8 + jl
                                for tb in range(4):
                                    MM(PS[:, tb * 512:(tb + 1) * 512], PL[:, j * 512 + tb * 128: j * 512 + (tb + 1) * 128],
                                       W_[:, jl * 512:(jl + 1) * 512], j == 0, j == NJ - 1, R=[PLB[j], WB], W=[bankB[tb]])
                        for tb in range(4):
                            COPY(Fsb[:, tb * 1024 + fh * 512: tb * 1024 + (fh + 1) * 512], PS[:, tb * 512:(tb + 1) * 512],
                                 R=[bankB[tb]], W=[FsbB[tb]])
                    st["bank"] = 4
                    for tb in range(4):
                        postres(Fsb[:, tb * 1024:(tb + 1) * 1024], [FsbB[tb]], 2, tb, oi)
                        oi += 1
                        sch.dma("pool", out_d[s, tt * 512 + tb * 128: tt * 512 + (tb + 1) * 128, :],
                                Ht[:, tb * 1024:(tb + 1) * 1024], R=[HtB[tb]])
                sch.barrier(bar[:, 0:1])
        sch.final_wait()

    with nc.Block() as block:
        @block.sync
        def _(sync):
            body()
    es.close()
    return nc, sch


def _kc(W):
    K, N = W.shape
    return np.ascontiguousarray(W.reshape(K // 128, 128, N).transpose(1, 0, 2)).reshape(128, -1)


def _cols(v):
    return np.ascontiguousarray(v.reshape(-1, 128).T)


def pack_shared(inp):
    f = np.float32
    w_in = inp["w_in"][0]
    winA = _kc(w_in[:, 0:1024])
    winB = _kc(np.concatenate([w_in[:, 1024:1408], w_in[:, 1664:1728], w_in[:, 1664:1728], w_in[:, 1408:1664]], axis=1))
    w_uq = inp["w_uq"][0].reshape(384, 4, 192)
    wuq = _kc(np.concatenate([w_uq[:, :, 0:128].reshape(384, 512), w_uq[:, :, 128:192].reshape(384, 256)], axis=1))
    w_ukv = inp["w_ukv"][0].reshape(256, 4, 256)
    wukv = _kc(np.concatenate([w_ukv[:, :, 0:128].reshape(256, 512), w_ukv[:, :, 128:256].reshape(256, 512)], axis=1))
    wlru = np.zeros((128, 4, 2, 128), f)
    for c in range(4):
        for g, key in enumerate(("lru_wa", "lru_wx")):
            for hh in range(2):
                wlru[hh * 64:(hh + 1) * 64, c, g, hh * 64:(hh + 1) * 64] = inp[key][0][2 * c + hh]
    wlru = wlru.reshape(128, -1)
    slots = np.zeros((NSLOT, 128, 4096), f)

    def moving(W, fh):
        return _kc(W[:, fh * 512:(fh + 1) * 512])

    def stationary(W, sl):
        Wr = W.reshape(8, 128, 8, 128)[:, :, sl * 4:(sl + 1) * 4, :]
        return np.ascontiguousarray(Wr.transpose(1, 2, 0, 3)).reshape(128, -1)

    for fh in range(2):
        slots[0 + fh] = moving(inp["w_out"][0], fh)
        slots[2 + fh] = stationary(inp["w_mq"][0], fh)
        slots[4 + fh] = moving(inp["w_mo"][0], fh)
        slots[SLOT_MK + fh] = stationary(inp["w_mk"][0], fh)
        slots[SLOT_MV + fh] = moving(inp["w_mv"][0], fh)
    wg = inp["w_gate"][0].reshape(8, 128, NJ, 128)
    wu = inp["w_up"][0].reshape(8, 128, NJ, 128)
    for jj in range(11):
        blk = np.zeros((128, 2, 2, 8, 128), f)
        for jl in range(2):
            j = jj * 2 + jl
            blk[:, jl, 0] = wg[:, :, j, :].transpose(1, 0, 2)
            blk[:, jl, 1] = wu[:, :, j, :].transpose(1, 0, 2)
        slots[6 + jj] = blk.reshape(128, -1)
    wd = inp["w_down"][0].reshape(NJ, 128, 1024)
    for fh in range(2):
        for sl3 in range(3):
            nj = 8 if sl3 < 2 else 6
            blk = np.zeros((128, 8, 512), f)
            blk[:, 0:nj, :] = wd[sl3 * 8: sl3 * 8 + nj, :, fh * 512:(fh + 1) * 512].transpose(1, 0, 2)
            slots[17 + fh * 3 + sl3] = blk.reshape(128, -1)
    pv = np.zeros((128, NPV), f)
    pv[:, GPM:GPM + 8] = _cols(inp["g_pre_mix"][0])
    for k in range(4):
        pv[:, CW + k * 4: CW + k * 4 + 4] = _cols(inp["conv_w"][0][k])
    pv[:, CB:CB + 4] = _cols(inp["conv_b"][0])
    pv[:, BA:BA + 4] = _cols(inp["lru_ba"][0])
    pv[:, BX:BX + 4] = _cols(inp["lru_bx"][0])
    pv[:, LAM:LAM + 4] = _cols(inp["lru_lambda"][0])
    pv[:, GQ:GQ + 3] = _cols(inp["g_q_lat"][0])
    pv[:, GKV:GKV + 2] = _cols(inp["g_kv_lat"][0])
    pv[:, GLO:GLO + 4] = _cols(inp["g_lru_out"][0])
    pv[:, GMO:GMO + 4] = _cols(inp["g_mla_out"][0])
    pv[:, GPMEM:GPMEM + 8] = _cols(inp["g_pre_mem"][0])
    pv[:, GMKV:GMKV + 8] = _cols(inp["g_mem_kv"][0])
    pv[:, GPF:GPF + 8] = _cols(inp["g_pre_ffn"][0])
    grow = np.stack([np.broadcast_to(inp[k][0][None, :], (128, 1024)) for k in
                     ("g_post_mix", "g_post_mem", "g_post_ffn")]).astype(f)
    invf = (10000.0 ** (-np.arange(0, 64, 2, dtype=np.float32) / 64)).astype(f)
    invf = np.ascontiguousarray(np.broadcast_to(invf[None, :], (128, 32)))
    ident = np.eye(128, dtype=f)
    tri = (np.arange(128)[None, :] >= np.arange(128)[:, None]).astype(f)
    return dict(pv=pv, grow=np.ascontiguousarray(grow), invf=invf, ident=ident, tri=tri, winA=winA, winB=winB,
                wuq=wuq, wukv=wukv, wlru=wlru, wslots=slots)


def pack_core(inp, b0, nseq, S):
    x = np.ascontiguousarray(inp["x"][b0:b0 + nseq], dtype=np.float32)
    mem = np.ascontiguousarray(inp["mem"][b0:b0 + nseq], dtype=np.float32)
    pos = np.asarray(inp["positions"][b0:b0 + nseq], dtype=np.int32)
    pos = np.ascontiguousarray(pos.reshape(nseq, S // 128, 128).transpose(2, 0, 1).reshape(128, -1))
    return dict(x=x, mem=mem, pos=pos)


_CACHE = {}


def kernel(**inputs):
    inp = {k: np.asarray(v) for k, v in inputs.items()}
    B, S, _ = inp["x"].shape
    nseq = B // NCORES
    key = (S, nseq)
    if key not in _CACHE:
        _CACHE[key] = build(S, nseq)[0]
    nc = _CACHE[key]
    shared = pack_shared(inp)
    in_maps = []
    for c in range(NCORES):
        m = dict(shared)
        m.update(pack_core(inp, c * nseq, nseq, S))
        in_maps.append(m)
    res = run_bass_kernel_spmd(nc, in_maps, core_ids=list(range(NCORES)))
    out = np.concatenate([np.asarray(r["out"]) for r in res.results], axis=0)
    return out.astype(np.float32)
```

```python
import math
from contextlib import ExitStack

import numpy as np
import concourse.bass as bass
import concourse.mybir as mybir
from concourse.bass_utils import run_bass_kernel_spmd

F32 = mybir.dt.float32
BF16 = mybir.dt.bfloat16
I32 = mybir.dt.int32
AF = mybir.ActivationFunctionType
ALU = mybir.AluOpType

NCORES = 8
D = 1024
DFF = 2816
NJ = DFF // 128
NMEM = 256
EPS = 1e-6
NSLOT = 27
SLOT_MK, SLOT_MV = 23, 25
GPM, CW, CB, BA, BX, LAM, GQ, GKV, GLO, GMO, GPMEM, GMKV, GPF = 0, 8, 24, 28, 32, 36, 40, 43, 45, 49, 53, 61, 69
NPV = 77
HC, H2, BAH, BXH = 77, 81, 85, 89
NPVT = 96
SEM_LIMIT = 12000
NRING = 5
import os
F_PIPEA = os.environ.get('K_PIPEA', '1') == '1'
F_LATSKEW = os.environ.get('K_LATSKEW', '1') == '1'
F_P2EARLY = os.environ.get('K_P2EARLY', '0') == '1'
F_WPRE = os.environ.get('K_WPRE', '0') == '1'


class Buf:
    __slots__ = ("name", "w", "r", "ld", "ldc", "ldk", "st", "stc", "stk")

    def __init__(self, name):
        self.name = name
        self.w = {}
        self.r = {}
        self.ld = None
        self.ldc = 0
        self.ldk = None
        self.st = None
        self.stc = 0
        self.stk = None


class _Eng:
    def __init__(self, sch, name, h):
        self.sch = sch
        self.name = name
        self.h = h
        self.sem = None
        self.key = None
        self.cnt = 0
        self.nsem = 0
        self.known = {}
        self.pending = False

    def newsem(self):
        self.sem = self.sch.alloc_sem(f"e_{self.name}{self.nsem}")
        self.key = (self.name, self.nsem)
        self.nsem += 1
        self.cnt = 0


class Sched:
    def __init__(self, nc, es):
        self.nc = nc
        self.es = es
        self.nsems = 0
        self.E = {
            "pe": _Eng(self, "pe", nc.tensor),
            "act": _Eng(self, "act", nc.scalar),
            "dve": _Eng(self, "dve", nc.vector),
            "pool": _Eng(self, "pool", nc.gpsimd),
            "sp": _Eng(self, "sp", nc.sync),
        }
        for e in self.E.values():
            e.newsem()
        self.dma_tokens = {}
        self.nwaits = 0
        self.ninst = 0

    def alloc_sem(self, name):
        self.nsems += 1
        return self.es.enter_context(self.nc.semaphore(name))

    @staticmethod
    def _merge(d, src):
        for k, (s, v) in src.items():
            if k not in d or d[k][1] < v:
                d[k] = (s, v)

    def _wait(self, E, key, sem, val):
        if E.known.get(key, 0) >= val:
            return
        E.h.wait_ge(sem, val)
        E.known[key] = val
        self.nwaits += 1

    def _deps(self, E, R, W, skip_keys=()):
        deps = {}
        for b in R:
            self._merge(deps, b.w)
        for b in W:
            self._merge(deps, b.w)
            self._merge(deps, b.r)
        for key, (sem, val) in deps.items():
            if key in skip_keys:
                continue
            if E.name == "pe" and key[0] == "pe":
                continue
            self._wait(E, key, sem, val)

    def op(self, en, fn, R=(), W=(), inc=True):
        E = self.E[en]
        self._deps(E, R, W)
        ins = fn(E.h)
        self.ninst += 1
        if inc:
            if E.cnt >= SEM_LIMIT and not E.pending:
                E.newsem()
            E.cnt += 1
            ins.then_inc(E.sem, 1)
            E.pending = False
            tok = (E.key, E.sem, E.cnt)
        else:
            E.pending = True
            tok = (E.key, E.sem, E.cnt + 1)
        for b in R:
            k = tok[0]
            if k not in b.r or b.r[k][1] < tok[2]:
                b.r[k] = (tok[1], tok[2])
        for b in W:
            b.w = {tok[0]: (tok[1], tok[2])}
            b.r = {}
        return ins

    def dma(self, q, out, in_, R=(), W=(), **kw):
        E = self.E[q]
        skip = ()
        if W and W[0].ldk is not None:
            skip = (W[0].ldk,)
        self._deps(E, R, W, skip_keys=skip)
        ins = E.h.dma_start(out=out, in_=in_, **kw)
        self.ninst += 1
        if W:
            b = W[0]
            if b.ld is None:
                b.ld = self.alloc_sem("ld_" + b.name)
                b.ldk = ("ld", b.name)
            b.ldc += 16
            ins.then_inc(b.ld, 16)
            tok = (b.ldk, b.ld, b.ldc)
        else:
            b = R[0]
            if b.st is None:
                b.st = self.alloc_sem("st_" + b.name)
                b.stk = ("st", b.name)
            b.stc += 16
            ins.then_inc(b.st, 16)
            tok = (b.stk, b.st, b.stc)
        self.dma_tokens[tok[0]] = (tok[1], tok[2])
        for bb in R:
            k = tok[0]
            if k not in bb.r or bb.r[k][1] < tok[2]:
                bb.r[k] = (tok[1], tok[2])
        for bb in W:
            if bb is W[0] and skip:
                bb.w[tok[0]] = (tok[1], tok[2])
                bb.r = {}
            else:
                bb.w = {tok[0]: (tok[1], tok[2])}
                bb.r = {}
        return ins

    def barrier(self, bar_ap):
        Dv = self.E["dve"]
        for E in self.E.values():
            if E is Dv:
                continue
            assert not E.pending
            if E.cnt > 0:
                self._wait(Dv, E.key, E.sem, E.cnt)
        for k, (s, v) in self.dma_tokens.items():
            self._wait(Dv, k, s, v)
        if Dv.cnt > 0:
            self._wait(Dv, Dv.key, Dv.sem, Dv.cnt)
        ins = Dv.h.memset(bar_ap, 0.0)
        if Dv.cnt >= SEM_LIMIT:
            Dv.newsem()
        Dv.cnt += 1
        ins.then_inc(Dv.sem, 1)
        for E in self.E.values():
            if E is Dv:
                continue
            self._wait(E, Dv.key, Dv.sem, Dv.cnt)

    def final_wait(self):
        sp = self.E["sp"]
        for k, (s, v) in self.dma_tokens.items():
            self._wait(sp, k, s, v)
        for E in self.E.values():
            if E is sp or E.cnt == 0:
                continue
            self._wait(sp, E.key, E.sem, E.cnt)


def build(S=2048, NSEQ=2, dbg=False):
    NTB = S // 128
    NQT = S // 512
    nc = bass.Bass("TRN2", target_bir_lowering=False)

    def din(name, shape, dt=F32):
        return nc.dram_tensor(name, list(shape), dt, kind="ExternalInput").ap()

    x_d = din("x", [NSEQ, S, D])
    mem_d = din("mem", [NSEQ, NMEM, D])
    pos_d = din("pos", [128, NSEQ * NTB], I32)
    pv_d = din("pv", [128, NPV])
    grow_d = din("grow", [3, 128, D])
    invf_d = din("invf", [128, 32])
    ident_d = din("ident", [128, 128])
    tri_d = din("tri", [128, 128])
    winA_d = din("winA", [128, 8 * 1024])
    winB_d = din("winB", [128, 8 * 768])
    wuq_d = din("wuq", [128, 3 * 768])
    wukv_d = din("wukv", [128, 2 * 1024])
    wlru_d = din("wlru", [128, 4 * 2 * 128])
    wsl_d = din("wslots", [NSLOT, 128, 4096])
    scr_d = nc.dram_tensor("scr", [NSLOT, 128, 4096], BF16, kind="Internal").ap()
    out_d = nc.dram_tensor("out", [NSEQ, S, D], F32, kind="ExternalOutput").ap()
    if dbg:
        dbg_yT = nc.dram_tensor("dbg_yT", [NSEQ, 128, 8 * S], BF16, kind="ExternalOutput").ap()

    es = ExitStack()
    sch = Sched(nc, es)

    def T(scope, name, cols, dt):
        return scope.enter_context(nc.sbuf_tensor("sb_" + name, [128, cols], dt))

    mla_scale = 1.0 / math.sqrt(192.0)
    mem_scale = 1.0 / math.sqrt(256.0)

    def body():
        pers = es
        ident = T(pers, "ident", 128, BF16)
        tri = T(pers, "tri", 128, BF16)
        onesb = T(pers, "onesb", 128, BF16)
        onesf = T(pers, "onesf", 128, F32)
        pv = T(pers, "pv", NPVT, F32)
        cst = T(pers, "cst", 8, F32)
        invf = T(pers, "invf", 32, F32)
        posi = T(pers, "posi", NSEQ * NTB, I32)
        smallt = T(pers, "smallt", 32 * 4, F32)
        bar = T(pers, "bar", 4, F32)
        yT = T(pers, "yT", 8 * S, BF16)
        PS = pers.enter_context(nc.psum_tensor("ps", [128, 6 * 512], F32))
        TPS = pers.enter_context(nc.psum_tensor("tps", [128, 2 * 1024], BF16))

        constB = Buf("const")
        identB = Buf("identb")
        triB = Buf("trib")
        pvB = Buf("pvb")
        invfB = Buf("invfb")
        posB = Buf("posb")
        bankB = [Buf(f"bank{i}") for i in range(6)]
        tpB = [Buf(f"tpb{i}") for i in range(2)]
        smallB = [Buf(f"small{i}") for i in range(32)]
        yTB = [Buf(f"yT{i}") for i in range(NQT)]
        scrB = Buf("scr")
        _bufcache = {}

        def MB(name):
            import re as _re
            key = _re.sub(r"@\d", "@", name)
            if key not in _bufcache:
                _bufcache[key] = Buf(key.replace("@", "_"))
            return _bufcache[key]

        st = {"small": 0, "bank": 0, "pair": 0, "tp": 0, "alt": 0}

        def small():
            i = st["small"] % 32
            st["small"] += 1
            return smallt[:, i * 4:(i + 1) * 4], smallB[i]

        def nbank():
            i = st["bank"] % 6
            st["bank"] += 1
            return PS[:, i * 512:(i + 1) * 512], bankB[i]

        def npair():
            i = st["pair"] % 3
            st["pair"] += 1
            return PS[:, i * 1024:(i + 1) * 1024], [bankB[2 * i], bankB[2 * i + 1]]

        def ntp():
            i = st["tp"] % 2
            st["tp"] += 1
            return TPS[:, i * 1024:(i + 1) * 1024], tpB[i]

        def MM(out, lhsT, rhs, start, stop, R, W, inc=None):
            return sch.op("pe", lambda e: e.matmul(out, lhsT=lhsT, rhs=rhs, start=start, stop=stop),
                          R=R, W=W, inc=(stop if inc is None else inc))

        def TR(out, in_, R, W, last):
            return sch.op("pe", lambda e: e.transpose(out, in_, ident[:]), R=list(R) + [identB], W=W, inc=last)

        def ACT(out, in_, func, R, W, **kw):
            return sch.op("act", lambda e: e.activation(out=out, in_=in_, func=func, **kw), R=R, W=W)

        def TS(out, in0, s1, s2, op0, op1, R, W, eng="dve"):
            if s2 is None:
                return sch.op(eng, lambda e: e.tensor_scalar(out=out, in0=in0, scalar1=s1, scalar2=None, op0=op0),
                              R=R, W=W)
            return sch.op(eng, lambda e: e.tensor_scalar(out=out, in0=in0, scalar1=s1, scalar2=s2, op0=op0, op1=op1),
                          R=R, W=W)

        def TT(out, in0, in1, op, R, W, eng="dve"):
            return sch.op(eng, lambda e: e.tensor_tensor(out=out, in0=in0, in1=in1, op=op), R=R, W=W)

        def STT(out, in0, scalar, in1, op0, op1, R, W):
            return sch.op("dve", lambda e: e.scalar_tensor_tensor(out=out, in0=in0, scalar=scalar, in1=in1,
                                                                  op0=op0, op1=op1), R=R, W=W)

        def COPY(out, in_, R, W, eng=None):
            if eng is None:
                eng = "act" if st["alt"] % 2 == 0 else "dve"
                st["alt"] += 1
            if eng == "act":
                return sch.op("act", lambda e: e.activation(out=out, in_=in_, func=AF.Copy), R=R, W=W)
            return sch.op(eng, lambda e: e.tensor_copy(out=out, in_=in_), R=R, W=W)

        def RECIP(out, in_, R, W):
            return sch.op("dve", lambda e: e.reciprocal(out=out, in_=in_), R=R, W=W)

        def rstd_from_ss(ss, ssB, n, invd):
            for i in range(n):
                TS(ss[:, i:i + 1], ss[:, i:i + 1], invd[i], EPS, ALU.mult, ALU.add, R=[ssB], W=[ssB])
            ACT(ss[:, 0:n], ss[:, 0:n], AF.Sqrt, R=[ssB], W=[ssB])
            RECIP(ss[:, 0:n], ss[:, 0:n], R=[ssB], W=[ssB])

        sch.dma("sp", pv[:, 0:NPV], pv_d[:, :], W=[pvB])
        sch.dma("sp", invf[:], invf_d[:, :], W=[invfB])
        sch.dma("sp", posi[:], pos_d[:, :], W=[posB])
        sch.dma("pool", ident[:], ident_d[:, :], W=[identB])
        sch.dma("pool", tri[:], tri_d[:, :], W=[triB])
        sch.op("dve", lambda e: e.memset(onesb[:], 1.0), W=[constB])
        sch.op("dve", lambda e: e.memset(onesf[:], 1.0), W=[constB])
        sch.op("dve", lambda e: e.memset(cst[:, 0:1], 1.0), W=[constB])
        sch.op("dve", lambda e: e.memset(cst[:, 1:2], math.pi), W=[constB])
        tmpc, tmpB = small()
        tmpc2, tmpB2 = small()
        TS(tmpc2, pv[:, LAM:LAM + 4], -1.0, None, ALU.mult, None, R=[pvB], W=[tmpB2])
        TT(tmpc, pv[:, LAM:LAM + 4], tmpc2, ALU.max, R=[pvB, tmpB2], W=[tmpB])
        ACT(tmpc, tmpc, AF.Exp, R=[tmpB], W=[tmpB], scale=-1.0)
        TS(tmpc, tmpc, 1.0, None, ALU.add, None, R=[tmpB], W=[tmpB])
        ACT(tmpc, tmpc, AF.Ln, R=[tmpB], W=[tmpB])
        TS(tmpc2, tmpc2, 0.0, None, ALU.max, None, R=[tmpB2], W=[tmpB2])
        TT(tmpc, tmpc, tmpc2, ALU.add, R=[tmpB, tmpB2], W=[tmpB])
        TS(pv[:, HC:HC + 4], tmpc, -4.0, None, ALU.mult, None, R=[tmpB], W=[pvB])
        TS(pv[:, H2:H2 + 4], tmpc, -8.0, None, ALU.mult, None, R=[tmpB], W=[pvB])
        TS(pv[:, BAH:BAH + 4], pv[:, BA:BA + 4], 0.5, None, ALU.mult, None, R=[pvB], W=[pvB])
        TS(pv[:, BXH:BXH + 4], pv[:, BX:BX + 4], 0.5, None, ALU.mult, None, R=[pvB], W=[pvB])

        scr_state = {"done": False}

        def convert_scratch():
            if scr_state["done"]:
                return
            scr_state["done"] = True
            order = [23, 24, 25, 26] + list(range(23))
            for sl in order:
                sch.dma("pool", scr_d[sl].rearrange("p (a b) -> p a b", a=2),
                        wsl_d[sl].rearrange("p (a b) -> p a b", a=2), W=[scrB])

        def nt_a(src, srcB, Dn, junk, junkB):
            ss, ssB = small()
            ACT(junk[:, 0:Dn], src, AF.Square, R=srcB, W=[junkB, ssB], accum_out=ss[:, 0:1])
            rstd_from_ss(ss, ssB, 1, [1.0 / Dn])
            return ss, ssB

        def nt_b(src, srcB, Dn, ss, ssB, gcols, dst3, dstB, xs, xsB):
            nk = Dn // 128
            ACT(xs[:, 0:Dn], src, AF.Copy, R=list(srcB) + [ssB], W=[xsB], scale=ss[:, 0:1])
            tp, tB = ntp()
            for k in range(nk):
                TR(tp[:, k * 128:(k + 1) * 128], xs[:, k * 128:(k + 1) * 128], R=[xsB], W=[tB], last=(k == nk - 1))
            g3 = gcols.unsqueeze(2).broadcast_to([128, nk, 128])
            TT(dst3, tp[:, 0:Dn].rearrange("p (k t) -> p k t", k=nk), g3, ALU.mult, R=[tB, pvB], W=dstB)

        def nt(src, srcB, Dn, gcols, dst3, dstB, xs, xsB, junk, junkB):
            ss, ssB = nt_a(src, srcB, Dn, junk, junkB)
            nt_b(src, srcB, Dn, ss, ssB, gcols, dst3, dstB, xs, xsB)

        def rope(src3, srcB, G, cos, sin, csB, dst3, dstB, tA, tAB, tB_, tBB, extra=()):
            srcB = list(srcB) + list(extra)
            x1 = src3[:, :, 0:32]
            x2 = src3[:, :, 32:64]
            cb = cos.unsqueeze(1).broadcast_to([128, G, 32])
            sb = sin.unsqueeze(1).broadcast_to([128, G, 32])
            a3 = tA[:, 0:G * 32].rearrange("p (g d) -> p g d", g=G)
            b3 = tB_[:, 0:G * 32].rearrange("p (g d) -> p g d", g=G)
            TT(a3, x1, cb, ALU.mult, R=srcB + [csB], W=[tAB])
            TT(b3, x2, sb, ALU.mult, R=srcB + [csB], W=[tBB])
            TT(dst3[:, :, 0:32], a3, b3, ALU.subtract, R=[tAB, tBB], W=dstB)
            TT(a3, x2, cb, ALU.mult, R=srcB + [csB], W=[tAB])
            TT(b3, x1, sb, ALU.mult, R=srcB + [csB], W=[tBB])
            TT(dst3[:, :, 32:64], a3, b3, ALU.add, R=[tAB, tBB], W=dstB)

        def feat_norm(srcs, srcBs, gcol0, Dn, tt, dst_off, sq, sqB, rt, rtB):
            n = len(srcs)
            bank, bB = nbank()
            for i in range(n):
                q, qB = sq[i % 2], sqB[i % 2]
                ACT(q[:, :], srcs[i][:, tt * 512:(tt + 1) * 512], AF.Square, R=[srcBs[i]], W=[qB])
                MM(bank, onesf[:], q[:, :], i == 0, i == n - 1, R=[qB, constB], W=[bB], inc=True)
            TS(rt[:, :], bank, 1.0 / Dn, EPS, ALU.mult, ALU.add, R=[bB], W=[rtB])
            ACT(rt[:, :], rt[:, :], AF.Sqrt, R=[rtB], W=[rtB])
            RECIP(rt[:, :], rt[:, :], R=[rtB], W=[rtB])
            for i in range(n):
                c = dst_off + i
                STT(yT[:, c * S + tt * 512: c * S + (tt + 1) * 512], srcs[i][:, tt * 512:(tt + 1) * 512],
                    pv[:, gcol0 + i:gcol0 + i + 1], rt[:, :], ALU.mult, ALU.mult,
                    R=[srcBs[i], pvB, rtB], W=[yTB[tt]])

        for s in range(NSEQ):
            sfx = f"_{s}"
            with ExitStack() as s1:
                cqnT = T(s1, "cqnT" + sfx, 3 * S, BF16)
                ckvnT = T(s1, "ckvnT" + sfx, 2 * S, BF16)
                kpeT = T(s1, "kpeT" + sfx, S, BF16)
                cs = T(s1, "cs" + sfx, NTB * 64, F32)
                latB = [MB(f"lat@{s}_{i}") for i in range(NTB)]
                csB = MB(f"cs@{s}")
                with ExitStack() as s2:
                    xnT = T(s2, "xnT" + sfx, 8 * S, BF16)
                    xnT3 = xnT[:, :].rearrange("p (k t) -> p k t", k=8)
                    xnTB = [MB(f"xnT@{s}_{i}") for i in range(NTB)]
                    winA = T(s2, "winA" + sfx, 8 * 1024, BF16)
                    wlru = T(s2, "wlru" + sfx, 4 * 2 * 128, BF16)
                    winAB = MB(f"winA@{s}")
                    wlruB = MB(f"wlru@{s}")
                    sch.dma("pool", winA[:, :].rearrange("p (k n) -> p k n", k=8),
                            winA_d[:, :].rearrange("p (k n) -> p k n", k=8), W=[winAB])
                    sch.dma("pool", wlru[:], wlru_d[:, :], W=[wlruB])
                    if F_WPRE:
                        convert_scratch()
                    with ExitStack() as s2a:
                        xtmp = [T(s2a, f"xtmp{i}" + sfx, 1024, F32) for i in range(3)]
                        xtmpB = [MB(f"xtmp@{s}_{i}") for i in range(3)]
                        xs = [T(s2a, f"xs{i}" + sfx, 1024, BF16) for i in range(2)]
                        xsB = [MB(f"xs@{s}_{i}") for i in range(2)]
                        junk = T(s2a, "junk" + sfx, 1024, BF16)
                        junkB = MB(f"junk@{s}")
                        ang = T(s2a, "ang" + sfx, NTB * 64, F32)
                        kf = T(s2a, "kf" + sfx, NTB * 64, F32)
                        ki = T(s2a, "ki" + sfx, NTB * 64, I32)
                        posf = T(s2a, "posf" + sfx, NTB, F32)
                        angB = MB(f"ang@{s}")
                        kfB = MB(f"kf@{s}")
                        kiB = MB(f"ki@{s}")
                        posfB = MB(f"posf@{s}")
                        pend = None
                        for tb in range(NTB + 1):
                            cur = None
                            if tb < NTB:
                                i = tb % 3
                                sch.dma("sp", xtmp[i][:], x_d[s, tb * 128:(tb + 1) * 128, :], W=[xtmpB[i]])
                                cur = (tb,) + nt_a(xtmp[i][:, :], [xtmpB[i]], 1024, junk, junkB)
                            if not F_PIPEA:
                                pend = cur
                                cur = None
                            if pend is not None:
                                ptb, pss, pssB = pend
                                pi = ptb % 3
                                nt_b(xtmp[pi][:, :], [xtmpB[pi]], 1024, pss, pssB, pv[:, GPM:GPM + 8],
                                     xnT3[:, :, ptb * 128:(ptb + 1) * 128], [xnTB[ptb]], xs[ptb % 2], xsB[ptb % 2])
                            pend = cur
                        COPY(posf[:], posi[:, s * NTB:(s + 1) * NTB], R=[posB], W=[posfB], eng="dve")
                        ang3 = ang[:, :].rearrange("p (t d) -> p t d", t=NTB)
                        TT(ang3[:, :, 0:32], posf[:, :].unsqueeze(2).broadcast_to([128, NTB, 32]),
                           invf[:, :].unsqueeze(1).broadcast_to([128, NTB, 32]), ALU.mult,
                           R=[posfB, invfB], W=[angB])
                        TS(ang3[:, :, 32:64], ang3[:, :, 0:32], math.pi / 2, None, ALU.add, None, R=[angB], W=[angB])
                        TS(kf[:], ang[:], 1.0 / (2 * math.pi), None, ALU.mult, None, R=[angB], W=[kfB])
                        COPY(ki[:], kf[:], R=[kfB], W=[kiB], eng="dve")
                        COPY(kf[:], ki[:], R=[kiB], W=[kfB], eng="dve")
                        C1 = 6.28125
                        C2 = 2 * math.pi - C1
                        STT(ang[:], kf[:], -C1, ang[:], ALU.mult, ALU.add, R=[kfB, angB], W=[angB])
                        STT(ang[:], kf[:], -C2, ang[:], ALU.mult, ALU.add, R=[kfB, angB], W=[angB])
                        TS(kf[:], ang[:], math.pi, None, ALU.is_gt, None, R=[angB], W=[kfB])
                        STT(ang[:], kf[:], -2 * math.pi, ang[:], ALU.mult, ALU.add, R=[kfB, angB], W=[angB])
                        TS(kf[:], ang[:], -math.pi, None, ALU.is_lt, None, R=[angB], W=[kfB])
                        STT(ang[:], kf[:], 2 * math.pi, ang[:], ALU.mult, ALU.add, R=[kfB, angB], W=[angB])
                        TS(ang[:], ang[:], math.pi, -math.pi, ALU.min, ALU.max, R=[angB], W=[angB])
                        ACT(cs[:], ang[:], AF.Sin, R=[angB], W=[csB])
                        sch.barrier(bar[:, 0:1])

                    with ExitStack() as s3:
                        convert_scratch()
                        NSEG = NQT
                        Rb = [T(s3, f"R{i}" + sfx, S, F32) for i in range(6)]
                        RB = [[MB(f"R@{s}_{i}_{g}") for g in range(NSEG)] for i in range(6)]
                        Ubf = T(s3, "Ubf" + sfx, S, BF16)
                        UbfB = [MB(f"Ubf@{s}_{g}") for g in range(NSEG)]
                        YL = [T(s3, f"YL{i}" + sfx, S, F32) for i in range(4)]
                        YLB = [[MB(f"YL@{s}_{i}_{g}") for g in range(NSEG)] for i in range(4)]
                        sq = [T(s3, f"sq{i}" + sfx, 512, F32) for i in range(2)]
                        sqB = [MB(f"sq@{s}_{i}") for i in range(2)]
                        rt = T(s3, "rt" + sfx, 512, F32)
                        rtB = MB(f"rt@{s}")
                        LX, LG, U, A, TI, R6 = Rb
                        LXB, LGB, UB, AB, TIB, R6B = RB
                        segs = [slice(g * 512, (g + 1) * 512) for g in range(NSEG)]
                        for c in range(4):
                            def col(base):
                                return pv[:, base + c:base + c + 1]
                            for g in range(NSEG):
                                sl = segs[g]
                                bank, bB = nbank()
                                for kc in range(8):
                                    MM(bank, winA[:, kc * 1024 + c * 128: kc * 1024 + (c + 1) * 128],
                                       xnT[:, kc * S + g * 512: kc * S + (g + 1) * 512], kc == 0, kc == 7,
                                       R=[winAB] + xnTB[g * 4:(g + 1) * 4], W=[bB])
                                COPY(LX[:, sl], bank, R=[bB], W=[LXB[g]], eng="act")
                                bank, bB = nbank()
                                for kc in range(8):
                                    MM(bank, winA[:, kc * 1024 + 512 + c * 128: kc * 1024 + 512 + (c + 1) * 128],
                                       xnT[:, kc * S + g * 512: kc * S + (g + 1) * 512], kc == 0, kc == 7,
                                       R=[winAB] + xnTB[g * 4:(g + 1) * 4], W=[bB])
                                COPY(LG[:, sl], bank, R=[bB], W=[LGB[g]], eng="dve")
                            for g in range(NSEG):
                                sl = segs[g]
                                e = (g + 1) * 512
                                TS(U[:, sl], LX[:, sl], pv[:, CW + 3 * 4 + c:CW + 3 * 4 + c + 1], col(CB), ALU.mult, ALU.add,
                                   R=[LXB[g], pvB], W=[UB[g]])
                                for k, sh in ((2, 1), (1, 2), (0, 3)):
                                    lo = max(g * 512, sh)
                                    rr = [LXB[g], pvB, UB[g]] + ([LXB[g - 1]] if g > 0 else [])
                                    STT(U[:, lo:e], LX[:, lo - sh:e - sh], pv[:, CW + k * 4 + c:CW + k * 4 + c + 1], U[:, lo:e],
                                        ALU.mult, ALU.add, R=rr, W=[UB[g]])
                                COPY(Ubf[:, sl], U[:, sl], R=[UB[g]], W=[UbfB[g]], eng="act")
                            for g in range(NSEG):
                                sl = segs[g]
                                bank, bB = nbank()
                                MM(bank, wlru[:, (c * 2 + 0) * 128:(c * 2 + 1) * 128], Ubf[:, sl], True, True,
                                   R=[wlruB, UbfB[g]], W=[bB])
                                ACT(LX[:, sl], bank, AF.Tanh, R=[bB, pvB], W=[LXB[g]], scale=0.5, bias=col(BAH))
                                bank, bB = nbank()
                                MM(bank, wlru[:, (c * 2 + 1) * 128:(c * 2 + 2) * 128], Ubf[:, sl], True, True,
                                   R=[wlruB, UbfB[g]], W=[bB])
                                ACT(TI[:, sl], bank, AF.Tanh, R=[bB, pvB], W=[TIB[g]], scale=0.5, bias=col(BXH))
                            for g in range(NSEG):
                                sl = segs[g]
                                ACT(A[:, sl], LX[:, sl], AF.Exp, R=[LXB[g], pvB], W=[AB[g]], scale=col(HC), bias=col(HC))
                                ACT(R6[:, sl], LX[:, sl], AF.Exp, R=[LXB[g], pvB], W=[R6B[g]], scale=col(H2), bias=col(H2))
                            for g in range(NSEG):
                                sl = segs[g]
                                ACT(LX[:, sl], LG[:, sl], AF.Square, R=[LGB[g]], W=[LXB[g]])
                                TS(LX[:, sl], LX[:, sl], 0.044715, 1.0, ALU.mult, ALU.add, R=[LXB[g]], W=[LXB[g]])
                                TT(LX[:, sl], LX[:, sl], LG[:, sl], ALU.mult, R=[LXB[g], LGB[g]], W=[LXB[g]])
                            for g in range(NSEG):
                                sl = segs[g]
                                ACT(LX[:, sl], LX[:, sl], AF.Tanh, R=[LXB[g]], W=[LXB[g]], scale=0.7978845608028654)
                            for g in range(NSEG):
                                sl = segs[g]
                                ACT(R6[:, sl], R6[:, sl], AF.Sqrt, R=[R6B[g], constB], W=[R6B[g]], scale=-1.0, bias=cst[:, 0:1])
                            for g in range(NSEG):
                                sl = segs[g]
                                STT(TI[:, sl], TI[:, sl], 1.0, U[:, sl], ALU.add, ALU.mult, R=[TIB[g], UB[g]], W=[TIB[g]])
                                STT(TI[:, sl], TI[:, sl], 0.5, R6[:, sl], ALU.mult, ALU.mult, R=[TIB[g], R6B[g]], W=[TIB[g]])
                            for g in range(NSEG):
                                sl = segs[g]
                                if g == 0:
                                    sch.op("dve", lambda e, sl=sl: e.tensor_tensor_scan(
                                        out=U[:, sl], data0=A[:, sl], data1=TI[:, sl], initial=0.0,
                                        op0=ALU.mult, op1=ALU.add), R=[AB[g], TIB[g]], W=[UB[g]])
                                else:
                                    sch.op("dve", lambda e, sl=sl, g=g: e.tensor_tensor_scan(
                                        out=U[:, sl], data0=A[:, sl], data1=TI[:, sl],
                                        initial=U[:, g * 512 - 1:g * 512],
                                        op0=ALU.mult, op1=ALU.add), R=[AB[g], TIB[g], UB[g - 1]], W=[UB[g]])
                                STT(LX[:, sl], LX[:, sl], 1.0, LG[:, sl], ALU.add, ALU.mult, R=[LXB[g], LGB[g]], W=[LXB[g]])
                                STT(YL[c][:, sl], LX[:, sl], 0.5, U[:, sl], ALU.mult, ALU.mult, R=[LXB[g], UB[g]], W=[YLB[c][g]])
                        for tt in range(NQT):
                            feat_norm([YL[i] for i in range(4)], [YLB[i][tt] for i in range(4)], GLO, 512, tt, 0, sq, sqB, rt, rtB)
                        sch.barrier(bar[:, 0:1])

                    with ExitStack() as s4:
                        winB = T(s4, "winB" + sfx, 8 * 768, BF16)
                        winBB = MB(f"winB@{s}")
                        sch.dma("pool", winB[:, :].rearrange("p (k n) -> p k n", k=8),
                                winB_d[:, :].rearrange("p (k n) -> p k n", k=8), W=[winBB])
                        lat = [T(s4, f"latbf{i}" + sfx, 768, BF16) for i in range(2)]
                        latbB = [MB(f"latbf@{s}_{i}") for i in range(2)]
                        junk = T(s4, "junkb" + sfx, 512, BF16)
                        junkB = MB(f"junkb@{s}")
                        rtmp = [T(s4, f"rtmp{i}" + sfx, 128, F32) for i in range(2)]
                        rtmpB = [MB(f"rtmp@{s}_{i}") for i in range(2)]
                        cqnT3 = cqnT[:, :].rearrange("p (k t) -> p k t", k=3)
                        ckvnT3 = ckvnT[:, :].rearrange("p (k t) -> p k t", k=2)
                        def lat_front(tb):
                            bA, bAB = nbank()
                            bBk, bBB = nbank()
                            for kc in range(8):
                                lhs = xnT[:, kc * S + tb * 128: kc * S + (tb + 1) * 128]
                                MM(bA, lhs, winB[:, kc * 768: kc * 768 + 512], kc == 0, kc == 7,
                                   R=[winBB, xnTB[tb]], W=[bAB])
                            for kc in range(8):
                                lhs = xnT[:, kc * S + tb * 128: kc * S + (tb + 1) * 128]
                                MM(bBk[:, 0:256], lhs, winB[:, kc * 768 + 512: kc * 768 + 768], kc == 0, kc == 7,
                                   R=[winBB, xnTB[tb]], W=[bBB])
                            ss, ssB = small()
                            ACT(junk[:, 0:384], bA[:, 0:384], AF.Square, R=[bAB], W=[junkB, ssB], accum_out=ss[:, 0:1])
                            ACT(junk[:, 0:256], bBk[:, 0:256], AF.Square, R=[bBB], W=[junkB, ssB], accum_out=ss[:, 1:2])
                            rstd_from_ss(ss, ssB, 2, [1.0 / 384, 1.0 / 256])
                            return (tb, bA, bAB, bBk, bBB, ss, ssB)

                        def lat_back(stt):
                            tb, bA, bAB, bBk, bBB, ss, ssB = stt
                            bsl = slice(tb * 128, (tb + 1) * 128)
                            L = lat[tb % 2]
                            LB = latbB[tb % 2]
                            ACT(L[:, 0:384], bA[:, 0:384], AF.Copy, R=[bAB, ssB], W=[LB], scale=ss[:, 0:1])
                            ACT(L[:, 384:640], bBk[:, 0:256], AF.Copy, R=[bBB, ssB], W=[LB], scale=ss[:, 1:2])
                            rope(bA[:, 384:512].rearrange("p (g d) -> p g d", g=2), [bAB], 2,
                                 cs[:, tb * 64 + 32: tb * 64 + 64], cs[:, tb * 64: tb * 64 + 32], csB,
                                 L[:, 640:768].rearrange("p (g d) -> p g d", g=2), [LB],
                                 rtmp[0], rtmpB[0], rtmp[1], rtmpB[1], extra=[LB])
                            tp, tB = ntp()
                            for k in range(6):
                                TR(tp[:, k * 128:(k + 1) * 128], L[:, k * 128:(k + 1) * 128], R=[LB], W=[tB], last=(k == 5))
                            TT(cqnT3[:, :, bsl], tp[:, 0:384].rearrange("p (k t) -> p k t", k=3),
                               pv[:, GQ:GQ + 3].unsqueeze(2).broadcast_to([128, 3, 128]), ALU.mult,
                               R=[tB, pvB], W=[latB[tb]])
                            TT(ckvnT3[:, :, bsl], tp[:, 384:640].rearrange("p (k t) -> p k t", k=2),
                               pv[:, GKV:GKV + 2].unsqueeze(2).broadcast_to([128, 2, 128]), ALU.mult,
                               R=[tB, pvB], W=[latB[tb]])
                            COPY(kpeT[:, bsl], tp[:, 640:768], R=[tB], W=[latB[tb]], eng="dve")

                        pend = None
                        for tb in range(NTB + 1):
                            cur = lat_front(tb) if tb < NTB else None
                            if not F_LATSKEW:
                                pend = cur
                                cur = None
                            if pend is not None:
                                lat_back(pend)
                            pend = cur
                        sch.barrier(bar[:, 0:1])
                with ExitStack() as s5:
                    qnT = T(s5, "qnT" + sfx, 4 * S, BF16)
                    qpeT = T(s5, "qpeT" + sfx, 2 * S, BF16)
                    knT = T(s5, "knT" + sfx, 4 * S, BF16)
                    Vt = T(s5, "Vt" + sfx, NTB * 512, BF16)
                    qB = [MB(f"q@{s}_{i}") for i in range(NQT)]
                    kB = [MB(f"k@{s}_{i}") for i in range(NQT)]
                    qpeB = [MB(f"qpe@{s}_{i}") for i in range(NTB)]
                    vB = [MB(f"v@{s}_{i}") for i in range(NTB)]
                    qpeT3 = qpeT[:, :].rearrange("p (k t) -> p k t", k=2)
                    with ExitStack() as s5b:
                        wuq = T(s5b, "wuq" + sfx, 3 * 768, BF16)
                        wukv = T(s5b, "wukv" + sfx, 2 * 1024, BF16)
                        wuqB = MB(f"wuq@{s}")
                        wukvB = MB(f"wukv@{s}")
                        sch.dma("pool", wuq[:, :].rearrange("p (k n) -> p k n", k=3),
                                wuq_d[:, :].rearrange("p (k n) -> p k n", k=3), W=[wuqB])
                        sch.dma("pool", wukv[:, :].rearrange("p (k n) -> p k n", k=2),
                                wukv_d[:, :].rearrange("p (k n) -> p k n", k=2), W=[wukvB])
                        qpb = [T(s5b, f"qpb{i}" + sfx, 256, BF16) for i in range(2)]
                        qpbB = [MB(f"qpb@{s}_{i}") for i in range(2)]
                        rtmp = [T(s5b, f"rtq{i}" + sfx, 128, F32) for i in range(2)]
                        rtmpB = [MB(f"rtq@{s}_{i}") for i in range(2)]
                        for tt in range(NQT):
                            tsl = slice(tt * 512, (tt + 1) * 512)
                            lb4 = latB[tt * 4:(tt + 1) * 4]
                            for h in range(4):
                                bank, bB = nbank()
                                for kc in range(3):
                                    MM(bank, wuq[:, kc * 768 + h * 128: kc * 768 + (h + 1) * 128],
                                       cqnT[:, kc * S + tt * 512: kc * S + (tt + 1) * 512], kc == 0, kc == 2,
                                       R=[wuqB] + lb4, W=[bB])
                                COPY(qnT[:, h * S + tt * 512: h * S + (tt + 1) * 512], bank, R=[bB], W=[qB[tt]])
                            for h in range(4):
                                bank, bB = nbank()
                                for kc in range(2):
                                    MM(bank, wukv[:, kc * 1024 + h * 128: kc * 1024 + (h + 1) * 128],
                                       ckvnT[:, kc * S + tt * 512: kc * S + (tt + 1) * 512], kc == 0, kc == 1,
                                       R=[wukvB] + lb4, W=[bB])
                                COPY(knT[:, h * S + tt * 512: h * S + (tt + 1) * 512], bank, R=[bB], W=[kB[tt]])
                            for tb in range(tt * 4, tt * 4 + 4):
                                bsl = slice(tb * 128, (tb + 1) * 128)
                                bank, bB = nbank()
                                for kc in range(3):
                                    MM(bank[:, 0:256], cqnT[:, kc * S + tb * 128: kc * S + (tb + 1) * 128],
                                       wuq[:, kc * 768 + 512: kc * 768 + 768], kc == 0, kc == 2,
                                       R=[wuqB, latB[tb]], W=[bB])
                                Q = qpb[tb % 2]
                                QB = qpbB[tb % 2]
                                rope(bank[:, 0:256].rearrange("p (g d) -> p g d", g=4), [bB], 4,
                                     cs[:, tb * 64 + 32: tb * 64 + 64], cs[:, tb * 64: tb * 64 + 32], csB,
                                     Q[:, :].rearrange("p (g d) -> p g d", g=4), [QB],
                                     rtmp[0], rtmpB[0], rtmp[1], rtmpB[1])
                                tp, tB = ntp()
                                for k in range(2):
                                    TR(tp[:, k * 128:(k + 1) * 128], Q[:, k * 128:(k + 1) * 128], R=[QB], W=[tB], last=(k == 1))
                                COPY(qpeT3[:, :, bsl], tp[:, 0:256].rearrange("p (k t) -> p k t", k=2),
                                     R=[tB], W=[qpeB[tb]], eng="act")
                                bank, bB = nbank()
                                for kc in range(2):
                                    MM(bank, ckvnT[:, kc * S + tb * 128: kc * S + (tb + 1) * 128],
                                       wukv[:, kc * 1024 + 512: kc * 1024 + 1024], kc == 0, kc == 1,
                                       R=[wukvB, latB[tb]], W=[bB])
                                COPY(Vt[:, tb * 512:(tb + 1) * 512], bank, R=[bB], W=[vB[tb]], eng="dve")
                        sch.barrier(bar[:, 0:1])
                    with ExitStack() as s6:
                        YM = [T(s6, f"YM{i}" + sfx, S, F32) for i in range(4)]
                        YMB = [MB(f"YM@{s}_{i}") for i in range(4)]
                        Pb = [T(s6, f"Pb{i}" + sfx, 512, BF16) for i in range(3)]
                        PbB = [MB(f"Pb@{s}_{i}") for i in range(3)]
                        rden = [T(s6, f"rden{i}" + sfx, 512, F32) for i in range(2)]
                        rdenB = [MB(f"rden@{s}_{i}") for i in range(2)]
                        sq = [T(s6, f"sqm{i}" + sfx, 512, F32) for i in range(2)]
                        sqB = [MB(f"sqm@{s}_{i}") for i in range(2)]
                        rt = T(s6, "rtm" + sfx, 512, F32)
                        rtB = MB(f"rtm@{s}")
                        it = 0
                        pcount = 0
                        for qi in range(NQT):
                            for h in range(4):
                                hb = (h % 2) * 64
                                nkc = 4 * qi + 4
                                ob = 4 if it % 2 == 0 else 2
                                O, OB = PS[:, ob * 512:(ob + 1) * 512], bankB[ob]
                                DN, DNB = PS[:, (ob + 1) * 512:(ob + 2) * 512], bankB[ob + 1]

                                def c0_of(kc):
                                    return 0 if kc < 4 * qi else (kc - 4 * qi) * 128

                                def emitS(kc):
                                    sbi = kc % 2
                                    Sb, SB = PS[:, sbi * 512:(sbi + 1) * 512], bankB[sbi]
                                    c0 = c0_of(kc)
                                    MM(Sb[:, c0:512], knT[:, h * S + kc * 128: h * S + (kc + 1) * 128],
                                       qnT[:, h * S + qi * 512 + c0: h * S + (qi + 1) * 512], True, False,
                                       R=[kB[kc // 4], qB[qi]], W=[SB])
                                    MM(Sb[:, c0:512], kpeT[hb:hb + 64, kc * 128:(kc + 1) * 128],
                                       qpeT[hb:hb + 64, (h // 2) * S + qi * 512 + c0: (h // 2) * S + (qi + 1) * 512],
                                       False, True, R=[latB[kc]] + qpeB[qi * 4:(qi + 1) * 4], W=[SB])

                                emitS(0)
                                for kc in range(nkc):
                                    if kc + 1 < nkc:
                                        emitS(kc + 1)
                                    sbi = kc % 2
                                    Sb, SB = PS[:, sbi * 512:(sbi + 1) * 512], bankB[sbi]
                                    c0 = c0_of(kc)
                                    P, PB = Pb[pcount % 3], PbB[pcount % 3]
                                    pcount += 1
                                    ACT(P[:, c0:512], Sb[:, c0:512], AF.Exp, R=[SB], W=[PB], scale=mla_scale)
                                    if kc >= 4 * qi:
                                        TT(P[:, c0:c0 + 128], P[:, c0:c0 + 128], tri[:, :], ALU.mult, R=[PB, triB], W=[PB])
                                    MM(O[:, c0:512], Vt[:, kc * 512 + h * 128: kc * 512 + (h + 1) * 128], P[:, c0:512],
                                       kc == 0, kc == nkc - 1, R=[vB[kc], PB], W=[OB])
                                    MM(DN[:, c0:512], onesb[:, :], P[:, c0:512], kc == 0, kc == nkc - 1,
                                       R=[constB, PB], W=[DNB])
                                rd, rdB = rden[it % 2], rdenB[it % 2]
                                RECIP(rd[:, :], DN, R=[DNB], W=[rdB])
                                TT(YM[h][:, qi * 512:(qi + 1) * 512], O, rd[:, :], ALU.mult, R=[OB, rdB], W=[YMB[h]])
                                it += 1
                            feat_norm(YM, YMB, GMO, 512, qi, 4, sq, sqB, rt, rtB)
                        sch.barrier(bar[:, 0:1])
            if dbg:
                dB = MB(f"dbgy@{s}")
                sch.dma("sp", dbg_yT[s], yT[:, :], R=yTB + [dB])

            with ExitStack() as p2:
                G = [T(p2, f"G{i}" + sfx, 1024, F32) for i in range(3)]
                GB = [MB(f"G@{s}_{i}") for i in range(3)]
                for i in range(3):
                    sch.dma("sp", G[i][:], grow_d[i], W=[GB[i]])
                mkT = T(p2, "mkT" + sfx, 8 * 256, BF16)
                mv = T(p2, "mv" + sfx, 2 * 1024, BF16)
                mkB = MB(f"mk@{s}")
                mvB = MB(f"mv@{s}")
                with ExitStack() as pm:
                    mnT = T(pm, "mnT" + sfx, 8 * 256, BF16)
                    mnB = [MB(f"mn@{s}_{i}") for i in range(2)]
                    memt = [T(pm, f"memt{i}" + sfx, 1024, F32) for i in range(2)]
                    memtB = [MB(f"memt@{s}_{i}") for i in range(2)]
                    xs = [T(pm, f"xsm{i}" + sfx, 1024, BF16) for i in range(2)]
                    xsB = [MB(f"xsm@{s}_{i}") for i in range(2)]
                    junk = T(pm, "junkm" + sfx, 1024, BF16)
                    junkB = MB(f"junkm@{s}")
                    wm = T(pm, "wm" + sfx, 4 * 4096, BF16)
                    wmB = [MB(f"wm@{s}_{i}") for i in range(4)]
                    for i in range(4):
                        sch.dma("sp", wm[:, i * 4096:(i + 1) * 4096], scr_d[SLOT_MK + i], R=[scrB], W=[wmB[i]])
                    mnT3 = mnT[:, :].rearrange("p (k t) -> p k t", k=8)
                    for mc in range(2):
                        sch.dma("sp", memt[mc][:], mem_d[s, mc * 128:(mc + 1) * 128, :], W=[memtB[mc]])
                    for mc in range(2):
                        nt(memt[mc][:, :], [memtB[mc]], 1024, pv[:, GMKV:GMKV + 8],
                           mnT3[:, :, mc * 128:(mc + 1) * 128], [mnB[mc]], xs[mc], xsB[mc], junk, junkB)
                    for sl in range(2):
                        W_ = wm[:, sl * 4096:(sl + 1) * 4096]
                        for cc in range(4):
                            c = sl * 4 + cc
                            bank, bB = nbank()
                            for kc in range(8):
                                MM(bank[:, 0:256], W_[:, cc * 1024 + kc * 128: cc * 1024 + (kc + 1) * 128],
                                   mnT[:, kc * 256:(kc + 1) * 256], kc == 0, kc == 7, R=[wmB[sl]] + mnB, W=[bB])
                            COPY(mkT[:, c * 256:(c + 1) * 256], bank[:, 0:256], R=[bB], W=[mkB])
                    for fh in range(2):
                        W_ = wm[:, (2 + fh) * 4096:(3 + fh) * 4096]
                        for mc in range(2):
                            bank, bB = nbank()
                            for kc in range(8):
                                MM(bank, mnT[:, kc * 256 + mc * 128: kc * 256 + (mc + 1) * 128],
                                   W_[:, kc * 512:(kc + 1) * 512], kc == 0, kc == 7, R=[wmB[2 + fh], mnB[mc]], W=[bB])
                            COPY(mv[:, mc * 1024 + fh * 512: mc * 1024 + (fh + 1) * 512], bank, R=[bB], W=[mvB])
                    sch.barrier(bar[:, 0:1])

                NRG, NRF = 3, 2
                ringG = T(p2, "ringG" + sfx, NRG * 4096, BF16)
                ringF = T(p2, "ringF" + sfx, NRF * 4096, BF16)
                Ht = [T(p2, f"Ht{i}" + sfx, 4 * 1024, F32) for i in range(2)]
                HtB = [[MB(f"Ht@{s}_{i}_{tb}") for tb in range(4)] for i in range(2)]
                xnF = T(p2, "xnF" + sfx, 8 * 512, BF16)
                xnFB = [MB(f"xnF@{s}_{i}") for i in range(4)]
                xnF3 = xnF[:, :].rearrange("p (k t) -> p k t", k=8)
                xnG = T(p2, "xnG" + sfx, 8 * 512, BF16)
                xnGB = [MB(f"xnG@{s}_{i}") for i in range(4)]
                xnG3 = xnG[:, :].rearrange("p (k t) -> p k t", k=8)
                PLh = T(p2, "PLh" + sfx, NJ * 512, BF16)
                PLhB = [MB(f"PLh@{s}_{i}") for i in range(NJ)]
                PLq = T(p2, "PLq" + sfx, 8 * 512, BF16)
                PLqB = [MB(f"PLq@{s}_{i}") for i in range(8)]
                Pb = [T(p2, f"Pm{i}" + sfx, 512, BF16) for i in range(2)]
                PbB = [MB(f"Pm@{s}_{i}") for i in range(2)]
                rden = [T(p2, f"rdm{i}" + sfx, 512, F32) for i in range(1)]
                rdenB = [MB(f"rdm@{s}_{i}") for i in range(1)]
                Fsb = T(p2, "Fsb" + sfx, 4 * 1024, F32)
                FsbB = [MB(f"Fsb@{s}_{i}") for i in range(4)]
                sg = [T(p2, f"sg{i}" + sfx, 512, F32) for i in range(2)]
                sgB = [MB(f"sg@{s}_{i}") for i in range(2)]
                tmpo = T(p2, "tmpo" + sfx, 1024, F32)
                tmpoB = MB(f"tmpo@{s}")
                xs = [T(p2, f"xs2{i}" + sfx, 1024, BF16) for i in range(2)]
                xsB = [MB(f"xs2@{s}_{i}") for i in range(2)]
                junk = T(p2, "junk2" + sfx, 1024, BF16)
                junkB = MB(f"junk2@{s}")

                class Ring:
                    def __init__(self, tensor, nslots, uses, name):
                        self.t, self.n, self.uses = tensor, nslots, uses
                        self.B = [MB(f"{name}@{s}_{i}") for i in range(nslots)]
                        self.k = self.issued = self.released = 0

                    def pump(self):
                        while self.issued < len(self.uses) and self.issued < self.released + self.n:
                            i = self.issued
                            r = i % self.n
                            sch.dma("sp", self.t[:, r * 4096:(r + 1) * 4096], scr_d[self.uses[i]], R=[scrB], W=[self.B[r]])
                            self.issued += 1

                    def acquire(self, expect):
                        k = self.k
                        self.k += 1
                        assert k < self.issued, "ring: acquire before load issued"
                        assert self.uses[k] == expect, (self.uses[k], expect)
                        r = k % self.n
                        return self.t[:, r * 4096:(r + 1) * 4096], self.B[r]

                    def release(self):
                        self.released += 1
                        self.pump()

                rG = Ring(ringG, NRG, (list(range(6, 17)) + list(range(17, 23))) * NQT, "rG")
                rF = Ring(ringF, NRF, [0, 1, 2, 3, 4, 5] * NQT, "rF")
                rG.pump()
                rF.pump()
                FB = [(PS[:, 4 * 512:5 * 512], bankB[4]), (PS[:, 5 * 512:6 * 512], bankB[5])]
                FPAIR = (PS[:, 4 * 512:6 * 512], [bankB[4], bankB[5]])
                fbs = {"i": 0}

                def fbank():
                    fbs["i"] += 1
                    return FB[fbs["i"] % 2]

                def postres(src, srcB, gi, HtAP, HtBuf):
                    ss, ssB = small()
                    ACT(junk[:, :], src, AF.Square, R=srcB, W=[junkB, ssB], accum_out=ss[:, 0:1])
                    rstd_from_ss(ss, ssB, 1, [1.0 / 1024])
                    TT(tmpo[:, :], src, G[gi][:, :], ALU.mult, R=list(srcB) + [GB[gi], ssB], W=[tmpoB])
                    STT(HtAP, tmpo[:, :], ss[:, 0:1], HtAP, ALU.mult, ALU.add, R=[tmpoB, ssB, HtBuf], W=[HtBuf])

                def front_items(t):
                    p = t % 2
                    H = Ht[p]
                    HB = HtB[p]
                    stt = {}

                    def xload():
                        for tb in range(4):
                            sch.dma("pool", H[:, tb * 1024:(tb + 1) * 1024],
                                    x_d[s, t * 512 + tb * 128: t * 512 + (tb + 1) * 128, :], W=[HB[tb]])

                    def proj(tb, which):
                        if tb == 0:
                            stt["W"] = [rF.acquire(0 + 4 * which), rF.acquire(1 + 4 * which)]
                        pair, pB = FPAIR
                        for fh in range(2):
                            W_, WB = stt["W"][fh]
                            for kc in range(8):
                                if which == 0:
                                    lhs = yT[:, kc * S + t * 512 + tb * 128: kc * S + t * 512 + (tb + 1) * 128]
                                    rr = [yTB[t], WB]
                                else:
                                    lhs = PLq[:, kc * 512 + tb * 128: kc * 512 + (tb + 1) * 128]
                                    rr = [PLqB[kc], WB]
                                MM(pair[:, fh * 512:(fh + 1) * 512], lhs, W_[:, kc * 512:(kc + 1) * 512],
                                   kc == 0, kc == 7, R=rr, W=[pB[fh]])
                        if tb == 3:
                            rF.release()
                            rF.release()
                        hap = H[:, tb * 1024:(tb + 1) * 1024]
                        postres(pair, pB, which, hap, HB[tb])
                        ss, ssB = nt_a(hap, [HB[tb]], 1024, junk, junkB)
                        ACT(xs[tb % 2][:, :], hap, AF.Copy, R=[HB[tb], ssB], W=[xsB[tb % 2]], scale=ss[:, 0:1])

                    def trn(tb, which):
                        x_, xB_ = xs[tb % 2], xsB[tb % 2]
                        tp, tB = ntp()
                        for k in range(8):
                            TR(tp[:, k * 128:(k + 1) * 128], x_[:, k * 128:(k + 1) * 128], R=[xB_], W=[tB], last=(k == 7))
                        gc = (pv[:, GPMEM:GPMEM + 8] if which == 0 else pv[:, GPF:GPF + 8])
                        g3 = gc.unsqueeze(2).broadcast_to([128, 8, 128])
                        dst3 = (xnF3 if which == 0 else xnG3)[:, :, tb * 128:(tb + 1) * 128]
                        dB = (xnFB if which == 0 else xnGB)[tb]
                        TT(dst3, tp[:, 0:1024].rearrange("p (k t) -> p k t", k=8), g3, ALU.mult, R=[tB, pvB], W=[dB])

                    def mq(c):
                        if c == 0:
                            stt["Q"] = [rF.acquire(2), rF.acquire(3)]
                        W_, WB = stt["Q"][c // 4]
                        cc = c % 4
                        bank, bB = fbank()
                        for kc in range(8):
                            MM(bank, W_[:, cc * 1024 + kc * 128: cc * 1024 + (kc + 1) * 128],
                               xnF[:, kc * 512:(kc + 1) * 512], kc == 0, kc == 7, R=[WB] + xnFB, W=[bB])
                        COPY(PLq[:, c * 512:(c + 1) * 512], bank, R=[bB], W=[PLqB[c]])
                        if c == 7:
                            rF.release()
                            rF.release()

                    def attS(h):
                        Ps = []
                        for mc in range(2):
                            bank, bB = fbank()
                            for dc in range(2):
                                c = h * 2 + dc
                                MM(bank, mkT[:, c * 256 + mc * 128: c * 256 + (mc + 1) * 128],
                                   PLq[:, c * 512:(c + 1) * 512], dc == 0, dc == 1, R=[mkB, PLqB[c]], W=[bB])
                            P, PB = Pb[mc], PbB[mc]
                            ACT(P[:, :], bank, AF.Exp, R=[bB], W=[PB], scale=mem_scale)
                            Ps.append((P, PB))
                        stt["Ps"] = Ps

                    def attPV(h):
                        Ps = stt["Ps"]
                        dn, dnB = fbank()
                        for mc in range(2):
                            MM(dn, onesb[:, :], Ps[mc][0][:, :], mc == 0, mc == 1, R=[constB, Ps[mc][1]], W=[dnB])
                        rd, rdB = rden[0], rdenB[0]
                        RECIP(rd[:, :], dn, R=[dnB], W=[rdB])
                        for dvc in range(2):
                            c = h * 2 + dvc
                            bank, bB = fbank()
                            for mc in range(2):
                                MM(bank, mv[:, mc * 1024 + c * 128: mc * 1024 + (c + 1) * 128], Ps[mc][0][:, :],
                                   mc == 0, mc == 1, R=[mvB, Ps[mc][1]], W=[bB])
                            TT(PLq[:, c * 512:(c + 1) * 512], bank, rd[:, :], ALU.mult, R=[bB, rdB], W=[PLqB[c]])

                    A = [xload]
                    A += [lambda: proj(0, 0), lambda: proj(1, 0), lambda: trn(0, 0), lambda: proj(2, 0),
                          lambda: trn(1, 0), lambda: proj(3, 0), lambda: trn(2, 0), lambda: trn(3, 0)]
                    A += [(lambda c=c: mq(c)) for c in range(8)]
                    A += [lambda: attS(0)]
                    for h in range(4):
                        A += [(lambda h=h: attPV(h))]
                        if h + 1 < 4:
                            A += [(lambda h=h: attS(h + 1))]
                    Bq = [lambda: proj(0, 1), lambda: proj(1, 1), lambda: trn(0, 1), lambda: proj(2, 1),
                          lambda: trn(1, 1), lambda: proj(3, 1), lambda: trn(2, 1), lambda: trn(3, 1)]
                    return A, Bq

                def ffn_items(t):
                    p = t % 2
                    H = Ht[p]
                    HB = HtB[p]

                    def gu(jj):
                        W_, WB = rG.acquire(6 + jj)
                        for jl in range(2):
                            j = jj * 2 + jl
                            b0 = 0 if jl == 0 else 2
                            bg, bgB = PS[:, b0 * 512:(b0 + 1) * 512], bankB[b0]
                            bu, buB = PS[:, (b0 + 1) * 512:(b0 + 2) * 512], bankB[b0 + 1]
                            for kc in range(8):
                                MM(bg, W_[:, (jl * 2 + 0) * 1024 + kc * 128: (jl * 2 + 0) * 1024 + (kc + 1) * 128],
                                   xnG[:, kc * 512:(kc + 1) * 512], kc == 0, kc == 7, R=[WB] + xnGB, W=[bgB])
                            for kc in range(8):
                                MM(bu, W_[:, (jl * 2 + 1) * 1024 + kc * 128: (jl * 2 + 1) * 1024 + (kc + 1) * 128],
                                   xnG[:, kc * 512:(kc + 1) * 512], kc == 0, kc == 7, R=[WB] + xnGB, W=[buB])
                            g_, gB_ = sg[j % 2], sgB[j % 2]
                            ACT(g_[:, :], bg, AF.Silu, R=[bgB], W=[gB_])
                            TT(PLh[:, j * 512:(j + 1) * 512], bu, g_[:, :], ALU.mult, R=[buB, gB_], W=[PLhB[j]])
                        rG.release()

                    def down(fh, sl3):
                        W_, WB = rG.acquire(17 + fh * 3 + sl3)
                        nj = 8 if sl3 < 2 else 6
                        for jl in range(nj):
                            j = sl3 * 8 + jl
                            for tb in range(4):
                                last = (jl == nj - 1 and tb == 3)
                                MM(PS[:, tb * 512:(tb + 1) * 512], PLh[:, j * 512 + tb * 128: j * 512 + (tb + 1) * 128],
                                   W_[:, jl * 512:(jl + 1) * 512], j == 0, j == NJ - 1, R=[PLhB[j], WB], W=[bankB[tb]],
                                   inc=(True if last else None))
                        rG.release()

                    def evac(fh):
                        for tb in range(4):
                            COPY(Fsb[:, tb * 1024 + fh * 512: tb * 1024 + (fh + 1) * 512], PS[:, tb * 512:(tb + 1) * 512],
                                 R=[bankB[tb]], W=[FsbB[tb]])

                    def fin():
                        for tb in range(4):
                            hap = H[:, tb * 1024:(tb + 1) * 1024]
                            postres(Fsb[:, tb * 1024:(tb + 1) * 1024], [FsbB[tb]], 2, hap, HB[tb])
                            sch.dma("pool", out_d[s, t * 512 + tb * 128: t * 512 + (tb + 1) * 128, :], hap, R=[HB[tb]])

                    GU = [(lambda jj=jj: gu(jj)) for jj in range(11)]
                    DN = []
                    for fh in range(2):
                        for sl3 in range(3):
                            DN.append(lambda fh=fh, sl3=sl3: down(fh, sl3))
                        DN.append(lambda fh=fh: evac(fh))
                    DN.append(fin)
                    return GU, DN

                def interleave(Xs, Ys):
                    nx, ny = len(Xs), len(Ys)
                    yi = 0
                    for i, xf in enumerate(Xs):
                        xf()
                        tgt = ((i + 1) * ny + nx - 1) // nx if nx else ny
                        while yi < min(tgt, ny):
                            Ys[yi]()
                            yi += 1
                    while yi < ny:
                        Ys[yi]()
                        yi += 1

                A0, B0 = front_items(0)
                for f in A0 + B0:
                    f()
                for tt in range(NQT):
                    GU, DN = ffn_items(tt)
                    if tt + 1 < NQT:
                        A1, B1 = front_items(tt + 1)
                    else:
                        A1, B1 = [], []
                    interleave(GU, A1)
                    interleave(DN, B1)
                sch.barrier(bar[:, 0:1])
        sch.final_wait()

    with nc.Block() as block:
        @block.sync
        def _(sync):
            body()
    es.close()
    return nc, sch


def _kc(W):
    K, N = W.shape
    return np.ascontiguousarray(W.reshape(K // 128, 128, N).transpose(1, 0, 2)).reshape(128, -1)


def _cols(v):
    return np.ascontiguousarray(v.reshape(-1, 128).T)


def pack_shared(inp):
    f = np.float32
    w_in = inp["w_in"][0]
    winA = _kc(w_in[:, 0:1024])
    winB = _kc(np.concatenate([w_in[:, 1024:1408], w_in[:, 1664:1728], w_in[:, 1664:1728], w_in[:, 1408:1664]], axis=1))
    w_uq = inp["w_uq"][0].reshape(384, 4, 192)
    wuq = _kc(np.concatenate([w_uq[:, :, 0:128].reshape(384, 512), w_uq[:, :, 128:192].reshape(384, 256)], axis=1))
    w_ukv = inp["w_ukv"][0].reshape(256, 4, 256)
    wukv = _kc(np.concatenate([w_ukv[:, :, 0:128].reshape(256, 512), w_ukv[:, :, 128:256].reshape(256, 512)], axis=1))
    wlru = np.zeros((128, 4, 2, 128), f)
    for c in range(4):
        for g, key in enumerate(("lru_wa", "lru_wx")):
            for hh in range(2):
                wlru[hh * 64:(hh + 1) * 64, c, g, hh * 64:(hh + 1) * 64] = inp[key][0][2 * c + hh]
    wlru = wlru.reshape(128, -1)
    slots = np.zeros((NSLOT, 128, 4096), f)

    def moving(W, fh):
        return _kc(W[:, fh * 512:(fh + 1) * 512])

    def stationary(W, sl):
        Wr = W.reshape(8, 128, 8, 128)[:, :, sl * 4:(sl + 1) * 4, :]
        return np.ascontiguousarray(Wr.transpose(1, 2, 0, 3)).reshape(128, -1)

    for fh in range(2):
        slots[0 + fh] = moving(inp["w_out"][0], fh)
        slots[2 + fh] = stationary(inp["w_mq"][0], fh)
        slots[4 + fh] = moving(inp["w_mo"][0], fh)
        slots[SLOT_MK + fh] = stationary(inp["w_mk"][0], fh)
        slots[SLOT_MV + fh] = moving(inp["w_mv"][0], fh)
    wg = inp["w_gate"][0].reshape(8, 128, NJ, 128)
    wu = inp["w_up"][0].reshape(8, 128, NJ, 128)
    for jj in range(11):
        blk = np.zeros((128, 2, 2, 8, 128), f)
        for jl in range(2):
            j = jj * 2 + jl
            blk[:, jl, 0] = wg[:, :, j, :].transpose(1, 0, 2)
            blk[:, jl, 1] = wu[:, :, j, :].transpose(1, 0, 2)
        slots[6 + jj] = blk.reshape(128, -1)
    wd = inp["w_down"][0].reshape(NJ, 128, 1024)
    for fh in range(2):
        for sl3 in range(3):
            nj = 8 if sl3 < 2 else 6
            blk = np.zeros((128, 8, 512), f)
            blk[:, 0:nj, :] = wd[sl3 * 8: sl3 * 8 + nj, :, fh * 512:(fh + 1) * 512].transpose(1, 0, 2)
            slots[17 + fh * 3 + sl3] = blk.reshape(128, -1)
    pv = np.zeros((128, NPV), f)
    pv[:, GPM:GPM + 8] = _cols(inp["g_pre_mix"][0])
    for k in range(4):
        pv[:, CW + k * 4: CW + k * 4 + 4] = _cols(inp["conv_w"][0][k])
    pv[:, CB:CB + 4] = _cols(inp["conv_b"][0])
    pv[:, BA:BA + 4] = _cols(inp["lru_ba"][0])
    pv[:, BX:BX + 4] = _cols(inp["lru_bx"][0])
    pv[:, LAM:LAM + 4] = _cols(inp["lru_lambda"][0])
    pv[:, GQ:GQ + 3] = _cols(inp["g_q_lat"][0])
    pv[:, GKV:GKV + 2] = _cols(inp["g_kv_lat"][0])
    pv[:, GLO:GLO + 4] = _cols(inp["g_lru_out"][0])
    pv[:, GMO:GMO + 4] = _cols(inp["g_mla_out"][0])
    pv[:, GPMEM:GPMEM + 8] = _cols(inp["g_pre_mem"][0])
    pv[:, GMKV:GMKV + 8] = _cols(inp["g_mem_kv"][0])
    pv[:, GPF:GPF + 8] = _cols(inp["g_pre_ffn"][0])
    grow = np.stack([np.broadcast_to(inp[k][0][None, :], (128, 1024)) for k in
                     ("g_post_mix", "g_post_mem", "g_post_ffn")]).astype(f)
    invf = (10000.0 ** (-np.arange(0, 64, 2, dtype=np.float32) / 64)).astype(f)
    invf = np.ascontiguousarray(np.broadcast_to(invf[None, :], (128, 32)))
    ident = np.eye(128, dtype=f)
    tri = (np.arange(128)[None, :] >= np.arange(128)[:, None]).astype(f)
    return dict(pv=pv, grow=np.ascontiguousarray(grow), invf=invf, ident=ident, tri=tri, winA=winA, winB=winB,
                wuq=wuq, wukv=wukv, wlru=wlru, wslots=slots)


def pack_core(inp, b0, nseq, S):
    x = np.ascontiguousarray(inp["x"][b0:b0 + nseq], dtype=np.float32)
    mem = np.ascontiguousarray(inp["mem"][b0:b0 + nseq], dtype=np.float32)
    pos = np.asarray(inp["positions"][b0:b0 + nseq], dtype=np.int32)
    pos = np.ascontiguousarray(pos.reshape(nseq, S // 128, 128).transpose(2, 0, 1).reshape(128, -1))
    return dict(x=x, mem=mem, pos=pos)


_CACHE = {}


def kernel(**inputs):
    inp = {k: np.asarray(v) for k, v in inputs.items()}
    B, S, _ = inp["x"].shape
    nseq = B // NCORES
    key = (S, nseq)
    if key not in _CACHE:
        _CACHE[key] = build(S, nseq)[0]
    nc = _CACHE[key]
    shared = pack_shared(inp)
    in_maps = []
    for c in range(NCORES):
        m = dict(shared)
        m.update(pack_core(inp, c * nseq, nseq, S))
        in_maps.append(m)
    res = run_bass_kernel_spmd(nc, in_maps, core_ids=list(range(NCORES)))
    out = np.concatenate([np.asarray(r["out"]) for r in res.results], axis=0)
    return out.astype(np.float32)
```

```python
import math
from contextlib import ExitStack

import numpy as np
import concourse.bass as bass
import concourse.mybir as mybir
from concourse.bass_utils import run_bass_kernel_spmd

F32 = mybir.dt.float32
BF16 = mybir.dt.bfloat16
I32 = mybir.dt.int32
AF = mybir.ActivationFunctionType
ALU = mybir.AluOpType

NCORES = 8
D = 1024
DFF = 2816
NJ = DFF // 128
NMEM = 256
EPS = 1e-6
NSLOT = 27
SLOT_MK, SLOT_MV = 23, 25
GPM, CW, CB, BA, BX, LAM, GQ, GKV, GLO, GMO, GPMEM, GMKV, GPF = 0, 8, 24, 28, 32, 36, 40, 43, 45, 49, 53, 61, 69
NPV = 77
HC, H2, BAH, BXH = 77, 81, 85, 89
NPVT = 96
SEM_LIMIT = 12000
NRING = 5
import os
F_PIPEA = os.environ.get('K_PIPEA', '1') == '1'
F_LATSKEW = os.environ.get('K_LATSKEW', '1') == '1'
F_P2EARLY = os.environ.get('K_P2EARLY', '0') == '1'
F_WPRE = os.environ.get('K_WPRE', '0') == '1'
LAT_DEPTH = int(os.environ.get('K_LATDEPTH', '2'))
F_QKVSKEW = os.environ.get('K_QKVSKEW', '1') == '1'


class Buf:
    __slots__ = ("name", "w", "r", "ld", "ldc", "ldk", "st", "stc", "stk")

    def __init__(self, name):
        self.name = name
        self.w = {}
        self.r = {}
        self.ld = None
        self.ldc = 0
        self.ldk = None
        self.st = None
        self.stc = 0
        self.stk = None


class _Eng:
    def __init__(self, sch, name, h):
        self.sch = sch
        self.name = name
        self.h = h
        self.sem = None
        self.key = None
        self.cnt = 0
        self.nsem = 0
        self.known = {}
        self.pending = False

    def newsem(self):
        self.sem = self.sch.alloc_sem(f"e_{self.name}{self.nsem}")
        self.key = (self.name, self.nsem)
        self.nsem += 1
        self.cnt = 0


class Sched:
    def __init__(self, nc, es):
        self.nc = nc
        self.es = es
        self.nsems = 0
        self.E = {
            "pe": _Eng(self, "pe", nc.tensor),
            "act": _Eng(self, "act", nc.scalar),
            "dve": _Eng(self, "dve", nc.vector),
            "pool": _Eng(self, "pool", nc.gpsimd),
            "sp": _Eng(self, "sp", nc.sync),
        }
        for e in self.E.values():
            e.newsem()
        self.dma_tokens = {}
        self.nwaits = 0
        self.ninst = 0

    def alloc_sem(self, name):
        self.nsems += 1
        return self.es.enter_context(self.nc.semaphore(name))

    @staticmethod
    def _merge(d, src):
        for k, (s, v) in src.items():
            if k not in d or d[k][1] < v:
                d[k] = (s, v)

    def _wait(self, E, key, sem, val):
        if E.known.get(key, 0) >= val:
            return
        E.h.wait_ge(sem, val)
        E.known[key] = val
        self.nwaits += 1

    def _deps(self, E, R, W, skip_keys=()):
        deps = {}
        for b in R:
            self._merge(deps, b.w)
        for b in W:
            self._merge(deps, b.w)
            self._merge(deps, b.r)
        for key, (sem, val) in deps.items():
            if key in skip_keys:
                continue
            if E.name == "pe" and key[0] == "pe":
                continue
            self._wait(E, key, sem, val)

    def op(self, en, fn, R=(), W=(), inc=True):
        E = self.E[en]
        self._deps(E, R, W)
        ins = fn(E.h)
        self.ninst += 1
        if inc:
            if E.cnt >= SEM_LIMIT and not E.pending:
                E.newsem()
            E.cnt += 1
            ins.then_inc(E.sem, 1)
            E.pending = False
            tok = (E.key, E.sem, E.cnt)
        else:
            E.pending = True
            tok = (E.key, E.sem, E.cnt + 1)
        for b in R:
            k = tok[0]
            if k not in b.r or b.r[k][1] < tok[2]:
                b.r[k] = (tok[1], tok[2])
        for b in W:
            b.w = {tok[0]: (tok[1], tok[2])}
            b.r = {}
        return ins

    def dma(self, q, out, in_, R=(), W=(), **kw):
        E = self.E[q]
        skip = ()
        if W and W[0].ldk is not None:
            skip = (W[0].ldk,)
        self._deps(E, R, W, skip_keys=skip)
        ins = E.h.dma_start(out=out, in_=in_, **kw)
        self.ninst += 1
        if W:
            b = W[0]
            if b.ld is None:
                b.ld = self.alloc_sem("ld_" + b.name)
                b.ldk = ("ld", b.name)
            b.ldc += 16
            ins.then_inc(b.ld, 16)
            tok = (b.ldk, b.ld, b.ldc)
        else:
            b = R[0]
            if b.st is None:
                b.st = self.alloc_sem("st_" + b.name)
                b.stk = ("st", b.name)
            b.stc += 16
            ins.then_inc(b.st, 16)
            tok = (b.stk, b.st, b.stc)
        self.dma_tokens[tok[0]] = (tok[1], tok[2])
        for bb in R:
            k = tok[0]
            if k not in bb.r or bb.r[k][1] < tok[2]:
                bb.r[k] = (tok[1], tok[2])
        for bb in W:
            if bb is W[0] and skip:
                bb.w[tok[0]] = (tok[1], tok[2])
                bb.r = {}
            else:
                bb.w = {tok[0]: (tok[1], tok[2])}
                bb.r = {}
        return ins

    def barrier(self, bar_ap):
        Dv = self.E["dve"]
        for E in self.E.values():
            if E is Dv:
                continue
            assert not E.pending
            if E.cnt > 0:
                self._wait(Dv, E.key, E.sem, E.cnt)
        for k, (s, v) in self.dma_tokens.items():
            self._wait(Dv, k, s, v)
        if Dv.cnt > 0:
            self._wait(Dv, Dv.key, Dv.sem, Dv.cnt)
        ins = Dv.h.memset(bar_ap, 0.0)
        if Dv.cnt >= SEM_LIMIT:
            Dv.newsem()
        Dv.cnt += 1
        ins.then_inc(Dv.sem, 1)
        for E in self.E.values():
            if E is Dv:
                continue
            self._wait(E, Dv.key, Dv.sem, Dv.cnt)

    def final_wait(self):
        sp = self.E["sp"]
        for k, (s, v) in self.dma_tokens.items():
            self._wait(sp, k, s, v)
        for E in self.E.values():
            if E is sp or E.cnt == 0:
                continue
            self._wait(sp, E.key, E.sem, E.cnt)


def build(S=2048, NSEQ=2, dbg=False):
    NTB = S // 128
    NQT = S // 512
    nc = bass.Bass("TRN2", target_bir_lowering=False)

    def din(name, shape, dt=F32):
        return nc.dram_tensor(name, list(shape), dt, kind="ExternalInput").ap()

    x_d = din("x", [NSEQ, S, D])
    mem_d = din("mem", [NSEQ, NMEM, D])
    pos_d = din("pos", [128, NSEQ * NTB], I32)
    pv_d = din("pv", [128, NPV])
    grow_d = din("grow", [3, 128, D])
    invf_d = din("invf", [128, 32])
    ident_d = din("ident", [128, 128])
    tri_d = din("tri", [128, 128])
    winA_d = din("winA", [128, 8 * 1024])
    winB_d = din("winB", [128, 8 * 768])
    wuq_d = din("wuq", [128, 3 * 768])
    wukv_d = din("wukv", [128, 2 * 1024])
    wlru_d = din("wlru", [128, 4 * 2 * 128])
    wsl_d = din("wslots", [NSLOT, 128, 4096])
    scr_d = nc.dram_tensor("scr", [NSLOT, 128, 4096], BF16, kind="Internal").ap()
    out_d = nc.dram_tensor("out", [NSEQ, S, D], F32, kind="ExternalOutput").ap()
    if dbg:
        dbg_yT = nc.dram_tensor("dbg_yT", [NSEQ, 128, 8 * S], BF16, kind="ExternalOutput").ap()

    es = ExitStack()
    sch = Sched(nc, es)

    def T(scope, name, cols, dt):
        return scope.enter_context(nc.sbuf_tensor("sb_" + name, [128, cols], dt))

    mla_scale = 1.0 / math.sqrt(192.0)
    mem_scale = 1.0 / math.sqrt(256.0)

    def body():
        pers = es
        ident = T(pers, "ident", 128, BF16)
        tri = T(pers, "tri", 128, BF16)
        onesb = T(pers, "onesb", 128, BF16)
        onesf = T(pers, "onesf", 128, F32)
        pv = T(pers, "pv", NPVT, F32)
        cst = T(pers, "cst", 8, F32)
        invf = T(pers, "invf", 32, F32)
        posi = T(pers, "posi", NSEQ * NTB, I32)
        smallt = T(pers, "smallt", 32 * 4, F32)
        bar = T(pers, "bar", 4, F32)
        yT = T(pers, "yT", 8 * S, BF16)
        PS = pers.enter_context(nc.psum_tensor("ps", [128, 6 * 512], F32))
        TPS = pers.enter_context(nc.psum_tensor("tps", [128, 2 * 1024], BF16))

        constB = Buf("const")
        identB = Buf("identb")
        triB = Buf("trib")
        pvB = Buf("pvb")
        invfB = Buf("invfb")
        posB = Buf("posb")
        bankB = [Buf(f"bank{i}") for i in range(6)]
        tpB = [Buf(f"tpb{i}") for i in range(2)]
        smallB = [Buf(f"small{i}") for i in range(32)]
        yTB = [Buf(f"yT{i}") for i in range(NQT)]
        scrB = Buf("scr")
        _bufcache = {}

        def MB(name):
            import re as _re
            key = _re.sub(r"@\d", "@", name)
            if key not in _bufcache:
                _bufcache[key] = Buf(key.replace("@", "_"))
            return _bufcache[key]

        st = {"small": 0, "bank": 0, "pair": 0, "tp": 0, "alt": 0}

        def small():
            i = st["small"] % 32
            st["small"] += 1
            return smallt[:, i * 4:(i + 1) * 4], smallB[i]

        def nbank():
            i = st["bank"] % 6
            st["bank"] += 1
            return PS[:, i * 512:(i + 1) * 512], bankB[i]

        def npair():
            i = st["pair"] % 3
            st["pair"] += 1
            return PS[:, i * 1024:(i + 1) * 1024], [bankB[2 * i], bankB[2 * i + 1]]

        def ntp():
            i = st["tp"] % 2
            st["tp"] += 1
            return TPS[:, i * 1024:(i + 1) * 1024], tpB[i]

        def MM(out, lhsT, rhs, start, stop, R, W, inc=None):
            return sch.op("pe", lambda e: e.matmul(out, lhsT=lhsT, rhs=rhs, start=start, stop=stop),
                          R=R, W=W, inc=(stop if inc is None else inc))

        def TR(out, in_, R, W, last):
            return sch.op("pe", lambda e: e.transpose(out, in_, ident[:]), R=list(R) + [identB], W=W, inc=last)

        def ACT(out, in_, func, R, W, **kw):
            return sch.op("act", lambda e: e.activation(out=out, in_=in_, func=func, **kw), R=R, W=W)

        def TS(out, in0, s1, s2, op0, op1, R, W, eng="dve"):
            if s2 is None:
                return sch.op(eng, lambda e: e.tensor_scalar(out=out, in0=in0, scalar1=s1, scalar2=None, op0=op0),
                              R=R, W=W)
            return sch.op(eng, lambda e: e.tensor_scalar(out=out, in0=in0, scalar1=s1, scalar2=s2, op0=op0, op1=op1),
                          R=R, W=W)

        def TT(out, in0, in1, op, R, W, eng="dve"):
            return sch.op(eng, lambda e: e.tensor_tensor(out=out, in0=in0, in1=in1, op=op), R=R, W=W)

        def STT(out, in0, scalar, in1, op0, op1, R, W):
            return sch.op("dve", lambda e: e.scalar_tensor_tensor(out=out, in0=in0, scalar=scalar, in1=in1,
                                                                  op0=op0, op1=op1), R=R, W=W)

        def COPY(out, in_, R, W, eng=None):
            if eng is None:
                eng = "act" if st["alt"] % 2 == 0 else "dve"
                st["alt"] += 1
            if eng == "act":
                return sch.op("act", lambda e: e.activation(out=out, in_=in_, func=AF.Copy), R=R, W=W)
            return sch.op(eng, lambda e: e.tensor_copy(out=out, in_=in_), R=R, W=W)

        def RECIP(out, in_, R, W):
            return sch.op("dve", lambda e: e.reciprocal(out=out, in_=in_), R=R, W=W)

        def rstd_from_ss(ss, ssB, n, invd):
            for i in range(n):
                TS(ss[:, i:i + 1], ss[:, i:i + 1], invd[i], EPS, ALU.mult, ALU.add, R=[ssB], W=[ssB])
            ACT(ss[:, 0:n], ss[:, 0:n], AF.Sqrt, R=[ssB], W=[ssB])
            RECIP(ss[:, 0:n], ss[:, 0:n], R=[ssB], W=[ssB])

        sch.dma("sp", pv[:, 0:NPV], pv_d[:, :], W=[pvB])
        sch.dma("sp", invf[:], invf_d[:, :], W=[invfB])
        sch.dma("sp", posi[:], pos_d[:, :], W=[posB])
        sch.dma("pool", ident[:], ident_d[:, :], W=[identB])
        sch.dma("pool", tri[:], tri_d[:, :], W=[triB])
        sch.op("dve", lambda e: e.memset(onesb[:], 1.0), W=[constB])
        sch.op("dve", lambda e: e.memset(onesf[:], 1.0), W=[constB])
        sch.op("dve", lambda e: e.memset(cst[:, 0:1], 1.0), W=[constB])
        sch.op("dve", lambda e: e.memset(cst[:, 1:2], math.pi), W=[constB])
        tmpc, tmpB = small()
        tmpc2, tmpB2 = small()
        TS(tmpc2, pv[:, LAM:LAM + 4], -1.0, None, ALU.mult, None, R=[pvB], W=[tmpB2])
        TT(tmpc, pv[:, LAM:LAM + 4], tmpc2, ALU.max, R=[pvB, tmpB2], W=[tmpB])
        ACT(tmpc, tmpc, AF.Exp, R=[tmpB], W=[tmpB], scale=-1.0)
        TS(tmpc, tmpc, 1.0, None, ALU.add, None, R=[tmpB], W=[tmpB])
        ACT(tmpc, tmpc, AF.Ln, R=[tmpB], W=[tmpB])
        TS(tmpc2, tmpc2, 0.0, None, ALU.max, None, R=[tmpB2], W=[tmpB2])
        TT(tmpc, tmpc, tmpc2, ALU.add, R=[tmpB, tmpB2], W=[tmpB])
        TS(pv[:, HC:HC + 4], tmpc, -4.0, None, ALU.mult, None, R=[tmpB], W=[pvB])
        TS(pv[:, H2:H2 + 4], tmpc, -8.0, None, ALU.mult, None, R=[tmpB], W=[pvB])
        TS(pv[:, BAH:BAH + 4], pv[:, BA:BA + 4], 0.5, None, ALU.mult, None, R=[pvB], W=[pvB])
        TS(pv[:, BXH:BXH + 4], pv[:, BX:BX + 4], 0.5, None, ALU.mult, None, R=[pvB], W=[pvB])

        scr_state = {"done": False}

        def convert_scratch():
            if scr_state["done"]:
                return
            scr_state["done"] = True
            order = [23, 24, 25, 26] + list(range(23))
            for sl in order:
                sch.dma("pool", scr_d[sl].rearrange("p (a b) -> p a b", a=2),
                        wsl_d[sl].rearrange("p (a b) -> p a b", a=2), W=[scrB])

        def nt_a(src, srcB, Dn, junk, junkB):
            ss, ssB = small()
            ACT(junk[:, 0:Dn], src, AF.Square, R=srcB, W=[junkB, ssB], accum_out=ss[:, 0:1])
            rstd_from_ss(ss, ssB, 1, [1.0 / Dn])
            return ss, ssB

        def nt_b(src, srcB, Dn, ss, ssB, gcols, dst3, dstB, xs, xsB):
            nk = Dn // 128
            ACT(xs[:, 0:Dn], src, AF.Copy, R=list(srcB) + [ssB], W=[xsB], scale=ss[:, 0:1])
            tp, tB = ntp()
            for k in range(nk):
                TR(tp[:, k * 128:(k + 1) * 128], xs[:, k * 128:(k + 1) * 128], R=[xsB], W=[tB], last=(k == nk - 1))
            g3 = gcols.unsqueeze(2).broadcast_to([128, nk, 128])
            TT(dst3, tp[:, 0:Dn].rearrange("p (k t) -> p k t", k=nk), g3, ALU.mult, R=[tB, pvB], W=dstB)

        def nt(src, srcB, Dn, gcols, dst3, dstB, xs, xsB, junk, junkB):
            ss, ssB = nt_a(src, srcB, Dn, junk, junkB)
            nt_b(src, srcB, Dn, ss, ssB, gcols, dst3, dstB, xs, xsB)

        def rope(src3, srcB, G, cos, sin, csB, dst3, dstB, tA, tAB, tB_, tBB, extra=()):
            srcB = list(srcB) + list(extra)
            x1 = src3[:, :, 0:32]
            x2 = src3[:, :, 32:64]
            cb = cos.unsqueeze(1).broadcast_to([128, G, 32])
            sb = sin.unsqueeze(1).broadcast_to([128, G, 32])
            a3 = tA[:, 0:G * 32].rearrange("p (g d) -> p g d", g=G)
            b3 = tB_[:, 0:G * 32].rearrange("p (g d) -> p g d", g=G)
            TT(a3, x1, cb, ALU.mult, R=srcB + [csB], W=[tAB])
            TT(b3, x2, sb, ALU.mult, R=srcB + [csB], W=[tBB])
            TT(dst3[:, :, 0:32], a3, b3, ALU.subtract, R=[tAB, tBB], W=dstB)
            TT(a3, x2, cb, ALU.mult, R=srcB + [csB], W=[tAB])
            TT(b3, x1, sb, ALU.mult, R=srcB + [csB], W=[tBB])
            TT(dst3[:, :, 32:64], a3, b3, ALU.add, R=[tAB, tBB], W=dstB)

        def feat_norm(srcs, srcBs, gcol0, Dn, tt, dst_off, sq, sqB, rt, rtB):
            n = len(srcs)
            bank, bB = nbank()
            for i in range(n):
                q, qB = sq[i % 2], sqB[i % 2]
                ACT(q[:, :], srcs[i][:, tt * 512:(tt + 1) * 512], AF.Square, R=[srcBs[i]], W=[qB])
                MM(bank, onesf[:], q[:, :], i == 0, i == n - 1, R=[qB, constB], W=[bB], inc=True)
            TS(rt[:, :], bank, 1.0 / Dn, EPS, ALU.mult, ALU.add, R=[bB], W=[rtB])
            ACT(rt[:, :], rt[:, :], AF.Sqrt, R=[rtB], W=[rtB])
            RECIP(rt[:, :], rt[:, :], R=[rtB], W=[rtB])
            for i in range(n):
                c = dst_off + i
                STT(yT[:, c * S + tt * 512: c * S + (tt + 1) * 512], srcs[i][:, tt * 512:(tt + 1) * 512],
                    pv[:, gcol0 + i:gcol0 + i + 1], rt[:, :], ALU.mult, ALU.mult,
                    R=[srcBs[i], pvB, rtB], W=[yTB[tt]])

        for s in range(NSEQ):
            sfx = f"_{s}"
            with ExitStack() as s1:
                cqnT = T(s1, "cqnT" + sfx, 3 * S, BF16)
                ckvnT = T(s1, "ckvnT" + sfx, 2 * S, BF16)
                kpeT = T(s1, "kpeT" + sfx, S, BF16)
                cs = T(s1, "cs" + sfx, NTB * 64, F32)
                latB = [MB(f"lat@{s}_{i}") for i in range(NTB)]
                csB = MB(f"cs@{s}")
                with ExitStack() as s2:
                    xnT = T(s2, "xnT" + sfx, 8 * S, BF16)
                    xnT3 = xnT[:, :].rearrange("p (k t) -> p k t", k=8)
                    xnTB = [MB(f"xnT@{s}_{i}") for i in range(NTB)]
                    winA = T(s2, "winA" + sfx, 8 * 1024, BF16)
                    wlru = T(s2, "wlru" + sfx, 4 * 2 * 128, BF16)
                    winAB = MB(f"winA@{s}")
                    wlruB = MB(f"wlru@{s}")
                    sch.dma("pool", winA[:, :].rearrange("p (k n) -> p k n", k=8),
                            winA_d[:, :].rearrange("p (k n) -> p k n", k=8), W=[winAB])
                    sch.dma("pool", wlru[:], wlru_d[:, :], W=[wlruB])
                    if F_WPRE:
                        convert_scratch()
                    with ExitStack() as s2a:
                        xtmp = [T(s2a, f"xtmp{i}" + sfx, 1024, F32) for i in range(3)]
                        xtmpB = [MB(f"xtmp@{s}_{i}") for i in range(3)]
                        xs = [T(s2a, f"xs{i}" + sfx, 1024, BF16) for i in range(2)]
                        xsB = [MB(f"xs@{s}_{i}") for i in range(2)]
                        junk = T(s2a, "junk" + sfx, 1024, BF16)
                        junkB = MB(f"junk@{s}")
                        ang = T(s2a, "ang" + sfx, NTB * 64, F32)
                        kf = T(s2a, "kf" + sfx, NTB * 64, F32)
                        ki = T(s2a, "ki" + sfx, NTB * 64, I32)
                        posf = T(s2a, "posf" + sfx, NTB, F32)
                        angB = MB(f"ang@{s}")
                        kfB = MB(f"kf@{s}")
                        kiB = MB(f"ki@{s}")
                        posfB = MB(f"posf@{s}")
                        pend = None
                        for tb in range(NTB + 1):
                            cur = None
                            if tb < NTB:
                                i = tb % 3
                                sch.dma("sp", xtmp[i][:], x_d[s, tb * 128:(tb + 1) * 128, :], W=[xtmpB[i]])
                                cur = (tb,) + nt_a(xtmp[i][:, :], [xtmpB[i]], 1024, junk, junkB)
                            if not F_PIPEA:
                                pend = cur
                                cur = None
                            if pend is not None:
                                ptb, pss, pssB = pend
                                pi = ptb % 3
                                nt_b(xtmp[pi][:, :], [xtmpB[pi]], 1024, pss, pssB, pv[:, GPM:GPM + 8],
                                     xnT3[:, :, ptb * 128:(ptb + 1) * 128], [xnTB[ptb]], xs[ptb % 2], xsB[ptb % 2])
                            pend = cur
                        COPY(posf[:], posi[:, s * NTB:(s + 1) * NTB], R=[posB], W=[posfB], eng="dve")
                        ang3 = ang[:, :].rearrange("p (t d) -> p t d", t=NTB)
                        TT(ang3[:, :, 0:32], posf[:, :].unsqueeze(2).broadcast_to([128, NTB, 32]),
                           invf[:, :].unsqueeze(1).broadcast_to([128, NTB, 32]), ALU.mult,
                           R=[posfB, invfB], W=[angB])
                        TS(ang3[:, :, 32:64], ang3[:, :, 0:32], math.pi / 2, None, ALU.add, None, R=[angB], W=[angB])
                        TS(kf[:], ang[:], 1.0 / (2 * math.pi), None, ALU.mult, None, R=[angB], W=[kfB])
                        COPY(ki[:], kf[:], R=[kfB], W=[kiB], eng="dve")
                        COPY(kf[:], ki[:], R=[kiB], W=[kfB], eng="dve")
                        C1 = 6.28125
                        C2 = 2 * math.pi - C1
                        STT(ang[:], kf[:], -C1, ang[:], ALU.mult, ALU.add, R=[kfB, angB], W=[angB])
                        STT(ang[:], kf[:], -C2, ang[:], ALU.mult, ALU.add, R=[kfB, angB], W=[angB])
                        TS(kf[:], ang[:], math.pi, None, ALU.is_gt, None, R=[angB], W=[kfB])
                        STT(ang[:], kf[:], -2 * math.pi, ang[:], ALU.mult, ALU.add, R=[kfB, angB], W=[angB])
                        TS(kf[:], ang[:], -math.pi, None, ALU.is_lt, None, R=[angB], W=[kfB])
                        STT(ang[:], kf[:], 2 * math.pi, ang[:], ALU.mult, ALU.add, R=[kfB, angB], W=[angB])
                        TS(ang[:], ang[:], math.pi, -math.pi, ALU.min, ALU.max, R=[angB], W=[angB])
                        ACT(cs[:], ang[:], AF.Sin, R=[angB], W=[csB])
                        sch.barrier(bar[:, 0:1])

                    with ExitStack() as s3:
                        convert_scratch()
                        NSEG = NQT
                        Rb = [T(s3, f"R{i}" + sfx, S, F32) for i in range(6)]
                        RB = [[MB(f"R@{s}_{i}_{g}") for g in range(NSEG)] for i in range(6)]
                        Ubf = T(s3, "Ubf" + sfx, S, BF16)
                        UbfB = [MB(f"Ubf@{s}_{g}") for g in range(NSEG)]
                        YL = [T(s3, f"YL{i}" + sfx, S, F32) for i in range(4)]
                        YLB = [[MB(f"YL@{s}_{i}_{g}") for g in range(NSEG)] for i in range(4)]
                        sq = [T(s3, f"sq{i}" + sfx, 512, F32) for i in range(2)]
                        sqB = [MB(f"sq@{s}_{i}") for i in range(2)]
                        rt = T(s3, "rt" + sfx, 512, F32)
                        rtB = MB(f"rt@{s}")
                        LX, LG, U, A, TI, R6 = Rb
                        LXB, LGB, UB, AB, TIB, R6B = RB
                        segs = [slice(g * 512, (g + 1) * 512) for g in range(NSEG)]
                        for c in range(4):
                            def col(base):
                                return pv[:, base + c:base + c + 1]
                            for g in range(NSEG):
                                sl = segs[g]
                                bank, bB = nbank()
                                for kc in range(8):
                                    MM(bank, winA[:, kc * 1024 + c * 128: kc * 1024 + (c + 1) * 128],
                                       xnT[:, kc * S + g * 512: kc * S + (g + 1) * 512], kc == 0, kc == 7,
                                       R=[winAB] + xnTB[g * 4:(g + 1) * 4], W=[bB])
                                COPY(LX[:, sl], bank, R=[bB], W=[LXB[g]], eng="act")
                                bank, bB = nbank()
                                for kc in range(8):
                                    MM(bank, winA[:, kc * 1024 + 512 + c * 128: kc * 1024 + 512 + (c + 1) * 128],
                                       xnT[:, kc * S + g * 512: kc * S + (g + 1) * 512], kc == 0, kc == 7,
                                       R=[winAB] + xnTB[g * 4:(g + 1) * 4], W=[bB])
                                COPY(LG[:, sl], bank, R=[bB], W=[LGB[g]], eng="dve")
                            for g in range(NSEG):
                                sl = segs[g]
                                e = (g + 1) * 512
                                TS(U[:, sl], LX[:, sl], pv[:, CW + 3 * 4 + c:CW + 3 * 4 + c + 1], col(CB), ALU.mult, ALU.add,
                                   R=[LXB[g], pvB], W=[UB[g]])
                                for k, sh in ((2, 1), (1, 2), (0, 3)):
                                    lo = max(g * 512, sh)
                                    rr = [LXB[g], pvB, UB[g]] + ([LXB[g - 1]] if g > 0 else [])
                                    STT(U[:, lo:e], LX[:, lo - sh:e - sh], pv[:, CW + k * 4 + c:CW + k * 4 + c + 1], U[:, lo:e],
                                        ALU.mult, ALU.add, R=rr, W=[UB[g]])
                                COPY(Ubf[:, sl], U[:, sl], R=[UB[g]], W=[UbfB[g]], eng="act")
                            for g in range(NSEG):
                                sl = segs[g]
                                bank, bB = nbank()
                                MM(bank, wlru[:, (c * 2 + 0) * 128:(c * 2 + 1) * 128], Ubf[:, sl], True, True,
                                   R=[wlruB, UbfB[g]], W=[bB])
                                ACT(LX[:, sl], bank, AF.Tanh, R=[bB, pvB], W=[LXB[g]], scale=0.5, bias=col(BAH))
                                bank, bB = nbank()
                                MM(bank, wlru[:, (c * 2 + 1) * 128:(c * 2 + 2) * 128], Ubf[:, sl], True, True,
                                   R=[wlruB, UbfB[g]], W=[bB])
                                ACT(TI[:, sl], bank, AF.Tanh, R=[bB, pvB], W=[TIB[g]], scale=0.5, bias=col(BXH))
                            for g in range(NSEG):
                                sl = segs[g]
                                ACT(A[:, sl], LX[:, sl], AF.Exp, R=[LXB[g], pvB], W=[AB[g]], scale=col(HC), bias=col(HC))
                                ACT(R6[:, sl], LX[:, sl], AF.Exp, R=[LXB[g], pvB], W=[R6B[g]], scale=col(H2), bias=col(H2))
                            for g in range(NSEG):
                                sl = segs[g]
                                ACT(LX[:, sl], LG[:, sl], AF.Square, R=[LGB[g]], W=[LXB[g]])
                                TS(LX[:, sl], LX[:, sl], 0.044715, 1.0, ALU.mult, ALU.add, R=[LXB[g]], W=[LXB[g]])
                                TT(LX[:, sl], LX[:, sl], LG[:, sl], ALU.mult, R=[LXB[g], LGB[g]], W=[LXB[g]])
                            for g in range(NSEG):
                                sl = segs[g]
                                ACT(LX[:, sl], LX[:, sl], AF.Tanh, R=[LXB[g]], W=[LXB[g]], scale=0.7978845608028654)
                            for g in range(NSEG):
                                sl = segs[g]
                                ACT(R6[:, sl], R6[:, sl], AF.Sqrt, R=[R6B[g], constB], W=[R6B[g]], scale=-1.0, bias=cst[:, 0:1])
                            for g in range(NSEG):
                                sl = segs[g]
                                STT(TI[:, sl], TI[:, sl], 1.0, U[:, sl], ALU.add, ALU.mult, R=[TIB[g], UB[g]], W=[TIB[g]])
                                STT(TI[:, sl], TI[:, sl], 0.5, R6[:, sl], ALU.mult, ALU.mult, R=[TIB[g], R6B[g]], W=[TIB[g]])
                            for g in range(NSEG):
                                sl = segs[g]
                                if g == 0:
                                    sch.op("dve", lambda e, sl=sl: e.tensor_tensor_scan(
                                        out=U[:, sl], data0=A[:, sl], data1=TI[:, sl], initial=0.0,
                                        op0=ALU.mult, op1=ALU.add), R=[AB[g], TIB[g]], W=[UB[g]])
                                else:
                                    sch.op("dve", lambda e, sl=sl, g=g: e.tensor_tensor_scan(
                                        out=U[:, sl], data0=A[:, sl], data1=TI[:, sl],
                                        initial=U[:, g * 512 - 1:g * 512],
                                        op0=ALU.mult, op1=ALU.add), R=[AB[g], TIB[g], UB[g - 1]], W=[UB[g]])
                                STT(LX[:, sl], LX[:, sl], 1.0, LG[:, sl], ALU.add, ALU.mult, R=[LXB[g], LGB[g]], W=[LXB[g]])
                                STT(YL[c][:, sl], LX[:, sl], 0.5, U[:, sl], ALU.mult, ALU.mult, R=[LXB[g], UB[g]], W=[YLB[c][g]])
                        for tt in range(NQT):
                            feat_norm([YL[i] for i in range(4)], [YLB[i][tt] for i in range(4)], GLO, 512, tt, 0, sq, sqB, rt, rtB)
                        sch.barrier(bar[:, 0:1])

                    with ExitStack() as s4:
                        winB = T(s4, "winB" + sfx, 8 * 768, BF16)
                        winBB = MB(f"winB@{s}")
                        sch.dma("pool", winB[:, :].rearrange("p (k n) -> p k n", k=8),
                                winB_d[:, :].rearrange("p (k n) -> p k n", k=8), W=[winBB])
                        lat = [T(s4, f"latbf{i}" + sfx, 768, BF16) for i in range(2)]
                        latbB = [MB(f"latbf@{s}_{i}") for i in range(2)]
                        junk = T(s4, "junkb" + sfx, 512, BF16)
                        junkB = MB(f"junkb@{s}")
                        rtmp = [T(s4, f"rtmp{i}" + sfx, 128, F32) for i in range(2)]
                        rtmpB = [MB(f"rtmp@{s}_{i}") for i in range(2)]
                        cqnT3 = cqnT[:, :].rearrange("p (k t) -> p k t", k=3)
                        ckvnT3 = ckvnT[:, :].rearrange("p (k t) -> p k t", k=2)
                        def lat_front(tb):
                            bA, bAB = nbank()
                            bBk, bBB = nbank()
                            for kc in range(8):
                                lhs = xnT[:, kc * S + tb * 128: kc * S + (tb + 1) * 128]
                                MM(bA, lhs, winB[:, kc * 768: kc * 768 + 512], kc == 0, kc == 7,
                                   R=[winBB, xnTB[tb]], W=[bAB])
                            for kc in range(8):
                                lhs = xnT[:, kc * S + tb * 128: kc * S + (tb + 1) * 128]
                                MM(bBk[:, 0:256], lhs, winB[:, kc * 768 + 512: kc * 768 + 768], kc == 0, kc == 7,
                                   R=[winBB, xnTB[tb]], W=[bBB])
                            ss, ssB = small()
                            ACT(junk[:, 0:384], bA[:, 0:384], AF.Square, R=[bAB], W=[junkB, ssB], accum_out=ss[:, 0:1])
                            ACT(junk[:, 0:256], bBk[:, 0:256], AF.Square, R=[bBB], W=[junkB, ssB], accum_out=ss[:, 1:2])
                            rstd_from_ss(ss, ssB, 2, [1.0 / 384, 1.0 / 256])
                            return (tb, bA, bAB, bBk, bBB, ss, ssB)

                        def lat_back(stt):
                            tb, bA, bAB, bBk, bBB, ss, ssB = stt
                            bsl = slice(tb * 128, (tb + 1) * 128)
                            L = lat[tb % 2]
                            LB = latbB[tb % 2]
                            ACT(L[:, 0:384], bA[:, 0:384], AF.Copy, R=[bAB, ssB], W=[LB], scale=ss[:, 0:1])
                            ACT(L[:, 384:640], bBk[:, 0:256], AF.Copy, R=[bBB, ssB], W=[LB], scale=ss[:, 1:2])
                            rope(bA[:, 384:512].rearrange("p (g d) -> p g d", g=2), [bAB], 2,
                                 cs[:, tb * 64 + 32: tb * 64 + 64], cs[:, tb * 64: tb * 64 + 32], csB,
                                 L[:, 640:768].rearrange("p (g d) -> p g d", g=2), [LB],
                                 rtmp[0], rtmpB[0], rtmp[1], rtmpB[1], extra=[LB])
                            tp, tB = ntp()
                            for k in range(6):
                                TR(tp[:, k * 128:(k + 1) * 128], L[:, k * 128:(k + 1) * 128], R=[LB], W=[tB], last=(k == 5))
                            TT(cqnT3[:, :, bsl], tp[:, 0:384].rearrange("p (k t) -> p k t", k=3),
                               pv[:, GQ:GQ + 3].unsqueeze(2).broadcast_to([128, 3, 128]), ALU.mult,
                               R=[tB, pvB], W=[latB[tb]])
                            TT(ckvnT3[:, :, bsl], tp[:, 384:640].rearrange("p (k t) -> p k t", k=2),
                               pv[:, GKV:GKV + 2].unsqueeze(2).broadcast_to([128, 2, 128]), ALU.mult,
                               R=[tB, pvB], W=[latB[tb]])
                            COPY(kpeT[:, bsl], tp[:, 640:768], R=[tB], W=[latB[tb]], eng="dve")

                        pendq = []
                        depth = LAT_DEPTH if F_LATSKEW else 0
                        for tb in range(NTB):
                            pendq.append(lat_front(tb))
                            if len(pendq) > depth:
                                lat_back(pendq.pop(0))
                        while pendq:
                            lat_back(pendq.pop(0))
                        sch.barrier(bar[:, 0:1])
                with ExitStack() as s5:
                    qnT = T(s5, "qnT" + sfx, 4 * S, BF16)
                    qpeT = T(s5, "qpeT" + sfx, 2 * S, BF16)
                    knT = T(s5, "knT" + sfx, 4 * S, BF16)
                    Vt = T(s5, "Vt" + sfx, NTB * 512, BF16)
                    qB = [MB(f"q@{s}_{i}") for i in range(NQT)]
                    kB = [MB(f"k@{s}_{i}") for i in range(NQT)]
                    qpeB = [MB(f"qpe@{s}_{i}") for i in range(NTB)]
                    vB = [MB(f"v@{s}_{i}") for i in range(NTB)]
                    qpeT3 = qpeT[:, :].rearrange("p (k t) -> p k t", k=2)
                    with ExitStack() as s5b:
                        wuq = T(s5b, "wuq" + sfx, 3 * 768, BF16)
                        wukv = T(s5b, "wukv" + sfx, 2 * 1024, BF16)
                        wuqB = MB(f"wuq@{s}")
                        wukvB = MB(f"wukv@{s}")
                        sch.dma("pool", wuq[:, :].rearrange("p (k n) -> p k n", k=3),
                                wuq_d[:, :].rearrange("p (k n) -> p k n", k=3), W=[wuqB])
                        sch.dma("pool", wukv[:, :].rearrange("p (k n) -> p k n", k=2),
                                wukv_d[:, :].rearrange("p (k n) -> p k n", k=2), W=[wukvB])
                        qpb = [T(s5b, f"qpb{i}" + sfx, 256, BF16) for i in range(2)]
                        qpbB = [MB(f"qpb@{s}_{i}") for i in range(2)]
                        rtmp = [T(s5b, f"rtq{i}" + sfx, 128, F32) for i in range(2)]
                        rtmpB = [MB(f"rtq@{s}_{i}") for i in range(2)]
                        for tt in range(NQT):
                            tsl = slice(tt * 512, (tt + 1) * 512)
                            lb4 = latB[tt * 4:(tt + 1) * 4]
                            for h in range(4):
                                bank, bB = nbank()
                                for kc in range(3):
                                    MM(bank, wuq[:, kc * 768 + h * 128: kc * 768 + (h + 1) * 128],
                                       cqnT[:, kc * S + tt * 512: kc * S + (tt + 1) * 512], kc == 0, kc == 2,
                                       R=[wuqB] + lb4, W=[bB])
                                COPY(qnT[:, h * S + tt * 512: h * S + (tt + 1) * 512], bank, R=[bB], W=[qB[tt]])
                            for h in range(4):
                                bank, bB = nbank()
                                for kc in range(2):
                                    MM(bank, wukv[:, kc * 1024 + h * 128: kc * 1024 + (h + 1) * 128],
                                       ckvnT[:, kc * S + tt * 512: kc * S + (tt + 1) * 512], kc == 0, kc == 1,
                                       R=[wukvB] + lb4, W=[bB])
                                COPY(knT[:, h * S + tt * 512: h * S + (tt + 1) * 512], bank, R=[bB], W=[kB[tt]])
                            def stA(tb):
                                bank, bB = nbank()
                                for kc in range(3):
                                    MM(bank[:, 0:256], cqnT[:, kc * S + tb * 128: kc * S + (tb + 1) * 128],
                                       wuq[:, kc * 768 + 512: kc * 768 + 768], kc == 0, kc == 2,
                                       R=[wuqB, latB[tb]], W=[bB])
                                Q = qpb[tb % 2]
                                QB = qpbB[tb % 2]
                                rope(bank[:, 0:256].rearrange("p (g d) -> p g d", g=4), [bB], 4,
                                     cs[:, tb * 64 + 32: tb * 64 + 64], cs[:, tb * 64: tb * 64 + 32], csB,
                                     Q[:, :].rearrange("p (g d) -> p g d", g=4), [QB],
                                     rtmp[0], rtmpB[0], rtmp[1], rtmpB[1])

                            def stB(tb):
                                bsl = slice(tb * 128, (tb + 1) * 128)
                                Q = qpb[tb % 2]
                                QB = qpbB[tb % 2]
                                tp, tB = ntp()
                                for k in range(2):
                                    TR(tp[:, k * 128:(k + 1) * 128], Q[:, k * 128:(k + 1) * 128], R=[QB], W=[tB], last=(k == 1))
                                COPY(qpeT3[:, :, bsl], tp[:, 0:256].rearrange("p (k t) -> p k t", k=2),
                                     R=[tB], W=[qpeB[tb]], eng="act")

                            def stV(tb):
                                bank, bB = nbank()
                                for kc in range(2):
                                    MM(bank, ckvnT[:, kc * S + tb * 128: kc * S + (tb + 1) * 128],
                                       wukv[:, kc * 1024 + 512: kc * 1024 + 1024], kc == 0, kc == 1,
                                       R=[wukvB, latB[tb]], W=[bB])
                                COPY(Vt[:, tb * 512:(tb + 1) * 512], bank, R=[bB], W=[vB[tb]], eng="dve")

                            t0 = tt * 4
                            if F_QKVSKEW:
                                stA(t0); stV(t0); stA(t0 + 1); stV(t0 + 1); stB(t0)
                                stA(t0 + 2); stV(t0 + 2); stB(t0 + 1)
                                stA(t0 + 3); stV(t0 + 3); stB(t0 + 2); stB(t0 + 3)
                            else:
                                for tb in range(t0, t0 + 4):
                                    stA(tb); stB(tb); stV(tb)
                        sch.barrier(bar[:, 0:1])
                    with ExitStack() as s6:
                        YM = [T(s6, f"YM{i}" + sfx, S, F32) for i in range(4)]
                        YMB = [MB(f"YM@{s}_{i}") for i in range(4)]
                        Pb = [T(s6, f"Pb{i}" + sfx, 512, BF16) for i in range(3)]
                        PbB = [MB(f"Pb@{s}_{i}") for i in range(3)]
                        rden = [T(s6, f"rden{i}" + sfx, 512, F32) for i in range(2)]
                        rdenB = [MB(f"rden@{s}_{i}") for i in range(2)]
                        sq = [T(s6, f"sqm{i}" + sfx, 512, F32) for i in range(2)]
                        sqB = [MB(f"sqm@{s}_{i}") for i in range(2)]
                        rt = T(s6, "rtm" + sfx, 512, F32)
                        rtB = MB(f"rtm@{s}")
                        it = 0
                        pcount = 0
                        for qi in range(NQT):
                            for h in range(4):
                                hb = (h % 2) * 64
                                nkc = 4 * qi + 4
                                ob = 4 if it % 2 == 0 else 2
                                O, OB = PS[:, ob * 512:(ob + 1) * 512], bankB[ob]
                                DN, DNB = PS[:, (ob + 1) * 512:(ob + 2) * 512], bankB[ob + 1]

                                def c0_of(kc):
                                    return 0 if kc < 4 * qi else (kc - 4 * qi) * 128

                                def emitS(kc):
                                    sbi = kc % 2
                                    Sb, SB = PS[:, sbi * 512:(sbi + 1) * 512], bankB[sbi]
                                    c0 = c0_of(kc)
                                    MM(Sb[:, c0:512], knT[:, h * S + kc * 128: h * S + (kc + 1) * 128],
                                       qnT[:, h * S + qi * 512 + c0: h * S + (qi + 1) * 512], True, False,
                                       R=[kB[kc // 4], qB[qi]], W=[SB])
                                    MM(Sb[:, c0:512], kpeT[hb:hb + 64, kc * 128:(kc + 1) * 128],
                                       qpeT[hb:hb + 64, (h // 2) * S + qi * 512 + c0: (h // 2) * S + (qi + 1) * 512],
                                       False, True, R=[latB[kc]] + qpeB[qi * 4:(qi + 1) * 4], W=[SB])

                                emitS(0)
                                for kc in range(nkc):
                                    if kc + 1 < nkc:
                                        emitS(kc + 1)
                                    sbi = kc % 2
                                    Sb, SB = PS[:, sbi * 512:(sbi + 1) * 512], bankB[sbi]
                                    c0 = c0_of(kc)
                                    P, PB = Pb[pcount % 3], PbB[pcount % 3]
                                    pcount += 1
                                    ACT(P[:, c0:512], Sb[:, c0:512], AF.Exp, R=[SB], W=[PB], scale=mla_scale)
                                    if kc >= 4 * qi:
                                        TT(P[:, c0:c0 + 128], P[:, c0:c0 + 128], tri[:, :], ALU.mult, R=[PB, triB], W=[PB])
                                    MM(O[:, c0:512], Vt[:, kc * 512 + h * 128: kc * 512 + (h + 1) * 128], P[:, c0:512],
                                       kc == 0, kc == nkc - 1, R=[vB[kc], PB], W=[OB])
                                    MM(DN[:, c0:512], onesb[:, :], P[:, c0:512], kc == 0, kc == nkc - 1,
                                       R=[constB, PB], W=[DNB])
                                rd, rdB = rden[it % 2], rdenB[it % 2]
                                RECIP(rd[:, :], DN, R=[DNB], W=[rdB])
                                TT(YM[h][:, qi * 512:(qi + 1) * 512], O, rd[:, :], ALU.mult, R=[OB, rdB], W=[YMB[h]])
                                it += 1
                            feat_norm(YM, YMB, GMO, 512, qi, 4, sq, sqB, rt, rtB)
                        sch.barrier(bar[:, 0:1])
            if dbg:
                dB = MB(f"dbgy@{s}")
                sch.dma("sp", dbg_yT[s], yT[:, :], R=yTB + [dB])

            with ExitStack() as p2:
                G = [T(p2, f"G{i}" + sfx, 1024, F32) for i in range(3)]
                GB = [MB(f"G@{s}_{i}") for i in range(3)]
                for i in range(3):
                    sch.dma("sp", G[i][:], grow_d[i], W=[GB[i]])
                mkT = T(p2, "mkT" + sfx, 8 * 256, BF16)
                mv = T(p2, "mv" + sfx, 2 * 1024, BF16)
                mkB = MB(f"mk@{s}")
                mvB = MB(f"mv@{s}")
                with ExitStack() as pm:
                    mnT = T(pm, "mnT" + sfx, 8 * 256, BF16)
                    mnB = [MB(f"mn@{s}_{i}") for i in range(2)]
                    memt = [T(pm, f"memt{i}" + sfx, 1024, F32) for i in range(2)]
                    memtB = [MB(f"memt@{s}_{i}") for i in range(2)]
                    xs = [T(pm, f"xsm{i}" + sfx, 1024, BF16) for i in range(2)]
                    xsB = [MB(f"xsm@{s}_{i}") for i in range(2)]
                    junk = T(pm, "junkm" + sfx, 1024, BF16)
                    junkB = MB(f"junkm@{s}")
                    wm = T(pm, "wm" + sfx, 4 * 4096, BF16)
                    wmB = [MB(f"wm@{s}_{i}") for i in range(4)]
                    for i in range(4):
                        sch.dma("sp", wm[:, i * 4096:(i + 1) * 4096], scr_d[SLOT_MK + i], R=[scrB], W=[wmB[i]])
                    mnT3 = mnT[:, :].rearrange("p (k t) -> p k t", k=8)
                    for mc in range(2):
                        sch.dma("sp", memt[mc][:], mem_d[s, mc * 128:(mc + 1) * 128, :], W=[memtB[mc]])
                    for mc in range(2):
                        nt(memt[mc][:, :], [memtB[mc]], 1024, pv[:, GMKV:GMKV + 8],
                           mnT3[:, :, mc * 128:(mc + 1) * 128], [mnB[mc]], xs[mc], xsB[mc], junk, junkB)
                    for sl in range(2):
                        W_ = wm[:, sl * 4096:(sl + 1) * 4096]
                        for cc in range(4):
                            c = sl * 4 + cc
                            bank, bB = nbank()
                            for kc in range(8):
                                MM(bank[:, 0:256], W_[:, cc * 1024 + kc * 128: cc * 1024 + (kc + 1) * 128],
                                   mnT[:, kc * 256:(kc + 1) * 256], kc == 0, kc == 7, R=[wmB[sl]] + mnB, W=[bB])
                            COPY(mkT[:, c * 256:(c + 1) * 256], bank[:, 0:256], R=[bB], W=[mkB])
                    for fh in range(2):
                        W_ = wm[:, (2 + fh) * 4096:(3 + fh) * 4096]
                        for mc in range(2):
                            bank, bB = nbank()
                            for kc in range(8):
                                MM(bank, mnT[:, kc * 256 + mc * 128: kc * 256 + (mc + 1) * 128],
                                   W_[:, kc * 512:(kc + 1) * 512], kc == 0, kc == 7, R=[wmB[2 + fh], mnB[mc]], W=[bB])
                            COPY(mv[:, mc * 1024 + fh * 512: mc * 1024 + (fh + 1) * 512], bank, R=[bB], W=[mvB])
                    sch.barrier(bar[:, 0:1])

                NRG, NRF = 3, 2
                ringG = T(p2, "ringG" + sfx, NRG * 4096, BF16)
                ringF = T(p2, "ringF" + sfx, NRF * 4096, BF16)
                Ht = [T(p2, f"Ht{i}" + sfx, 4 * 1024, F32) for i in range(2)]
                HtB = [[MB(f"Ht@{s}_{i}_{tb}") for tb in range(4)] for i in range(2)]
                xnF = T(p2, "xnF" + sfx, 8 * 512, BF16)
                xnFB = [MB(f"xnF@{s}_{i}") for i in range(4)]
                xnF3 = xnF[:, :].rearrange("p (k t) -> p k t", k=8)
                xnG = T(p2, "xnG" + sfx, 8 * 512, BF16)
                xnGB = [MB(f"xnG@{s}_{i}") for i in range(4)]
                xnG3 = xnG[:, :].rearrange("p (k t) -> p k t", k=8)
                PLh = T(p2, "PLh" + sfx, NJ * 512, BF16)
                PLhB = [MB(f"PLh@{s}_{i}") for i in range(NJ)]
                PLq = T(p2, "PLq" + sfx, 8 * 512, BF16)
                PLqB = [MB(f"PLq@{s}_{i}") for i in range(8)]
                Pb = [T(p2, f"Pm{i}" + sfx, 512, BF16) for i in range(2)]
                PbB = [MB(f"Pm@{s}_{i}") for i in range(2)]
                rden = [T(p2, f"rdm{i}" + sfx, 512, F32) for i in range(1)]
                rdenB = [MB(f"rdm@{s}_{i}") for i in range(1)]
                Fsb = T(p2, "Fsb" + sfx, 4 * 1024, F32)
                FsbB = [MB(f"Fsb@{s}_{i}") for i in range(4)]
                sg = [T(p2, f"sg{i}" + sfx, 512, F32) for i in range(2)]
                sgB = [MB(f"sg@{s}_{i}") for i in range(2)]
                tmpo = T(p2, "tmpo" + sfx, 1024, F32)
                tmpoB = MB(f"tmpo@{s}")
                xs = [T(p2, f"xs2{i}" + sfx, 1024, BF16) for i in range(2)]
                xsB = [MB(f"xs2@{s}_{i}") for i in range(2)]
                junk = T(p2, "junk2" + sfx, 1024, BF16)
                junkB = MB(f"junk2@{s}")

                class Ring:
                    def __init__(self, tensor, nslots, uses, name):
                        self.t, self.n, self.uses = tensor, nslots, uses
                        self.B = [MB(f"{name}@{s}_{i}") for i in range(nslots)]
                        self.k = self.issued = self.released = 0

                    def pump(self):
                        while self.issued < len(self.uses) and self.issued < self.released + self.n:
                            i = self.issued
                            r = i % self.n
                            sch.dma("sp", self.t[:, r * 4096:(r + 1) * 4096], scr_d[self.uses[i]], R=[scrB], W=[self.B[r]])
                            self.issued += 1

                    def acquire(self, expect):
                        k = self.k
                        self.k += 1
                        assert k < self.issued, "ring: acquire before load issued"
                        assert self.uses[k] == expect, (self.uses[k], expect)
                        r = k % self.n
                        return self.t[:, r * 4096:(r + 1) * 4096], self.B[r]

                    def release(self):
                        self.released += 1
                        self.pump()

                rG = Ring(ringG, NRG, (list(range(6, 17)) + list(range(17, 23))) * NQT, "rG")
                rF = Ring(ringF, NRF, [0, 1, 2, 3, 4, 5] * NQT, "rF")
                rG.pump()
                rF.pump()
                FB = [(PS[:, 4 * 512:5 * 512], bankB[4]), (PS[:, 5 * 512:6 * 512], bankB[5])]
                FPAIR = (PS[:, 4 * 512:6 * 512], [bankB[4], bankB[5]])
                fbs = {"i": 0}

                def fbank():
                    fbs["i"] += 1
                    return FB[fbs["i"] % 2]

                def postres(src, srcB, gi, HtAP, HtBuf):
                    ss, ssB = small()
                    ACT(junk[:, :], src, AF.Square, R=srcB, W=[junkB, ssB], accum_out=ss[:, 0:1])
                    rstd_from_ss(ss, ssB, 1, [1.0 / 1024])
                    TT(tmpo[:, :], src, G[gi][:, :], ALU.mult, R=list(srcB) + [GB[gi], ssB], W=[tmpoB])
                    STT(HtAP, tmpo[:, :], ss[:, 0:1], HtAP, ALU.mult, ALU.add, R=[tmpoB, ssB, HtBuf], W=[HtBuf])

                def front_items(t):
                    p = t % 2
                    H = Ht[p]
                    HB = HtB[p]
                    stt = {}

                    def xload():
                        for tb in range(4):
                            sch.dma("pool", H[:, tb * 1024:(tb + 1) * 1024],
                                    x_d[s, t * 512 + tb * 128: t * 512 + (tb + 1) * 128, :], W=[HB[tb]])

                    def proj(tb, which):
                        if tb == 0:
                            stt["W"] = [rF.acquire(0 + 4 * which), rF.acquire(1 + 4 * which)]
                        pair, pB = FPAIR
                        for fh in range(2):
                            W_, WB = stt["W"][fh]
                            for kc in range(8):
                                if which == 0:
                                    lhs = yT[:, kc * S + t * 512 + tb * 128: kc * S + t * 512 + (tb + 1) * 128]
                                    rr = [yTB[t], WB]
                                else:
                                    lhs = PLq[:, kc * 512 + tb * 128: kc * 512 + (tb + 1) * 128]
                                    rr = [PLqB[kc], WB]
                                MM(pair[:, fh * 512:(fh + 1) * 512], lhs, W_[:, kc * 512:(kc + 1) * 512],
                                   kc == 0, kc == 7, R=rr, W=[pB[fh]])
                        if tb == 3:
                            rF.release()
                            rF.release()
                        hap = H[:, tb * 1024:(tb + 1) * 1024]
                        postres(pair, pB, which, hap, HB[tb])
                        ss, ssB = nt_a(hap, [HB[tb]], 1024, junk, junkB)
                        ACT(xs[tb % 2][:, :], hap, AF.Copy, R=[HB[tb], ssB], W=[xsB[tb % 2]], scale=ss[:, 0:1])

                    def trn(tb, which):
                        x_, xB_ = xs[tb % 2], xsB[tb % 2]
                        tp, tB = ntp()
                        for k in range(8):
                            TR(tp[:, k * 128:(k + 1) * 128], x_[:, k * 128:(k + 1) * 128], R=[xB_], W=[tB], last=(k == 7))
                        gc = (pv[:, GPMEM:GPMEM + 8] if which == 0 else pv[:, GPF:GPF + 8])
                        g3 = gc.unsqueeze(2).broadcast_to([128, 8, 128])
                        dst3 = (xnF3 if which == 0 else xnG3)[:, :, tb * 128:(tb + 1) * 128]
                        dB = (xnFB if which == 0 else xnGB)[tb]
                        TT(dst3, tp[:, 0:1024].rearrange("p (k t) -> p k t", k=8), g3, ALU.mult, R=[tB, pvB], W=[dB])

                    def mq(c):
                        if c == 0:
                            stt["Q"] = [rF.acquire(2), rF.acquire(3)]
                        W_, WB = stt["Q"][c // 4]
                        cc = c % 4
                        bank, bB = fbank()
                        for kc in range(8):
                            MM(bank, W_[:, cc * 1024 + kc * 128: cc * 1024 + (kc + 1) * 128],
                               xnF[:, kc * 512:(kc + 1) * 512], kc == 0, kc == 7, R=[WB] + xnFB, W=[bB])
                        COPY(PLq[:, c * 512:(c + 1) * 512], bank, R=[bB], W=[PLqB[c]])
                        if c == 7:
                            rF.release()
                            rF.release()

                    def attS(h):
                        Ps = []
                        for mc in range(2):
                            bank, bB = fbank()
                            for dc in range(2):
                                c = h * 2 + dc
                                MM(bank, mkT[:, c * 256 + mc * 128: c * 256 + (mc + 1) * 128],
                                   PLq[:, c * 512:(c + 1) * 512], dc == 0, dc == 1, R=[mkB, PLqB[c]], W=[bB])
                            P, PB = Pb[mc], PbB[mc]
                            ACT(P[:, :], bank, AF.Exp, R=[bB], W=[PB], scale=mem_scale)
                            Ps.append((P, PB))
                        stt["Ps"] = Ps

                    def attPV(h):
                        Ps = stt["Ps"]
                        dn, dnB = fbank()
                        for mc in range(2):
                            MM(dn, onesb[:, :], Ps[mc][0][:, :], mc == 0, mc == 1, R=[constB, Ps[mc][1]], W=[dnB])
                        rd, rdB = rden[0], rdenB[0]
                        RECIP(rd[:, :], dn, R=[dnB], W=[rdB])
                        for dvc in range(2):
                            c = h * 2 + dvc
                            bank, bB = fbank()
                            for mc in range(2):
                                MM(bank, mv[:, mc * 1024 + c * 128: mc * 1024 + (c + 1) * 128], Ps[mc][0][:, :],
                                   mc == 0, mc == 1, R=[mvB, Ps[mc][1]], W=[bB])
                            TT(PLq[:, c * 512:(c + 1) * 512], bank, rd[:, :], ALU.mult, R=[bB, rdB], W=[PLqB[c]])

                    A = [xload]
                    A += [lambda: proj(0, 0), lambda: proj(1, 0), lambda: trn(0, 0), lambda: proj(2, 0),
                          lambda: trn(1, 0), lambda: proj(3, 0), lambda: trn(2, 0), lambda: trn(3, 0)]
                    A += [(lambda c=c: mq(c)) for c in range(8)]
                    A += [lambda: attS(0)]
                    for h in range(4):
                        A += [(lambda h=h: attPV(h))]
                        if h + 1 < 4:
                            A += [(lambda h=h: attS(h + 1))]
                    Bq = [lambda: proj(0, 1), lambda: proj(1, 1), lambda: trn(0, 1), lambda: proj(2, 1),
                          lambda: trn(1, 1), lambda: proj(3, 1), lambda: trn(2, 1), lambda: trn(3, 1)]
                    return A, Bq

                def ffn_items(t):
                    p = t % 2
                    H = Ht[p]
                    HB = HtB[p]

                    def gu(jj):
                        W_, WB = rG.acquire(6 + jj)
                        for jl in range(2):
                            j = jj * 2 + jl
                            b0 = 0 if jl == 0 else 2
                            bg, bgB = PS[:, b0 * 512:(b0 + 1) * 512], bankB[b0]
                            bu, buB = PS[:, (b0 + 1) * 512:(b0 + 2) * 512], bankB[b0 + 1]
                            for kc in range(8):
                                MM(bg, W_[:, (jl * 2 + 0) * 1024 + kc * 128: (jl * 2 + 0) * 1024 + (kc + 1) * 128],
                                   xnG[:, kc * 512:(kc + 1) * 512], kc == 0, kc == 7, R=[WB] + xnGB, W=[bgB])
                            for kc in range(8):
                                MM(bu, W_[:, (jl * 2 + 1) * 1024 + kc * 128: (jl * 2 + 1) * 1024 + (kc + 1) * 128],
                                   xnG[:, kc * 512:(kc + 1) * 512], kc == 0, kc == 7, R=[WB] + xnGB, W=[buB])
                            g_, gB_ = sg[j % 2], sgB[j % 2]
                            ACT(g_[:, :], bg, AF.Silu, R=[bgB], W=[gB_])
                            TT(PLh[:, j * 512:(j + 1) * 512], bu, g_[:, :], ALU.mult, R=[buB, gB_], W=[PLhB[j]])
                        rG.release()

                    def down(fh, sl3):
                        W_, WB = rG.acquire(17 + fh * 3 + sl3)
                        nj = 8 if sl3 < 2 else 6
                        for jl in range(nj):
                            j = sl3 * 8 + jl
                            for tb in range(4):
                                last = (jl == nj - 1 and tb == 3)
                                MM(PS[:, tb * 512:(tb + 1) * 512], PLh[:, j * 512 + tb * 128: j * 512 + (tb + 1) * 128],
                                   W_[:, jl * 512:(jl + 1) * 512], j == 0, j == NJ - 1, R=[PLhB[j], WB], W=[bankB[tb]],
                                   inc=(True if last else None))
                        rG.release()

                    def evac(fh):
                        for tb in range(4):
                            COPY(Fsb[:, tb * 1024 + fh * 512: tb * 1024 + (fh + 1) * 512], PS[:, tb * 512:(tb + 1) * 512],
                                 R=[bankB[tb]], W=[FsbB[tb]])

                    def fin():
                        for tb in range(4):
                            hap = H[:, tb * 1024:(tb + 1) * 1024]
                            postres(Fsb[:, tb * 1024:(tb + 1) * 1024], [FsbB[tb]], 2, hap, HB[tb])
                            sch.dma("pool", out_d[s, t * 512 + tb * 128: t * 512 + (tb + 1) * 128, :], hap, R=[HB[tb]])

                    GU = [(lambda jj=jj: gu(jj)) for jj in range(11)]
                    DN = []
                    for fh in range(2):
                        for sl3 in range(3):
                            DN.append(lambda fh=fh, sl3=sl3: down(fh, sl3))
                        DN.append(lambda fh=fh: evac(fh))
                    DN.append(fin)
                    return GU, DN

                def interleave(Xs, Ys):
                    nx, ny = len(Xs), len(Ys)
                    yi = 0
                    for i, xf in enumerate(Xs):
                        xf()
                        tgt = ((i + 1) * ny + nx - 1) // nx if nx else ny
                        while yi < min(tgt, ny):
                            Ys[yi]()
                            yi += 1
                    while yi < ny:
                        Ys[yi]()
                        yi += 1

                A0, B0 = front_items(0)
                for f in A0 + B0:
                    f()
                for tt in range(NQT):
                    GU, DN = ffn_items(tt)
                    if tt + 1 < NQT:
                        A1, B1 = front_items(tt + 1)
                    else:
                        A1, B1 = [], []
                    interleave(GU, A1)
                    interleave(DN, B1)
                sch.barrier(bar[:, 0:1])
        sch.final_wait()

    with nc.Block() as block:
        @block.sync
        def _(sync):
            body()
    es.close()
    return nc, sch


def _kc(W):
    K, N = W.shape
    return np.ascontiguousarray(W.reshape(K // 128, 128, N).transpose(1, 0, 2)).reshape(128, -1)


def _cols(v):
    return np.ascontiguousarray(v.reshape(-1, 128).T)


def pack_shared(inp):
    f = np.float32
    w_in = inp["w_in"][0]
    winA = _kc(w_in[:, 0:1024])
    winB = _kc(np.concatenate([w_in[:, 1024:1408], w_in[:, 1664:1728], w_in[:, 1664:1728], w_in[:, 1408:1664]], axis=1))
    w_uq = inp["w_uq"][0].reshape(384, 4, 192)
    wuq = _kc(np.concatenate([w_uq[:, :, 0:128].reshape(384, 512), w_uq[:, :, 128:192].reshape(384, 256)], axis=1))
    w_ukv = inp["w_ukv"][0].reshape(256, 4, 256)
    wukv = _kc(np.concatenate([w_ukv[:, :, 0:128].reshape(256, 512), w_ukv[:, :, 128:256].reshape(256, 512)], axis=1))
    wlru = np.zeros((128, 4, 2, 128), f)
    for c in range(4):
        for g, key in enumerate(("lru_wa", "lru_wx")):
            for hh in range(2):
                wlru[hh * 64:(hh + 1) * 64, c, g, hh * 64:(hh + 1) * 64] = inp[key][0][2 * c + hh]
    wlru = wlru.reshape(128, -1)
    slots = np.zeros((NSLOT, 128, 4096), f)

    def moving(W, fh):
        return _kc(W[:, fh * 512:(fh + 1) * 512])

    def stationary(W, sl):
        Wr = W.reshape(8, 128, 8, 128)[:, :, sl * 4:(sl + 1) * 4, :]
        return np.ascontiguousarray(Wr.transpose(1, 2, 0, 3)).reshape(128, -1)

    for fh in range(2):
        slots[0 + fh] = moving(inp["w_out"][0], fh)
        slots[2 + fh] = stationary(inp["w_mq"][0], fh)
        slots[4 + fh] = moving(inp["w_mo"][0], fh)
        slots[SLOT_MK + fh] = stationary(inp["w_mk"][0], fh)
        slots[SLOT_MV + fh] = moving(inp["w_mv"][0], fh)
    wg = inp["w_gate"][0].reshape(8, 128, NJ, 128)
    wu = inp["w_up"][0].reshape(8, 128, NJ, 128)
    for jj in range(11):
        blk = np.zeros((128, 2, 2, 8, 128), f)
        for jl in range(2):
            j = jj * 2 + jl
            blk[:, jl, 0] = wg[:, :, j, :].transpose(1, 0, 2)
            blk[:, jl, 1] = wu[:, :, j, :].transpose(1, 0, 2)
        slots[6 + jj] = blk.reshape(128, -1)
    wd = inp["w_down"][0].reshape(NJ, 128, 1024)
    for fh in range(2):
        for sl3 in range(3):
            nj = 8 if sl3 < 2 else 6
            blk = np.zeros((128, 8, 512), f)
            blk[:, 0:nj, :] = wd[sl3 * 8: sl3 * 8 + nj, :, fh * 512:(fh + 1) * 512].transpose(1, 0, 2)
            slots[17 + fh * 3 + sl3] = blk.reshape(128, -1)
    pv = np.zeros((128, NPV), f)
    pv[:, GPM:GPM + 8] = _cols(inp["g_pre_mix"][0])
    for k in range(4):
        pv[:, CW + k * 4: CW + k * 4 + 4] = _cols(inp["conv_w"][0][k])
    pv[:, CB:CB + 4] = _cols(inp["conv_b"][0])
    pv[:, BA:BA + 4] = _cols(inp["lru_ba"][0])
    pv[:, BX:BX + 4] = _cols(inp["lru_bx"][0])
    pv[:, LAM:LAM + 4] = _cols(inp["lru_lambda"][0])
    pv[:, GQ:GQ + 3] = _cols(inp["g_q_lat"][0])
    pv[:, GKV:GKV + 2] = _cols(inp["g_kv_lat"][0])
    pv[:, GLO:GLO + 4] = _cols(inp["g_lru_out"][0])
    pv[:, GMO:GMO + 4] = _cols(inp["g_mla_out"][0])
    pv[:, GPMEM:GPMEM + 8] = _cols(inp["g_pre_mem"][0])
    pv[:, GMKV:GMKV + 8] = _cols(inp["g_mem_kv"][0])
    pv[:, GPF:GPF + 8] = _cols(inp["g_pre_ffn"][0])
    grow = np.stack([np.broadcast_to(inp[k][0][None, :], (128, 1024)) for k in
                     ("g_post_mix", "g_post_mem", "g_post_ffn")]).astype(f)
    invf = (10000.0 ** (-np.arange(0, 64, 2, dtype=np.float32) / 64)).astype(f)
    invf = np.ascontiguousarray(np.broadcast_to(invf[None, :], (128, 32)))
    ident = np.eye(128, dtype=f)
    tri = (np.arange(128)[None, :] >= np.arange(128)[:, None]).astype(f)
    return dict(pv=pv, grow=np.ascontiguousarray(grow), invf=invf, ident=ident, tri=tri, winA=winA, winB=winB,
                wuq=wuq, wukv=wukv, wlru=wlru, wslots=slots)


def pack_core(inp, b0, nseq, S):
    x = np.ascontiguousarray(inp["x"][b0:b0 + nseq], dtype=np.float32)
    mem = np.ascontiguousarray(inp["mem"][b0:b0 + nseq], dtype=np.float32)
    pos = np.asarray(inp["positions"][b0:b0 + nseq], dtype=np.int32)
    pos = np.ascontiguousarray(pos.reshape(nseq, S // 128, 128).transpose(2, 0, 1).reshape(128, -1))
    return dict(x=x, mem=mem, pos=pos)


_CACHE = {}


def kernel(**inputs):
    inp = {k: np.asarray(v) for k, v in inputs.items()}
    B, S, _ = inp["x"].shape
    nseq = B // NCORES
    key = (S, nseq)
    if key not in _CACHE:
        _CACHE[key] = build(S, nseq)[0]
    nc = _CACHE[key]
    shared = pack_shared(inp)
    in_maps = []
    for c in range(NCORES):
        m = dict(shared)
        m.update(pack_core(inp, c * nseq, nseq, S))
        in_maps.append(m)
    res = run_bass_kernel_spmd(nc, in_maps, core_ids=list(range(NCORES)))
    out = np.concatenate([np.asarray(r["out"]) for r in res.results], axis=0)
    return out.astype(np.float32)
```

```python
import math
from contextlib import ExitStack

import numpy as np
import concourse.bass as bass
import concourse.mybir as mybir
from concourse.bass_utils import run_bass_kernel_spmd

F32 = mybir.dt.float32
BF16 = mybir.dt.bfloat16
I32 = mybir.dt.int32
AF = mybir.ActivationFunctionType
ALU = mybir.AluOpType

NCORES = 8
D = 1024
DFF = 2816
NJ = DFF // 128
NMEM = 256
EPS = 1e-6
NSLOT = 27
SLOT_MK, SLOT_MV = 23, 25
GPM, CW, CB, BA, BX, LAM, GQ, GKV, GLO, GMO, GPMEM, GMKV, GPF = 0, 8, 24, 28, 32, 36, 40, 43, 45, 49, 53, 61, 69
NPV = 77
HC, H2, BAH, BXH = 77, 81, 85, 89
NPVT = 96
SEM_LIMIT = 12000
NRING = 5
import os
F_PIPEA = os.environ.get('K_PIPEA', '1') == '1'
F_LATSKEW = os.environ.get('K_LATSKEW', '1') == '1'
F_P2EARLY = os.environ.get('K_P2EARLY', '0') == '1'
F_WPRE = os.environ.get('K_WPRE', '0') == '1'
LAT_DEPTH = int(os.environ.get('K_LATDEPTH', '2'))
F_QKVSKEW = os.environ.get('K_QKVSKEW', '1') == '1'


class Buf:
    __slots__ = ("name", "w", "r", "ld", "ldc", "ldk", "st", "stc", "stk")

    def __init__(self, name):
        self.name = name
        self.w = {}
        self.r = {}
        self.ld = None
        self.ldc = 0
        self.ldk = None
        self.st = None
        self.stc = 0
        self.stk = None


class _Eng:
    def __init__(self, sch, name, h):
        self.sch = sch
        self.name = name
        self.h = h
        self.sem = None
        self.key = None
        self.cnt = 0
        self.nsem = 0
        self.known = {}
        self.pending = False

    def newsem(self):
        self.sem = self.sch.alloc_sem(f"e_{self.name}{self.nsem}")
        self.key = (self.name, self.nsem)
        self.nsem += 1
        self.cnt = 0


class Sched:
    def __init__(self, nc, es):
        self.nc = nc
        self.es = es
        self.nsems = 0
        self.E = {
            "pe": _Eng(self, "pe", nc.tensor),
            "act": _Eng(self, "act", nc.scalar),
            "dve": _Eng(self, "dve", nc.vector),
            "pool": _Eng(self, "pool", nc.gpsimd),
            "sp": _Eng(self, "sp", nc.sync),
        }
        for e in self.E.values():
            e.newsem()
        self.dma_tokens = {}
        self.nwaits = 0
        self.ninst = 0

    def alloc_sem(self, name):
        self.nsems += 1
        return self.es.enter_context(self.nc.semaphore(name))

    @staticmethod
    def _merge(d, src):
        for k, (s, v) in src.items():
            if k not in d or d[k][1] < v:
                d[k] = (s, v)

    def _wait(self, E, key, sem, val):
        if E.known.get(key, 0) >= val:
            return
        E.h.wait_ge(sem, val)
        E.known[key] = val
        self.nwaits += 1

    def _deps(self, E, R, W, skip_keys=()):
        deps = {}
        for b in R:
            self._merge(deps, b.w)
        for b in W:
            self._merge(deps, b.w)
            self._merge(deps, b.r)
        for key, (sem, val) in deps.items():
            if key in skip_keys:
                continue
            if E.name == "pe" and key[0] == "pe":
                continue
            self._wait(E, key, sem, val)

    def op(self, en, fn, R=(), W=(), inc=True):
        E = self.E[en]
        self._deps(E, R, W)
        ins = fn(E.h)
        self.ninst += 1
        if inc:
            if E.cnt >= SEM_LIMIT and not E.pending:
                E.newsem()
            E.cnt += 1
            ins.then_inc(E.sem, 1)
            E.pending = False
            tok = (E.key, E.sem, E.cnt)
        else:
            E.pending = True
            tok = (E.key, E.sem, E.cnt + 1)
        for b in R:
            k = tok[0]
            if k not in b.r or b.r[k][1] < tok[2]:
                b.r[k] = (tok[1], tok[2])
        for b in W:
            b.w = {tok[0]: (tok[1], tok[2])}
            b.r = {}
        return ins

    def dma(self, q, out, in_, R=(), W=(), **kw):
        E = self.E[q]
        skip = ()
        if W and W[0].ldk is not None:
            skip = (W[0].ldk,)
        self._deps(E, R, W, skip_keys=skip)
        ins = E.h.dma_start(out=out, in_=in_, **kw)
        self.ninst += 1
        if W:
            b = W[0]
            if b.ld is None:
                b.ld = self.alloc_sem("ld_" + b.name)
                b.ldk = ("ld", b.name)
            b.ldc += 16
            ins.then_inc(b.ld, 16)
            tok = (b.ldk, b.ld, b.ldc)
        else:
            b = R[0]
            if b.st is None:
                b.st = self.alloc_sem("st_" + b.name)
                b.stk = ("st", b.name)
            b.stc += 16
            ins.then_inc(b.st, 16)
            tok = (b.stk, b.st, b.stc)
        self.dma_tokens[tok[0]] = (tok[1], tok[2])
        for bb in R:
            k = tok[0]
            if k not in bb.r or bb.r[k][1] < tok[2]:
                bb.r[k] = (tok[1], tok[2])
        for bb in W:
            if bb is W[0] and skip:
                bb.w[tok[0]] = (tok[1], tok[2])
                bb.r = {}
            else:
                bb.w = {tok[0]: (tok[1], tok[2])}
                bb.r = {}
        return ins

    def barrier(self, bar_ap):
        Dv = self.E["dve"]
        for E in self.E.values():
            if E is Dv:
                continue
            assert not E.pending
            if E.cnt > 0:
                self._wait(Dv, E.key, E.sem, E.cnt)
        for k, (s, v) in self.dma_tokens.items():
            self._wait(Dv, k, s, v)
        if Dv.cnt > 0:
            self._wait(Dv, Dv.key, Dv.sem, Dv.cnt)
        ins = Dv.h.memset(bar_ap, 0.0)
        if Dv.cnt >= SEM_LIMIT:
            Dv.newsem()
        Dv.cnt += 1
        ins.then_inc(Dv.sem, 1)
        for E in self.E.values():
            if E is Dv:
                continue
            self._wait(E, Dv.key, Dv.sem, Dv.cnt)

    def final_wait(self):
        sp = self.E["sp"]
        for k, (s, v) in self.dma_tokens.items():
            self._wait(sp, k, s, v)
        for E in self.E.values():
            if E is sp or E.cnt == 0:
                continue
            self._wait(sp, E.key, E.sem, E.cnt)


def build(S=2048, NSEQ=2, dbg=False):
    NTB = S // 128
    NQT = S // 512
    nc = bass.Bass("TRN2", target_bir_lowering=False)

    def din(name, shape, dt=F32):
        return nc.dram_tensor(name, list(shape), dt, kind="ExternalInput").ap()

    x_d = din("x", [NSEQ, S, D])
    mem_d = din("mem", [NSEQ, NMEM, D])
    pos_d = din("pos", [128, NSEQ * NTB], I32)
    pv_d = din("pv", [128, NPV])
    grow_d = din("grow", [3, 128, D])
    invf_d = din("invf", [128, 32])
    ident_d = din("ident", [128, 128])
    tri_d = din("tri", [128, 128])
    winA_d = din("winA", [128, 8 * 1024])
    winB_d = din("winB", [128, 8 * 768])
    wuq_d = din("wuq", [128, 3 * 768])
    wukv_d = din("wukv", [128, 2 * 1024])
    wlru_d = din("wlru", [128, 4 * 2 * 128])
    wsl_d = din("wslots", [NSLOT, 128, 4096])
    scr_d = nc.dram_tensor("scr", [NSLOT, 128, 4096], BF16, kind="Internal").ap()
    out_d = nc.dram_tensor("out", [NSEQ, S, D], F32, kind="ExternalOutput").ap()
    if dbg:
        dbg_yT = nc.dram_tensor("dbg_yT", [NSEQ, 128, 8 * S], BF16, kind="ExternalOutput").ap()

    es = ExitStack()
    sch = Sched(nc, es)

    def T(scope, name, cols, dt):
        return scope.enter_context(nc.sbuf_tensor("sb_" + name, [128, cols], dt))

    mla_scale = 1.0 / math.sqrt(192.0)
    mem_scale = 1.0 / math.sqrt(256.0)

    def body():
        pers = es
        ident = T(pers, "ident", 128, BF16)
        tri = T(pers, "tri", 128, BF16)
        onesb = T(pers, "onesb", 128, BF16)
        onesf = T(pers, "onesf", 128, F32)
        pv = T(pers, "pv", NPVT, F32)
        cst = T(pers, "cst", 8, F32)
        invf = T(pers, "invf", 32, F32)
        posi = T(pers, "posi", NSEQ * NTB, I32)
        smallt = T(pers, "smallt", 32 * 4, F32)
        bar = T(pers, "bar", 4, F32)
        yT = T(pers, "yT", 8 * S, BF16)
        PS = pers.enter_context(nc.psum_tensor("ps", [128, 6 * 512], F32))
        TPS = pers.enter_context(nc.psum_tensor("tps", [128, 2 * 1024], BF16))

        constB = Buf("const")
        identB = Buf("identb")
        triB = Buf("trib")
        pvB = Buf("pvb")
        invfB = Buf("invfb")
        posB = Buf("posb")
        bankB = [Buf(f"bank{i}") for i in range(6)]
        tpB = [Buf(f"tpb{i}") for i in range(2)]
        smallB = [Buf(f"small{i}") for i in range(32)]
        yTB = [Buf(f"yT{i}") for i in range(NQT)]
        scrB = Buf("scr")
        _bufcache = {}

        def MB(name):
            import re as _re
            key = _re.sub(r"@\d", "@", name)
            if key not in _bufcache:
                _bufcache[key] = Buf(key.replace("@", "_"))
            return _bufcache[key]

        st = {"small": 0, "bank": 0, "pair": 0, "tp": 0, "alt": 0}

        def small():
            i = st["small"] % 32
            st["small"] += 1
            return smallt[:, i * 4:(i + 1) * 4], smallB[i]

        def nbank():
            i = st["bank"] % 6
            st["bank"] += 1
            return PS[:, i * 512:(i + 1) * 512], bankB[i]

        def npair():
            i = st["pair"] % 3
            st["pair"] += 1
            return PS[:, i * 1024:(i + 1) * 1024], [bankB[2 * i], bankB[2 * i + 1]]

        def ntp():
            i = st["tp"] % 2
            st["tp"] += 1
            return TPS[:, i * 1024:(i + 1) * 1024], tpB[i]

        def MM(out, lhsT, rhs, start, stop, R, W, inc=None):
            return sch.op("pe", lambda e: e.matmul(out, lhsT=lhsT, rhs=rhs, start=start, stop=stop),
                          R=R, W=W, inc=(stop if inc is None else inc))

        def TR(out, in_, R, W, last):
            return sch.op("pe", lambda e: e.transpose(out, in_, ident[:]), R=list(R) + [identB], W=W, inc=last)

        def ACT(out, in_, func, R, W, **kw):
            return sch.op("act", lambda e: e.activation(out=out, in_=in_, func=func, **kw), R=R, W=W)

        def TS(out, in0, s1, s2, op0, op1, R, W, eng="dve"):
            if s2 is None:
                return sch.op(eng, lambda e: e.tensor_scalar(out=out, in0=in0, scalar1=s1, scalar2=None, op0=op0),
                              R=R, W=W)
            return sch.op(eng, lambda e: e.tensor_scalar(out=out, in0=in0, scalar1=s1, scalar2=s2, op0=op0, op1=op1),
                          R=R, W=W)

        def TT(out, in0, in1, op, R, W, eng="dve"):
            return sch.op(eng, lambda e: e.tensor_tensor(out=out, in0=in0, in1=in1, op=op), R=R, W=W)

        def STT(out, in0, scalar, in1, op0, op1, R, W):
            return sch.op("dve", lambda e: e.scalar_tensor_tensor(out=out, in0=in0, scalar=scalar, in1=in1,
                                                                  op0=op0, op1=op1), R=R, W=W)

        def COPY(out, in_, R, W, eng=None):
            if eng is None:
                eng = "act" if st["alt"] % 2 == 0 else "dve"
                st["alt"] += 1
            if eng == "act":
                return sch.op("act", lambda e: e.activation(out=out, in_=in_, func=AF.Copy), R=R, W=W)
            return sch.op(eng, lambda e: e.tensor_copy(out=out, in_=in_), R=R, W=W)

        def RECIP(out, in_, R, W):
            return sch.op("dve", lambda e: e.reciprocal(out=out, in_=in_), R=R, W=W)

        def rstd_from_ss(ss, ssB, n, invd):
            for i in range(n):
                TS(ss[:, i:i + 1], ss[:, i:i + 1], invd[i], EPS, ALU.mult, ALU.add, R=[ssB], W=[ssB])
            ACT(ss[:, 0:n], ss[:, 0:n], AF.Sqrt, R=[ssB], W=[ssB])
            RECIP(ss[:, 0:n], ss[:, 0:n], R=[ssB], W=[ssB])

        sch.dma("sp", pv[:, 0:NPV], pv_d[:, :], W=[pvB])
        sch.dma("sp", invf[:], invf_d[:, :], W=[invfB])
        sch.dma("sp", posi[:], pos_d[:, :], W=[posB])
        sch.dma("pool", ident[:], ident_d[:, :], W=[identB])
        sch.dma("pool", tri[:], tri_d[:, :], W=[triB])
        sch.op("dve", lambda e: e.memset(onesb[:], 1.0), W=[constB])
        sch.op("dve", lambda e: e.memset(onesf[:], 1.0), W=[constB])
        sch.op("dve", lambda e: e.memset(cst[:, 0:1], 1.0), W=[constB])
        sch.op("dve", lambda e: e.memset(cst[:, 1:2], math.pi), W=[constB])
        tmpc, tmpB = small()
        tmpc2, tmpB2 = small()
        TS(tmpc2, pv[:, LAM:LAM + 4], -1.0, None, ALU.mult, None, R=[pvB], W=[tmpB2])
        TT(tmpc, pv[:, LAM:LAM + 4], tmpc2, ALU.max, R=[pvB, tmpB2], W=[tmpB])
        ACT(tmpc, tmpc, AF.Exp, R=[tmpB], W=[tmpB], scale=-1.0)
        TS(tmpc, tmpc, 1.0, None, ALU.add, None, R=[tmpB], W=[tmpB])
        ACT(tmpc, tmpc, AF.Ln, R=[tmpB], W=[tmpB])
        TS(tmpc2, tmpc2, 0.0, None, ALU.max, None, R=[tmpB2], W=[tmpB2])
        TT(tmpc, tmpc, tmpc2, ALU.add, R=[tmpB, tmpB2], W=[tmpB])
        TS(pv[:, HC:HC + 4], tmpc, -4.0, None, ALU.mult, None, R=[tmpB], W=[pvB])
        TS(pv[:, H2:H2 + 4], tmpc, -8.0, None, ALU.mult, None, R=[tmpB], W=[pvB])
        TS(pv[:, BAH:BAH + 4], pv[:, BA:BA + 4], 0.5, None, ALU.mult, None, R=[pvB], W=[pvB])
        TS(pv[:, BXH:BXH + 4], pv[:, BX:BX + 4], 0.5, None, ALU.mult, None, R=[pvB], W=[pvB])

        scr_state = {"done": False}

        def convert_scratch():
            if scr_state["done"]:
                return
            scr_state["done"] = True
            order = [23, 24, 25, 26] + list(range(23))
            for sl in order:
                sch.dma("pool", scr_d[sl].rearrange("p (a b) -> p a b", a=2),
                        wsl_d[sl].rearrange("p (a b) -> p a b", a=2), W=[scrB])

        def nt_a(src, srcB, Dn, junk, junkB):
            ss, ssB = small()
            ACT(junk[:, 0:Dn], src, AF.Square, R=srcB, W=[junkB, ssB], accum_out=ss[:, 0:1])
            rstd_from_ss(ss, ssB, 1, [1.0 / Dn])
            return ss, ssB

        def nt_b(src, srcB, Dn, ss, ssB, gcols, dst3, dstB, xs, xsB):
            nk = Dn // 128
            ACT(xs[:, 0:Dn], src, AF.Copy, R=list(srcB) + [ssB], W=[xsB], scale=ss[:, 0:1])
            tp, tB = ntp()
            for k in range(nk):
                TR(tp[:, k * 128:(k + 1) * 128], xs[:, k * 128:(k + 1) * 128], R=[xsB], W=[tB], last=(k == nk - 1))
            g3 = gcols.unsqueeze(2).broadcast_to([128, nk, 128])
            TT(dst3, tp[:, 0:Dn].rearrange("p (k t) -> p k t", k=nk), g3, ALU.mult, R=[tB, pvB], W=dstB)

        def nt(src, srcB, Dn, gcols, dst3, dstB, xs, xsB, junk, junkB):
            ss, ssB = nt_a(src, srcB, Dn, junk, junkB)
            nt_b(src, srcB, Dn, ss, ssB, gcols, dst3, dstB, xs, xsB)

        def rope(src3, srcB, G, cos, sin, csB, dst3, dstB, tA, tAB, tB_, tBB, extra=()):
            srcB = list(srcB) + list(extra)
            x1 = src3[:, :, 0:32]
            x2 = src3[:, :, 32:64]
            cb = cos.unsqueeze(1).broadcast_to([128, G, 32])
            sb = sin.unsqueeze(1).broadcast_to([128, G, 32])
            a3 = tA[:, 0:G * 32].rearrange("p (g d) -> p g d", g=G)
            b3 = tB_[:, 0:G * 32].rearrange("p (g d) -> p g d", g=G)
            TT(a3, x1, cb, ALU.mult, R=srcB + [csB], W=[tAB])
            TT(b3, x2, sb, ALU.mult, R=srcB + [csB], W=[tBB])
            TT(dst3[:, :, 0:32], a3, b3, ALU.subtract, R=[tAB, tBB], W=dstB)
            TT(a3, x2, cb, ALU.mult, R=srcB + [csB], W=[tAB])
            TT(b3, x1, sb, ALU.mult, R=srcB + [csB], W=[tBB])
            TT(dst3[:, :, 32:64], a3, b3, ALU.add, R=[tAB, tBB], W=dstB)

        def feat_norm(srcs, srcBs, gcol0, Dn, tt, dst_off, sq, sqB, rt, rtB):
            n = len(srcs)
            bank, bB = nbank()
            for i in range(n):
                q, qB = sq[i % 2], sqB[i % 2]
                ACT(q[:, :], srcs[i][:, tt * 512:(tt + 1) * 512], AF.Square, R=[srcBs[i]], W=[qB])
                MM(bank, onesf[:], q[:, :], i == 0, i == n - 1, R=[qB, constB], W=[bB], inc=True)
            TS(rt[:, :], bank, 1.0 / Dn, EPS, ALU.mult, ALU.add, R=[bB], W=[rtB])
            ACT(rt[:, :], rt[:, :], AF.Sqrt, R=[rtB], W=[rtB])
            RECIP(rt[:, :], rt[:, :], R=[rtB], W=[rtB])
            for i in range(n):
                c = dst_off + i
                STT(yT[:, c * S + tt * 512: c * S + (tt + 1) * 512], srcs[i][:, tt * 512:(tt + 1) * 512],
                    pv[:, gcol0 + i:gcol0 + i + 1], rt[:, :], ALU.mult, ALU.mult,
                    R=[srcBs[i], pvB, rtB], W=[yTB[tt]])

        for s in range(NSEQ):
            sfx = f"_{s}"
            with ExitStack() as s1:
                cqnT = T(s1, "cqnT" + sfx, 3 * S, BF16)
                ckvnT = T(s1, "ckvnT" + sfx, 2 * S, BF16)
                kpeT = T(s1, "kpeT" + sfx, S, BF16)
                cs = T(s1, "cs" + sfx, NTB * 64, F32)
                latB = [MB(f"lat@{s}_{i}") for i in range(NTB)]
                csB = MB(f"cs@{s}")
                with ExitStack() as s2:
                    xnT = T(s2, "xnT" + sfx, 8 * S, BF16)
                    xnT3 = xnT[:, :].rearrange("p (k t) -> p k t", k=8)
                    xnTB = [MB(f"xnT@{s}_{i}") for i in range(NTB)]
                    winA = T(s2, "winA" + sfx, 8 * 1024, BF16)
                    wlru = T(s2, "wlru" + sfx, 4 * 2 * 128, BF16)
                    winAB = MB(f"winA@{s}")
                    wlruB = MB(f"wlru@{s}")
                    sch.dma("pool", winA[:, :].rearrange("p (k n) -> p k n", k=8),
                            winA_d[:, :].rearrange("p (k n) -> p k n", k=8), W=[winAB])
                    sch.dma("pool", wlru[:], wlru_d[:, :], W=[wlruB])
                    if F_WPRE:
                        convert_scratch()
                    with ExitStack() as s2a:
                        xtmp = [T(s2a, f"xtmp{i}" + sfx, 1024, F32) for i in range(3)]
                        xtmpB = [MB(f"xtmp@{s}_{i}") for i in range(3)]
                        xs = [T(s2a, f"xs{i}" + sfx, 1024, BF16) for i in range(2)]
                        xsB = [MB(f"xs@{s}_{i}") for i in range(2)]
                        junk = T(s2a, "junk" + sfx, 1024, BF16)
                        junkB = MB(f"junk@{s}")
                        ang = T(s2a, "ang" + sfx, NTB * 64, F32)
                        kf = T(s2a, "kf" + sfx, NTB * 64, F32)
                        ki = T(s2a, "ki" + sfx, NTB * 64, I32)
                        posf = T(s2a, "posf" + sfx, NTB, F32)
                        angB = MB(f"ang@{s}")
                        kfB = MB(f"kf@{s}")
                        kiB = MB(f"ki@{s}")
                        posfB = MB(f"posf@{s}")
                        pend = None
                        for tb in range(NTB + 1):
                            cur = None
                            if tb < NTB:
                                i = tb % 3
                                sch.dma("sp", xtmp[i][:], x_d[s, tb * 128:(tb + 1) * 128, :], W=[xtmpB[i]])
                                cur = (tb,) + nt_a(xtmp[i][:, :], [xtmpB[i]], 1024, junk, junkB)
                            if not F_PIPEA:
                                pend = cur
                                cur = None
                            if pend is not None:
                                ptb, pss, pssB = pend
                                pi = ptb % 3
                                nt_b(xtmp[pi][:, :], [xtmpB[pi]], 1024, pss, pssB, pv[:, GPM:GPM + 8],
                                     xnT3[:, :, ptb * 128:(ptb + 1) * 128], [xnTB[ptb]], xs[ptb % 2], xsB[ptb % 2])
                            pend = cur
                        COPY(posf[:], posi[:, s * NTB:(s + 1) * NTB], R=[posB], W=[posfB], eng="dve")
                        ang3 = ang[:, :].rearrange("p (t d) -> p t d", t=NTB)
                        TT(ang3[:, :, 0:32], posf[:, :].unsqueeze(2).broadcast_to([128, NTB, 32]),
                           invf[:, :].unsqueeze(1).broadcast_to([128, NTB, 32]), ALU.mult,
                           R=[posfB, invfB], W=[angB])
                        TS(ang3[:, :, 32:64], ang3[:, :, 0:32], math.pi / 2, None, ALU.add, None, R=[angB], W=[angB])
                        TS(kf[:], ang[:], 1.0 / (2 * math.pi), None, ALU.mult, None, R=[angB], W=[kfB])
                        COPY(ki[:], kf[:], R=[kfB], W=[kiB], eng="dve")
                        COPY(kf[:], ki[:], R=[kiB], W=[kfB], eng="dve")
                        C1 = 6.28125
                        C2 = 2 * math.pi - C1
                        STT(ang[:], kf[:], -C1, ang[:], ALU.mult, ALU.add, R=[kfB, angB], W=[angB])
                        STT(ang[:], kf[:], -C2, ang[:], ALU.mult, ALU.add, R=[kfB, angB], W=[angB])
                        TS(kf[:], ang[:], math.pi, None, ALU.is_gt, None, R=[angB], W=[kfB])
                        STT(ang[:], kf[:], -2 * math.pi, ang[:], ALU.mult, ALU.add, R=[kfB, angB], W=[angB])
                        TS(kf[:], ang[:], -math.pi, None, ALU.is_lt, None, R=[angB], W=[kfB])
                        STT(ang[:], kf[:], 2 * math.pi, ang[:], ALU.mult, ALU.add, R=[kfB, angB], W=[angB])
                        TS(ang[:], ang[:], math.pi, -math.pi, ALU.min, ALU.max, R=[angB], W=[angB])
                        ACT(cs[:], ang[:], AF.Sin, R=[angB], W=[csB])
                        sch.barrier(bar[:, 0:1])

                    with ExitStack() as s3:
                        convert_scratch()
                        NSEG = NQT
                        Rb = [T(s3, f"R{i}" + sfx, S, F32) for i in range(6)]
                        RB = [[MB(f"R@{s}_{i}_{g}") for g in range(NSEG)] for i in range(6)]
                        Ubf = T(s3, "Ubf" + sfx, S, BF16)
                        UbfB = [MB(f"Ubf@{s}_{g}") for g in range(NSEG)]
                        YL = [T(s3, f"YL{i}" + sfx, S, F32) for i in range(4)]
                        YLB = [[MB(f"YL@{s}_{i}_{g}") for g in range(NSEG)] for i in range(4)]
                        sq = [T(s3, f"sq{i}" + sfx, 512, F32) for i in range(2)]
                        sqB = [MB(f"sq@{s}_{i}") for i in range(2)]
                        rt = T(s3, "rt" + sfx, 512, F32)
                        rtB = MB(f"rt@{s}")
                        LX, LG, U, A, TI, R6 = Rb
                        LXB, LGB, UB, AB, TIB, R6B = RB
                        segs = [slice(g * 512, (g + 1) * 512) for g in range(NSEG)]
                        for c in range(4):
                            def col(base):
                                return pv[:, base + c:base + c + 1]
                            for g in range(NSEG):
                                sl = segs[g]
                                bank, bB = nbank()
                                for kc in range(8):
                                    MM(bank, winA[:, kc * 1024 + c * 128: kc * 1024 + (c + 1) * 128],
                                       xnT[:, kc * S + g * 512: kc * S + (g + 1) * 512], kc == 0, kc == 7,
                                       R=[winAB] + xnTB[g * 4:(g + 1) * 4], W=[bB])
                                COPY(LX[:, sl], bank, R=[bB], W=[LXB[g]], eng="act")
                                bank, bB = nbank()
                                for kc in range(8):
                                    MM(bank, winA[:, kc * 1024 + 512 + c * 128: kc * 1024 + 512 + (c + 1) * 128],
                                       xnT[:, kc * S + g * 512: kc * S + (g + 1) * 512], kc == 0, kc == 7,
                                       R=[winAB] + xnTB[g * 4:(g + 1) * 4], W=[bB])
                                COPY(LG[:, sl], bank, R=[bB], W=[LGB[g]], eng="dve")
                            for g in range(NSEG):
                                sl = segs[g]
                                e = (g + 1) * 512
                                TS(U[:, sl], LX[:, sl], pv[:, CW + 3 * 4 + c:CW + 3 * 4 + c + 1], col(CB), ALU.mult, ALU.add,
                                   R=[LXB[g], pvB], W=[UB[g]])
                                for k, sh in ((2, 1), (1, 2), (0, 3)):
                                    lo = max(g * 512, sh)
                                    rr = [LXB[g], pvB, UB[g]] + ([LXB[g - 1]] if g > 0 else [])
                                    STT(U[:, lo:e], LX[:, lo - sh:e - sh], pv[:, CW + k * 4 + c:CW + k * 4 + c + 1], U[:, lo:e],
                                        ALU.mult, ALU.add, R=rr, W=[UB[g]])
                                COPY(Ubf[:, sl], U[:, sl], R=[UB[g]], W=[UbfB[g]], eng="act")
                            for g in range(NSEG):
                                sl = segs[g]
                                bank, bB = nbank()
                                MM(bank, wlru[:, (c * 2 + 0) * 128:(c * 2 + 1) * 128], Ubf[:, sl], True, True,
                                   R=[wlruB, UbfB[g]], W=[bB])
                                ACT(LX[:, sl], bank, AF.Tanh, R=[bB, pvB], W=[LXB[g]], scale=0.5, bias=col(BAH))
                                bank, bB = nbank()
                                MM(bank, wlru[:, (c * 2 + 1) * 128:(c * 2 + 2) * 128], Ubf[:, sl], True, True,
                                   R=[wlruB, UbfB[g]], W=[bB])
                                ACT(TI[:, sl], bank, AF.Tanh, R=[bB, pvB], W=[TIB[g]], scale=0.5, bias=col(BXH))
                            for g in range(NSEG):
                                sl = segs[g]
                                ACT(A[:, sl], LX[:, sl], AF.Exp, R=[LXB[g], pvB], W=[AB[g]], scale=col(HC), bias=col(HC))
                                ACT(R6[:, sl], LX[:, sl], AF.Exp, R=[LXB[g], pvB], W=[R6B[g]], scale=col(H2), bias=col(H2))
                            for g in range(NSEG):
                                sl = segs[g]
                                ACT(LX[:, sl], LG[:, sl], AF.Square, R=[LGB[g]], W=[LXB[g]])
                                TS(LX[:, sl], LX[:, sl], 0.044715, 1.0, ALU.mult, ALU.add, R=[LXB[g]], W=[LXB[g]])
                                TT(LX[:, sl], LX[:, sl], LG[:, sl], ALU.mult, R=[LXB[g], LGB[g]], W=[LXB[g]])
                            for g in range(NSEG):
                                sl = segs[g]
                                ACT(LX[:, sl], LX[:, sl], AF.Tanh, R=[LXB[g]], W=[LXB[g]], scale=0.7978845608028654)
                            for g in range(NSEG):
                                sl = segs[g]
                                ACT(R6[:, sl], R6[:, sl], AF.Sqrt, R=[R6B[g], constB], W=[R6B[g]], scale=-1.0, bias=cst[:, 0:1])
                            for g in range(NSEG):
                                sl = segs[g]
                                STT(TI[:, sl], TI[:, sl], 1.0, U[:, sl], ALU.add, ALU.mult, R=[TIB[g], UB[g]], W=[TIB[g]])
                                STT(TI[:, sl], TI[:, sl], 0.5, R6[:, sl], ALU.mult, ALU.mult, R=[TIB[g], R6B[g]], W=[TIB[g]])
                            for g in range(NSEG):
                                sl = segs[g]
                                if g == 0:
                                    sch.op("dve", lambda e, sl=sl: e.tensor_tensor_scan(
                                        out=U[:, sl], data0=A[:, sl], data1=TI[:, sl], initial=0.0,
                                        op0=ALU.mult, op1=ALU.add), R=[AB[g], TIB[g]], W=[UB[g]])
                                else:
                                    sch.op("dve", lambda e, sl=sl, g=g: e.tensor_tensor_scan(
                                        out=U[:, sl], data0=A[:, sl], data1=TI[:, sl],
                                        initial=U[:, g * 512 - 1:g * 512],
                                        op0=ALU.mult, op1=ALU.add), R=[AB[g], TIB[g], UB[g - 1]], W=[UB[g]])
                                STT(LX[:, sl], LX[:, sl], 1.0, LG[:, sl], ALU.add, ALU.mult, R=[LXB[g], LGB[g]], W=[LXB[g]])
                                STT(YL[c][:, sl], LX[:, sl], 0.5, U[:, sl], ALU.mult, ALU.mult, R=[LXB[g], UB[g]], W=[YLB[c][g]])
                        for tt in range(NQT):
                            feat_norm([YL[i] for i in range(4)], [YLB[i][tt] for i in range(4)], GLO, 512, tt, 0, sq, sqB, rt, rtB)
                        sch.barrier(bar[:, 0:1])

                    with ExitStack() as s4:
                        winB = T(s4, "winB" + sfx, 8 * 768, BF16)
                        winBB = MB(f"winB@{s}")
                        sch.dma("pool", winB[:, :].rearrange("p (k n) -> p k n", k=8),
                                winB_d[:, :].rearrange("p (k n) -> p k n", k=8), W=[winBB])
                        lat = [T(s4, f"latbf{i}" + sfx, 768, BF16) for i in range(2)]
                        latbB = [MB(f"latbf@{s}_{i}") for i in range(2)]
                        junk = T(s4, "junkb" + sfx, 512, BF16)
                        junkB = MB(f"junkb@{s}")
                        rtmp = [T(s4, f"rtmp{i}" + sfx, 128, F32) for i in range(2)]
                        rtmpB = [MB(f"rtmp@{s}_{i}") for i in range(2)]
                        cqnT3 = cqnT[:, :].rearrange("p (k t) -> p k t", k=3)
                        ckvnT3 = ckvnT[:, :].rearrange("p (k t) -> p k t", k=2)
                        def lat_front(tb):
                            bA, bAB = nbank()
                            bBk, bBB = nbank()
                            for kc in range(8):
                                lhs = xnT[:, kc * S + tb * 128: kc * S + (tb + 1) * 128]
                                MM(bA, lhs, winB[:, kc * 768: kc * 768 + 512], kc == 0, kc == 7,
                                   R=[winBB, xnTB[tb]], W=[bAB])
                            for kc in range(8):
                                lhs = xnT[:, kc * S + tb * 128: kc * S + (tb + 1) * 128]
                                MM(bBk[:, 0:256], lhs, winB[:, kc * 768 + 512: kc * 768 + 768], kc == 0, kc == 7,
                                   R=[winBB, xnTB[tb]], W=[bBB])
                            ss, ssB = small()
                            return (tb, bA, bAB, bBk, bBB, ss, ssB)

                        def lat_stats(stt):
                            tb, bA, bAB, bBk, bBB, ss, ssB = stt
                            ACT(junk[:, 0:384], bA[:, 0:384], AF.Square, R=[bAB], W=[junkB, ssB], accum_out=ss[:, 0:1])
                            ACT(junk[:, 0:256], bBk[:, 0:256], AF.Square, R=[bBB], W=[junkB, ssB], accum_out=ss[:, 1:2])
                            rstd_from_ss(ss, ssB, 2, [1.0 / 384, 1.0 / 256])

                        def lat_back(stt):
                            tb, bA, bAB, bBk, bBB, ss, ssB = stt
                            bsl = slice(tb * 128, (tb + 1) * 128)
                            L = lat[tb % 2]
                            LB = latbB[tb % 2]
                            ACT(L[:, 0:384], bA[:, 0:384], AF.Copy, R=[bAB, ssB], W=[LB], scale=ss[:, 0:1])
                            ACT(L[:, 384:640], bBk[:, 0:256], AF.Copy, R=[bBB, ssB], W=[LB], scale=ss[:, 1:2])
                            rope(bA[:, 384:512].rearrange("p (g d) -> p g d", g=2), [bAB], 2,
                                 cs[:, tb * 64 + 32: tb * 64 + 64], cs[:, tb * 64: tb * 64 + 32], csB,
                                 L[:, 640:768].rearrange("p (g d) -> p g d", g=2), [LB],
                                 rtmp[0], rtmpB[0], rtmp[1], rtmpB[1], extra=[LB])
                            tp, tB = ntp()
                            for k in range(6):
                                TR(tp[:, k * 128:(k + 1) * 128], L[:, k * 128:(k + 1) * 128], R=[LB], W=[tB], last=(k == 5))
                            TT(cqnT3[:, :, bsl], tp[:, 0:384].rearrange("p (k t) -> p k t", k=3),
                               pv[:, GQ:GQ + 3].unsqueeze(2).broadcast_to([128, 3, 128]), ALU.mult,
                               R=[tB, pvB], W=[latB[tb]])
                            TT(ckvnT3[:, :, bsl], tp[:, 384:640].rearrange("p (k t) -> p k t", k=2),
                               pv[:, GKV:GKV + 2].unsqueeze(2).broadcast_to([128, 2, 128]), ALU.mult,
                               R=[tB, pvB], W=[latB[tb]])
                            COPY(kpeT[:, bsl], tp[:, 640:768], R=[tB], W=[latB[tb]], eng="dve")

                        depth = LAT_DEPTH if F_LATSKEW else 0
                        sts = {}
                        for tb in range(NTB + depth):
                            if tb < NTB:
                                sts[tb] = lat_front(tb)
                            if depth > 0 and tb - depth >= 0:
                                lat_back(sts.pop(tb - depth))
                            if tb < NTB:
                                lat_stats(sts[tb])
                                if depth == 0:
                                    lat_back(sts.pop(tb))
                        sch.barrier(bar[:, 0:1])
                with ExitStack() as s5:
                    qnT = T(s5, "qnT" + sfx, 4 * S, BF16)
                    qpeT = T(s5, "qpeT" + sfx, 2 * S, BF16)
                    knT = T(s5, "knT" + sfx, 4 * S, BF16)
                    Vt = T(s5, "Vt" + sfx, NTB * 512, BF16)
                    qB = [MB(f"q@{s}_{i}") for i in range(NQT)]
                    kB = [MB(f"k@{s}_{i}") for i in range(NQT)]
                    qpeB = [MB(f"qpe@{s}_{i}") for i in range(NTB)]
                    vB = [MB(f"v@{s}_{i}") for i in range(NTB)]
                    qpeT3 = qpeT[:, :].rearrange("p (k t) -> p k t", k=2)
                    with ExitStack() as s5b:
                        wuq = T(s5b, "wuq" + sfx, 3 * 768, BF16)
                        wukv = T(s5b, "wukv" + sfx, 2 * 1024, BF16)
                        wuqB = MB(f"wuq@{s}")
                        wukvB = MB(f"wukv@{s}")
                        sch.dma("pool", wuq[:, :].rearrange("p (k n) -> p k n", k=3),
                                wuq_d[:, :].rearrange("p (k n) -> p k n", k=3), W=[wuqB])
                        sch.dma("pool", wukv[:, :].rearrange("p (k n) -> p k n", k=2),
                                wukv_d[:, :].rearrange("p (k n) -> p k n", k=2), W=[wukvB])
                        qpb = [T(s5b, f"qpb{i}" + sfx, 256, BF16) for i in range(2)]
                        qpbB = [MB(f"qpb@{s}_{i}") for i in range(2)]
                        rtmp = [T(s5b, f"rtq{i}" + sfx, 128, F32) for i in range(2)]
                        rtmpB = [MB(f"rtq@{s}_{i}") for i in range(2)]
                        for tt in range(NQT):
                            tsl = slice(tt * 512, (tt + 1) * 512)
                            lb4 = latB[tt * 4:(tt + 1) * 4]
                            for h in range(4):
                                bank, bB = nbank()
                                for kc in range(3):
                                    MM(bank, wuq[:, kc * 768 + h * 128: kc * 768 + (h + 1) * 128],
                                       cqnT[:, kc * S + tt * 512: kc * S + (tt + 1) * 512], kc == 0, kc == 2,
                                       R=[wuqB] + lb4, W=[bB])
                                COPY(qnT[:, h * S + tt * 512: h * S + (tt + 1) * 512], bank, R=[bB], W=[qB[tt]])
                            for h in range(4):
                                bank, bB = nbank()
                                for kc in range(2):
                                    MM(bank, wukv[:, kc * 1024 + h * 128: kc * 1024 + (h + 1) * 128],
                                       ckvnT[:, kc * S + tt * 512: kc * S + (tt + 1) * 512], kc == 0, kc == 1,
                                       R=[wukvB] + lb4, W=[bB])
                                COPY(knT[:, h * S + tt * 512: h * S + (tt + 1) * 512], bank, R=[bB], W=[kB[tt]])
                            def stA(tb):
                                bank, bB = nbank()
                                for kc in range(3):
                                    MM(bank[:, 0:256], cqnT[:, kc * S + tb * 128: kc * S + (tb + 1) * 128],
                                       wuq[:, kc * 768 + 512: kc * 768 + 768], kc == 0, kc == 2,
                                       R=[wuqB, latB[tb]], W=[bB])
                                Q = qpb[tb % 2]
                                QB = qpbB[tb % 2]
                                rope(bank[:, 0:256].rearrange("p (g d) -> p g d", g=4), [bB], 4,
                                     cs[:, tb * 64 + 32: tb * 64 + 64], cs[:, tb * 64: tb * 64 + 32], csB,
                                     Q[:, :].rearrange("p (g d) -> p g d", g=4), [QB],
                                     rtmp[0], rtmpB[0], rtmp[1], rtmpB[1])

                            def stB(tb):
                                bsl = slice(tb * 128, (tb + 1) * 128)
                                Q = qpb[tb % 2]
                                QB = qpbB[tb % 2]
                                tp, tB = ntp()
                                for k in range(2):
                                    TR(tp[:, k * 128:(k + 1) * 128], Q[:, k * 128:(k + 1) * 128], R=[QB], W=[tB], last=(k == 1))
                                COPY(qpeT3[:, :, bsl], tp[:, 0:256].rearrange("p (k t) -> p k t", k=2),
                                     R=[tB], W=[qpeB[tb]], eng="act")

                            def stV(tb):
                                bank, bB = nbank()
                                for kc in range(2):
                                    MM(bank, ckvnT[:, kc * S + tb * 128: kc * S + (tb + 1) * 128],
                                       wukv[:, kc * 1024 + 512: kc * 1024 + 1024], kc == 0, kc == 1,
                                       R=[wukvB, latB[tb]], W=[bB])
                                COPY(Vt[:, tb * 512:(tb + 1) * 512], bank, R=[bB], W=[vB[tb]], eng="dve")

                            t0 = tt * 4
                            if F_QKVSKEW:
                                stA(t0); stV(t0); stA(t0 + 1); stV(t0 + 1); stB(t0)
                                stA(t0 + 2); stV(t0 + 2); stB(t0 + 1)
                                stA(t0 + 3); stV(t0 + 3); stB(t0 + 2); stB(t0 + 3)
                            else:
                                for tb in range(t0, t0 + 4):
                                    stA(tb); stB(tb); stV(tb)
                        sch.barrier(bar[:, 0:1])
                    with ExitStack() as s6:
                        YM = [T(s6, f"YM{i}" + sfx, S, F32) for i in range(4)]
                        YMB = [MB(f"YM@{s}_{i}") for i in range(4)]
                        Pb = [T(s6, f"Pb{i}" + sfx, 512, BF16) for i in range(3)]
                        PbB = [MB(f"Pb@{s}_{i}") for i in range(3)]
                        rden = [T(s6, f"rden{i}" + sfx, 512, F32) for i in range(2)]
                        rdenB = [MB(f"rden@{s}_{i}") for i in range(2)]
                        sq = [T(s6, f"sqm{i}" + sfx, 512, F32) for i in range(2)]
                        sqB = [MB(f"sqm@{s}_{i}") for i in range(2)]
                        rt = T(s6, "rtm" + sfx, 512, F32)
                        rtB = MB(f"rtm@{s}")
                        it = 0
                        pcount = 0
                        for qi in range(NQT):
                            for h in range(4):
                                hb = (h % 2) * 64
                                nkc = 4 * qi + 4
                                ob = 4 if it % 2 == 0 else 2
                                O, OB = PS[:, ob * 512:(ob + 1) * 512], bankB[ob]
                                DN, DNB = PS[:, (ob + 1) * 512:(ob + 2) * 512], bankB[ob + 1]

                                def c0_of(kc):
                                    return 0 if kc < 4 * qi else (kc - 4 * qi) * 128

                                def emitS(kc):
                                    sbi = kc % 2
                                    Sb, SB = PS[:, sbi * 512:(sbi + 1) * 512], bankB[sbi]
                                    c0 = c0_of(kc)
                                    MM(Sb[:, c0:512], knT[:, h * S + kc * 128: h * S + (kc + 1) * 128],
                                       qnT[:, h * S + qi * 512 + c0: h * S + (qi + 1) * 512], True, False,
                                       R=[kB[kc // 4], qB[qi]], W=[SB])
                                    MM(Sb[:, c0:512], kpeT[hb:hb + 64, kc * 128:(kc + 1) * 128],
                                       qpeT[hb:hb + 64, (h // 2) * S + qi * 512 + c0: (h // 2) * S + (qi + 1) * 512],
                                       False, True, R=[latB[kc]] + qpeB[qi * 4:(qi + 1) * 4], W=[SB])

                                emitS(0)
                                for kc in range(nkc):
                                    if kc + 1 < nkc:
                                        emitS(kc + 1)
                                    sbi = kc % 2
                                    Sb, SB = PS[:, sbi * 512:(sbi + 1) * 512], bankB[sbi]
                                    c0 = c0_of(kc)
                                    P, PB = Pb[pcount % 3], PbB[pcount % 3]
                                    pcount += 1
                                    ACT(P[:, c0:512], Sb[:, c0:512], AF.Exp, R=[SB], W=[PB], scale=mla_scale)
                                    if kc >= 4 * qi:
                                        TT(P[:, c0:c0 + 128], P[:, c0:c0 + 128], tri[:, :], ALU.mult, R=[PB, triB], W=[PB])
                                    MM(O[:, c0:512], Vt[:, kc * 512 + h * 128: kc * 512 + (h + 1) * 128], P[:, c0:512],
                                       kc == 0, kc == nkc - 1, R=[vB[kc], PB], W=[OB])
                                    MM(DN[:, c0:512], onesb[:, :], P[:, c0:512], kc == 0, kc == nkc - 1,
                                       R=[constB, PB], W=[DNB])
                                rd, rdB = rden[it % 2], rdenB[it % 2]
                                RECIP(rd[:, :], DN, R=[DNB], W=[rdB])
                                TT(YM[h][:, qi * 512:(qi + 1) * 512], O, rd[:, :], ALU.mult, R=[OB, rdB], W=[YMB[h]])
                                it += 1
                            feat_norm(YM, YMB, GMO, 512, qi, 4, sq, sqB, rt, rtB)
                        sch.barrier(bar[:, 0:1])
            if dbg:
                dB = MB(f"dbgy@{s}")
                sch.dma("sp", dbg_yT[s], yT[:, :], R=yTB + [dB])

            with ExitStack() as p2:
                G = [T(p2, f"G{i}" + sfx, 1024, F32) for i in range(3)]
                GB = [MB(f"G@{s}_{i}") for i in range(3)]
                for i in range(3):
                    sch.dma("sp", G[i][:], grow_d[i], W=[GB[i]])
                mkT = T(p2, "mkT" + sfx, 8 * 256, BF16)
                mv = T(p2, "mv" + sfx, 2 * 1024, BF16)
                mkB = MB(f"mk@{s}")
                mvB = MB(f"mv@{s}")
                with ExitStack() as pm:
                    mnT = T(pm, "mnT" + sfx, 8 * 256, BF16)
                    mnB = [MB(f"mn@{s}_{i}") for i in range(2)]
                    memt = [T(pm, f"memt{i}" + sfx, 1024, F32) for i in range(2)]
                    memtB = [MB(f"memt@{s}_{i}") for i in range(2)]
                    xs = [T(pm, f"xsm{i}" + sfx, 1024, BF16) for i in range(2)]
                    xsB = [MB(f"xsm@{s}_{i}") for i in range(2)]
                    junk = T(pm, "junkm" + sfx, 1024, BF16)
                    junkB = MB(f"junkm@{s}")
                    wm = T(pm, "wm" + sfx, 4 * 4096, BF16)
                    wmB = [MB(f"wm@{s}_{i}") for i in range(4)]
                    for i in range(4):
                        sch.dma("sp", wm[:, i * 4096:(i + 1) * 4096], scr_d[SLOT_MK + i], R=[scrB], W=[wmB[i]])
                    mnT3 = mnT[:, :].rearrange("p (k t) -> p k t", k=8)
                    for mc in range(2):
                        sch.dma("sp", memt[mc][:], mem_d[s, mc * 128:(mc + 1) * 128, :], W=[memtB[mc]])
                    for mc in range(2):
                        nt(memt[mc][:, :], [memtB[mc]], 1024, pv[:, GMKV:GMKV + 8],
                           mnT3[:, :, mc * 128:(mc + 1) * 128], [mnB[mc]], xs[mc], xsB[mc], junk, junkB)
                    for sl in range(2):
                        W_ = wm[:, sl * 4096:(sl + 1) * 4096]
                        for cc in range(4):
                            c = sl * 4 + cc
                            bank, bB = nbank()
                            for kc in range(8):
                                MM(bank[:, 0:256], W_[:, cc * 1024 + kc * 128: cc * 1024 + (kc + 1) * 128],
                                   mnT[:, kc * 256:(kc + 1) * 256], kc == 0, kc == 7, R=[wmB[sl]] + mnB, W=[bB])
                            COPY(mkT[:, c * 256:(c + 1) * 256], bank[:, 0:256], R=[bB], W=[mkB])
                    for fh in range(2):
                        W_ = wm[:, (2 + fh) * 4096:(3 + fh) * 4096]
                        for mc in range(2):
                            bank, bB = nbank()
                            for kc in range(8):
                                MM(bank, mnT[:, kc * 256 + mc * 128: kc * 256 + (mc + 1) * 128],
                                   W_[:, kc * 512:(kc + 1) * 512], kc == 0, kc == 7, R=[wmB[2 + fh], mnB[mc]], W=[bB])
                            COPY(mv[:, mc * 1024 + fh * 512: mc * 1024 + (fh + 1) * 512], bank, R=[bB], W=[mvB])
                    sch.barrier(bar[:, 0:1])

                NRG, NRF = 3, 2
                ringG = T(p2, "ringG" + sfx, NRG * 4096, BF16)
                ringF = T(p2, "ringF" + sfx, NRF * 4096, BF16)
                Ht = [T(p2, f"Ht{i}" + sfx, 4 * 1024, F32) for i in range(2)]
                HtB = [[MB(f"Ht@{s}_{i}_{tb}") for tb in range(4)] for i in range(2)]
                xnF = T(p2, "xnF" + sfx, 8 * 512, BF16)
                xnFB = [MB(f"xnF@{s}_{i}") for i in range(4)]
                xnF3 = xnF[:, :].rearrange("p (k t) -> p k t", k=8)
                xnG = T(p2, "xnG" + sfx, 8 * 512, BF16)
                xnGB = [MB(f"xnG@{s}_{i}") for i in range(4)]
                xnG3 = xnG[:, :].rearrange("p (k t) -> p k t", k=8)
                PLh = T(p2, "PLh" + sfx, NJ * 512, BF16)
                PLhB = [MB(f"PLh@{s}_{i}") for i in range(NJ)]
                PLq = T(p2, "PLq" + sfx, 8 * 512, BF16)
                PLqB = [MB(f"PLq@{s}_{i}") for i in range(8)]
                Pb = [T(p2, f"Pm{i}" + sfx, 512, BF16) for i in range(2)]
                PbB = [MB(f"Pm@{s}_{i}") for i in range(2)]
                rden = [T(p2, f"rdm{i}" + sfx, 512, F32) for i in range(1)]
                rdenB = [MB(f"rdm@{s}_{i}") for i in range(1)]
                Fsb = T(p2, "Fsb" + sfx, 4 * 1024, F32)
                FsbB = [MB(f"Fsb@{s}_{i}") for i in range(4)]
                sg = [T(p2, f"sg{i}" + sfx, 512, F32) for i in range(2)]
                sgB = [MB(f"sg@{s}_{i}") for i in range(2)]
                tmpo = T(p2, "tmpo" + sfx, 1024, F32)
                tmpoB = MB(f"tmpo@{s}")
                xs = [T(p2, f"xs2{i}" + sfx, 1024, BF16) for i in range(2)]
                xsB = [MB(f"xs2@{s}_{i}") for i in range(2)]
                junk = T(p2, "junk2" + sfx, 1024, BF16)
                junkB = MB(f"junk2@{s}")

                class Ring:
                    def __init__(self, tensor, nslots, uses, name):
                        self.t, self.n, self.uses = tensor, nslots, uses
                        self.B = [MB(f"{name}@{s}_{i}") for i in range(nslots)]
                        self.k = self.issued = self.released = 0

                    def pump(self):
                        while self.issued < len(self.uses) and self.issued < self.released + self.n:
                            i = self.issued
                            r = i % self.n
                            sch.dma("sp", self.t[:, r * 4096:(r + 1) * 4096], scr_d[self.uses[i]], R=[scrB], W=[self.B[r]])
                            self.issued += 1

                    def acquire(self, expect):
                        k = self.k
                        self.k += 1
                        assert k < self.issued, "ring: acquire before load issued"
                        assert self.uses[k] == expect, (self.uses[k], expect)
                        r = k % self.n
                        return self.t[:, r * 4096:(r + 1) * 4096], self.B[r]

                    def release(self):
                        self.released += 1
                        self.pump()

                rG = Ring(ringG, NRG, (list(range(6, 17)) + list(range(17, 23))) * NQT, "rG")
                rF = Ring(ringF, NRF, [0, 1, 2, 3, 4, 5] * NQT, "rF")
                rG.pump()
                rF.pump()
                FB = [(PS[:, 4 * 512:5 * 512], bankB[4]), (PS[:, 5 * 512:6 * 512], bankB[5])]
                FPAIR = (PS[:, 4 * 512:6 * 512], [bankB[4], bankB[5]])
                fbs = {"i": 0}

                def fbank():
                    fbs["i"] += 1
                    return FB[fbs["i"] % 2]

                def postres(src, srcB, gi, HtAP, HtBuf):
                    ss, ssB = small()
                    ACT(junk[:, :], src, AF.Square, R=srcB, W=[junkB, ssB], accum_out=ss[:, 0:1])
                    rstd_from_ss(ss, ssB, 1, [1.0 / 1024])
                    TT(tmpo[:, :], src, G[gi][:, :], ALU.mult, R=list(srcB) + [GB[gi], ssB], W=[tmpoB])
                    STT(HtAP, tmpo[:, :], ss[:, 0:1], HtAP, ALU.mult, ALU.add, R=[tmpoB, ssB, HtBuf], W=[HtBuf])

                def front_items(t):
                    p = t % 2
                    H = Ht[p]
                    HB = HtB[p]
                    stt = {}

                    def xload():
                        for tb in range(4):
                            sch.dma("pool", H[:, tb * 1024:(tb + 1) * 1024],
                                    x_d[s, t * 512 + tb * 128: t * 512 + (tb + 1) * 128, :], W=[HB[tb]])

                    def proj(tb, which):
                        if tb == 0:
                            stt["W"] = [rF.acquire(0 + 4 * which), rF.acquire(1 + 4 * which)]
                        pair, pB = FPAIR
                        for fh in range(2):
                            W_, WB = stt["W"][fh]
                            for kc in range(8):
                                if which == 0:
                                    lhs = yT[:, kc * S + t * 512 + tb * 128: kc * S + t * 512 + (tb + 1) * 128]
                                    rr = [yTB[t], WB]
                                else:
                                    lhs = PLq[:, kc * 512 + tb * 128: kc * 512 + (tb + 1) * 128]
                                    rr = [PLqB[kc], WB]
                                MM(pair[:, fh * 512:(fh + 1) * 512], lhs, W_[:, kc * 512:(kc + 1) * 512],
                                   kc == 0, kc == 7, R=rr, W=[pB[fh]])
                        if tb == 3:
                            rF.release()
                            rF.release()
                        hap = H[:, tb * 1024:(tb + 1) * 1024]
                        postres(pair, pB, which, hap, HB[tb])
                        ss, ssB = nt_a(hap, [HB[tb]], 1024, junk, junkB)
                        ACT(xs[tb % 2][:, :], hap, AF.Copy, R=[HB[tb], ssB], W=[xsB[tb % 2]], scale=ss[:, 0:1])

                    def trn(tb, which):
                        x_, xB_ = xs[tb % 2], xsB[tb % 2]
                        tp, tB = ntp()
                        for k in range(8):
                            TR(tp[:, k * 128:(k + 1) * 128], x_[:, k * 128:(k + 1) * 128], R=[xB_], W=[tB], last=(k == 7))
                        gc = (pv[:, GPMEM:GPMEM + 8] if which == 0 else pv[:, GPF:GPF + 8])
                        g3 = gc.unsqueeze(2).broadcast_to([128, 8, 128])
                        dst3 = (xnF3 if which == 0 else xnG3)[:, :, tb * 128:(tb + 1) * 128]
                        dB = (xnFB if which == 0 else xnGB)[tb]
                        TT(dst3, tp[:, 0:1024].rearrange("p (k t) -> p k t", k=8), g3, ALU.mult, R=[tB, pvB], W=[dB])

                    def mq(c):
                        if c == 0:
                            stt["Q"] = [rF.acquire(2), rF.acquire(3)]
                        W_, WB = stt["Q"][c // 4]
                        cc = c % 4
                        bank, bB = fbank()
                        for kc in range(8):
                            MM(bank, W_[:, cc * 1024 + kc * 128: cc * 1024 + (kc + 1) * 128],
                               xnF[:, kc * 512:(kc + 1) * 512], kc == 0, kc == 7, R=[WB] + xnFB, W=[bB])
                        COPY(PLq[:, c * 512:(c + 1) * 512], bank, R=[bB], W=[PLqB[c]])
                        if c == 7:
                            rF.release()
                            rF.release()

                    def attS(h):
                        Ps = []
                        for mc in range(2):
                            bank, bB = fbank()
                            for dc in range(2):
                                c = h * 2 + dc
                                MM(bank, mkT[:, c * 256 + mc * 128: c * 256 + (mc + 1) * 128],
                                   PLq[:, c * 512:(c + 1) * 512], dc == 0, dc == 1, R=[mkB, PLqB[c]], W=[bB])
                            P, PB = Pb[mc], PbB[mc]
                            ACT(P[:, :], bank, AF.Exp, R=[bB], W=[PB], scale=mem_scale)
                            Ps.append((P, PB))
                        stt["Ps"] = Ps

                    def attPV(h):
                        Ps = stt["Ps"]
                        dn, dnB = fbank()
                        for mc in range(2):
                            MM(dn, onesb[:, :], Ps[mc][0][:, :], mc == 0, mc == 1, R=[constB, Ps[mc][1]], W=[dnB])
                        rd, rdB = rden[0], rdenB[0]
                        RECIP(rd[:, :], dn, R=[dnB], W=[rdB])
                        for dvc in range(2):
                            c = h * 2 + dvc
                            bank, bB = fbank()
                            for mc in range(2):
                                MM(bank, mv[:, mc * 1024 + c * 128: mc * 1024 + (c + 1) * 128], Ps[mc][0][:, :],
                                   mc == 0, mc == 1, R=[mvB, Ps[mc][1]], W=[bB])
                            TT(PLq[:, c * 512:(c + 1) * 512], bank, rd[:, :], ALU.mult, R=[bB, rdB], W=[PLqB[c]])

                    A = [xload]
                    A += [lambda: proj(0, 0), lambda: proj(1, 0), lambda: trn(0, 0), lambda: proj(2, 0),
                          lambda: trn(1, 0), lambda: proj(3, 0), lambda: trn(2, 0), lambda: trn(3, 0)]
                    A += [(lambda c=c: mq(c)) for c in range(8)]
                    A += [lambda: attS(0)]
                    for h in range(4):
                        A += [(lambda h=h: attPV(h))]
                        if h + 1 < 4:
                            A += [(lambda h=h: attS(h + 1))]
                    Bq = [lambda: proj(0, 1), lambda: proj(1, 1), lambda: trn(0, 1), lambda: proj(2, 1),
                          lambda: trn(1, 1), lambda: proj(3, 1), lambda: trn(2, 1), lambda: trn(3, 1)]
                    return A, Bq

                def ffn_items(t):
                    p = t % 2
                    H = Ht[p]
                    HB = HtB[p]

                    def gu(jj):
                        W_, WB = rG.acquire(6 + jj)
                        for jl in range(2):
                            j = jj * 2 + jl
                            b0 = 0 if jl == 0 else 2
                            bg, bgB = PS[:, b0 * 512:(b0 + 1) * 512], bankB[b0]
                            bu, buB = PS[:, (b0 + 1) * 512:(b0 + 2) * 512], bankB[b0 + 1]
                            for kc in range(8):
                                MM(bg, W_[:, (jl * 2 + 0) * 1024 + kc * 128: (jl * 2 + 0) * 1024 + (kc + 1) * 128],
                                   xnG[:, kc * 512:(kc + 1) * 512], kc == 0, kc == 7, R=[WB] + xnGB, W=[bgB])
                            for kc in range(8):
                                MM(bu, W_[:, (jl * 2 + 1) * 1024 + kc * 128: (jl * 2 + 1) * 1024 + (kc + 1) * 128],
                                   xnG[:, kc * 512:(kc + 1) * 512], kc == 0, kc == 7, R=[WB] + xnGB, W=[buB])
                            g_, gB_ = sg[j % 2], sgB[j % 2]
                            ACT(g_[:, :], bg, AF.Silu, R=[bgB], W=[gB_])
                            TT(PLh[:, j * 512:(j + 1) * 512], bu, g_[:, :], ALU.mult, R=[buB, gB_], W=[PLhB[j]])
                        rG.release()

                    def down(fh, sl3):
                        W_, WB = rG.acquire(17 + fh * 3 + sl3)
                        nj = 8 if sl3 < 2 else 6
                        for jl in range(nj):
                            j = sl3 * 8 + jl
                            for tb in range(4):
                                last = (jl == nj - 1 and tb == 3)
                                MM(PS[:, tb * 512:(tb + 1) * 512], PLh[:, j * 512 + tb * 128: j * 512 + (tb + 1) * 128],
                                   W_[:, jl * 512:(jl + 1) * 512], j == 0, j == NJ - 1, R=[PLhB[j], WB], W=[bankB[tb]],
                                   inc=(True if last else None))
                        rG.release()

                    def evac(fh):
                        for tb in range(4):
                            COPY(Fsb[:, tb * 1024 + fh * 512: tb * 1024 + (fh + 1) * 512], PS[:, tb * 512:(tb + 1) * 512],
                                 R=[bankB[tb]], W=[FsbB[tb]])

                    def fin():
                        for tb in range(4):
                            hap = H[:, tb * 1024:(tb + 1) * 1024]
                            postres(Fsb[:, tb * 1024:(tb + 1) * 1024], [FsbB[tb]], 2, hap, HB[tb])
                            sch.dma("pool", out_d[s, t * 512 + tb * 128: t * 512 + (tb + 1) * 128, :], hap, R=[HB[tb]])

                    GU = [(lambda jj=jj: gu(jj)) for jj in range(11)]
                    DN = []
                    for fh in range(2):
                        for sl3 in range(3):
                            DN.append(lambda fh=fh, sl3=sl3: down(fh, sl3))
                        DN.append(lambda fh=fh: evac(fh))
                    DN.append(fin)
                    return GU, DN

                def interleave(Xs, Ys):
                    nx, ny = len(Xs), len(Ys)
                    yi = 0
                    for i, xf in enumerate(Xs):
                        xf()
                        tgt = ((i + 1) * ny + nx - 1) // nx if nx else ny
                        while yi < min(tgt, ny):
                            Ys[yi]()
                            yi += 1
                    while yi < ny:
                        Ys[yi]()
                        yi += 1

                A0, B0 = front_items(0)
                for f in A0 + B0:
                    f()
                for tt in range(NQT):
                    GU, DN = ffn_items(tt)
                    if tt + 1 < NQT:
                        A1, B1 = front_items(tt + 1)
                    else:
                        A1, B1 = [], []
                    interleave(GU, A1)
                    interleave(DN, B1)
                sch.barrier(bar[:, 0:1])
        sch.final_wait()

    with nc.Block() as block:
        @block.sync
        def _(sync):
            body()
    es.close()
    return nc, sch


def _kc(W):
    K, N = W.shape
    return np.ascontiguousarray(W.reshape(K // 128, 128, N).transpose(1, 0, 2)).reshape(128, -1)


def _cols(v):
    return np.ascontiguousarray(v.reshape(-1, 128).T)


def pack_shared(inp):
    f = np.float32
    w_in = inp["w_in"][0]
    winA = _kc(w_in[:, 0:1024])
    winB = _kc(np.concatenate([w_in[:, 1024:1408], w_in[:, 1664:1728], w_in[:, 1664:1728], w_in[:, 1408:1664]], axis=1))
    w_uq = inp["w_uq"][0].reshape(384, 4, 192)
    wuq = _kc(np.concatenate([w_uq[:, :, 0:128].reshape(384, 512), w_uq[:, :, 128:192].reshape(384, 256)], axis=1))
    w_ukv = inp["w_ukv"][0].reshape(256, 4, 256)
    wukv = _kc(np.concatenate([w_ukv[:, :, 0:128].reshape(256, 512), w_ukv[:, :, 128:256].reshape(256, 512)], axis=1))
    wlru = np.zeros((128, 4, 2, 128), f)
    for c in range(4):
        for g, key in enumerate(("lru_wa", "lru_wx")):
            for hh in range(2):
                wlru[hh * 64:(hh + 1) * 64, c, g, hh * 64:(hh + 1) * 64] = inp[key][0][2 * c + hh]
    wlru = wlru.reshape(128, -1)
    slots = np.zeros((NSLOT, 128, 4096), f)

    def moving(W, fh):
        return _kc(W[:, fh * 512:(fh + 1) * 512])

    def stationary(W, sl):
        Wr = W.reshape(8, 128, 8, 128)[:, :, sl * 4:(sl + 1) * 4, :]
        return np.ascontiguousarray(Wr.transpose(1, 2, 0, 3)).reshape(128, -1)

    for fh in range(2):
        slots[0 + fh] = moving(inp["w_out"][0], fh)
        slots[2 + fh] = stationary(inp["w_mq"][0], fh)
        slots[4 + fh] = moving(inp["w_mo"][0], fh)
        slots[SLOT_MK + fh] = stationary(inp["w_mk"][0], fh)
        slots[SLOT_MV + fh] = moving(inp["w_mv"][0], fh)
    wg = inp["w_gate"][0].reshape(8, 128, NJ, 128)
    wu = inp["w_up"][0].reshape(8, 128, NJ, 128)
    for jj in range(11):
        blk = np.zeros((128, 2, 2, 8, 128), f)
        for jl in range(2):
            j = jj * 2 + jl
            blk[:, jl, 0] = wg[:, :, j, :].transpose(1, 0, 2)
            blk[:, jl, 1] = wu[:, :, j, :].transpose(1, 0, 2)
        slots[6 + jj] = blk.reshape(128, -1)
    wd = inp["w_down"][0].reshape(NJ, 128, 1024)
    for fh in range(2):
        for sl3 in range(3):
            nj = 8 if sl3 < 2 else 6
            blk = np.zeros((128, 8, 512), f)
            blk[:, 0:nj, :] = wd[sl3 * 8: sl3 * 8 + nj, :, fh * 512:(fh + 1) * 512].transpose(1, 0, 2)
            slots[17 + fh * 3 + sl3] = blk.reshape(128, -1)
    pv = np.zeros((128, NPV), f)
    pv[:, GPM:GPM + 8] = _cols(inp["g_pre_mix"][0])
    for k in range(4):
        pv[:, CW + k * 4: CW + k * 4 + 4] = _cols(inp["conv_w"][0][k])
    pv[:, CB:CB + 4] = _cols(inp["conv_b"][0])
    pv[:, BA:BA + 4] = _cols(inp["lru_ba"][0])
    pv[:, BX:BX + 4] = _cols(inp["lru_bx"][0])
    pv[:, LAM:LAM + 4] = _cols(inp["lru_lambda"][0])
    pv[:, GQ:GQ + 3] = _cols(inp["g_q_lat"][0])
    pv[:, GKV:GKV + 2] = _cols(inp["g_kv_lat"][0])
    pv[:, GLO:GLO + 4] = _cols(inp["g_lru_out"][0])
    pv[:, GMO:GMO + 4] = _cols(inp["g_mla_out"][0])
    pv[:, GPMEM:GPMEM + 8] = _cols(inp["g_pre_mem"][0])
    pv[:, GMKV:GMKV + 8] = _cols(inp["g_mem_kv"][0])
    pv[:, GPF:GPF + 8] = _cols(inp["g_pre_ffn"][0])
    grow = np.stack([np.broadcast_to(inp[k][0][None, :], (128, 1024)) for k in
                     ("g_post_mix", "g_post_mem", "g_post_ffn")]).astype(f)
    invf = (10000.0 ** (-np.arange(0, 64, 2, dtype=np.float32) / 64)).astype(f)
    invf = np.ascontiguousarray(np.broadcast_to(invf[None, :], (128, 32)))
    ident = np.eye(128, dtype=f)
    tri = (np.arange(128)[None, :] >= np.arange(128)[:, None]).astype(f)
    return dict(pv=pv, grow=np.ascontiguousarray(grow), invf=invf, ident=ident, tri=tri, winA=winA, winB=winB,
                wuq=wuq, wukv=wukv, wlru=wlru, wslots=slots)


def pack_core(inp, b0, nseq, S):
    x = np.ascontiguousarray(inp["x"][b0:b0 + nseq], dtype=np.float32)
    mem = np.ascontiguousarray(inp["mem"][b0:b0 + nseq], dtype=np.float32)
    pos = np.asarray(inp["positions"][b0:b0 + nseq], dtype=np.int32)
    pos = np.ascontiguousarray(pos.reshape(nseq, S // 128, 128).transpose(2, 0, 1).reshape(128, -1))
    return dict(x=x, mem=mem, pos=pos)


_CACHE = {}


def kernel(**inputs):
    inp = {k: np.asarray(v) for k, v in inputs.items()}
    B, S, _ = inp["x"].shape
    nseq = B // NCORES
    key = (S, nseq)
    if key not in _CACHE:
        _CACHE[key] = build(S, nseq)[0]
    nc = _CACHE[key]
    shared = pack_shared(inp)
    in_maps = []
    for c in range(NCORES):
        m = dict(shared)
        m.update(pack_core(inp, c * nseq, nseq, S))
        in_maps.append(m)
    res = run_bass_kernel_spmd(nc, in_maps, core_ids=list(range(NCORES)))
    out = np.concatenate([np.asarray(r["out"]) for r in res.results], axis=0)
    return out.astype(np.float32)
```

```python
import math
from contextlib import ExitStack

import numpy as np
import concourse.bass as bass
import concourse.mybir as mybir
from concourse.bass_utils import run_bass_kernel_spmd

F32 = mybir.dt.float32
BF16 = mybir.dt.bfloat16
I32 = mybir.dt.int32
AF = mybir.ActivationFunctionType
ALU = mybir.AluOpType

NCORES = 8
D = 1024
DFF = 2816
NJ = DFF // 128
NMEM = 256
EPS = 1e-6
NSLOT = 27
SLOT_MK, SLOT_MV = 23, 25
GPM, CW, CB, BA, BX, LAM, GQ, GKV, GLO, GMO, GPMEM, GMKV, GPF = 0, 8, 24, 28, 32, 36, 40, 43, 45, 49, 53, 61, 69
NPV = 77
HC, H2, BAH, BXH = 77, 81, 85, 89
NPVT = 96
SEM_LIMIT = 12000
NRING = 5
import os
F_PIPEA = os.environ.get('K_PIPEA', '1') == '1'
F_LATSKEW = os.environ.get('K_LATSKEW', '1') == '1'
F_P2EARLY = os.environ.get('K_P2EARLY', '0') == '1'
F_WPRE = os.environ.get('K_WPRE', '0') == '1'
LAT_DEPTH = int(os.environ.get('K_LATDEPTH', '2'))
F_QKVSKEW = os.environ.get('K_QKVSKEW', '1') == '1'


class Buf:
    __slots__ = ("name", "w", "r", "ld", "ldc", "ldk", "st", "stc", "stk")

    def __init__(self, name):
        self.name = name
        self.w = {}
        self.r = {}
        self.ld = None
        self.ldc = 0
        self.ldk = None
        self.st = None
        self.stc = 0
        self.stk = None


class _Eng:
    def __init__(self, sch, name, h):
        self.sch = sch
        self.name = name
        self.h = h
        self.sem = None
        self.key = None
        self.cnt = 0
        self.nsem = 0
        self.known = {}
        self.pending = False

    def newsem(self):
        self.sem = self.sch.alloc_sem(f"e_{self.name}{self.nsem}")
        self.key = (self.name, self.nsem)
        self.nsem += 1
        self.cnt = 0


class Sched:
    def __init__(self, nc, es):
        self.nc = nc
        self.es = es
        self.nsems = 0
        self.E = {
            "pe": _Eng(self, "pe", nc.tensor),
            "act": _Eng(self, "act", nc.scalar),
            "dve": _Eng(self, "dve", nc.vector),
            "pool": _Eng(self, "pool", nc.gpsimd),
            "sp": _Eng(self, "sp", nc.sync),
        }
        for e in self.E.values():
            e.newsem()
        self.dma_tokens = {}
        self.nwaits = 0
        self.ninst = 0

    def alloc_sem(self, name):
        self.nsems += 1
        return self.es.enter_context(self.nc.semaphore(name))

    @staticmethod
    def _merge(d, src):
        for k, (s, v) in src.items():
            if k not in d or d[k][1] < v:
                d[k] = (s, v)

    def _wait(self, E, key, sem, val):
        if E.known.get(key, 0) >= val:
            return
        E.h.wait_ge(sem, val)
        E.known[key] = val
        self.nwaits += 1

    def _deps(self, E, R, W, skip_keys=()):
        deps = {}
        for b in R:
            self._merge(deps, b.w)
        for b in W:
            self._merge(deps, b.w)
            self._merge(deps, b.r)
        for key, (sem, val) in deps.items():
            if key in skip_keys:
                continue
            if E.name == "pe" and key[0] == "pe":
                continue
            self._wait(E, key, sem, val)

    def op(self, en, fn, R=(), W=(), inc=True):
        E = self.E[en]
        self._deps(E, R, W)
        ins = fn(E.h)
        self.ninst += 1
        if inc:
            if E.cnt >= SEM_LIMIT and not E.pending:
                E.newsem()
            E.cnt += 1
            ins.then_inc(E.sem, 1)
            E.pending = False
            tok = (E.key, E.sem, E.cnt)
        else:
            E.pending = True
            tok = (E.key, E.sem, E.cnt + 1)
        for b in R:
            k = tok[0]
            if k not in b.r or b.r[k][1] < tok[2]:
                b.r[k] = (tok[1], tok[2])
        for b in W:
            b.w = {tok[0]: (tok[1], tok[2])}
            b.r = {}
        return ins

    def dma(self, q, out, in_, R=(), W=(), **kw):
        E = self.E[q]
        skip = ()
        if W and W[0].ldk is not None:
            skip = (W[0].ldk,)
        self._deps(E, R, W, skip_keys=skip)
        ins = E.h.dma_start(out=out, in_=in_, **kw)
        self.ninst += 1
        if W:
            b = W[0]
            if b.ld is None:
                b.ld = self.alloc_sem("ld_" + b.name)
                b.ldk = ("ld", b.name)
            b.ldc += 16
            ins.then_inc(b.ld, 16)
            tok = (b.ldk, b.ld, b.ldc)
        else:
            b = R[0]
            if b.st is None:
                b.st = self.alloc_sem("st_" + b.name)
                b.stk = ("st", b.name)
            b.stc += 16
            ins.then_inc(b.st, 16)
            tok = (b.stk, b.st, b.stc)
        self.dma_tokens[tok[0]] = (tok[1], tok[2])
        for bb in R:
            k = tok[0]
            if k not in bb.r or bb.r[k][1] < tok[2]:
                bb.r[k] = (tok[1], tok[2])
        for bb in W:
            if bb is W[0] and skip:
                bb.w[tok[0]] = (tok[1], tok[2])
                bb.r = {}
            else:
                bb.w = {tok[0]: (tok[1], tok[2])}
                bb.r = {}
        return ins

    def barrier(self, bar_ap):
        Dv = self.E["dve"]
        for E in self.E.values():
            if E is Dv:
                continue
            assert not E.pending
            if E.cnt > 0:
                self._wait(Dv, E.key, E.sem, E.cnt)
        for k, (s, v) in self.dma_tokens.items():
            self._wait(Dv, k, s, v)
        if Dv.cnt > 0:
            self._wait(Dv, Dv.key, Dv.sem, Dv.cnt)
        ins = Dv.h.memset(bar_ap, 0.0)
        if Dv.cnt >= SEM_LIMIT:
            Dv.newsem()
        Dv.cnt += 1
        ins.then_inc(Dv.sem, 1)
        for E in self.E.values():
            if E is Dv:
                continue
            self._wait(E, Dv.key, Dv.sem, Dv.cnt)

    def final_wait(self):
        sp = self.E["sp"]
        for k, (s, v) in self.dma_tokens.items():
            self._wait(sp, k, s, v)
        for E in self.E.values():
            if E is sp or E.cnt == 0:
                continue
            self._wait(sp, E.key, E.sem, E.cnt)


def build(S=2048, NSEQ=2, dbg=False):
    NTB = S // 128
    NQT = S // 512
    nc = bass.Bass("TRN2", target_bir_lowering=False)

    def din(name, shape, dt=F32):
        return nc.dram_tensor(name, list(shape), dt, kind="ExternalInput").ap()

    x_d = din("x", [NSEQ, S, D])
    mem_d = din("mem", [NSEQ, NMEM, D])
    pos_d = din("pos", [128, NSEQ * NTB], I32)
    pv_d = din("pv", [128, NPV])
    grow_d = din("grow", [3, 128, D])
    invf_d = din("invf", [128, 32])
    ident_d = din("ident", [128, 128])
    tri_d = din("tri", [128, 128])
    winA_d = din("winA", [128, 8 * 1024])
    winB_d = din("winB", [128, 8 * 768])
    wuq_d = din("wuq", [128, 3 * 768])
    wukv_d = din("wukv", [128, 2 * 1024])
    wlru_d = din("wlru", [128, 4 * 2 * 128])
    wsl_d = din("wslots", [NSLOT, 128, 4096])
    scr_d = nc.dram_tensor("scr", [NSLOT, 128, 4096], BF16, kind="Internal").ap()
    out_d = nc.dram_tensor("out", [NSEQ, S, D], F32, kind="ExternalOutput").ap()
    if dbg:
        dbg_yT = nc.dram_tensor("dbg_yT", [NSEQ, 128, 8 * S], BF16, kind="ExternalOutput").ap()

    es = ExitStack()
    sch = Sched(nc, es)

    def T(scope, name, cols, dt):
        return scope.enter_context(nc.sbuf_tensor("sb_" + name, [128, cols], dt))

    mla_scale = 1.0 / math.sqrt(192.0)
    mem_scale = 1.0 / math.sqrt(256.0)

    def body():
        pers = es
        ident = T(pers, "ident", 128, BF16)
        tri = T(pers, "tri", 128, BF16)
        onesb = T(pers, "onesb", 128, BF16)
        onesf = T(pers, "onesf", 128, F32)
        pv = T(pers, "pv", NPVT, F32)
        cst = T(pers, "cst", 8, F32)
        invf = T(pers, "invf", 32, F32)
        posi = T(pers, "posi", NSEQ * NTB, I32)
        smallt = T(pers, "smallt", 32 * 4, F32)
        bar = T(pers, "bar", 4, F32)
        yT = T(pers, "yT", 8 * S, BF16)
        PS = pers.enter_context(nc.psum_tensor("ps", [128, 6 * 512], F32))
        TPS = pers.enter_context(nc.psum_tensor("tps", [128, 2 * 1024], BF16))

        constB = Buf("const")
        identB = Buf("identb")
        triB = Buf("trib")
        pvB = Buf("pvb")
        invfB = Buf("invfb")
        posB = Buf("posb")
        bankB = [Buf(f"bank{i}") for i in range(6)]
        tpB = [Buf(f"tpb{i}") for i in range(2)]
        smallB = [Buf(f"small{i}") for i in range(32)]
        yTB = [Buf(f"yT{i}") for i in range(NQT)]
        scrB = Buf("scr")
        _bufcache = {}

        def MB(name):
            import re as _re
            key = _re.sub(r"@\d", "@", name)
            if key not in _bufcache:
                _bufcache[key] = Buf(key.replace("@", "_"))
            return _bufcache[key]

        st = {"small": 0, "bank": 0, "pair": 0, "tp": 0, "alt": 0}

        def small():
            i = st["small"] % 32
            st["small"] += 1
            return smallt[:, i * 4:(i + 1) * 4], smallB[i]

        def nbank():
            i = st["bank"] % 6
            st["bank"] += 1
            return PS[:, i * 512:(i + 1) * 512], bankB[i]

        def npair():
            i = st["pair"] % 3
            st["pair"] += 1
            return PS[:, i * 1024:(i + 1) * 1024], [bankB[2 * i], bankB[2 * i + 1]]

        def ntp():
            i = st["tp"] % 2
            st["tp"] += 1
            return TPS[:, i * 1024:(i + 1) * 1024], tpB[i]

        def MM(out, lhsT, rhs, start, stop, R, W, inc=None):
            return sch.op("pe", lambda e: e.matmul(out, lhsT=lhsT, rhs=rhs, start=start, stop=stop),
                          R=R, W=W, inc=(stop if inc is None else inc))

        def TR(out, in_, R, W, last):
            return sch.op("pe", lambda e: e.transpose(out, in_, ident[:]), R=list(R) + [identB], W=W, inc=last)

        def ACT(out, in_, func, R, W, **kw):
            return sch.op("act", lambda e: e.activation(out=out, in_=in_, func=func, **kw), R=R, W=W)

        def TS(out, in0, s1, s2, op0, op1, R, W, eng="dve"):
            if s2 is None:
                return sch.op(eng, lambda e: e.tensor_scalar(out=out, in0=in0, scalar1=s1, scalar2=None, op0=op0),
                              R=R, W=W)
            return sch.op(eng, lambda e: e.tensor_scalar(out=out, in0=in0, scalar1=s1, scalar2=s2, op0=op0, op1=op1),
                          R=R, W=W)

        def TT(out, in0, in1, op, R, W, eng="dve"):
            return sch.op(eng, lambda e: e.tensor_tensor(out=out, in0=in0, in1=in1, op=op), R=R, W=W)

        def STT(out, in0, scalar, in1, op0, op1, R, W):
            return sch.op("dve", lambda e: e.scalar_tensor_tensor(out=out, in0=in0, scalar=scalar, in1=in1,
                                                                  op0=op0, op1=op1), R=R, W=W)

        def COPY(out, in_, R, W, eng=None):
            if eng is None:
                eng = "act" if st["alt"] % 2 == 0 else "dve"
                st["alt"] += 1
            if eng == "act":
                return sch.op("act", lambda e: e.activation(out=out, in_=in_, func=AF.Copy), R=R, W=W)
            return sch.op(eng, lambda e: e.tensor_copy(out=out, in_=in_), R=R, W=W)

        def RECIP(out, in_, R, W):
            return sch.op("dve", lambda e: e.reciprocal(out=out, in_=in_), R=R, W=W)

        def rstd_from_ss(ss, ssB, n, invd):
            for i in range(n):
                TS(ss[:, i:i + 1], ss[:, i:i + 1], invd[i], EPS, ALU.mult, ALU.add, R=[ssB], W=[ssB])
            ACT(ss[:, 0:n], ss[:, 0:n], AF.Sqrt, R=[ssB], W=[ssB])
            RECIP(ss[:, 0:n], ss[:, 0:n], R=[ssB], W=[ssB])

        sch.dma("sp", pv[:, 0:NPV], pv_d[:, :], W=[pvB])
        sch.dma("sp", invf[:], invf_d[:, :], W=[invfB])
        sch.dma("sp", posi[:], pos_d[:, :], W=[posB])
        sch.dma("pool", ident[:], ident_d[:, :], W=[identB])
        sch.dma("pool", tri[:], tri_d[:, :], W=[triB])
        sch.op("dve", lambda e: e.memset(onesb[:], 1.0), W=[constB])
        sch.op("dve", lambda e: e.memset(onesf[:], 1.0), W=[constB])
        sch.op("dve", lambda e: e.memset(cst[:, 0:1], 1.0), W=[constB])
        sch.op("dve", lambda e: e.memset(cst[:, 1:2], math.pi), W=[constB])
        tmpc, tmpB = small()
        tmpc2, tmpB2 = small()
        TS(tmpc2, pv[:, LAM:LAM + 4], -1.0, None, ALU.mult, None, R=[pvB], W=[tmpB2])
        TT(tmpc, pv[:, LAM:LAM + 4], tmpc2, ALU.max, R=[pvB, tmpB2], W=[tmpB])
        ACT(tmpc, tmpc, AF.Exp, R=[tmpB], W=[tmpB], scale=-1.0)
        TS(tmpc, tmpc, 1.0, None, ALU.add, None, R=[tmpB], W=[tmpB])
        ACT(tmpc, tmpc, AF.Ln, R=[tmpB], W=[tmpB])
        TS(tmpc2, tmpc2, 0.0, None, ALU.max, None, R=[tmpB2], W=[tmpB2])
        TT(tmpc, tmpc, tmpc2, ALU.add, R=[tmpB, tmpB2], W=[tmpB])
        TS(pv[:, HC:HC + 4], tmpc, -4.0, None, ALU.mult, None, R=[tmpB], W=[pvB])
        TS(pv[:, H2:H2 + 4], tmpc, -8.0, None, ALU.mult, None, R=[tmpB], W=[pvB])
        TS(pv[:, BAH:BAH + 4], pv[:, BA:BA + 4], 0.5, None, ALU.mult, None, R=[pvB], W=[pvB])
        TS(pv[:, BXH:BXH + 4], pv[:, BX:BX + 4], 0.5, None, ALU.mult, None, R=[pvB], W=[pvB])

        scr_state = {"done": False}

        def convert_scratch():
            if scr_state["done"]:
                return
            scr_state["done"] = True
            order = [23, 24, 25, 26] + list(range(23))
            for sl in order:
                sch.dma("pool", scr_d[sl].rearrange("p (a b) -> p a b", a=2),
                        wsl_d[sl].rearrange("p (a b) -> p a b", a=2), W=[scrB])

        def nt_a(src, srcB, Dn, junk, junkB):
            ss, ssB = small()
            ACT(junk[:, 0:Dn], src, AF.Square, R=srcB, W=[junkB, ssB], accum_out=ss[:, 0:1])
            rstd_from_ss(ss, ssB, 1, [1.0 / Dn])
            return ss, ssB

        def nt_b(src, srcB, Dn, ss, ssB, gcols, dst3, dstB, xs, xsB):
            nk = Dn // 128
            ACT(xs[:, 0:Dn], src, AF.Copy, R=list(srcB) + [ssB], W=[xsB], scale=ss[:, 0:1])
            tp, tB = ntp()
            for k in range(nk):
                TR(tp[:, k * 128:(k + 1) * 128], xs[:, k * 128:(k + 1) * 128], R=[xsB], W=[tB], last=(k == nk - 1))
            g3 = gcols.unsqueeze(2).broadcast_to([128, nk, 128])
            TT(dst3, tp[:, 0:Dn].rearrange("p (k t) -> p k t", k=nk), g3, ALU.mult, R=[tB, pvB], W=dstB)

        def nt(src, srcB, Dn, gcols, dst3, dstB, xs, xsB, junk, junkB):
            ss, ssB = nt_a(src, srcB, Dn, junk, junkB)
            nt_b(src, srcB, Dn, ss, ssB, gcols, dst3, dstB, xs, xsB)

        def rope(src3, srcB, G, cos, sin, csB, dst3, dstB, tA, tAB, tB_, tBB, extra=()):
            srcB = list(srcB) + list(extra)
            x1 = src3[:, :, 0:32]
            x2 = src3[:, :, 32:64]
            cb = cos.unsqueeze(1).broadcast_to([128, G, 32])
            sb = sin.unsqueeze(1).broadcast_to([128, G, 32])
            a3 = tA[:, 0:G * 32].rearrange("p (g d) -> p g d", g=G)
            b3 = tB_[:, 0:G * 32].rearrange("p (g d) -> p g d", g=G)
            TT(a3, x1, cb, ALU.mult, R=srcB + [csB], W=[tAB])
            TT(b3, x2, sb, ALU.mult, R=srcB + [csB], W=[tBB])
            TT(dst3[:, :, 0:32], a3, b3, ALU.subtract, R=[tAB, tBB], W=dstB)
            TT(a3, x2, cb, ALU.mult, R=srcB + [csB], W=[tAB])
            TT(b3, x1, sb, ALU.mult, R=srcB + [csB], W=[tBB])
            TT(dst3[:, :, 32:64], a3, b3, ALU.add, R=[tAB, tBB], W=dstB)

        def feat_norm(srcs, srcBs, gcol0, Dn, tt, dst_off, sq, sqB, rt, rtB):
            n = len(srcs)
            bank, bB = nbank()
            for i in range(n):
                q, qB = sq[i % 2], sqB[i % 2]
                ACT(q[:, :], srcs[i][:, tt * 512:(tt + 1) * 512], AF.Square, R=[srcBs[i]], W=[qB])
                MM(bank, onesf[:], q[:, :], i == 0, i == n - 1, R=[qB, constB], W=[bB], inc=True)
            TS(rt[:, :], bank, 1.0 / Dn, EPS, ALU.mult, ALU.add, R=[bB], W=[rtB])
            ACT(rt[:, :], rt[:, :], AF.Sqrt, R=[rtB], W=[rtB])
            RECIP(rt[:, :], rt[:, :], R=[rtB], W=[rtB])
            for i in range(n):
                c = dst_off + i
                STT(yT[:, c * S + tt * 512: c * S + (tt + 1) * 512], srcs[i][:, tt * 512:(tt + 1) * 512],
                    pv[:, gcol0 + i:gcol0 + i + 1], rt[:, :], ALU.mult, ALU.mult,
                    R=[srcBs[i], pvB, rtB], W=[yTB[tt]])

        for s in range(NSEQ):
            sfx = f"_{s}"
            with ExitStack() as s1:
                cqnT = T(s1, "cqnT" + sfx, 3 * S, BF16)
                ckvnT = T(s1, "ckvnT" + sfx, 2 * S, BF16)
                kpeT = T(s1, "kpeT" + sfx, S, BF16)
                cs = T(s1, "cs" + sfx, NTB * 64, F32)
                wuq = T(s1, "wuq" + sfx, 3 * 768, BF16)
                wukv = T(s1, "wukv" + sfx, 2 * 1024, BF16)
                wuqB = MB(f"wuq@{s}")
                wukvB = MB(f"wukv@{s}")
                latB = [MB(f"lat@{s}_{i}") for i in range(NTB)]
                csB = MB(f"cs@{s}")
                with ExitStack() as s2:
                    xnT = T(s2, "xnT" + sfx, 8 * S, BF16)
                    xnT3 = xnT[:, :].rearrange("p (k t) -> p k t", k=8)
                    xnTB = [MB(f"xnT@{s}_{i}") for i in range(NTB)]
                    winA = T(s2, "winA" + sfx, 8 * 1024, BF16)
                    wlru = T(s2, "wlru" + sfx, 4 * 2 * 128, BF16)
                    winAB = MB(f"winA@{s}")
                    wlruB = MB(f"wlru@{s}")
                    sch.dma("pool", winA[:, :].rearrange("p (k n) -> p k n", k=8),
                            winA_d[:, :].rearrange("p (k n) -> p k n", k=8), W=[winAB])
                    sch.dma("pool", wlru[:], wlru_d[:, :], W=[wlruB])
                    winB = T(s2, "winB" + sfx, 8 * 768, BF16)
                    winBB = MB(f"winB@{s}")
                    sch.dma("pool", winB[:, :].rearrange("p (k n) -> p k n", k=8),
                            winB_d[:, :].rearrange("p (k n) -> p k n", k=8), W=[winBB])
                    sch.dma("pool", wuq[:, :].rearrange("p (k n) -> p k n", k=3),
                            wuq_d[:, :].rearrange("p (k n) -> p k n", k=3), W=[wuqB])
                    sch.dma("pool", wukv[:, :].rearrange("p (k n) -> p k n", k=2),
                            wukv_d[:, :].rearrange("p (k n) -> p k n", k=2), W=[wukvB])
                    if F_WPRE:
                        convert_scratch()
                    with ExitStack() as s2a:
                        xtmp = [T(s2a, f"xtmp{i}" + sfx, 1024, F32) for i in range(3)]
                        xtmpB = [MB(f"xtmp@{s}_{i}") for i in range(3)]
                        xs = [T(s2a, f"xs{i}" + sfx, 1024, BF16) for i in range(2)]
                        xsB = [MB(f"xs@{s}_{i}") for i in range(2)]
                        junk = T(s2a, "junk" + sfx, 1024, BF16)
                        junkB = MB(f"junk@{s}")
                        ang = T(s2a, "ang" + sfx, NTB * 64, F32)
                        kf = T(s2a, "kf" + sfx, NTB * 64, F32)
                        ki = T(s2a, "ki" + sfx, NTB * 64, I32)
                        posf = T(s2a, "posf" + sfx, NTB, F32)
                        angB = MB(f"ang@{s}")
                        kfB = MB(f"kf@{s}")
                        kiB = MB(f"ki@{s}")
                        posfB = MB(f"posf@{s}")
                        pend = None
                        for tb in range(NTB + 1):
                            cur = None
                            if tb < NTB:
                                i = tb % 3
                                sch.dma("sp", xtmp[i][:], x_d[s, tb * 128:(tb + 1) * 128, :], W=[xtmpB[i]])
                                cur = (tb,) + nt_a(xtmp[i][:, :], [xtmpB[i]], 1024, junk, junkB)
                            if not F_PIPEA:
                                pend = cur
                                cur = None
                            if pend is not None:
                                ptb, pss, pssB = pend
                                pi = ptb % 3
                                nt_b(xtmp[pi][:, :], [xtmpB[pi]], 1024, pss, pssB, pv[:, GPM:GPM + 8],
                                     xnT3[:, :, ptb * 128:(ptb + 1) * 128], [xnTB[ptb]], xs[ptb % 2], xsB[ptb % 2])
                            pend = cur
                        COPY(posf[:], posi[:, s * NTB:(s + 1) * NTB], R=[posB], W=[posfB], eng="dve")
                        ang3 = ang[:, :].rearrange("p (t d) -> p t d", t=NTB)
                        TT(ang3[:, :, 0:32], posf[:, :].unsqueeze(2).broadcast_to([128, NTB, 32]),
                           invf[:, :].unsqueeze(1).broadcast_to([128, NTB, 32]), ALU.mult,
                           R=[posfB, invfB], W=[angB])
                        TS(ang3[:, :, 32:64], ang3[:, :, 0:32], math.pi / 2, None, ALU.add, None, R=[angB], W=[angB])
                        TS(kf[:], ang[:], 1.0 / (2 * math.pi), None, ALU.mult, None, R=[angB], W=[kfB])
                        COPY(ki[:], kf[:], R=[kfB], W=[kiB], eng="dve")
                        COPY(kf[:], ki[:], R=[kiB], W=[kfB], eng="dve")
                        C1 = 6.28125
                        C2 = 2 * math.pi - C1
                        STT(ang[:], kf[:], -C1, ang[:], ALU.mult, ALU.add, R=[kfB, angB], W=[angB])
                        STT(ang[:], kf[:], -C2, ang[:], ALU.mult, ALU.add, R=[kfB, angB], W=[angB])
                        TS(kf[:], ang[:], math.pi, None, ALU.is_gt, None, R=[angB], W=[kfB])
                        STT(ang[:], kf[:], -2 * math.pi, ang[:], ALU.mult, ALU.add, R=[kfB, angB], W=[angB])
                        TS(kf[:], ang[:], -math.pi, None, ALU.is_lt, None, R=[angB], W=[kfB])
                        STT(ang[:], kf[:], 2 * math.pi, ang[:], ALU.mult, ALU.add, R=[kfB, angB], W=[angB])
                        TS(ang[:], ang[:], math.pi, -math.pi, ALU.min, ALU.max, R=[angB], W=[angB])
                        ACT(cs[:], ang[:], AF.Sin, R=[angB], W=[csB])
                        sch.barrier(bar[:, 0:1])

                    with ExitStack() as s3:
                        convert_scratch()
                        NSEG = NQT
                        Rb = [T(s3, f"R{i}" + sfx, S, F32) for i in range(6)]
                        RB = [[MB(f"R@{s}_{i}_{g}") for g in range(NSEG)] for i in range(6)]
                        Ubf = T(s3, "Ubf" + sfx, S, BF16)
                        UbfB = [MB(f"Ubf@{s}_{g}") for g in range(NSEG)]
                        YL = [T(s3, f"YL{i}" + sfx, S, BF16) for i in range(4)]
                        YLB = [[MB(f"YL@{s}_{i}_{g}") for g in range(NSEG)] for i in range(4)]
                        sq = [T(s3, f"sq{i}" + sfx, 512, F32) for i in range(2)]
                        sqB = [MB(f"sq@{s}_{i}") for i in range(2)]
                        rt = T(s3, "rt" + sfx, 512, F32)
                        rtB = MB(f"rt@{s}")
                        LX, LG, U, A, TI, R6 = Rb
                        LXB, LGB, UB, AB, TIB, R6B = RB
                        segs = [slice(g * 512, (g + 1) * 512) for g in range(NSEG)]
                        for c in range(4):
                            def col(base):
                                return pv[:, base + c:base + c + 1]
                            for g in range(NSEG):
                                sl = segs[g]
                                bank, bB = nbank()
                                for kc in range(8):
                                    MM(bank, winA[:, kc * 1024 + c * 128: kc * 1024 + (c + 1) * 128],
                                       xnT[:, kc * S + g * 512: kc * S + (g + 1) * 512], kc == 0, kc == 7,
                                       R=[winAB] + xnTB[g * 4:(g + 1) * 4], W=[bB])
                                COPY(LX[:, sl], bank, R=[bB], W=[LXB[g]], eng="act")
                                bank, bB = nbank()
                                for kc in range(8):
                                    MM(bank, winA[:, kc * 1024 + 512 + c * 128: kc * 1024 + 512 + (c + 1) * 128],
                                       xnT[:, kc * S + g * 512: kc * S + (g + 1) * 512], kc == 0, kc == 7,
                                       R=[winAB] + xnTB[g * 4:(g + 1) * 4], W=[bB])
                                COPY(LG[:, sl], bank, R=[bB], W=[LGB[g]], eng="dve")
                            for g in range(NSEG):
                                sl = segs[g]
                                e = (g + 1) * 512
                                TS(U[:, sl], LX[:, sl], pv[:, CW + 3 * 4 + c:CW + 3 * 4 + c + 1], col(CB), ALU.mult, ALU.add,
                                   R=[LXB[g], pvB], W=[UB[g]])
                                for k, sh in ((2, 1), (1, 2), (0, 3)):
                                    lo = max(g * 512, sh)
                                    rr = [LXB[g], pvB, UB[g]] + ([LXB[g - 1]] if g > 0 else [])
                                    STT(U[:, lo:e], LX[:, lo - sh:e - sh], pv[:, CW + k * 4 + c:CW + k * 4 + c + 1], U[:, lo:e],
                                        ALU.mult, ALU.add, R=rr, W=[UB[g]])
                                COPY(Ubf[:, sl], U[:, sl], R=[UB[g]], W=[UbfB[g]], eng="act")
                            for g in range(NSEG):
                                sl = segs[g]
                                bank, bB = nbank()
                                MM(bank, wlru[:, (c * 2 + 0) * 128:(c * 2 + 1) * 128], Ubf[:, sl], True, True,
                                   R=[wlruB, UbfB[g]], W=[bB])
                                ACT(LX[:, sl], bank, AF.Tanh, R=[bB, pvB], W=[LXB[g]], scale=0.5, bias=col(BAH))
                                bank, bB = nbank()
                                MM(bank, wlru[:, (c * 2 + 1) * 128:(c * 2 + 2) * 128], Ubf[:, sl], True, True,
                                   R=[wlruB, UbfB[g]], W=[bB])
                                ACT(TI[:, sl], bank, AF.Tanh, R=[bB, pvB], W=[TIB[g]], scale=0.5, bias=col(BXH))
                            for g in range(NSEG):
                                sl = segs[g]
                                ACT(A[:, sl], LX[:, sl], AF.Exp, R=[LXB[g], pvB], W=[AB[g]], scale=col(HC), bias=col(HC))
                                ACT(R6[:, sl], LX[:, sl], AF.Exp, R=[LXB[g], pvB], W=[R6B[g]], scale=col(H2), bias=col(H2))
                            for g in range(NSEG):
                                sl = segs[g]
                                ACT(LX[:, sl], LG[:, sl], AF.Square, R=[LGB[g]], W=[LXB[g]])
                                TS(LX[:, sl], LX[:, sl], 0.044715, 1.0, ALU.mult, ALU.add, R=[LXB[g]], W=[LXB[g]])
                                TT(LX[:, sl], LX[:, sl], LG[:, sl], ALU.mult, R=[LXB[g], LGB[g]], W=[LXB[g]])
                            for g in range(NSEG):
                                sl = segs[g]
                                ACT(LX[:, sl], LX[:, sl], AF.Tanh, R=[LXB[g]], W=[LXB[g]], scale=0.7978845608028654)
                            for g in range(NSEG):
                                sl = segs[g]
                                ACT(R6[:, sl], R6[:, sl], AF.Sqrt, R=[R6B[g], constB], W=[R6B[g]], scale=-1.0, bias=cst[:, 0:1])
                            for g in range(NSEG):
                                sl = segs[g]
                                STT(TI[:, sl], TI[:, sl], 1.0, U[:, sl], ALU.add, ALU.mult, R=[TIB[g], UB[g]], W=[TIB[g]])
                                STT(TI[:, sl], TI[:, sl], 0.5, R6[:, sl], ALU.mult, ALU.mult, R=[TIB[g], R6B[g]], W=[TIB[g]])
                            for g in range(NSEG):
                                sl = segs[g]
                                if g == 0:
                                    sch.op("dve", lambda e, sl=sl: e.tensor_tensor_scan(
                                        out=U[:, sl], data0=A[:, sl], data1=TI[:, sl], initial=0.0,
                                        op0=ALU.mult, op1=ALU.add), R=[AB[g], TIB[g]], W=[UB[g]])
                                else:
                                    sch.op("dve", lambda e, sl=sl, g=g: e.tensor_tensor_scan(
                                        out=U[:, sl], data0=A[:, sl], data1=TI[:, sl],
                                        initial=U[:, g * 512 - 1:g * 512],
                                        op0=ALU.mult, op1=ALU.add), R=[AB[g], TIB[g], UB[g - 1]], W=[UB[g]])
                                STT(LX[:, sl], LX[:, sl], 1.0, LG[:, sl], ALU.add, ALU.mult, R=[LXB[g], LGB[g]], W=[LXB[g]])
                                STT(YL[c][:, sl], LX[:, sl], 0.5, U[:, sl], ALU.mult, ALU.mult, R=[LXB[g], UB[g]], W=[YLB[c][g]])
                        for tt in range(NQT):
                            feat_norm([YL[i] for i in range(4)], [YLB[i][tt] for i in range(4)], GLO, 512, tt, 0, sq, sqB, rt, rtB)
                        sch.barrier(bar[:, 0:1])

                    with ExitStack() as s4:
                        lat = [T(s4, f"latbf{i}" + sfx, 768, BF16) for i in range(2)]
                        latbB = [MB(f"latbf@{s}_{i}") for i in range(2)]
                        junk = T(s4, "junkb" + sfx, 512, BF16)
                        junkB = MB(f"junkb@{s}")
                        rtmp = [T(s4, f"rtmp{i}" + sfx, 128, F32) for i in range(2)]
                        rtmpB = [MB(f"rtmp@{s}_{i}") for i in range(2)]
                        cqnT3 = cqnT[:, :].rearrange("p (k t) -> p k t", k=3)
                        ckvnT3 = ckvnT[:, :].rearrange("p (k t) -> p k t", k=2)
                        def lat_front(tb):
                            bA, bAB = nbank()
                            bBk, bBB = nbank()
                            for kc in range(8):
                                lhs = xnT[:, kc * S + tb * 128: kc * S + (tb + 1) * 128]
                                MM(bA, lhs, winB[:, kc * 768: kc * 768 + 512], kc == 0, kc == 7,
                                   R=[winBB, xnTB[tb]], W=[bAB])
                            for kc in range(8):
                                lhs = xnT[:, kc * S + tb * 128: kc * S + (tb + 1) * 128]
                                MM(bBk[:, 0:256], lhs, winB[:, kc * 768 + 512: kc * 768 + 768], kc == 0, kc == 7,
                                   R=[winBB, xnTB[tb]], W=[bBB])
                            ss, ssB = small()
                            return (tb, bA, bAB, bBk, bBB, ss, ssB)

                        def lat_stats(stt):
                            tb, bA, bAB, bBk, bBB, ss, ssB = stt
                            ACT(junk[:, 0:384], bA[:, 0:384], AF.Square, R=[bAB], W=[junkB, ssB], accum_out=ss[:, 0:1])
                            ACT(junk[:, 0:256], bBk[:, 0:256], AF.Square, R=[bBB], W=[junkB, ssB], accum_out=ss[:, 1:2])
                            rstd_from_ss(ss, ssB, 2, [1.0 / 384, 1.0 / 256])

                        def lat_back(stt):
                            tb, bA, bAB, bBk, bBB, ss, ssB = stt
                            bsl = slice(tb * 128, (tb + 1) * 128)
                            L = lat[tb % 2]
                            LB = latbB[tb % 2]
                            ACT(L[:, 0:384], bA[:, 0:384], AF.Copy, R=[bAB, ssB], W=[LB], scale=ss[:, 0:1])
                            ACT(L[:, 384:640], bBk[:, 0:256], AF.Copy, R=[bBB, ssB], W=[LB], scale=ss[:, 1:2])
                            rope(bA[:, 384:512].rearrange("p (g d) -> p g d", g=2), [bAB], 2,
                                 cs[:, tb * 64 + 32: tb * 64 + 64], cs[:, tb * 64: tb * 64 + 32], csB,
                                 L[:, 640:768].rearrange("p (g d) -> p g d", g=2), [LB],
                                 rtmp[0], rtmpB[0], rtmp[1], rtmpB[1], extra=[LB])
                            tp, tB = ntp()
                            for k in range(6):
                                TR(tp[:, k * 128:(k + 1) * 128], L[:, k * 128:(k + 1) * 128], R=[LB], W=[tB], last=(k == 5))
                            TT(cqnT3[:, :, bsl], tp[:, 0:384].rearrange("p (k t) -> p k t", k=3),
                               pv[:, GQ:GQ + 3].unsqueeze(2).broadcast_to([128, 3, 128]), ALU.mult,
                               R=[tB, pvB], W=[latB[tb]])
                            TT(ckvnT3[:, :, bsl], tp[:, 384:640].rearrange("p (k t) -> p k t", k=2),
                               pv[:, GKV:GKV + 2].unsqueeze(2).broadcast_to([128, 2, 128]), ALU.mult,
                               R=[tB, pvB], W=[latB[tb]])
                            COPY(kpeT[:, bsl], tp[:, 640:768], R=[tB], W=[latB[tb]], eng="dve")

                        depth = LAT_DEPTH if F_LATSKEW else 0
                        sts = {}
                        for tb in range(NTB + depth):
                            if tb < NTB:
                                sts[tb] = lat_front(tb)
                            if depth > 0 and tb - depth >= 0:
                                lat_back(sts.pop(tb - depth))
                            if tb < NTB:
                                lat_stats(sts[tb])
                                if depth == 0:
                                    lat_back(sts.pop(tb))
                        sch.barrier(bar[:, 0:1])
                with ExitStack() as s5:
                    qnT = T(s5, "qnT" + sfx, 4 * S, BF16)
                    qpeT = T(s5, "qpeT" + sfx, 2 * S, BF16)
                    knT = T(s5, "knT" + sfx, 4 * S, BF16)
                    Vt = T(s5, "Vt" + sfx, NTB * 512, BF16)
                    qB = [MB(f"q@{s}_{i}") for i in range(NQT)]
                    kB = [MB(f"k@{s}_{i}") for i in range(NQT)]
                    qpeB = [MB(f"qpe@{s}_{i}") for i in range(NTB)]
                    vB = [MB(f"v@{s}_{i}") for i in range(NTB)]
                    qpeT3 = qpeT[:, :].rearrange("p (k t) -> p k t", k=2)
                    with ExitStack() as s5b:
                        qpb = [T(s5b, f"qpb{i}" + sfx, 256, BF16) for i in range(2)]
                        qpbB = [MB(f"qpb@{s}_{i}") for i in range(2)]
                        rtmp = [T(s5b, f"rtq{i}" + sfx, 128, F32) for i in range(2)]
                        rtmpB = [MB(f"rtq@{s}_{i}") for i in range(2)]
                        for tt in range(NQT):
                            tsl = slice(tt * 512, (tt + 1) * 512)
                            lb4 = latB[tt * 4:(tt + 1) * 4]
                            for h in range(4):
                                bank, bB = nbank()
                                for kc in range(3):
                                    MM(bank, wuq[:, kc * 768 + h * 128: kc * 768 + (h + 1) * 128],
                                       cqnT[:, kc * S + tt * 512: kc * S + (tt + 1) * 512], kc == 0, kc == 2,
                                       R=[wuqB] + lb4, W=[bB])
                                COPY(qnT[:, h * S + tt * 512: h * S + (tt + 1) * 512], bank, R=[bB], W=[qB[tt]])
                            for h in range(4):
                                bank, bB = nbank()
                                for kc in range(2):
                                    MM(bank, wukv[:, kc * 1024 + h * 128: kc * 1024 + (h + 1) * 128],
                                       ckvnT[:, kc * S + tt * 512: kc * S + (tt + 1) * 512], kc == 0, kc == 1,
                                       R=[wukvB] + lb4, W=[bB])
                                COPY(knT[:, h * S + tt * 512: h * S + (tt + 1) * 512], bank, R=[bB], W=[kB[tt]])
                            def stA(tb):
                                bank, bB = nbank()
                                for kc in range(3):
                                    MM(bank[:, 0:256], cqnT[:, kc * S + tb * 128: kc * S + (tb + 1) * 128],
                                       wuq[:, kc * 768 + 512: kc * 768 + 768], kc == 0, kc == 2,
                                       R=[wuqB, latB[tb]], W=[bB])
                                Q = qpb[tb % 2]
                                QB = qpbB[tb % 2]
                                rope(bank[:, 0:256].rearrange("p (g d) -> p g d", g=4), [bB], 4,
                                     cs[:, tb * 64 + 32: tb * 64 + 64], cs[:, tb * 64: tb * 64 + 32], csB,
                                     Q[:, :].rearrange("p (g d) -> p g d", g=4), [QB],
                                     rtmp[0], rtmpB[0], rtmp[1], rtmpB[1])

                            def stB(tb):
                                bsl = slice(tb * 128, (tb + 1) * 128)
                                Q = qpb[tb % 2]
                                QB = qpbB[tb % 2]
                                tp, tB = ntp()
                                for k in range(2):
                                    TR(tp[:, k * 128:(k + 1) * 128], Q[:, k * 128:(k + 1) * 128], R=[QB], W=[tB], last=(k == 1))
                                COPY(qpeT3[:, :, bsl], tp[:, 0:256].rearrange("p (k t) -> p k t", k=2),
                                     R=[tB], W=[qpeB[tb]], eng="act")

                            def stV(tb):
                                bank, bB = nbank()
                                for kc in range(2):
                                    MM(bank, ckvnT[:, kc * S + tb * 128: kc * S + (tb + 1) * 128],
                                       wukv[:, kc * 1024 + 512: kc * 1024 + 1024], kc == 0, kc == 1,
                                       R=[wukvB, latB[tb]], W=[bB])
                                COPY(Vt[:, tb * 512:(tb + 1) * 512], bank, R=[bB], W=[vB[tb]], eng="dve")

                            t0 = tt * 4
                            if F_QKVSKEW:
                                stA(t0); stV(t0); stA(t0 + 1); stV(t0 + 1); stB(t0)
                                stA(t0 + 2); stV(t0 + 2); stB(t0 + 1)
                                stA(t0 + 3); stV(t0 + 3); stB(t0 + 2); stB(t0 + 3)
                            else:
                                for tb in range(t0, t0 + 4):
                                    stA(tb); stB(tb); stV(tb)
                        sch.barrier(bar[:, 0:1])
                    with ExitStack() as s6:
                        YM = [T(s6, f"YM{i}" + sfx, S, F32) for i in range(4)]
                        YMB = [MB(f"YM@{s}_{i}") for i in range(4)]
                        Pb = [T(s6, f"Pb{i}" + sfx, 512, BF16) for i in range(3)]
                        PbB = [MB(f"Pb@{s}_{i}") for i in range(3)]
                        rden = [T(s6, f"rden{i}" + sfx, 512, F32) for i in range(2)]
                        rdenB = [MB(f"rden@{s}_{i}") for i in range(2)]
                        sq = [T(s6, f"sqm{i}" + sfx, 512, F32) for i in range(2)]
                        sqB = [MB(f"sqm@{s}_{i}") for i in range(2)]
                        rt = T(s6, "rtm" + sfx, 512, F32)
                        rtB = MB(f"rtm@{s}")
                        it = 0
                        pcount = 0
                        for qi in range(NQT):
                            for h in range(4):
                                hb = (h % 2) * 64
                                nkc = 4 * qi + 4
                                ob = 4 if it % 2 == 0 else 2
                                O, OB = PS[:, ob * 512:(ob + 1) * 512], bankB[ob]
                                DN, DNB = PS[:, (ob + 1) * 512:(ob + 2) * 512], bankB[ob + 1]

                                def c0_of(kc):
                                    return 0 if kc < 4 * qi else (kc - 4 * qi) * 128

                                def emitS(kc):
                                    sbi = kc % 2
                                    Sb, SB = PS[:, sbi * 512:(sbi + 1) * 512], bankB[sbi]
                                    c0 = c0_of(kc)
                                    MM(Sb[:, c0:512], knT[:, h * S + kc * 128: h * S + (kc + 1) * 128],
                                       qnT[:, h * S + qi * 512 + c0: h * S + (qi + 1) * 512], True, False,
                                       R=[kB[kc // 4], qB[qi]], W=[SB])
                                    MM(Sb[:, c0:512], kpeT[hb:hb + 64, kc * 128:(kc + 1) * 128],
                                       qpeT[hb:hb + 64, (h // 2) * S + qi * 512 + c0: (h // 2) * S + (qi + 1) * 512],
                                       False, True, R=[latB[kc]] + qpeB[qi * 4:(qi + 1) * 4], W=[SB])

                                emitS(0)
                                for kc in range(nkc):
                                    if kc + 1 < nkc:
                                        emitS(kc + 1)
                                    sbi = kc % 2
                                    Sb, SB = PS[:, sbi * 512:(sbi + 1) * 512], bankB[sbi]
                                    c0 = c0_of(kc)
                                    P, PB = Pb[pcount % 3], PbB[pcount % 3]
                                    pcount += 1
                                    ACT(P[:, c0:512], Sb[:, c0:512], AF.Exp, R=[SB], W=[PB], scale=mla_scale)
                                    if kc >= 4 * qi:
                                        TT(P[:, c0:c0 + 128], P[:, c0:c0 + 128], tri[:, :], ALU.mult, R=[PB, triB], W=[PB])
                                    MM(O[:, c0:512], Vt[:, kc * 512 + h * 128: kc * 512 + (h + 1) * 128], P[:, c0:512],
                                       kc == 0, kc == nkc - 1, R=[vB[kc], PB], W=[OB])
                                    MM(DN[:, c0:512], onesb[:, :], P[:, c0:512], kc == 0, kc == nkc - 1,
                                       R=[constB, PB], W=[DNB])
                                rd, rdB = rden[it % 2], rdenB[it % 2]
                                RECIP(rd[:, :], DN, R=[DNB], W=[rdB])
                                TT(YM[h][:, qi * 512:(qi + 1) * 512], O, rd[:, :], ALU.mult, R=[OB, rdB], W=[YMB[h]])
                                it += 1
                            feat_norm(YM, YMB, GMO, 512, qi, 4, sq, sqB, rt, rtB)
                        sch.barrier(bar[:, 0:1])
            if dbg:
                dB = MB(f"dbgy@{s}")
                sch.dma("sp", dbg_yT[s], yT[:, :], R=yTB + [dB])

            with ExitStack() as p2:
                G = [T(p2, f"G{i}" + sfx, 1024, F32) for i in range(3)]
                GB = [MB(f"G@{s}_{i}") for i in range(3)]
                for i in range(3):
                    sch.dma("sp", G[i][:], grow_d[i], W=[GB[i]])
                mkT = T(p2, "mkT" + sfx, 8 * 256, BF16)
                mv = T(p2, "mv" + sfx, 2 * 1024, BF16)
                mkB = MB(f"mk@{s}")
                mvB = MB(f"mv@{s}")
                with ExitStack() as pm:
                    mnT = T(pm, "mnT" + sfx, 8 * 256, BF16)
                    mnB = [MB(f"mn@{s}_{i}") for i in range(2)]
                    memt = [T(pm, f"memt{i}" + sfx, 1024, F32) for i in range(2)]
                    memtB = [MB(f"memt@{s}_{i}") for i in range(2)]
                    xs = [T(pm, f"xsm{i}" + sfx, 1024, BF16) for i in range(2)]
                    xsB = [MB(f"xsm@{s}_{i}") for i in range(2)]
                    junk = T(pm, "junkm" + sfx, 1024, BF16)
                    junkB = MB(f"junkm@{s}")
                    wm = T(pm, "wm" + sfx, 4 * 4096, BF16)
                    wmB = [MB(f"wm@{s}_{i}") for i in range(4)]
                    for i in range(4):
                        sch.dma("sp", wm[:, i * 4096:(i + 1) * 4096], scr_d[SLOT_MK + i], R=[scrB], W=[wmB[i]])
                    mnT3 = mnT[:, :].rearrange("p (k t) -> p k t", k=8)
                    for mc in range(2):
                        sch.dma("sp", memt[mc][:], mem_d[s, mc * 128:(mc + 1) * 128, :], W=[memtB[mc]])
                    for mc in range(2):
                        nt(memt[mc][:, :], [memtB[mc]], 1024, pv[:, GMKV:GMKV + 8],
                           mnT3[:, :, mc * 128:(mc + 1) * 128], [mnB[mc]], xs[mc], xsB[mc], junk, junkB)
                    for sl in range(2):
                        W_ = wm[:, sl * 4096:(sl + 1) * 4096]
                        for cc in range(4):
                            c = sl * 4 + cc
                            bank, bB = nbank()
                            for kc in range(8):
                                MM(bank[:, 0:256], W_[:, cc * 1024 + kc * 128: cc * 1024 + (kc + 1) * 128],
                                   mnT[:, kc * 256:(kc + 1) * 256], kc == 0, kc == 7, R=[wmB[sl]] + mnB, W=[bB])
                            COPY(mkT[:, c * 256:(c + 1) * 256], bank[:, 0:256], R=[bB], W=[mkB])
                    for fh in range(2):
                        W_ = wm[:, (2 + fh) * 4096:(3 + fh) * 4096]
                        for mc in range(2):
                            bank, bB = nbank()
                            for kc in range(8):
                                MM(bank, mnT[:, kc * 256 + mc * 128: kc * 256 + (mc + 1) * 128],
                                   W_[:, kc * 512:(kc + 1) * 512], kc == 0, kc == 7, R=[wmB[2 + fh], mnB[mc]], W=[bB])
                            COPY(mv[:, mc * 1024 + fh * 512: mc * 1024 + (fh + 1) * 512], bank, R=[bB], W=[mvB])
                    sch.barrier(bar[:, 0:1])

                NRG, NRF = 3, 2
                ringG = T(p2, "ringG" + sfx, NRG * 4096, BF16)
                ringF = T(p2, "ringF" + sfx, NRF * 4096, BF16)
                Ht = [T(p2, f"Ht{i}" + sfx, 4 * 1024, F32) for i in range(2)]
                HtB = [[MB(f"Ht@{s}_{i}_{tb}") for tb in range(4)] for i in range(2)]
                xnF = T(p2, "xnF" + sfx, 8 * 512, BF16)
                xnFB = [MB(f"xnF@{s}_{i}") for i in range(4)]
                xnF3 = xnF[:, :].rearrange("p (k t) -> p k t", k=8)
                xnG = T(p2, "xnG" + sfx, 8 * 512, BF16)
                xnGB = [MB(f"xnG@{s}_{i}") for i in range(4)]
                xnG3 = xnG[:, :].rearrange("p (k t) -> p k t", k=8)
                PLh = T(p2, "PLh" + sfx, NJ * 512, BF16)
                PLhB = [MB(f"PLh@{s}_{i}") for i in range(NJ)]
                PLq = T(p2, "PLq" + sfx, 8 * 512, BF16)
                PLqB = [MB(f"PLq@{s}_{i}") for i in range(8)]
                Pb = [T(p2, f"Pm{i}" + sfx, 512, BF16) for i in range(2)]
                PbB = [MB(f"Pm@{s}_{i}") for i in range(2)]
                rden = [T(p2, f"rdm{i}" + sfx, 512, F32) for i in range(1)]
                rdenB = [MB(f"rdm@{s}_{i}") for i in range(1)]
                Fsb = T(p2, "Fsb" + sfx, 4 * 1024, F32)
                FsbB = [MB(f"Fsb@{s}_{i}") for i in range(4)]
                sg = [T(p2, f"sg{i}" + sfx, 512, F32) for i in range(2)]
                sgB = [MB(f"sg@{s}_{i}") for i in range(2)]
                tmpo = T(p2, "tmpo" + sfx, 1024, F32)
                tmpoB = MB(f"tmpo@{s}")
                xs = [T(p2, f"xs2{i}" + sfx, 1024, BF16) for i in range(2)]
                xsB = [MB(f"xs2@{s}_{i}") for i in range(2)]
                junk = T(p2, "junk2" + sfx, 1024, BF16)
                junkB = MB(f"junk2@{s}")

                class Ring:
                    def __init__(self, tensor, nslots, uses, name):
                        self.t, self.n, self.uses = tensor, nslots, uses
                        self.B = [MB(f"{name}@{s}_{i}") for i in range(nslots)]
                        self.k = self.issued = self.released = 0

                    def pump(self):
                        while self.issued < len(self.uses) and self.issued < self.released + self.n:
                            i = self.issued
                            r = i % self.n
                            sch.dma("sp", self.t[:, r * 4096:(r + 1) * 4096], scr_d[self.uses[i]], R=[scrB], W=[self.B[r]])
                            self.issued += 1

                    def acquire(self, expect):
                        k = self.k
                        self.k += 1
                        assert k < self.issued, "ring: acquire before load issued"
                        assert self.uses[k] == expect, (self.uses[k], expect)
                        r = k % self.n
                        return self.t[:, r * 4096:(r + 1) * 4096], self.B[r]

                    def release(self):
                        self.released += 1
                        self.pump()

                rG = Ring(ringG, NRG, (list(range(6, 17)) + list(range(17, 23))) * NQT, "rG")
                rF = Ring(ringF, NRF, [0, 1, 2, 3, 4, 5] * NQT, "rF")
                rG.pump()
                rF.pump()
                FB = [(PS[:, 4 * 512:5 * 512], bankB[4]), (PS[:, 5 * 512:6 * 512], bankB[5])]
                FPAIR = (PS[:, 4 * 512:6 * 512], [bankB[4], bankB[5]])
                fbs = {"i": 0}

                def fbank():
                    fbs["i"] += 1
                    return FB[fbs["i"] % 2]

                def postres(src, srcB, gi, HtAP, HtBuf):
                    ss, ssB = small()
                    ACT(junk[:, :], src, AF.Square, R=srcB, W=[junkB, ssB], accum_out=ss[:, 0:1])
                    rstd_from_ss(ss, ssB, 1, [1.0 / 1024])
                    TT(tmpo[:, :], src, G[gi][:, :], ALU.mult, R=list(srcB) + [GB[gi], ssB], W=[tmpoB])
                    STT(HtAP, tmpo[:, :], ss[:, 0:1], HtAP, ALU.mult, ALU.add, R=[tmpoB, ssB, HtBuf], W=[HtBuf])

                def front_items(t):
                    p = t % 2
                    H = Ht[p]
                    HB = HtB[p]
                    stt = {}

                    def xload():
                        for tb in range(4):
                            sch.dma("pool", H[:, tb * 1024:(tb + 1) * 1024],
                                    x_d[s, t * 512 + tb * 128: t * 512 + (tb + 1) * 128, :], W=[HB[tb]])

                    def proj(tb, which):
                        if tb == 0:
                            stt["W"] = [rF.acquire(0 + 4 * which), rF.acquire(1 + 4 * which)]
                        pair, pB = FPAIR
                        for fh in range(2):
                            W_, WB = stt["W"][fh]
                            for kc in range(8):
                                if which == 0:
                                    lhs = yT[:, kc * S + t * 512 + tb * 128: kc * S + t * 512 + (tb + 1) * 128]
                                    rr = [yTB[t], WB]
                                else:
                                    lhs = PLq[:, kc * 512 + tb * 128: kc * 512 + (tb + 1) * 128]
                                    rr = [PLqB[kc], WB]
                                MM(pair[:, fh * 512:(fh + 1) * 512], lhs, W_[:, kc * 512:(kc + 1) * 512],
                                   kc == 0, kc == 7, R=rr, W=[pB[fh]])
                        if tb == 3:
                            rF.release()
                            rF.release()
                        hap = H[:, tb * 1024:(tb + 1) * 1024]
                        postres(pair, pB, which, hap, HB[tb])
                        ss, ssB = nt_a(hap, [HB[tb]], 1024, junk, junkB)
                        ACT(xs[tb % 2][:, :], hap, AF.Copy, R=[HB[tb], ssB], W=[xsB[tb % 2]], scale=ss[:, 0:1])

                    def trn(tb, which):
                        x_, xB_ = xs[tb % 2], xsB[tb % 2]
                        tp, tB = ntp()
                        for k in range(8):
                            TR(tp[:, k * 128:(k + 1) * 128], x_[:, k * 128:(k + 1) * 128], R=[xB_], W=[tB], last=(k == 7))
                        gc = (pv[:, GPMEM:GPMEM + 8] if which == 0 else pv[:, GPF:GPF + 8])
                        g3 = gc.unsqueeze(2).broadcast_to([128, 8, 128])
                        dst3 = (xnF3 if which == 0 else xnG3)[:, :, tb * 128:(tb + 1) * 128]
                        dB = (xnFB if which == 0 else xnGB)[tb]
                        TT(dst3, tp[:, 0:1024].rearrange("p (k t) -> p k t", k=8), g3, ALU.mult, R=[tB, pvB], W=[dB])

                    def mq(c):
                        if c == 0:
                            stt["Q"] = [rF.acquire(2), rF.acquire(3)]
                        W_, WB = stt["Q"][c // 4]
                        cc = c % 4
                        bank, bB = fbank()
                        for kc in range(8):
                            MM(bank, W_[:, cc * 1024 + kc * 128: cc * 1024 + (kc + 1) * 128],
                               xnF[:, kc * 512:(kc + 1) * 512], kc == 0, kc == 7, R=[WB] + xnFB, W=[bB])
                        COPY(PLq[:, c * 512:(c + 1) * 512], bank, R=[bB], W=[PLqB[c]])
                        if c == 7:
                            rF.release()
                            rF.release()

                    def attS(h):
                        Ps = []
                        for mc in range(2):
                            bank, bB = fbank()
                            for dc in range(2):
                                c = h * 2 + dc
                                MM(bank, mkT[:, c * 256 + mc * 128: c * 256 + (mc + 1) * 128],
                                   PLq[:, c * 512:(c + 1) * 512], dc == 0, dc == 1, R=[mkB, PLqB[c]], W=[bB])
                            P, PB = Pb[mc], PbB[mc]
                            ACT(P[:, :], bank, AF.Exp, R=[bB], W=[PB], scale=mem_scale)
                            Ps.append((P, PB))
                        stt["Ps"] = Ps

                    def attPV(h):
                        Ps = stt["Ps"]
                        dn, dnB = fbank()
                        for mc in range(2):
                            MM(dn, onesb[:, :], Ps[mc][0][:, :], mc == 0, mc == 1, R=[constB, Ps[mc][1]], W=[dnB])
                        rd, rdB = rden[0], rdenB[0]
                        RECIP(rd[:, :], dn, R=[dnB], W=[rdB])
                        for dvc in range(2):
                            c = h * 2 + dvc
                            bank, bB = fbank()
                            for mc in range(2):
                                MM(bank, mv[:, mc * 1024 + c * 128: mc * 1024 + (c + 1) * 128], Ps[mc][0][:, :],
                                   mc == 0, mc == 1, R=[mvB, Ps[mc][1]], W=[bB])
                            TT(PLq[:, c * 512:(c + 1) * 512], bank, rd[:, :], ALU.mult, R=[bB, rdB], W=[PLqB[c]])

                    A = [xload]
                    A += [lambda: proj(0, 0), lambda: proj(1, 0), lambda: trn(0, 0), lambda: proj(2, 0),
                          lambda: trn(1, 0), lambda: proj(3, 0), lambda: trn(2, 0), lambda: trn(3, 0)]
                    A += [(lambda c=c: mq(c)) for c in range(8)]
                    A += [lambda: attS(0)]
                    for h in range(4):
                        A += [(lambda h=h: attPV(h))]
                        if h + 1 < 4:
                            A += [(lambda h=h: attS(h + 1))]
                    Bq = [lambda: proj(0, 1), lambda: proj(1, 1), lambda: trn(0, 1), lambda: proj(2, 1),
                          lambda: trn(1, 1), lambda: proj(3, 1), lambda: trn(2, 1), lambda: trn(3, 1)]
                    return A, Bq

                def ffn_items(t):
                    p = t % 2
                    H = Ht[p]
                    HB = HtB[p]

                    def gu(jj):
                        W_, WB = rG.acquire(6 + jj)
                        for jl in range(2):
                            j = jj * 2 + jl
                            b0 = 0 if jl == 0 else 2
                            bg, bgB = PS[:, b0 * 512:(b0 + 1) * 512], bankB[b0]
                            bu, buB = PS[:, (b0 + 1) * 512:(b0 + 2) * 512], bankB[b0 + 1]
                            for kc in range(8):
                                MM(bg, W_[:, (jl * 2 + 0) * 1024 + kc * 128: (jl * 2 + 0) * 1024 + (kc + 1) * 128],
                                   xnG[:, kc * 512:(kc + 1) * 512], kc == 0, kc == 7, R=[WB] + xnGB, W=[bgB])
                            for kc in range(8):
                                MM(bu, W_[:, (jl * 2 + 1) * 1024 + kc * 128: (jl * 2 + 1) * 1024 + (kc + 1) * 128],
                                   xnG[:, kc * 512:(kc + 1) * 512], kc == 0, kc == 7, R=[WB] + xnGB, W=[buB])
                            g_, gB_ = sg[j % 2], sgB[j % 2]
                            ACT(g_[:, :], bg, AF.Silu, R=[bgB], W=[gB_])
                            TT(PLh[:, j * 512:(j + 1) * 512], bu, g_[:, :], ALU.mult, R=[buB, gB_], W=[PLhB[j]])
                        rG.release()

                    def down(fh, sl3):
                        W_, WB = rG.acquire(17 + fh * 3 + sl3)
                        nj = 8 if sl3 < 2 else 6
                        for jl in range(nj):
                            j = sl3 * 8 + jl
                            for tb in range(4):
                                last = (jl == nj - 1 and tb == 3)
                                MM(PS[:, tb * 512:(tb + 1) * 512], PLh[:, j * 512 + tb * 128: j * 512 + (tb + 1) * 128],
                                   W_[:, jl * 512:(jl + 1) * 512], j == 0, j == NJ - 1, R=[PLhB[j], WB], W=[bankB[tb]],
                                   inc=(True if last else None))
                        rG.release()

                    def evac(fh):
                        for tb in range(4):
                            COPY(Fsb[:, tb * 1024 + fh * 512: tb * 1024 + (fh + 1) * 512], PS[:, tb * 512:(tb + 1) * 512],
                                 R=[bankB[tb]], W=[FsbB[tb]])

                    def fin():
                        for tb in range(4):
                            hap = H[:, tb * 1024:(tb + 1) * 1024]
                            postres(Fsb[:, tb * 1024:(tb + 1) * 1024], [FsbB[tb]], 2, hap, HB[tb])
                            sch.dma("pool", out_d[s, t * 512 + tb * 128: t * 512 + (tb + 1) * 128, :], hap, R=[HB[tb]])

                    GU = [(lambda jj=jj: gu(jj)) for jj in range(11)]
                    DN = []
                    for fh in range(2):
                        for sl3 in range(3):
                            DN.append(lambda fh=fh, sl3=sl3: down(fh, sl3))
                        DN.append(lambda fh=fh: evac(fh))
                    DN.append(fin)
                    return GU, DN

                def interleave(Xs, Ys):
                    nx, ny = len(Xs), len(Ys)
                    yi = 0
                    for i, xf in enumerate(Xs):
                        xf()
                        tgt = ((i + 1) * ny + nx - 1) // nx if nx else ny
                        while yi < min(tgt, ny):
                            Ys[yi]()
                            yi += 1
                    while yi < ny:
                        Ys[yi]()
                        yi += 1

                A0, B0 = front_items(0)
                for f in A0 + B0:
                    f()
                for tt in range(NQT):
                    GU, DN = ffn_items(tt)
                    if tt + 1 < NQT:
                        A1, B1 = front_items(tt + 1)
                    else:
                        A1, B1 = [], []
                    interleave(GU, A1)
                    interleave(DN, B1)
                sch.barrier(bar[:, 0:1])
        sch.final_wait()

    with nc.Block() as block:
        @block.sync
        def _(sync):
            body()
    es.close()
    return nc, sch


def _kc(W):
    K, N = W.shape
    return np.ascontiguousarray(W.reshape(K // 128, 128, N).transpose(1, 0, 2)).reshape(128, -1)


def _cols(v):
    return np.ascontiguousarray(v.reshape(-1, 128).T)


def pack_shared(inp):
    f = np.float32
    w_in = inp["w_in"][0]
    winA = _kc(w_in[:, 0:1024])
    winB = _kc(np.concatenate([w_in[:, 1024:1408], w_in[:, 1664:1728], w_in[:, 1664:1728], w_in[:, 1408:1664]], axis=1))
    w_uq = inp["w_uq"][0].reshape(384, 4, 192)
    wuq = _kc(np.concatenate([w_uq[:, :, 0:128].reshape(384, 512), w_uq[:, :, 128:192].reshape(384, 256)], axis=1))
    w_ukv = inp["w_ukv"][0].reshape(256, 4, 256)
    wukv = _kc(np.concatenate([w_ukv[:, :, 0:128].reshape(256, 512), w_ukv[:, :, 128:256].reshape(256, 512)], axis=1))
    wlru = np.zeros((128, 4, 2, 128), f)
    for c in range(4):
        for g, key in enumerate(("lru_wa", "lru_wx")):
            for hh in range(2):
                wlru[hh * 64:(hh + 1) * 64, c, g, hh * 64:(hh + 1) * 64] = inp[key][0][2 * c + hh]
    wlru = wlru.reshape(128, -1)
    slots = np.zeros((NSLOT, 128, 4096), f)

    def moving(W, fh):
        return _kc(W[:, fh * 512:(fh + 1) * 512])

    def stationary(W, sl):
        Wr = W.reshape(8, 128, 8, 128)[:, :, sl * 4:(sl + 1) * 4, :]
        return np.ascontiguousarray(Wr.transpose(1, 2, 0, 3)).reshape(128, -1)

    for fh in range(2):
        slots[0 + fh] = moving(inp["w_out"][0], fh)
        slots[2 + fh] = stationary(inp["w_mq"][0], fh)
        slots[4 + fh] = moving(inp["w_mo"][0], fh)
        slots[SLOT_MK + fh] = stationary(inp["w_mk"][0], fh)
        slots[SLOT_MV + fh] = moving(inp["w_mv"][0], fh)
    wg = inp["w_gate"][0].reshape(8, 128, NJ, 128)
    wu = inp["w_up"][0].reshape(8, 128, NJ, 128)
    for jj in range(11):
        blk = np.zeros((128, 2, 2, 8, 128), f)
        for jl in range(2):
            j = jj * 2 + jl
            blk[:, jl, 0] = wg[:, :, j, :].transpose(1, 0, 2)
            blk[:, jl, 1] = wu[:, :, j, :].transpose(1, 0, 2)
        slots[6 + jj] = blk.reshape(128, -1)
    wd = inp["w_down"][0].reshape(NJ, 128, 1024)
    for fh in range(2):
        for sl3 in range(3):
            nj = 8 if sl3 < 2 else 6
            blk = np.zeros((128, 8, 512), f)
            blk[:, 0:nj, :] = wd[sl3 * 8: sl3 * 8 + nj, :, fh * 512:(fh + 1) * 512].transpose(1, 0, 2)
            slots[17 + fh * 3 + sl3] = blk.reshape(128, -1)
    pv = np.zeros((128, NPV), f)
    pv[:, GPM:GPM + 8] = _cols(inp["g_pre_mix"][0])
    for k in range(4):
        pv[:, CW + k * 4: CW + k * 4 + 4] = _cols(inp["conv_w"][0][k])
    pv[:, CB:CB + 4] = _cols(inp["conv_b"][0])
    pv[:, BA:BA + 4] = _cols(inp["lru_ba"][0])
    pv[:, BX:BX + 4] = _cols(inp["lru_bx"][0])
    pv[:, LAM:LAM + 4] = _cols(inp["lru_lambda"][0])
    pv[:, GQ:GQ + 3] = _cols(inp["g_q_lat"][0])
    pv[:, GKV:GKV + 2] = _cols(inp["g_kv_lat"][0])
    pv[:, GLO:GLO + 4] = _cols(inp["g_lru_out"][0])
    pv[:, GMO:GMO + 4] = _cols(inp["g_mla_out"][0])
    pv[:, GPMEM:GPMEM + 8] = _cols(inp["g_pre_mem"][0])
    pv[:, GMKV:GMKV + 8] = _cols(inp["g_mem_kv"][0])
    pv[:, GPF:GPF + 8] = _cols(inp["g_pre_ffn"][0])
    grow = np.stack([np.broadcast_to(inp[k][0][None, :], (128, 1024)) for k in
                     ("g_post_mix", "g_post_mem", "g_post_ffn")]).astype(f)
    invf = (10000.0 ** (-np.arange(0, 64, 2, dtype=np.float32) / 64)).astype(f)
    invf = np.ascontiguousarray(np.broadcast_to(invf[None, :], (128, 32)))
    ident = np.eye(128, dtype=f)
    tri = (np.arange(128)[None, :] >= np.arange(128)[:, None]).astype(f)
    return dict(pv=pv, grow=np.ascontiguousarray(grow), invf=invf, ident=ident, tri=tri, winA=winA, winB=winB,
                wuq=wuq, wukv=wukv, wlru=wlru, wslots=slots)


def pack_core(inp, b0, nseq, S):
    x = np.ascontiguousarray(inp["x"][b0:b0 + nseq], dtype=np.float32)
    mem = np.ascontiguousarray(inp["mem"][b0:b0 + nseq], dtype=np.float32)
    pos = np.asarray(inp["positions"][b0:b0 + nseq], dtype=np.int32)
    pos = np.ascontiguousarray(pos.reshape(nseq, S // 128, 128).transpose(2, 0, 1).reshape(128, -1))
    return dict(x=x, mem=mem, pos=pos)


_CACHE = {}


def kernel(**inputs):
    inp = {k: np.asarray(v) for k, v in inputs.items()}
    B, S, _ = inp["x"].shape
    nseq = B // NCORES
    key = (S, nseq)
    if key not in _CACHE:
        _CACHE[key] = build(S, nseq)[0]
    nc = _CACHE[key]
    shared = pack_shared(inp)
    in_maps = []
    for c in range(NCORES):
        m = dict(shared)
        m.update(pack_core(inp, c * nseq, nseq, S))
        in_maps.append(m)
    res = run_bass_kernel_spmd(nc, in_maps, core_ids=list(range(NCORES)))
    out = np.concatenate([np.asarray(r["out"]) for r in res.results], axis=0)
    return out.astype(np.float32)
```

```python
import math
from contextlib import ExitStack

import numpy as np
import concourse.bass as bass
import concourse.mybir as mybir
from concourse.bass_utils import run_bass_kernel_spmd

F32 = mybir.dt.float32
BF16 = mybir.dt.bfloat16
I32 = mybir.dt.int32
AF = mybir.ActivationFunctionType
ALU = mybir.AluOpType

NCORES = 8
D = 1024
DFF = 2816
NJ = DFF // 128
NMEM = 256
EPS = 1e-6
NSLOT = 27
SLOT_MK, SLOT_MV = 23, 25
GPM, CW, CB, BA, BX, LAM, GQ, GKV, GLO, GMO, GPMEM, GMKV, GPF = 0, 8, 24, 28, 32, 36, 40, 43, 45, 49, 53, 61, 69
NPV = 77
HC, H2, BAH, BXH = 77, 81, 85, 89
NPVT = 96
SEM_LIMIT = 12000
NRING = 5
import os
F_PIPEA = os.environ.get('K_PIPEA', '1') == '1'
F_LATSKEW = os.environ.get('K_LATSKEW', '1') == '1'
F_P2EARLY = os.environ.get('K_P2EARLY', '0') == '1'
F_WPRE = os.environ.get('K_WPRE', '0') == '1'
LAT_DEPTH = int(os.environ.get('K_LATDEPTH', '2'))
F_QKVSKEW = os.environ.get('K_QKVSKEW', '1') == '1'


class Buf:
    __slots__ = ("name", "w", "r", "ld", "ldc", "ldk", "st", "stc", "stk")

    def __init__(self, name):
        self.name = name
        self.w = {}
        self.r = {}
        self.ld = None
        self.ldc = 0
        self.ldk = None
        self.st = None
        self.stc = 0
        self.stk = None


class _Eng:
    def __init__(self, sch, name, h):
        self.sch = sch
        self.name = name
        self.h = h
        self.sem = None
        self.key = None
        self.cnt = 0
        self.nsem = 0
        self.known = {}
        self.pending = False

    def newsem(self):
        self.sem = self.sch.alloc_sem(f"e_{self.name}{self.nsem}")
        self.key = (self.name, self.nsem)
        self.nsem += 1
        self.cnt = 0


class Sched:
    def __init__(self, nc, es):
        self.nc = nc
        self.es = es
        self.nsems = 0
        self.E = {
            "pe": _Eng(self, "pe", nc.tensor),
            "act": _Eng(self, "act", nc.scalar),
            "dve": _Eng(self, "dve", nc.vector),
            "pool": _Eng(self, "pool", nc.gpsimd),
            "sp": _Eng(self, "sp", nc.sync),
        }
        for e in self.E.values():
            e.newsem()
        self.dma_tokens = {}
        self.nwaits = 0
        self.ninst = 0

    def alloc_sem(self, name):
        self.nsems += 1
        return self.es.enter_context(self.nc.semaphore(name))

    @staticmethod
    def _merge(d, src):
        for k, (s, v) in src.items():
            if k not in d or d[k][1] < v:
                d[k] = (s, v)

    def _wait(self, E, key, sem, val):
        if E.known.get(key, 0) >= val:
            return
        E.h.wait_ge(sem, val)
        E.known[key] = val
        self.nwaits += 1

    def _deps(self, E, R, W, skip_keys=()):
        deps = {}
        for b in R:
            self._merge(deps, b.w)
        for b in W:
            self._merge(deps, b.w)
            self._merge(deps, b.r)
        for key, (sem, val) in deps.items():
            if key in skip_keys:
                continue
            if E.name == "pe" and key[0] == "pe":
                continue
            self._wait(E, key, sem, val)

    def op(self, en, fn, R=(), W=(), inc=True):
        E = self.E[en]
        self._deps(E, R, W)
        ins = fn(E.h)
        self.ninst += 1
        if inc:
            if E.cnt >= SEM_LIMIT and not E.pending:
                E.newsem()
            E.cnt += 1
            ins.then_inc(E.sem, 1)
            E.pending = False
            tok = (E.key, E.sem, E.cnt)
        else:
            E.pending = True
            tok = (E.key, E.sem, E.cnt + 1)
        for b in R:
            k = tok[0]
            if k not in b.r or b.r[k][1] < tok[2]:
                b.r[k] = (tok[1], tok[2])
        for b in W:
            b.w = {tok[0]: (tok[1], tok[2])}
            b.r = {}
        return ins

    def dma(self, q, out, in_, R=(), W=(), **kw):
        E = self.E[q]
        skip = ()
        if W and W[0].ldk is not None:
            skip = (W[0].ldk,)
        self._deps(E, R, W, skip_keys=skip)
        ins = E.h.dma_start(out=out, in_=in_, **kw)
        self.ninst += 1
        if W:
            b = W[0]
            if b.ld is None:
                b.ld = self.alloc_sem("ld_" + b.name)
                b.ldk = ("ld", b.name)
            b.ldc += 16
            ins.then_inc(b.ld, 16)
            tok = (b.ldk, b.ld, b.ldc)
        else:
            b = R[0]
            if b.st is None:
                b.st = self.alloc_sem("st_" + b.name)
                b.stk = ("st", b.name)
            b.stc += 16
            ins.then_inc(b.st, 16)
            tok = (b.stk, b.st, b.stc)
        self.dma_tokens[tok[0]] = (tok[1], tok[2])
        for bb in R:
            k = tok[0]
            if k not in bb.r or bb.r[k][1] < tok[2]:
                bb.r[k] = (tok[1], tok[2])
        for bb in W:
            if bb is W[0] and skip:
                bb.w[tok[0]] = (tok[1], tok[2])
                bb.r = {}
            else:
                bb.w = {tok[0]: (tok[1], tok[2])}
                bb.r = {}
        return ins

    def barrier(self, bar_ap):
        Dv = self.E["dve"]
        for E in self.E.values():
            if E is Dv:
                continue
            assert not E.pending
            if E.cnt > 0:
                self._wait(Dv, E.key, E.sem, E.cnt)
        for k, (s, v) in self.dma_tokens.items():
            self._wait(Dv, k, s, v)
        if Dv.cnt > 0:
            self._wait(Dv, Dv.key, Dv.sem, Dv.cnt)
        ins = Dv.h.memset(bar_ap, 0.0)
        if Dv.cnt >= SEM_LIMIT:
            Dv.newsem()
        Dv.cnt += 1
        ins.then_inc(Dv.sem, 1)
        for E in self.E.values():
            if E is Dv:
                continue
            self._wait(E, Dv.key, Dv.sem, Dv.cnt)

    def final_wait(self):
        sp = self.E["sp"]
        for k, (s, v) in self.dma_tokens.items():
            self._wait(sp, k, s, v)
        for E in self.E.values():
            if E is sp or E.cnt == 0:
                continue
            self._wait(sp, E.key, E.sem, E.cnt)


def build(S=2048, NSEQ=2, dbg=False):
    NTB = S // 128
    NQT = S // 512
    nc = bass.Bass("TRN2", target_bir_lowering=False)

    def din(name, shape, dt=F32):
        return nc.dram_tensor(name, list(shape), dt, kind="ExternalInput").ap()

    x_d = din("x", [NSEQ, S, D])
    mem_d = din("mem", [NSEQ, NMEM, D])
    pos_d = din("pos", [128, NSEQ * NTB], I32)
    pv_d = din("pv", [128, NPV])
    grow_d = din("grow", [3, 128, D])
    invf_d = din("invf", [128, 32])
    ident_d = din("ident", [128, 128])
    tri_d = din("tri", [128, 128])
    winA_d = din("winA", [128, 8 * 1024])
    winB_d = din("winB", [128, 8 * 768])
    wuq_d = din("wuq", [128, 3 * 768])
    wukv_d = din("wukv", [128, 2 * 1024])
    wlru_d = din("wlru", [128, 4 * 2 * 128])
    wsl_d = din("wslots", [NSLOT, 128, 4096])
    scr_d = nc.dram_tensor("scr", [NSLOT, 128, 4096], BF16, kind="Internal").ap()
    out_d = nc.dram_tensor("out", [NSEQ, S, D], F32, kind="ExternalOutput").ap()
    if dbg:
        dbg_yT = nc.dram_tensor("dbg_yT", [NSEQ, 128, 8 * S], BF16, kind="ExternalOutput").ap()

    es = ExitStack()
    sch = Sched(nc, es)

    def T(scope, name, cols, dt):
        return scope.enter_context(nc.sbuf_tensor("sb_" + name, [128, cols], dt))

    mla_scale = 1.0 / math.sqrt(192.0)
    mem_scale = 1.0 / math.sqrt(256.0)

    def body():
        pers = es
        ident = T(pers, "ident", 128, BF16)
        tri = T(pers, "tri", 128, BF16)
        onesb = T(pers, "onesb", 128, BF16)
        onesf = T(pers, "onesf", 128, F32)
        pv = T(pers, "pv", NPVT, F32)
        cst = T(pers, "cst", 8, F32)
        invf = T(pers, "invf", 32, F32)
        posi = T(pers, "posi", NSEQ * NTB, I32)
        smallt = T(pers, "smallt", 32 * 4, F32)
        bar = T(pers, "bar", 4, F32)
        yT = T(pers, "yT", 8 * S, BF16)
        PS = pers.enter_context(nc.psum_tensor("ps", [128, 6 * 512], F32))
        TPS = pers.enter_context(nc.psum_tensor("tps", [128, 2 * 1024], BF16))

        constB = Buf("const")
        identB = Buf("identb")
        triB = Buf("trib")
        pvB = Buf("pvb")
        invfB = Buf("invfb")
        posB = Buf("posb")
        bankB = [Buf(f"bank{i}") for i in range(6)]
        tpB = [Buf(f"tpb{i}") for i in range(2)]
        smallB = [Buf(f"small{i}") for i in range(32)]
        yTB = [Buf(f"yT{i}") for i in range(NQT)]
        scrB = Buf("scr")
        _bufcache = {}

        def MB(name):
            import re as _re
            key = _re.sub(r"@\d", "@", name)
            if key not in _bufcache:
                _bufcache[key] = Buf(key.replace("@", "_"))
            return _bufcache[key]

        st = {"small": 0, "bank": 0, "pair": 0, "tp": 0, "alt": 0}

        def small():
            i = st["small"] % 32
            st["small"] += 1
            return smallt[:, i * 4:(i + 1) * 4], smallB[i]

        def nbank():
            i = st["bank"] % 6
            st["bank"] += 1
            return PS[:, i * 512:(i + 1) * 512], bankB[i]

        def npair():
            i = st["pair"] % 3
            st["pair"] += 1
            return PS[:, i * 1024:(i + 1) * 1024], [bankB[2 * i], bankB[2 * i + 1]]

        def ntp():
            i = st["tp"] % 2
            st["tp"] += 1
            return TPS[:, i * 1024:(i + 1) * 1024], tpB[i]

        def MM(out, lhsT, rhs, start, stop, R, W, inc=None):
            return sch.op("pe", lambda e: e.matmul(out, lhsT=lhsT, rhs=rhs, start=start, stop=stop),
                          R=R, W=W, inc=(stop if inc is None else inc))

        def TR(out, in_, R, W, last):
            return sch.op("pe", lambda e: e.transpose(out, in_, ident[:]), R=list(R) + [identB], W=W, inc=last)

        def ACT(out, in_, func, R, W, **kw):
            return sch.op("act", lambda e: e.activation(out=out, in_=in_, func=func, **kw), R=R, W=W)

        def TS(out, in0, s1, s2, op0, op1, R, W, eng="dve"):
            if s2 is None:
                return sch.op(eng, lambda e: e.tensor_scalar(out=out, in0=in0, scalar1=s1, scalar2=None, op0=op0),
                              R=R, W=W)
            return sch.op(eng, lambda e: e.tensor_scalar(out=out, in0=in0, scalar1=s1, scalar2=s2, op0=op0, op1=op1),
                          R=R, W=W)

        def TT(out, in0, in1, op, R, W, eng="dve"):
            return sch.op(eng, lambda e: e.tensor_tensor(out=out, in0=in0, in1=in1, op=op), R=R, W=W)

        def STT(out, in0, scalar, in1, op0, op1, R, W):
            return sch.op("dve", lambda e: e.scalar_tensor_tensor(out=out, in0=in0, scalar=scalar, in1=in1,
                                                                  op0=op0, op1=op1), R=R, W=W)

        def COPY(out, in_, R, W, eng=None):
            if eng is None:
                eng = "act" if st["alt"] % 2 == 0 else "dve"
                st["alt"] += 1
            if eng == "act":
                return sch.op("act", lambda e: e.activation(out=out, in_=in_, func=AF.Copy), R=R, W=W)
            return sch.op(eng, lambda e: e.tensor_copy(out=out, in_=in_), R=R, W=W)

        def RECIP(out, in_, R, W):
            return sch.op("dve", lambda e: e.reciprocal(out=out, in_=in_), R=R, W=W)

        def rstd_from_ss(ss, ssB, n, invd):
            for i in range(n):
                TS(ss[:, i:i + 1], ss[:, i:i + 1], invd[i], EPS, ALU.mult, ALU.add, R=[ssB], W=[ssB])
            ACT(ss[:, 0:n], ss[:, 0:n], AF.Sqrt, R=[ssB], W=[ssB])
            RECIP(ss[:, 0:n], ss[:, 0:n], R=[ssB], W=[ssB])

        sch.dma("sp", pv[:, 0:NPV], pv_d[:, :], W=[pvB])
        sch.dma("sp", invf[:], invf_d[:, :], W=[invfB])
        sch.dma("sp", posi[:], pos_d[:, :], W=[posB])
        sch.dma("pool", ident[:], ident_d[:, :], W=[identB])
        sch.dma("pool", tri[:], tri_d[:, :], W=[triB])
        sch.op("dve", lambda e: e.memset(onesb[:], 1.0), W=[constB])
        sch.op("dve", lambda e: e.memset(onesf[:], 1.0), W=[constB])
        sch.op("dve", lambda e: e.memset(cst[:, 0:1], 1.0), W=[constB])
        sch.op("dve", lambda e: e.memset(cst[:, 1:2], math.pi), W=[constB])
        tmpc, tmpB = small()
        tmpc2, tmpB2 = small()
        TS(tmpc2, pv[:, LAM:LAM + 4], -1.0, None, ALU.mult, None, R=[pvB], W=[tmpB2])
        TT(tmpc, pv[:, LAM:LAM + 4], tmpc2, ALU.max, R=[pvB, tmpB2], W=[tmpB])
        ACT(tmpc, tmpc, AF.Exp, R=[tmpB], W=[tmpB], scale=-1.0)
        TS(tmpc, tmpc, 1.0, None, ALU.add, None, R=[tmpB], W=[tmpB])
        ACT(tmpc, tmpc, AF.Ln, R=[tmpB], W=[tmpB])
        TS(tmpc2, tmpc2, 0.0, None, ALU.max, None, R=[tmpB2], W=[tmpB2])
        TT(tmpc, tmpc, tmpc2, ALU.add, R=[tmpB, tmpB2], W=[tmpB])
        TS(pv[:, HC:HC + 4], tmpc, -4.0, None, ALU.mult, None, R=[tmpB], W=[pvB])
        TS(pv[:, H2:H2 + 4], tmpc, -8.0, None, ALU.mult, None, R=[tmpB], W=[pvB])
        TS(pv[:, BAH:BAH + 4], pv[:, BA:BA + 4], 0.5, None, ALU.mult, None, R=[pvB], W=[pvB])
        TS(pv[:, BXH:BXH + 4], pv[:, BX:BX + 4], 0.5, None, ALU.mult, None, R=[pvB], W=[pvB])

        scr_state = {"done": False}

        def convert_scratch():
            if scr_state["done"]:
                return
            scr_state["done"] = True
            order = [23, 24, 25, 26] + list(range(23))
            for sl in order:
                sch.dma("pool", scr_d[sl].rearrange("p (a b) -> p a b", a=2),
                        wsl_d[sl].rearrange("p (a b) -> p a b", a=2), W=[scrB])

        def nt_a(src, srcB, Dn, junk, junkB):
            ss, ssB = small()
            ACT(junk[:, 0:Dn], src, AF.Square, R=srcB, W=[junkB, ssB], accum_out=ss[:, 0:1])
            rstd_from_ss(ss, ssB, 1, [1.0 / Dn])
            return ss, ssB

        def nt_b(src, srcB, Dn, ss, ssB, gcols, dst3, dstB, xs, xsB):
            nk = Dn // 128
            ACT(xs[:, 0:Dn], src, AF.Copy, R=list(srcB) + [ssB], W=[xsB], scale=ss[:, 0:1])
            tp, tB = ntp()
            for k in range(nk):
                TR(tp[:, k * 128:(k + 1) * 128], xs[:, k * 128:(k + 1) * 128], R=[xsB], W=[tB], last=(k == nk - 1))
            g3 = gcols.unsqueeze(2).broadcast_to([128, nk, 128])
            TT(dst3, tp[:, 0:Dn].rearrange("p (k t) -> p k t", k=nk), g3, ALU.mult, R=[tB, pvB], W=dstB)

        def nt(src, srcB, Dn, gcols, dst3, dstB, xs, xsB, junk, junkB):
            ss, ssB = nt_a(src, srcB, Dn, junk, junkB)
            nt_b(src, srcB, Dn, ss, ssB, gcols, dst3, dstB, xs, xsB)

        def rope(src3, srcB, G, cos, sin, csB, dst3, dstB, tA, tAB, tB_, tBB, extra=()):
            srcB = list(srcB) + list(extra)
            x1 = src3[:, :, 0:32]
            x2 = src3[:, :, 32:64]
            cb = cos.unsqueeze(1).broadcast_to([128, G, 32])
            sb = sin.unsqueeze(1).broadcast_to([128, G, 32])
            a3 = tA[:, 0:G * 32].rearrange("p (g d) -> p g d", g=G)
            b3 = tB_[:, 0:G * 32].rearrange("p (g d) -> p g d", g=G)
            TT(a3, x1, cb, ALU.mult, R=srcB + [csB], W=[tAB])
            TT(b3, x2, sb, ALU.mult, R=srcB + [csB], W=[tBB])
            TT(dst3[:, :, 0:32], a3, b3, ALU.subtract, R=[tAB, tBB], W=dstB)
            TT(a3, x2, cb, ALU.mult, R=srcB + [csB], W=[tAB])
            TT(b3, x1, sb, ALU.mult, R=srcB + [csB], W=[tBB])
            TT(dst3[:, :, 32:64], a3, b3, ALU.add, R=[tAB, tBB], W=dstB)

        def feat_norm(srcs, srcBs, gcol0, Dn, tt, dst_off, sq, sqB, rt, rtB):
            n = len(srcs)
            bank, bB = nbank()
            for i in range(n):
                q, qB = sq[i % 2], sqB[i % 2]
                ACT(q[:, :], srcs[i][:, tt * 512:(tt + 1) * 512], AF.Square, R=[srcBs[i]], W=[qB])
                MM(bank, onesf[:], q[:, :], i == 0, i == n - 1, R=[qB, constB], W=[bB], inc=True)
            TS(rt[:, :], bank, 1.0 / Dn, EPS, ALU.mult, ALU.add, R=[bB], W=[rtB])
            ACT(rt[:, :], rt[:, :], AF.Sqrt, R=[rtB], W=[rtB])
            RECIP(rt[:, :], rt[:, :], R=[rtB], W=[rtB])
            for i in range(n):
                c = dst_off + i
                STT(yT[:, c * S + tt * 512: c * S + (tt + 1) * 512], srcs[i][:, tt * 512:(tt + 1) * 512],
                    pv[:, gcol0 + i:gcol0 + i + 1], rt[:, :], ALU.mult, ALU.mult,
                    R=[srcBs[i], pvB, rtB], W=[yTB[tt]])

        for s in range(NSEQ):
            sfx = f"_{s}"
            with ExitStack() as s1:
                cqnT = T(s1, "cqnT" + sfx, 3 * S, BF16)
                ckvnT = T(s1, "ckvnT" + sfx, 2 * S, BF16)
                kpeT = T(s1, "kpeT" + sfx, S, BF16)
                cs = T(s1, "cs" + sfx, NTB * 64, F32)
                wuq = T(s1, "wuq" + sfx, 3 * 768, BF16)
                wukv = T(s1, "wukv" + sfx, 2 * 1024, BF16)
                wuqB = MB(f"wuq@{s}")
                wukvB = MB(f"wukv@{s}")
                latB = [MB(f"lat@{s}_{i}") for i in range(NTB)]
                csB = MB(f"cs@{s}")
                with ExitStack() as s2:
                    xnT = T(s2, "xnT" + sfx, 8 * S, BF16)
                    xnT3 = xnT[:, :].rearrange("p (k t) -> p k t", k=8)
                    xnTB = [MB(f"xnT@{s}_{i}") for i in range(NTB)]
                    winA = T(s2, "winA" + sfx, 8 * 1024, BF16)
                    wlru = T(s2, "wlru" + sfx, 4 * 2 * 128, BF16)
                    winAB = MB(f"winA@{s}")
                    wlruB = MB(f"wlru@{s}")
                    sch.dma("pool", winA[:, :].rearrange("p (k n) -> p k n", k=8),
                            winA_d[:, :].rearrange("p (k n) -> p k n", k=8), W=[winAB])
                    sch.dma("pool", wlru[:], wlru_d[:, :], W=[wlruB])
                    winB = T(s2, "winB" + sfx, 8 * 768, BF16)
                    winBB = MB(f"winB@{s}")
                    sch.dma("pool", winB[:, :].rearrange("p (k n) -> p k n", k=8),
                            winB_d[:, :].rearrange("p (k n) -> p k n", k=8), W=[winBB])
                    sch.dma("pool", wuq[:, :].rearrange("p (k n) -> p k n", k=3),
                            wuq_d[:, :].rearrange("p (k n) -> p k n", k=3), W=[wuqB])
                    sch.dma("pool", wukv[:, :].rearrange("p (k n) -> p k n", k=2),
                            wukv_d[:, :].rearrange("p (k n) -> p k n", k=2), W=[wukvB])
                    if F_WPRE:
                        convert_scratch()
                    with ExitStack() as s2a:
                        xtmp = [T(s2a, f"xtmp{i}" + sfx, 1024, F32) for i in range(3)]
                        xtmpB = [MB(f"xtmp@{s}_{i}") for i in range(3)]
                        xs = [T(s2a, f"xs{i}" + sfx, 1024, BF16) for i in range(2)]
                        xsB = [MB(f"xs@{s}_{i}") for i in range(2)]
                        junk = T(s2a, "junk" + sfx, 1024, BF16)
                        junkB = MB(f"junk@{s}")
                        ang = T(s2a, "ang" + sfx, NTB * 64, F32)
                        kf = T(s2a, "kf" + sfx, NTB * 64, F32)
                        ki = T(s2a, "ki" + sfx, NTB * 64, I32)
                        posf = T(s2a, "posf" + sfx, NTB, F32)
                        angB = MB(f"ang@{s}")
                        kfB = MB(f"kf@{s}")
                        kiB = MB(f"ki@{s}")
                        posfB = MB(f"posf@{s}")
                        pend = None
                        for tb in range(NTB + 1):
                            cur = None
                            if tb < NTB:
                                i = tb % 3
                                sch.dma("sp", xtmp[i][:], x_d[s, tb * 128:(tb + 1) * 128, :], W=[xtmpB[i]])
                                cur = (tb,) + nt_a(xtmp[i][:, :], [xtmpB[i]], 1024, junk, junkB)
                            if not F_PIPEA:
                                pend = cur
                                cur = None
                            if pend is not None:
                                ptb, pss, pssB = pend
                                pi = ptb % 3
                                nt_b(xtmp[pi][:, :], [xtmpB[pi]], 1024, pss, pssB, pv[:, GPM:GPM + 8],
                                     xnT3[:, :, ptb * 128:(ptb + 1) * 128], [xnTB[ptb]], xs[ptb % 2], xsB[ptb % 2])
                            pend = cur
                        COPY(posf[:], posi[:, s * NTB:(s + 1) * NTB], R=[posB], W=[posfB], eng="dve")
                        ang3 = ang[:, :].rearrange("p (t d) -> p t d", t=NTB)
                        TT(ang3[:, :, 0:32], posf[:, :].unsqueeze(2).broadcast_to([128, NTB, 32]),
                           invf[:, :].unsqueeze(1).broadcast_to([128, NTB, 32]), ALU.mult,
                           R=[posfB, invfB], W=[angB])
                        TS(ang3[:, :, 32:64], ang3[:, :, 0:32], math.pi / 2, None, ALU.add, None, R=[angB], W=[angB])
                        TS(kf[:], ang[:], 1.0 / (2 * math.pi), None, ALU.mult, None, R=[angB], W=[kfB])
                        COPY(ki[:], kf[:], R=[kfB], W=[kiB], eng="dve")
                        COPY(kf[:], ki[:], R=[kiB], W=[kfB], eng="dve")
                        C1 = 6.28125
                        C2 = 2 * math.pi - C1
                        STT(ang[:], kf[:], -C1, ang[:], ALU.mult, ALU.add, R=[kfB, angB], W=[angB])
                        STT(ang[:], kf[:], -C2, ang[:], ALU.mult, ALU.add, R=[kfB, angB], W=[angB])
                        TS(kf[:], ang[:], math.pi, None, ALU.is_gt, None, R=[angB], W=[kfB])
                        STT(ang[:], kf[:], -2 * math.pi, ang[:], ALU.mult, ALU.add, R=[kfB, angB], W=[angB])
                        TS(kf[:], ang[:], -math.pi, None, ALU.is_lt, None, R=[angB], W=[kfB])
                        STT(ang[:], kf[:], 2 * math.pi, ang[:], ALU.mult, ALU.add, R=[kfB, angB], W=[angB])
                        TS(ang[:], ang[:], math.pi, -math.pi, ALU.min, ALU.max, R=[angB], W=[angB])
                        ACT(cs[:], ang[:], AF.Sin, R=[angB], W=[csB])
                        sch.barrier(bar[:, 0:1])

                    with ExitStack() as s3:
                        convert_scratch()
                        NSEG = NQT
                        Rb = [T(s3, f"R{i}" + sfx, S, F32) for i in range(6)]
                        RB = [[MB(f"R@{s}_{i}_{g}") for g in range(NSEG)] for i in range(6)]
                        Ubf = T(s3, "Ubf" + sfx, S, BF16)
                        UbfB = [MB(f"Ubf@{s}_{g}") for g in range(NSEG)]
                        YL = [T(s3, f"YL{i}" + sfx, S, BF16) for i in range(4)]
                        YLB = [[MB(f"YL@{s}_{i}_{g}") for g in range(NSEG)] for i in range(4)]
                        sq = [T(s3, f"sq{i}" + sfx, 512, F32) for i in range(2)]
                        sqB = [MB(f"sq@{s}_{i}") for i in range(2)]
                        rt = T(s3, "rt" + sfx, 512, F32)
                        rtB = MB(f"rt@{s}")
                        LX, LG, U, A, TI, R6 = Rb
                        LXB, LGB, UB, AB, TIB, R6B = RB
                        segs = [slice(g * 512, (g + 1) * 512) for g in range(NSEG)]
                        for c in range(4):
                            def col(base):
                                return pv[:, base + c:base + c + 1]
                            for g in range(NSEG):
                                sl = segs[g]
                                bank, bB = nbank()
                                for kc in range(8):
                                    MM(bank, winA[:, kc * 1024 + c * 128: kc * 1024 + (c + 1) * 128],
                                       xnT[:, kc * S + g * 512: kc * S + (g + 1) * 512], kc == 0, kc == 7,
                                       R=[winAB] + xnTB[g * 4:(g + 1) * 4], W=[bB])
                                COPY(LX[:, sl], bank, R=[bB], W=[LXB[g]], eng="act")
                                bank, bB = nbank()
                                for kc in range(8):
                                    MM(bank, winA[:, kc * 1024 + 512 + c * 128: kc * 1024 + 512 + (c + 1) * 128],
                                       xnT[:, kc * S + g * 512: kc * S + (g + 1) * 512], kc == 0, kc == 7,
                                       R=[winAB] + xnTB[g * 4:(g + 1) * 4], W=[bB])
                                COPY(LG[:, sl], bank, R=[bB], W=[LGB[g]], eng="dve")
                            for g in range(NSEG):
                                sl = segs[g]
                                e = (g + 1) * 512
                                TS(U[:, sl], LX[:, sl], pv[:, CW + 3 * 4 + c:CW + 3 * 4 + c + 1], col(CB), ALU.mult, ALU.add,
                                   R=[LXB[g], pvB], W=[UB[g]])
                                for k, sh in ((2, 1), (1, 2), (0, 3)):
                                    lo = max(g * 512, sh)
                                    rr = [LXB[g], pvB, UB[g]] + ([LXB[g - 1]] if g > 0 else [])
                                    STT(U[:, lo:e], LX[:, lo - sh:e - sh], pv[:, CW + k * 4 + c:CW + k * 4 + c + 1], U[:, lo:e],
                                        ALU.mult, ALU.add, R=rr, W=[UB[g]])
                                COPY(Ubf[:, sl], U[:, sl], R=[UB[g]], W=[UbfB[g]], eng="act")
                            for g in range(NSEG):
                                sl = segs[g]
                                bank, bB = nbank()
                                MM(bank, wlru[:, (c * 2 + 0) * 128:(c * 2 + 1) * 128], Ubf[:, sl], True, True,
                                   R=[wlruB, UbfB[g]], W=[bB])
                                ACT(LX[:, sl], bank, AF.Tanh, R=[bB, pvB], W=[LXB[g]], scale=0.5, bias=col(BAH))
                                bank, bB = nbank()
                                MM(bank, wlru[:, (c * 2 + 1) * 128:(c * 2 + 2) * 128], Ubf[:, sl], True, True,
                                   R=[wlruB, UbfB[g]], W=[bB])
                                ACT(TI[:, sl], bank, AF.Tanh, R=[bB, pvB], W=[TIB[g]], scale=0.5, bias=col(BXH))
                            for g in range(NSEG):
                                sl = segs[g]
                                STT(TI[:, sl], TI[:, sl], 1.0, U[:, sl], ALU.add, ALU.mult, R=[TIB[g], UB[g]], W=[TIB[g]])
                            for g in range(NSEG):
                                sl = segs[g]
                                ACT(A[:, sl], LX[:, sl], AF.Exp, R=[LXB[g], pvB], W=[AB[g]], scale=col(HC), bias=col(HC))
                                ACT(R6[:, sl], LX[:, sl], AF.Exp, R=[LXB[g], pvB], W=[R6B[g]], scale=col(H2), bias=col(H2))
                            for g in range(NSEG):
                                sl = segs[g]
                                ACT(R6[:, sl], R6[:, sl], AF.Sqrt, R=[R6B[g], constB], W=[R6B[g]], scale=-1.0, bias=cst[:, 0:1])
                            for g in range(NSEG):
                                sl = segs[g]
                                ACT(LX[:, sl], LG[:, sl], AF.Square, R=[LGB[g]], W=[LXB[g]])
                            for g in range(NSEG):
                                sl = segs[g]
                                STT(TI[:, sl], TI[:, sl], 0.5, R6[:, sl], ALU.mult, ALU.mult, R=[TIB[g], R6B[g]], W=[TIB[g]])
                            for g in range(NSEG):
                                sl = segs[g]
                                if g == 0:
                                    sch.op("dve", lambda e, sl=sl: e.tensor_tensor_scan(
                                        out=U[:, sl], data0=A[:, sl], data1=TI[:, sl], initial=0.0,
                                        op0=ALU.mult, op1=ALU.add), R=[AB[g], TIB[g]], W=[UB[g]])
                                else:
                                    sch.op("dve", lambda e, sl=sl, g=g: e.tensor_tensor_scan(
                                        out=U[:, sl], data0=A[:, sl], data1=TI[:, sl],
                                        initial=U[:, g * 512 - 1:g * 512],
                                        op0=ALU.mult, op1=ALU.add), R=[AB[g], TIB[g], UB[g - 1]], W=[UB[g]])
                            for g in range(NSEG):
                                sl = segs[g]
                                TS(LX[:, sl], LX[:, sl], 0.044715, 1.0, ALU.mult, ALU.add, R=[LXB[g]], W=[LXB[g]])
                                TT(LX[:, sl], LX[:, sl], LG[:, sl], ALU.mult, R=[LXB[g], LGB[g]], W=[LXB[g]])
                            for g in range(NSEG):
                                sl = segs[g]
                                ACT(LX[:, sl], LX[:, sl], AF.Tanh, R=[LXB[g]], W=[LXB[g]], scale=0.7978845608028654)
                            for g in range(NSEG):
                                sl = segs[g]
                                STT(LX[:, sl], LX[:, sl], 1.0, LG[:, sl], ALU.add, ALU.mult, R=[LXB[g], LGB[g]], W=[LXB[g]])
                                STT(YL[c][:, sl], LX[:, sl], 0.5, U[:, sl], ALU.mult, ALU.mult, R=[LXB[g], UB[g]], W=[YLB[c][g]])
                        for tt in range(NQT):
                            feat_norm([YL[i] for i in range(4)], [YLB[i][tt] for i in range(4)], GLO, 512, tt, 0, sq, sqB, rt, rtB)
                        sch.barrier(bar[:, 0:1])

                    with ExitStack() as s4:
                        lat = [T(s4, f"latbf{i}" + sfx, 768, BF16) for i in range(2)]
                        latbB = [MB(f"latbf@{s}_{i}") for i in range(2)]
                        junk = T(s4, "junkb" + sfx, 512, BF16)
                        junkB = MB(f"junkb@{s}")
                        rtmp = [T(s4, f"rtmp{i}" + sfx, 128, F32) for i in range(2)]
                        rtmpB = [MB(f"rtmp@{s}_{i}") for i in range(2)]
                        cqnT3 = cqnT[:, :].rearrange("p (k t) -> p k t", k=3)
                        ckvnT3 = ckvnT[:, :].rearrange("p (k t) -> p k t", k=2)
                        def lat_front(tb):
                            bA, bAB = nbank()
                            bBk, bBB = nbank()
                            for kc in range(8):
                                lhs = xnT[:, kc * S + tb * 128: kc * S + (tb + 1) * 128]
                                MM(bA, lhs, winB[:, kc * 768: kc * 768 + 512], kc == 0, kc == 7,
                                   R=[winBB, xnTB[tb]], W=[bAB])
                            for kc in range(8):
                                lhs = xnT[:, kc * S + tb * 128: kc * S + (tb + 1) * 128]
                                MM(bBk[:, 0:256], lhs, winB[:, kc * 768 + 512: kc * 768 + 768], kc == 0, kc == 7,
                                   R=[winBB, xnTB[tb]], W=[bBB])
                            ss, ssB = small()
                            return (tb, bA, bAB, bBk, bBB, ss, ssB)

                        def lat_stats(stt):
                            tb, bA, bAB, bBk, bBB, ss, ssB = stt
                            ACT(junk[:, 0:384], bA[:, 0:384], AF.Square, R=[bAB], W=[junkB, ssB], accum_out=ss[:, 0:1])
                            ACT(junk[:, 0:256], bBk[:, 0:256], AF.Square, R=[bBB], W=[junkB, ssB], accum_out=ss[:, 1:2])
                            rstd_from_ss(ss, ssB, 2, [1.0 / 384, 1.0 / 256])

                        def lat_back(stt):
                            tb, bA, bAB, bBk, bBB, ss, ssB = stt
                            bsl = slice(tb * 128, (tb + 1) * 128)
                            L = lat[tb % 2]
                            LB = latbB[tb % 2]
                            ACT(L[:, 0:384], bA[:, 0:384], AF.Copy, R=[bAB, ssB], W=[LB], scale=ss[:, 0:1])
                            ACT(L[:, 384:640], bBk[:, 0:256], AF.Copy, R=[bBB, ssB], W=[LB], scale=ss[:, 1:2])
                            rope(bA[:, 384:512].rearrange("p (g d) -> p g d", g=2), [bAB], 2,
                                 cs[:, tb * 64 + 32: tb * 64 + 64], cs[:, tb * 64: tb * 64 + 32], csB,
                                 L[:, 640:768].rearrange("p (g d) -> p g d", g=2), [LB],
                                 rtmp[0], rtmpB[0], rtmp[1], rtmpB[1], extra=[LB])
                            tp, tB = ntp()
                            for k in range(6):
                                TR(tp[:, k * 128:(k + 1) * 128], L[:, k * 128:(k + 1) * 128], R=[LB], W=[tB], last=(k == 5))
                            TT(cqnT3[:, :, bsl], tp[:, 0:384].rearrange("p (k t) -> p k t", k=3),
                               pv[:, GQ:GQ + 3].unsqueeze(2).broadcast_to([128, 3, 128]), ALU.mult,
                               R=[tB, pvB], W=[latB[tb]])
                            TT(ckvnT3[:, :, bsl], tp[:, 384:640].rearrange("p (k t) -> p k t", k=2),
                               pv[:, GKV:GKV + 2].unsqueeze(2).broadcast_to([128, 2, 128]), ALU.mult,
                               R=[tB, pvB], W=[latB[tb]])
                            COPY(kpeT[:, bsl], tp[:, 640:768], R=[tB], W=[latB[tb]], eng="dve")

                        depth = LAT_DEPTH if F_LATSKEW else 0
                        sts = {}
                        for tb in range(NTB + depth):
                            if tb < NTB:
                                sts[tb] = lat_front(tb)
                            if depth > 0 and tb - depth >= 0:
                                lat_back(sts.pop(tb - depth))
                            if tb < NTB:
                                lat_stats(sts[tb])
                                if depth == 0:
                                    lat_back(sts.pop(tb))
                        sch.barrier(bar[:, 0:1])
                with ExitStack() as s5:
                    qnT = T(s5, "qnT" + sfx, 4 * S, BF16)
                    qpeT = T(s5, "qpeT" + sfx, 2 * S, BF16)
                    knT = T(s5, "knT" + sfx, 4 * S, BF16)
                    Vt = T(s5, "Vt" + sfx, NTB * 512, BF16)
                    qB = [MB(f"q@{s}_{i}") for i in range(NQT)]
                    kB = [MB(f"k@{s}_{i}") for i in range(NQT)]
                    qpeB = [MB(f"qpe@{s}_{i}") for i in range(NTB)]
                    vB = [MB(f"v@{s}_{i}") for i in range(NTB)]
                    qpeT3 = qpeT[:, :].rearrange("p (k t) -> p k t", k=2)
                    with ExitStack() as s5b:
                        qpb = [T(s5b, f"qpb{i}" + sfx, 256, BF16) for i in range(2)]
                        qpbB = [MB(f"qpb@{s}_{i}") for i in range(2)]
                        rtmp = [T(s5b, f"rtq{i}" + sfx, 128, F32) for i in range(2)]
                        rtmpB = [MB(f"rtq@{s}_{i}") for i in range(2)]
                        for tt in range(NQT):
                            tsl = slice(tt * 512, (tt + 1) * 512)
                            lb4 = latB[tt * 4:(tt + 1) * 4]
                            for h in range(4):
                                bank, bB = nbank()
                                for kc in range(3):
                                    MM(bank, wuq[:, kc * 768 + h * 128: kc * 768 + (h + 1) * 128],
                                       cqnT[:, kc * S + tt * 512: kc * S + (tt + 1) * 512], kc == 0, kc == 2,
                                       R=[wuqB] + lb4, W=[bB])
                                COPY(qnT[:, h * S + tt * 512: h * S + (tt + 1) * 512], bank, R=[bB], W=[qB[tt]])
                            for h in range(4):
                                bank, bB = nbank()
                                for kc in range(2):
                                    MM(bank, wukv[:, kc * 1024 + h * 128: kc * 1024 + (h + 1) * 128],
                                       ckvnT[:, kc * S + tt * 512: kc * S + (tt + 1) * 512], kc == 0, kc == 1,
                                       R=[wukvB] + lb4, W=[bB])
                                COPY(knT[:, h * S + tt * 512: h * S + (tt + 1) * 512], bank, R=[bB], W=[kB[tt]])
                            def stA(tb):
                                bank, bB = nbank()
                                for kc in range(3):
                                    MM(bank[:, 0:256], cqnT[:, kc * S + tb * 128: kc * S + (tb + 1) * 128],
                                       wuq[:, kc * 768 + 512: kc * 768 + 768], kc == 0, kc == 2,
                                       R=[wuqB, latB[tb]], W=[bB])
                                Q = qpb[tb % 2]
                                QB = qpbB[tb % 2]
                                rope(bank[:, 0:256].rearrange("p (g d) -> p g d", g=4), [bB], 4,
                                     cs[:, tb * 64 + 32: tb * 64 + 64], cs[:, tb * 64: tb * 64 + 32], csB,
                                     Q[:, :].rearrange("p (g d) -> p g d", g=4), [QB],
                                     rtmp[0], rtmpB[0], rtmp[1], rtmpB[1])

                            def stB(tb):
                                bsl = slice(tb * 128, (tb + 1) * 128)
                                Q = qpb[tb % 2]
                                QB = qpbB[tb % 2]
                                tp, tB = ntp()
                                for k in range(2):
                                    TR(tp[:, k * 128:(k + 1) * 128], Q[:, k * 128:(k + 1) * 128], R=[QB], W=[tB], last=(k == 1))
                                COPY(qpeT3[:, :, bsl], tp[:, 0:256].rearrange("p (k t) -> p k t", k=2),
                                     R=[tB], W=[qpeB[tb]], eng="act")

                            def stV(tb):
                                bank, bB = nbank()
                                for kc in range(2):
                                    MM(bank, ckvnT[:, kc * S + tb * 128: kc * S + (tb + 1) * 128],
                                       wukv[:, kc * 1024 + 512: kc * 1024 + 1024], kc == 0, kc == 1,
                                       R=[wukvB, latB[tb]], W=[bB])
                                COPY(Vt[:, tb * 512:(tb + 1) * 512], bank, R=[bB], W=[vB[tb]], eng="dve")

                            t0 = tt * 4
                            if F_QKVSKEW:
                                stA(t0); stV(t0); stA(t0 + 1); stV(t0 + 1); stB(t0)
                                stA(t0 + 2); stV(t0 + 2); stB(t0 + 1)
                                stA(t0 + 3); stV(t0 + 3); stB(t0 + 2); stB(t0 + 3)
                            else:
                                for tb in range(t0, t0 + 4):
                                    stA(tb); stB(tb); stV(tb)
                        sch.barrier(bar[:, 0:1])
                    with ExitStack() as s6:
                        YM = [T(s6, f"YM{i}" + sfx, S, F32) for i in range(4)]
                        YMB = [MB(f"YM@{s}_{i}") for i in range(4)]
                        Pb = [T(s6, f"Pb{i}" + sfx, 512, BF16) for i in range(3)]
                        PbB = [MB(f"Pb@{s}_{i}") for i in range(3)]
                        rden = [T(s6, f"rden{i}" + sfx, 512, F32) for i in range(2)]
                        rdenB = [MB(f"rden@{s}_{i}") for i in range(2)]
                        sq = [T(s6, f"sqm{i}" + sfx, 512, F32) for i in range(2)]
                        sqB = [MB(f"sqm@{s}_{i}") for i in range(2)]
                        rt = T(s6, "rtm" + sfx, 512, F32)
                        rtB = MB(f"rtm@{s}")
                        it = 0
                        pcount = 0
                        for qi in range(NQT):
                            for h in range(4):
                                hb = (h % 2) * 64
                                nkc = 4 * qi + 4
                                ob = 4 if it % 2 == 0 else 2
                                O, OB = PS[:, ob * 512:(ob + 1) * 512], bankB[ob]
                                DN, DNB = PS[:, (ob + 1) * 512:(ob + 2) * 512], bankB[ob + 1]

                                def c0_of(kc):
                                    return 0 if kc < 4 * qi else (kc - 4 * qi) * 128

                                def emitS(kc):
                                    sbi = kc % 2
                                    Sb, SB = PS[:, sbi * 512:(sbi + 1) * 512], bankB[sbi]
                                    c0 = c0_of(kc)
                                    MM(Sb[:, c0:512], knT[:, h * S + kc * 128: h * S + (kc + 1) * 128],
                                       qnT[:, h * S + qi * 512 + c0: h * S + (qi + 1) * 512], True, False,
                                       R=[kB[kc // 4], qB[qi]], W=[SB])
                                    MM(Sb[:, c0:512], kpeT[hb:hb + 64, kc * 128:(kc + 1) * 128],
                                       qpeT[hb:hb + 64, (h // 2) * S + qi * 512 + c0: (h // 2) * S + (qi + 1) * 512],
                                       False, True, R=[latB[kc]] + qpeB[qi * 4:(qi + 1) * 4], W=[SB])

                                emitS(0)
                                for kc in range(nkc):
                                    if kc + 1 < nkc:
                                        emitS(kc + 1)
                                    sbi = kc % 2
                                    Sb, SB = PS[:, sbi * 512:(sbi + 1) * 512], bankB[sbi]
                                    c0 = c0_of(kc)
                                    P, PB = Pb[pcount % 3], PbB[pcount % 3]
                                    pcount += 1
                                    ACT(P[:, c0:512], Sb[:, c0:512], AF.Exp, R=[SB], W=[PB], scale=mla_scale)
                                    if kc >= 4 * qi:
                                        TT(P[:, c0:c0 + 128], P[:, c0:c0 + 128], tri[:, :], ALU.mult, R=[PB, triB], W=[PB])
                                    MM(O[:, c0:512], Vt[:, kc * 512 + h * 128: kc * 512 + (h + 1) * 128], P[:, c0:512],
                                       kc == 0, kc == nkc - 1, R=[vB[kc], PB], W=[OB])
                                    MM(DN[:, c0:512], onesb[:, :], P[:, c0:512], kc == 0, kc == nkc - 1,
                                       R=[constB, PB], W=[DNB])
                                rd, rdB = rden[it % 2], rdenB[it % 2]
                                RECIP(rd[:, :], DN, R=[DNB], W=[rdB])
                                TT(YM[h][:, qi * 512:(qi + 1) * 512], O, rd[:, :], ALU.mult, R=[OB, rdB], W=[YMB[h]])
                                it += 1
                            feat_norm(YM, YMB, GMO, 512, qi, 4, sq, sqB, rt, rtB)
                        sch.barrier(bar[:, 0:1])
            if dbg:
                dB = MB(f"dbgy@{s}")
                sch.dma("sp", dbg_yT[s], yT[:, :], R=yTB + [dB])

            with ExitStack() as p2:
                G = [T(p2, f"G{i}" + sfx, 1024, F32) for i in range(3)]
                GB = [MB(f"G@{s}_{i}") for i in range(3)]
                for i in range(3):
                    sch.dma("sp", G[i][:], grow_d[i], W=[GB[i]])
                mkT = T(p2, "mkT" + sfx, 8 * 256, BF16)
                mv = T(p2, "mv" + sfx, 2 * 1024, BF16)
                mkB = MB(f"mk@{s}")
                mvB = MB(f"mv@{s}")
                with ExitStack() as pm:
                    mnT = T(pm, "mnT" + sfx, 8 * 256, BF16)
                    mnB = [MB(f"mn@{s}_{i}") for i in range(2)]
                    memt = [T(pm, f"memt{i}" + sfx, 1024, F32) for i in range(2)]
                    memtB = [MB(f"memt@{s}_{i}") for i in range(2)]
                    xs = [T(pm, f"xsm{i}" + sfx, 1024, BF16) for i in range(2)]
                    xsB = [MB(f"xsm@{s}_{i}") for i in range(2)]
                    junk = T(pm, "junkm" + sfx, 1024, BF16)
                    junkB = MB(f"junkm@{s}")
                    wm = T(pm, "wm" + sfx, 4 * 4096, BF16)
                    wmB = [MB(f"wm@{s}_{i}") for i in range(4)]
                    for i in range(4):
                        sch.dma("sp", wm[:, i * 4096:(i + 1) * 4096], scr_d[SLOT_MK + i], R=[scrB], W=[wmB[i]])
                    mnT3 = mnT[:, :].rearrange("p (k t) -> p k t", k=8)
                    for mc in range(2):
                        sch.dma("sp", memt[mc][:], mem_d[s, mc * 128:(mc + 1) * 128, :], W=[memtB[mc]])
                    for mc in range(2):
                        nt(memt[mc][:, :], [memtB[mc]], 1024, pv[:, GMKV:GMKV + 8],
                           mnT3[:, :, mc * 128:(mc + 1) * 128], [mnB[mc]], xs[mc], xsB[mc], junk, junkB)
                    for sl in range(2):
                        W_ = wm[:, sl * 4096:(sl + 1) * 4096]
                        for cc in range(4):
                            c = sl * 4 + cc
                            bank, bB = nbank()
                            for kc in range(8):
                                MM(bank[:, 0:256], W_[:, cc * 1024 + kc * 128: cc * 1024 + (kc + 1) * 128],
                                   mnT[:, kc * 256:(kc + 1) * 256], kc == 0, kc == 7, R=[wmB[sl]] + mnB, W=[bB])
                            COPY(mkT[:, c * 256:(c + 1) * 256], bank[:, 0:256], R=[bB], W=[mkB])
                    for fh in range(2):
                        W_ = wm[:, (2 + fh) * 4096:(3 + fh) * 4096]
                        for mc in range(2):
                            bank, bB = nbank()
                            for kc in range(8):
                                MM(bank, mnT[:, kc * 256 + mc * 128: kc * 256 + (mc + 1) * 128],
                                   W_[:, kc * 512:(kc + 1) * 512], kc == 0, kc == 7, R=[wmB[2 + fh], mnB[mc]], W=[bB])
                            COPY(mv[:, mc * 1024 + fh * 512: mc * 1024 + (fh + 1) * 512], bank, R=[bB], W=[mvB])
                    sch.barrier(bar[:, 0:1])

                NRG, NRF = 3, 2
                ringG = T(p2, "ringG" + sfx, NRG * 4096, BF16)
                ringF = T(p2, "ringF" + sfx, NRF * 4096, BF16)
                Ht = [T(p2, f"Ht{i}" + sfx, 4 * 1024, F32) for i in range(2)]
                HtB = [[MB(f"Ht@{s}_{i}_{tb}") for tb in range(4)] for i in range(2)]
                xnF = T(p2, "xnF" + sfx, 8 * 512, BF16)
                xnFB = [MB(f"xnF@{s}_{i}") for i in range(4)]
                xnF3 = xnF[:, :].rearrange("p (k t) -> p k t", k=8)
                xnG = T(p2, "xnG" + sfx, 8 * 512, BF16)
                xnGB = [MB(f"xnG@{s}_{i}") for i in range(4)]
                xnG3 = xnG[:, :].rearrange("p (k t) -> p k t", k=8)
                PLh = T(p2, "PLh" + sfx, NJ * 512, BF16)
                PLhB = [MB(f"PLh@{s}_{i}") for i in range(NJ)]
                PLq = T(p2, "PLq" + sfx, 8 * 512, BF16)
                PLqB = [MB(f"PLq@{s}_{i}") for i in range(8)]
                Pb = [T(p2, f"Pm{i}" + sfx, 512, BF16) for i in range(2)]
                PbB = [MB(f"Pm@{s}_{i}") for i in range(2)]
                rden = [T(p2, f"rdm{i}" + sfx, 512, F32) for i in range(1)]
                rdenB = [MB(f"rdm@{s}_{i}") for i in range(1)]
                Fsb = T(p2, "Fsb" + sfx, 4 * 1024, F32)
                FsbB = [MB(f"Fsb@{s}_{i}") for i in range(4)]
                sg = [T(p2, f"sg{i}" + sfx, 512, F32) for i in range(2)]
                sgB = [MB(f"sg@{s}_{i}") for i in range(2)]
                tmpo = T(p2, "tmpo" + sfx, 1024, F32)
                tmpoB = MB(f"tmpo@{s}")
                xs = [T(p2, f"xs2{i}" + sfx, 1024, BF16) for i in range(2)]
                xsB = [MB(f"xs2@{s}_{i}") for i in range(2)]
                junk = T(p2, "junk2" + sfx, 1024, BF16)
                junkB = MB(f"junk2@{s}")

                class Ring:
                    def __init__(self, tensor, nslots, uses, name):
                        self.t, self.n, self.uses = tensor, nslots, uses
                        self.B = [MB(f"{name}@{s}_{i}") for i in range(nslots)]
                        self.k = self.issued = self.released = 0

                    def pump(self):
                        while self.issued < len(self.uses) and self.issued < self.released + self.n:
                            i = self.issued
                            r = i % self.n
                            sch.dma("sp", self.t[:, r * 4096:(r + 1) * 4096], scr_d[self.uses[i]], R=[scrB], W=[self.B[r]])
                            self.issued += 1

                    def acquire(self, expect):
                        k = self.k
                        self.k += 1
                        assert k < self.issued, "ring: acquire before load issued"
                        assert self.uses[k] == expect, (self.uses[k], expect)
                        r = k % self.n
                        return self.t[:, r * 4096:(r + 1) * 4096], self.B[r]

                    def release(self):
                        self.released += 1
                        self.pump()

                rG = Ring(ringG, NRG, (list(range(6, 17)) + list(range(17, 23))) * NQT, "rG")
                rF = Ring(ringF, NRF, [0, 1, 2, 3, 4, 5] * NQT, "rF")
                rG.pump()
                rF.pump()
                FB = [(PS[:, 4 * 512:5 * 512], bankB[4]), (PS[:, 5 * 512:6 * 512], bankB[5])]
                FPAIR = (PS[:, 4 * 512:6 * 512], [bankB[4], bankB[5]])
                fbs = {"i": 0}

                def fbank():
                    fbs["i"] += 1
                    return FB[fbs["i"] % 2]

                def postres(src, srcB, gi, HtAP, HtBuf):
                    ss, ssB = small()
                    ACT(junk[:, :], src, AF.Square, R=srcB, W=[junkB, ssB], accum_out=ss[:, 0:1])
                    rstd_from_ss(ss, ssB, 1, [1.0 / 1024])
                    TT(tmpo[:, :], src, G[gi][:, :], ALU.mult, R=list(srcB) + [GB[gi], ssB], W=[tmpoB])
                    STT(HtAP, tmpo[:, :], ss[:, 0:1], HtAP, ALU.mult, ALU.add, R=[tmpoB, ssB, HtBuf], W=[HtBuf])

                def front_items(t):
                    p = t % 2
                    H = Ht[p]
                    HB = HtB[p]
                    stt = {}

                    def xload():
                        for tb in range(4):
                            sch.dma("pool", H[:, tb * 1024:(tb + 1) * 1024],
                                    x_d[s, t * 512 + tb * 128: t * 512 + (tb + 1) * 128, :], W=[HB[tb]])

                    def proj(tb, which):
                        if tb == 0:
                            stt["W"] = [rF.acquire(0 + 4 * which), rF.acquire(1 + 4 * which)]
                        pair, pB = FPAIR
                        for fh in range(2):
                            W_, WB = stt["W"][fh]
                            for kc in range(8):
                                if which == 0:
                                    lhs = yT[:, kc * S + t * 512 + tb * 128: kc * S + t * 512 + (tb + 1) * 128]
                                    rr = [yTB[t], WB]
                                else:
                                    lhs = PLq[:, kc * 512 + tb * 128: kc * 512 + (tb + 1) * 128]
                                    rr = [PLqB[kc], WB]
                                MM(pair[:, fh * 512:(fh + 1) * 512], lhs, W_[:, kc * 512:(kc + 1) * 512],
                                   kc == 0, kc == 7, R=rr, W=[pB[fh]])
                        if tb == 3:
                            rF.release()
                            rF.release()
                        hap = H[:, tb * 1024:(tb + 1) * 1024]
                        postres(pair, pB, which, hap, HB[tb])
                        ss, ssB = nt_a(hap, [HB[tb]], 1024, junk, junkB)
                        ACT(xs[tb % 2][:, :], hap, AF.Copy, R=[HB[tb], ssB], W=[xsB[tb % 2]], scale=ss[:, 0:1])

                    def trn(tb, which):
                        x_, xB_ = xs[tb % 2], xsB[tb % 2]
                        tp, tB = ntp()
                        for k in range(8):
                            TR(tp[:, k * 128:(k + 1) * 128], x_[:, k * 128:(k + 1) * 128], R=[xB_], W=[tB], last=(k == 7))
                        gc = (pv[:, GPMEM:GPMEM + 8] if which == 0 else pv[:, GPF:GPF + 8])
                        g3 = gc.unsqueeze(2).broadcast_to([128, 8, 128])
                        dst3 = (xnF3 if which == 0 else xnG3)[:, :, tb * 128:(tb + 1) * 128]
                        dB = (xnFB if which == 0 else xnGB)[tb]
                        TT(dst3, tp[:, 0:1024].rearrange("p (k t) -> p k t", k=8), g3, ALU.mult, R=[tB, pvB], W=[dB])

                    def mq(c):
                        if c == 0:
                            stt["Q"] = [rF.acquire(2), rF.acquire(3)]
                        W_, WB = stt["Q"][c // 4]
                        cc = c % 4
                        bank, bB = fbank()
                        for kc in range(8):
                            MM(bank, W_[:, cc * 1024 + kc * 128: cc * 1024 + (kc + 1) * 128],
                               xnF[:, kc * 512:(kc + 1) * 512], kc == 0, kc == 7, R=[WB] + xnFB, W=[bB])
                        COPY(PLq[:, c * 512:(c + 1) * 512], bank, R=[bB], W=[PLqB[c]])
                        if c == 7:
                            rF.release()
                            rF.release()

                    def attS(h):
                        Ps = []
                        for mc in range(2):
                            bank, bB = fbank()
                            for dc in range(2):
                                c = h * 2 + dc
                                MM(bank, mkT[:, c * 256 + mc * 128: c * 256 + (mc + 1) * 128],
                                   PLq[:, c * 512:(c + 1) * 512], dc == 0, dc == 1, R=[mkB, PLqB[c]], W=[bB])
                            P, PB = Pb[mc], PbB[mc]
                            ACT(P[:, :], bank, AF.Exp, R=[bB], W=[PB], scale=mem_scale)
                            Ps.append((P, PB))
                        stt["Ps"] = Ps

                    def attPV(h):
                        Ps = stt["Ps"]
                        dn, dnB = fbank()
                        for mc in range(2):
                            MM(dn, onesb[:, :], Ps[mc][0][:, :], mc == 0, mc == 1, R=[constB, Ps[mc][1]], W=[dnB])
                        rd, rdB = rden[0], rdenB[0]
                        RECIP(rd[:, :], dn, R=[dnB], W=[rdB])
                        for dvc in range(2):
                            c = h * 2 + dvc
                            bank, bB = fbank()
                            for mc in range(2):
                                MM(bank, mv[:, mc * 1024 + c * 128: mc * 1024 + (c + 1) * 128], Ps[mc][0][:, :],
                                   mc == 0, mc == 1, R=[mvB, Ps[mc][1]], W=[bB])
                            TT(PLq[:, c * 512:(c + 1) * 512], bank, rd[:, :], ALU.mult, R=[bB, rdB], W=[PLqB[c]])

                    A = [xload]
                    A += [lambda: proj(0, 0), lambda: proj(1, 0), lambda: trn(0, 0), lambda: proj(2, 0),
                          lambda: trn(1, 0), lambda: proj(3, 0), lambda: trn(2, 0), lambda: trn(3, 0)]
                    A += [(lambda c=c: mq(c)) for c in range(8)]
                    A += [lambda: attS(0)]
                    for h in range(4):
                        A += [(lambda h=h: attPV(h))]
                        if h + 1 < 4:
                            A += [(lambda h=h: attS(h + 1))]
                    Bq = [lambda: proj(0, 1), lambda: proj(1, 1), lambda: trn(0, 1), lambda: proj(2, 1),
                          lambda: trn(1, 1), lambda: proj(3, 1), lambda: trn(2, 1), lambda: trn(3, 1)]
                    return A, Bq

                def ffn_items(t):
                    p = t % 2
                    H = Ht[p]
                    HB = HtB[p]

                    def gu(jj):
                        W_, WB = rG.acquire(6 + jj)
                        for jl in range(2):
                            j = jj * 2 + jl
                            b0 = 0 if jl == 0 else 2
                            bg, bgB = PS[:, b0 * 512:(b0 + 1) * 512], bankB[b0]
                            bu, buB = PS[:, (b0 + 1) * 512:(b0 + 2) * 512], bankB[b0 + 1]
                            for kc in range(8):
                                MM(bg, W_[:, (jl * 2 + 0) * 1024 + kc * 128: (jl * 2 + 0) * 1024 + (kc + 1) * 128],
                                   xnG[:, kc * 512:(kc + 1) * 512], kc == 0, kc == 7, R=[WB] + xnGB, W=[bgB])
                            for kc in range(8):
                                MM(bu, W_[:, (jl * 2 + 1) * 1024 + kc * 128: (jl * 2 + 1) * 1024 + (kc + 1) * 128],
                                   xnG[:, kc * 512:(kc + 1) * 512], kc == 0, kc == 7, R=[WB] + xnGB, W=[buB])
                            g_, gB_ = sg[j % 2], sgB[j % 2]
                            ACT(g_[:, :], bg, AF.Silu, R=[bgB], W=[gB_])
                            TT(PLh[:, j * 512:(j + 1) * 512], bu, g_[:, :], ALU.mult, R=[buB, gB_], W=[PLhB[j]])
                        rG.release()

                    def down(fh, sl3):
                        W_, WB = rG.acquire(17 + fh * 3 + sl3)
                        nj = 8 if sl3 < 2 else 6
                        for jl in range(nj):
                            j = sl3 * 8 + jl
                            for tb in range(4):
                                last = (jl == nj - 1 and tb == 3)
                                MM(PS[:, tb * 512:(tb + 1) * 512], PLh[:, j * 512 + tb * 128: j * 512 + (tb + 1) * 128],
                                   W_[:, jl * 512:(jl + 1) * 512], j == 0, j == NJ - 1, R=[PLhB[j], WB], W=[bankB[tb]],
                                   inc=(True if last else None))
                        rG.release()

                    def evac(fh):
                        for tb in range(4):
                            COPY(Fsb[:, tb * 1024 + fh * 512: tb * 1024 + (fh + 1) * 512], PS[:, tb * 512:(tb + 1) * 512],
                                 R=[bankB[tb]], W=[FsbB[tb]])

                    def fin():
                        for tb in range(4):
                            hap = H[:, tb * 1024:(tb + 1) * 1024]
                            postres(Fsb[:, tb * 1024:(tb + 1) * 1024], [FsbB[tb]], 2, hap, HB[tb])
                            sch.dma("pool", out_d[s, t * 512 + tb * 128: t * 512 + (tb + 1) * 128, :], hap, R=[HB[tb]])

                    GU = [(lambda jj=jj: gu(jj)) for jj in range(11)]
                    DN = []
                    for fh in range(2):
                        for sl3 in range(3):
                            DN.append(lambda fh=fh, sl3=sl3: down(fh, sl3))
                        DN.append(lambda fh=fh: evac(fh))
                    DN.append(fin)
                    return GU, DN

                def interleave(Xs, Ys):
                    nx, ny = len(Xs), len(Ys)
                    yi = 0
                    for i, xf in enumerate(Xs):
                        xf()
                        tgt = ((i + 1) * ny + nx - 1) // nx if nx else ny
                        while yi < min(tgt, ny):
                            Ys[yi]()
                            yi += 1
                    while yi < ny:
                        Ys[yi]()
                        yi += 1

                A0, B0 = front_items(0)
                for f in A0 + B0:
                    f()
                for tt in range(NQT):
                    GU, DN = ffn_items(tt)
                    if tt + 1 < NQT:
                        A1, B1 = front_items(tt + 1)
                    else:
                        A1, B1 = [], []
                    interleave(GU, A1)
                    interleave(DN, B1)
                sch.barrier(bar[:, 0:1])
        sch.final_wait()

    with nc.Block() as block:
        @block.sync
        def _(sync):
            body()
    es.close()
    return nc, sch


def _kc(W):
    K, N = W.shape
    return np.ascontiguousarray(W.reshape(K // 128, 128, N).transpose(1, 0, 2)).reshape(128, -1)


def _cols(v):
    return np.ascontiguousarray(v.reshape(-1, 128).T)


def pack_shared(inp):
    f = np.float32
    w_in = inp["w_in"][0]
    winA = _kc(w_in[:, 0:1024])
    winB = _kc(np.concatenate([w_in[:, 1024:1408], w_in[:, 1664:1728], w_in[:, 1664:1728], w_in[:, 1408:1664]], axis=1))
    w_uq = inp["w_uq"][0].reshape(384, 4, 192)
    wuq = _kc(np.concatenate([w_uq[:, :, 0:128].reshape(384, 512), w_uq[:, :, 128:192].reshape(384, 256)], axis=1))
    w_ukv = inp["w_ukv"][0].reshape(256, 4, 256)
    wukv = _kc(np.concatenate([w_ukv[:, :, 0:128].reshape(256, 512), w_ukv[:, :, 128:256].reshape(256, 512)], axis=1))
    wlru = np.zeros((128, 4, 2, 128), f)
    for c in range(4):
        for g, key in enumerate(("lru_wa", "lru_wx")):
            for hh in range(2):
                wlru[hh * 64:(hh + 1) * 64, c, g, hh * 64:(hh + 1) * 64] = inp[key][0][2 * c + hh]
    wlru = wlru.reshape(128, -1)
    slots = np.zeros((NSLOT, 128, 4096), f)

    def moving(W, fh):
        return _kc(W[:, fh * 512:(fh + 1) * 512])

    def stationary(W, sl):
        Wr = W.reshape(8, 128, 8, 128)[:, :, sl * 4:(sl + 1) * 4, :]
        return np.ascontiguousarray(Wr.transpose(1, 2, 0, 3)).reshape(128, -1)

    for fh in range(2):
        slots[0 + fh] = moving(inp["w_out"][0], fh)
        slots[2 + fh] = stationary(inp["w_mq"][0], fh)
        slots[4 + fh] = moving(inp["w_mo"][0], fh)
        slots[SLOT_MK + fh] = stationary(inp["w_mk"][0], fh)
        slots[SLOT_MV + fh] = moving(inp["w_mv"][0], fh)
    wg = inp["w_gate"][0].reshape(8, 128, NJ, 128)
    wu = inp["w_up"][0].reshape(8, 128, NJ, 128)
    for jj in range(11):
        blk = np.zeros((128, 2, 2, 8, 128), f)
        for jl in range(2):
            j = jj * 2 + jl
            blk[:, jl, 0] = wg[:, :, j, :].transpose(1, 0, 2)
            blk[:, jl, 1] = wu[:, :, j, :].transpose(1, 0, 2)
        slots[6 + jj] = blk.reshape(128, -1)
    wd = inp["w_down"][0].reshape(NJ, 128, 1024)
    for fh in range(2):
        for sl3 in range(3):
            nj = 8 if sl3 < 2 else 6
            blk = np.zeros((128, 8, 512), f)
            blk[:, 0:nj, :] = wd[sl3 * 8: sl3 * 8 + nj, :, fh * 512:(fh + 1) * 512].transpose(1, 0, 2)
            slots[17 + fh * 3 + sl3] = blk.reshape(128, -1)
    pv = np.zeros((128, NPV), f)
    pv[:, GPM:GPM + 8] = _cols(inp["g_pre_mix"][0])
    for k in range(4):
        pv[:, CW + k * 4: CW + k * 4 + 4] = _cols(inp["conv_w"][0][k])
    pv[:, CB:CB + 4] = _cols(inp["conv_b"][0])
    pv[:, BA:BA + 4] = _cols(inp["lru_ba"][0])
    pv[:, BX:BX + 4] = _cols(inp["lru_bx"][0])
    pv[:, LAM:LAM + 4] = _cols(inp["lru_lambda"][0])
    pv[:, GQ:GQ + 3] = _cols(inp["g_q_lat"][0])
    pv[:, GKV:GKV + 2] = _cols(inp["g_kv_lat"][0])
    pv[:, GLO:GLO + 4] = _cols(inp["g_lru_out"][0])
    pv[:, GMO:GMO + 4] = _cols(inp["g_mla_out"][0])
    pv[:, GPMEM:GPMEM + 8] = _cols(inp["g_pre_mem"][0])
    pv[:, GMKV:GMKV + 8] = _cols(inp["g_mem_kv"][0])
    pv[:, GPF:GPF + 8] = _cols(inp["g_pre_ffn"][0])
    grow = np.stack([np.broadcast_to(inp[k][0][None, :], (128, 1024)) for k in
                     ("g_post_mix", "g_post_mem", "g_post_ffn")]).astype(f)
    invf = (10000.0 ** (-np.arange(0, 64, 2, dtype=np.float32) / 64)).astype(f)
    invf = np.ascontiguousarray(np.broadcast_to(invf[None, :], (128, 32)))
    ident = np.eye(128, dtype=f)
    tri = (np.arange(128)[None, :] >= np.arange(128)[:, None]).astype(f)
    return dict(pv=pv, grow=np.ascontiguousarray(grow), invf=invf, ident=ident, tri=tri, winA=winA, winB=winB,
                wuq=wuq, wukv=wukv, wlru=wlru, wslots=slots)


def pack_core(inp, b0, nseq, S):
    x = np.ascontiguousarray(inp["x"][b0:b0 + nseq], dtype=np.float32)
    mem = np.ascontiguousarray(inp["mem"][b0:b0 + nseq], dtype=np.float32)
    pos = np.asarray(inp["positions"][b0:b0 + nseq], dtype=np.int32)
    pos = np.ascontiguousarray(pos.reshape(nseq, S // 128, 128).transpose(2, 0, 1).reshape(128, -1))
    return dict(x=x, mem=mem, pos=pos)


_CACHE = {}


def kernel(**inputs):
    inp = {k: np.asarray(v) for k, v in inputs.items()}
    B, S, _ = inp["x"].shape
    nseq = B // NCORES
    key = (S, nseq)
    if key not in _CACHE:
        _CACHE[key] = build(S, nseq)[0]
    nc = _CACHE[key]
    shared = pack_shared(inp)
    in_maps = []
    for c in range(NCORES):
        m = dict(shared)
        m.update(pack_core(inp, c * nseq, nseq, S))
        in_maps.append(m)
    res = run_bass_kernel_spmd(nc, in_maps, core_ids=list(range(NCORES)))
    out = np.concatenate([np.asarray(r["out"]) for r in res.results], axis=0)
    return out.astype(np.float32)
```
